# Optimizing a Trainium2 kernel written in Bass

```python
import math
import jax
import jax.numpy as jnp
from jax import lax
import numpy as np

D_MODEL = 2048
BATCH = 4
SEQ = 2048
DEPTH = 1
DEC_BATCH = 128
DEC_SEQ = 8
PAST_LEN = 16384
PAGE_SIZE = 128

W_A = D_MODEL // 2
GROUP_A = 16
N_GROUPS_A = W_A // GROUP_A
P_STATE = 64
W_B = D_MODEL // 2
N_HEADS_B = 4
DK_B = W_B // 2 // N_HEADS_B
DV_B = W_B // N_HEADS_B
GATE_RANK = 16
GATE_TAU = 16.0
CHUNK = 16
EPS = 1e-6

IN_SPLITS = (W_A, W_A, N_HEADS_B * DK_B, N_HEADS_B * DK_B, W_B, W_B, GATE_RANK, D_MODEL, D_MODEL)
SPLIT_POINTS = tuple(int(s) for s in np.cumsum(IN_SPLITS)[:-1])
IN_COLS = int(sum(IN_SPLITS))

kernel_name = "hybrid_s5_gla_adaln_step"


def rms_norm(x, gain):
    xf = x.astype(jnp.float32)
    y = xf * lax.rsqrt(jnp.mean(xf * xf, axis=-1, keepdims=True) + EPS)
    return (y * gain.astype(jnp.float32)).astype(x.dtype)


def s5_discretise(lambda_re, lambda_im, log_dt):
    f32 = jnp.float32
    dt = jnp.exp(log_dt.astype(f32))[:, None]
    lr, li = lambda_re.astype(f32), lambda_im.astype(f32)
    mag = jnp.exp(lr * dt)
    ab_re, ab_im = mag * jnp.cos(li * dt), mag * jnp.sin(li * dt)
    nr, ni = ab_re - 1.0, ab_im
    den = lr * lr + li * li
    cf_re = (nr * lr + ni * li) / den
    cf_im = (ni * lr - nr * li) / den
    return ab_re, ab_im, cf_re, cf_im


def s5_branch(u, h0_re, h0_im, lambda_re, lambda_im, log_dt, b_re, b_im, c_re, c_im, d_skip, w_glu, b_glu):
    f32 = jnp.float32
    Bn, L, _ = u.shape
    uf = u.astype(f32)
    ab_re, ab_im, cf_re, cf_im = s5_discretise(lambda_re, lambda_im, log_dt)
    ug = uf.reshape(Bn, L, N_GROUPS_A, GROUP_A)
    bu_re = jnp.einsum('blgh,gph->blgp', ug, b_re.astype(f32))
    bu_im = jnp.einsum('blgh,gph->blgp', ug, b_im.astype(f32))
    xr = cf_re * bu_re - cf_im * bu_im
    xi = cf_re * bu_im + cf_im * bu_re
    h0r, h0i = h0_re.astype(f32), h0_im.astype(f32)
    xr = xr.at[:, 0].add(ab_re * h0r - ab_im * h0i)
    xi = xi.at[:, 0].add(ab_re * h0i + ab_im * h0r)
    ar = jnp.broadcast_to(ab_re, xr.shape)
    ai = jnp.broadcast_to(ab_im, xr.shape)

    def combine(e1, e2):
        a1r, a1i, b1r, b1i = e1
        a2r, a2i, b2r, b2i = e2
        return (a2r * a1r - a2i * a1i,
                a2r * a1i + a2i * a1r,
                a2r * b1r - a2i * b1i + b2r,
                a2r * b1i + a2i * b1r + b2i)

    _, _, hr, hi = lax.associative_scan(combine, (ar, ai, xr, xi), axis=1)
    y = (jnp.einsum('blgp,ghp->blgh', hr, c_re.astype(f32))
         - jnp.einsum('blgp,ghp->blgh', hi, c_im.astype(f32))).reshape(Bn, L, W_A)
    y = y + d_skip.astype(f32) * uf
    g = jax.nn.gelu(y)
    out = g * jax.nn.sigmoid(g @ w_glu.astype(f32) + b_glu.astype(f32))
    return out.astype(u.dtype), hr[:, -1], hi[:, -1]


def gla_branch(q, k, v, log_a, s0, head_gain):
    f32 = jnp.float32
    Bn, L = q.shape[0], q.shape[1]
    pad = (-L) % CHUNK

    def prep(t):
        t = jnp.pad(t.astype(f32), ((0, 0), (0, pad), (0, 0), (0, 0)))
        return t.reshape(Bn, (L + pad) // CHUNK, CHUNK, t.shape[2], t.shape[3])

    qc, kc, vc, gc = prep(q), prep(k), prep(v), prep(log_a)
    b = jnp.cumsum(gc, axis=2)
    b_end = b[:, :, -1]
    causal = jnp.tril(jnp.ones((CHUNK, CHUNK), dtype=bool))[None, None, :, :, None, None]
    diff = b[:, :, :, None] - b[:, :, None, :]
    decay = jnp.where(causal, jnp.exp(jnp.where(causal, diff, 0.0)), 0.0)
    scores = jnp.einsum('bntshk,bnthk,bnshk->bntsh', decay, qc, kc)
    o_intra = jnp.einsum('bntsh,bnshv->bnthv', scores, vc)
    q_dec = qc * jnp.exp(b)
    k_dec = kc * jnp.exp(b_end[:, :, None] - b)
    d_end = jnp.exp(b_end)

    def step(S, inp):
        qd, kd, vv, de = inp
        o = jnp.einsum('bchk,bhkv->bchv', qd, S)
        S = de[..., None] * S + jnp.einsum('bchk,bchv->bhkv', kd, vv)
        return S, o

    xs = (jnp.moveaxis(q_dec, 1, 0), jnp.moveaxis(k_dec, 1, 0), jnp.moveaxis(vc, 1, 0), jnp.moveaxis(d_end, 1, 0))
    s_fin, o_inter = lax.scan(step, s0.astype(f32), xs)
    o = (jnp.moveaxis(o_inter, 0, 1) + o_intra).reshape(Bn, -1, N_HEADS_B, DV_B)[:, :L]
    o = o * lax.rsqrt(jnp.mean(o * o, axis=-1, keepdims=True) + EPS)
    o = o.reshape(Bn, L, N_HEADS_B * DV_B) * head_gain.astype(f32)
    return o, s_fin


def hybrid_layer(x, c, s0_re, s0_im, s0_gla, lw):
    (w_ada, b_ada, norm_gain, w_in, lambda_re, lambda_im, log_dt, ssm_b_re, ssm_b_im,
     ssm_c_re, ssm_c_im, d_skip, w_glu, b_glu, w_gate_up, b_gate, gla_norm_gain,
     w_a_out, w_b_out, w_out) = lw
    Bn, L, _ = x.shape
    mod = jax.nn.silu(c) @ w_ada + b_ada
    shift, scale, gate = jnp.split(mod, 3, axis=-1)
    h = rms_norm(x, norm_gain) * (1 + scale[:, None]) + shift[:, None]
    u_a, z_a, q, k, v, z_b, g_lr, m_a, m_b = jnp.split(h @ w_in, SPLIT_POINTS, axis=-1)
    a_y, sa_re, sa_im = s5_branch(u_a, s0_re, s0_im, lambda_re, lambda_im, log_dt, ssm_b_re, ssm_b_im,
                                  ssm_c_re, ssm_c_im, d_skip, w_glu, b_glu)
    a_out = (a_y * jax.nn.silu(z_a)) @ w_a_out
    log_a = jax.nn.log_sigmoid((g_lr @ w_gate_up + b_gate).astype(jnp.float32)) / GATE_TAU
    b_y, s_gla = gla_branch(q.reshape(Bn, L, N_HEADS_B, DK_B) * DK_B ** -0.5,
                            k.reshape(Bn, L, N_HEADS_B, DK_B),
                            v.reshape(Bn, L, N_HEADS_B, DV_B),
                            log_a.reshape(Bn, L, N_HEADS_B, DK_B), s0_gla, gla_norm_gain)
    b_out = (b_y.astype(x.dtype) * jax.nn.silu(z_b)) @ w_b_out
    merged = jax.nn.sigmoid(m_a) * a_out + jax.nn.sigmoid(m_b) * b_out
    y = x + gate[:, None] * (merged @ w_out)
    return y, sa_re, sa_im, s_gla


def setup_inputs(seed: int = 0) -> dict:
    key = jax.random.key(seed)
    ks = jax.random.split(key, 32)
    f32 = jnp.float32

    def nrm(k, shape, scale):
        return jax.random.normal(k, shape, f32) * scale

    G, P, H = N_GROUPS_A, P_STATE, N_HEADS_B
    n = jnp.arange(P, dtype=f32)
    return {
        'x_prompt': nrm(ks[0], (BATCH, SEQ, D_MODEL), 1.0),
        'x_sample': nrm(ks[1], (DEC_BATCH, DEC_SEQ, D_MODEL), 1.0),
        'c_prompt': nrm(ks[2], (BATCH, D_MODEL), 1.0),
        'c_sample': nrm(ks[3], (DEC_BATCH, D_MODEL), 1.0),
        'state_ssm_re': nrm(ks[4], (DEPTH, DEC_BATCH, G, P), 0.5),
        'state_ssm_im': nrm(ks[5], (DEPTH, DEC_BATCH, G, P), 0.5),
        'state_gla': nrm(ks[6], (DEPTH, DEC_BATCH, H, DK_B, DV_B), 1.0),
        'w_ada': nrm(ks[7], (DEPTH, D_MODEL, 3 * D_MODEL), D_MODEL ** -0.5),
        'b_ada': nrm(ks[8], (DEPTH, 3 * D_MODEL), 0.02),
        'norm_gain': 1.0 + nrm(ks[9], (DEPTH, D_MODEL), 0.02),
        'w_in': nrm(ks[10], (DEPTH, D_MODEL, IN_COLS), D_MODEL ** -0.5),
        'lambda_re': -0.5 + nrm(ks[11], (DEPTH, G, P), 0.01),
        'lambda_im': math.pi * n + nrm(ks[12], (DEPTH, G, P), 0.01),
        'log_dt': jax.random.uniform(ks[13], (DEPTH, G), f32, math.log(1e-3), math.log(1e-1)),
        'ssm_b_re': nrm(ks[14], (DEPTH, G, P, GROUP_A), (2 * GROUP_A) ** -0.5),
        'ssm_b_im': nrm(ks[15], (DEPTH, G, P, GROUP_A), (2 * GROUP_A) ** -0.5),
        'ssm_c_re': nrm(ks[16], (DEPTH, G, GROUP_A, P), (2 * P) ** -0.5),
        'ssm_c_im': nrm(ks[17], (DEPTH, G, GROUP_A, P), (2 * P) ** -0.5),
        'd_skip': nrm(ks[18], (DEPTH, W_A), 1.0),
        'w_glu': nrm(ks[19], (DEPTH, W_A, W_A), W_A ** -0.5),
        'b_glu': nrm(ks[20], (DEPTH, W_A), 0.02),
        'w_gate_up': nrm(ks[21], (DEPTH, GATE_RANK, H * DK_B), GATE_RANK ** -0.5),
        'b_gate': nrm(ks[22], (DEPTH, H * DK_B), 0.1),
        'gla_norm_gain': 1.0 + nrm(ks[23], (DEPTH, W_B), 0.02),
        'w_a_out': nrm(ks[24], (DEPTH, W_A, D_MODEL), W_A ** -0.5),
        'w_b_out': nrm(ks[25], (DEPTH, W_B, D_MODEL), W_B ** -0.5),
        'w_out': nrm(ks[26], (DEPTH, D_MODEL, D_MODEL), D_MODEL ** -0.5),
        'final_norm_gain': 1.0 + nrm(ks[27], (D_MODEL,), 0.02),
    }


def reference(x_prompt, x_sample, c_prompt, c_sample, state_ssm_re, state_ssm_im, state_gla,
              w_ada, b_ada, norm_gain, w_in, lambda_re, lambda_im, log_dt, ssm_b_re, ssm_b_im,
              ssm_c_re, ssm_c_im, d_skip, w_glu, b_glu, w_gate_up, b_gate, gla_norm_gain,
              w_a_out, w_b_out, w_out, final_norm_gain):
    bp = x_prompt.shape[0]
    z_ssm = jnp.zeros((bp, N_GROUPS_A, P_STATE), state_ssm_re.dtype)
    z_gla = jnp.zeros((bp, N_HEADS_B, DK_B, DV_B), state_gla.dtype)
    hp, hs = x_prompt, x_sample
    p_re, p_im, p_gla, s_re, s_im, s_gla = [], [], [], [], [], []
    for l in range(DEPTH):
        lw = (w_ada[l], b_ada[l], norm_gain[l], w_in[l], lambda_re[l], lambda_im[l], log_dt[l],
              ssm_b_re[l], ssm_b_im[l], ssm_c_re[l], ssm_c_im[l], d_skip[l], w_glu[l], b_glu[l],
              w_gate_up[l], b_gate[l], gla_norm_gain[l], w_a_out[l], w_b_out[l], w_out[l])
        hp, r1, i1, g1 = hybrid_layer(hp, c_prompt, z_ssm, z_ssm, z_gla, lw)
        hs, r2, i2, g2 = hybrid_layer(hs, c_sample, state_ssm_re[l], state_ssm_im[l], state_gla[l], lw)
        p_re.append(r1); p_im.append(i1); p_gla.append(g1)
        s_re.append(r2); s_im.append(i2); s_gla.append(g2)
    y_prompt = rms_norm(hp, final_norm_gain)
    y_sample = rms_norm(hs, final_norm_gain)
    sd, gd = state_ssm_re.dtype, state_gla.dtype
    new_ssm_re_prompt = jnp.stack(p_re).astype(sd)
    new_ssm_im_prompt = jnp.stack(p_im).astype(sd)
    new_gla_prompt = jnp.stack(p_gla).astype(gd)
    new_ssm_re_sample = jnp.stack(s_re).astype(sd)
    new_ssm_im_sample = jnp.stack(s_im).astype(sd)
    new_gla_sample = jnp.stack(s_gla).astype(gd)
    return (y_prompt, y_sample, new_ssm_re_prompt, new_ssm_im_prompt, new_gla_prompt,
            new_ssm_re_sample, new_ssm_im_sample, new_gla_sample)
```

```python
import math
from contextlib import ExitStack

import numpy as np
import concourse.bass as bass
import concourse.mybir as mybir
from concourse.bass_utils import run_bass_kernel_spmd

F32 = mybir.dt.float32
BF16 = mybir.dt.bfloat16
AF = mybir.ActivationFunctionType
ALU = mybir.AluOpType

D = 2048
KT = 16
WA = 1024
NG = 64
NH = 4
DK = 128
DV = 256
EPS = 1e-6
INC = 9232
NSEQ = 16
TP = 1024
TM = TP + 128
WC = 256
NSLOT = 3
PI = math.pi


class Ev:
    __slots__ = ("sem", "val", "eng")

    def __init__(self, sem, val, eng):
        self.sem, self.val, self.eng = sem, val, eng


class Buf:
    def __init__(self, name):
        self.name = name
        self.w = None
        self.r = {}
        self.dsem = None
        self.dcnt = 0


class K:
    def __init__(self, nc, es):
        self.nc, self.es = nc, es
        self.E = {"pe": nc.tensor, "dve": nc.vector, "act": nc.scalar, "pool": nc.gpsimd, "sp": nc.sync}
        self.sem = {e: es.enter_context(nc.semaphore("s_" + e)) for e in ("pe", "dve", "act", "pool")}
        self.cnt = {e: 0 for e in self.sem}
        self.seen = {e: {} for e in self.E}
        self.nbuf = 0
        self.esR = es
        self.es_sem = es

    def sb(self, name, shape, dt, side="left"):
        st = self.es if side == "left" else self.esR
        self.nname = getattr(self, "nname", 0) + 1
        return st.enter_context(self.nc.sbuf_tensor("sb%d_%s" % (self.nname, name), shape, dt, side=side))

    def _deps(self, reads, writes):
        evs = []
        for b in reads:
            if b.w is not None:
                evs.append(b.w)
        for b in writes:
            if b.w is not None:
                evs.append(b.w)
            evs.extend(b.r.values())
        return evs

    def _waits(self, e, evs):
        best = {}
        for ev in evs:
            if ev.eng == "pe" and e == "pe":
                continue
            kk = id(ev.sem)
            if kk not in best or best[kk].val < ev.val:
                best[kk] = ev
        for kk, ev in best.items():
            if self.seen[e].get(kk, 0) >= ev.val:
                continue
            self.E[e].wait_ge(ev.sem, ev.val)
            self.seen[e][kk] = ev.val

    def _commit(self, ev, reads, writes):
        for b in writes:
            b.w = ev
            b.r = {}
        for b in reads:
            if b not in writes:
                kk = id(ev.sem)
                if kk not in b.r or b.r[kk].val < ev.val:
                    b.r[kk] = ev

    def op(self, e, fn, reads=(), writes=()):
        self._waits(e, self._deps(reads, writes))
        ins = fn(self.E[e])
        self.cnt[e] += 1
        ins.then_inc(self.sem[e], 1)
        ev = Ev(self.sem[e], self.cnt[e], e)
        self._commit(ev, reads, writes)
        return ev

    def dma(self, q, out, in_, reads=(), writes=(), sbuf=None, **kw):
        self._waits(q, self._deps(reads, writes))
        tb = sbuf if sbuf is not None else (writes[0] if writes else reads[0])
        if tb.dsem is None:
            self.nbuf += 1
            tb.dsem = self.es_sem.enter_context(self.nc.semaphore("d%d" % self.nbuf))
        tb.dcnt += 16
        self.E[q].dma_start(out=out, in_=in_, **kw).then_inc(tb.dsem, 16)
        ev = Ev(tb.dsem, tb.dcnt, "dma")
        self._commit(ev, reads, writes)
        return ev

    def wait_all(self, e, bufs):
        evs = []
        for b in bufs:
            if b.w is not None:
                evs.append(b.w)
            evs.extend(b.r.values())
        self._waits(e, evs)


class _Stop(Exception):
    pass


def build_nc(dbg=None, stop=None):
    nc = bass.Bass("TRN2", target_bir_lowering=False)
    dumps = []

    def din(name, shape):
        return nc.dram_tensor(name, list(shape), F32, kind="ExternalInput").ap()

    def dout(name, shape):
        return nc.dram_tensor(name, list(shape), F32, kind="ExternalOutput").ap()

    xpre = din("xpre", [TP, D]); xmain = din("xmain", [TP, D]); xsmp = din("xsmp", [128, D])
    c17 = din("c17", [17, D]); flag_d = din("flag", [128, 1])
    ssm_re_in = din("ssm_re_in", [NSEQ, NG, 64]); ssm_im_in = din("ssm_im_in", [NSEQ, NG, 64])
    gla_in = din("gla_in", [NSEQ, NH, DK, DV])
    w_ada = din("w_ada", [D, 3 * D]); b_ada = din("b_ada", [3 * D]); norm_gain = din("norm_gain", [D])
    w_in = din("w_in", [D, INC])
    lam_re = din("lambda_re", [NG, 64]); lam_im = din("lambda_im", [NG, 64]); log_dt = din("log_dt", [NG])
    b_re = din("ssm_b_re", [NG, 64, 16]); b_im = din("ssm_b_im", [NG, 64, 16])
    c_re = din("ssm_c_re", [NG, 16, 64]); c_im = din("ssm_c_im", [NG, 16, 64])
    d_skip = din("d_skip", [WA]); w_glu = din("w_glu", [WA, WA]); b_glu = din("b_glu", [WA])
    w_gate_up = din("w_gate_up", [16, NH * DK]); b_gate = din("b_gate", [NH * DK])
    gla_gain = din("gla_norm_gain", [WA])
    w_a_out = din("w_a_out", [WA, D]); w_b_out = din("w_b_out", [WA, D]); w_out = din("w_out", [D, D])
    fgain = din("final_norm_gain", [D])

    y_main = dout("y_main", [TP, D]); y_smp = dout("y_smp", [128, D])
    ssm_re_p = dout("ssm_re_p", [NG, 64]); ssm_im_p = dout("ssm_im_p", [NG, 64])
    gla_p = dout("gla_p", [NH, DK, DV])
    ssm_re_s = dout("ssm_re_s", [NSEQ, NG, 64]); ssm_im_s = dout("ssm_im_s", [NSEQ, NG, 64])
    gla_s = dout("gla_s", [NSEQ, NH, DK, DV])
    gate_scr = nc.dram_tensor("gate_scr", [17, D], F32, kind="Internal").ap()
    es = ExitStack()
    try:
      with es:
        k = K(nc, es)
        sb = k.sb
        Bdump = Buf("dump")

        def dump(name, ap, shape, bufs, dt=F32):
            if not dbg:
                return
            d = nc.dram_tensor("dbg_" + name, list(shape), dt, kind="ExternalOutput").ap()
            k.dma("sp", d, ap, reads=list(bufs), writes=[Bdump])
            dumps.append(name)

        def stage_end(n):
            if stop is not None and n >= stop:
                for e_ in ("pe", "dve", "act", "pool", "sp"):
                    k.wait_all(e_, [Bdump])
                raise _Stop()
        psum = [es.enter_context(nc.psum_tensor("ps%d" % i, [128, 512], F32)) for i in range(8)]
        psb = [Buf("ps%d" % i) for i in range(8)]
        pcur = [0]
        held = set()

        def bank(hold=False):
            i = pcur[0]
            while i in held:
                i = (i + 1) % 8
            pcur[0] = (i + 1) % 8
            if hold:
                held.add(i)
            return psum[i], psb[i]

        def unhold(pbuf):
            held.discard(psb.index(pbuf))

        Bc = Buf("consts")
        identf = sb("identf", [128, 128], F32); identb = sb("identb", [128, 128], BF16)
        onesb = sb("onesb", [128, 128], BF16); onesf = sb("onesf", [128, 128], F32)
        causal = sb("causal", [128, 128], F32); seqmask = sb("seqmask", [128, NSEQ], F32)
        smask = sb("smask", [128, 128], F32); rstmask = sb("rstmask", [128, 128], F32)
        mask3 = sb("mask3", [128, 8], F32)
        ohP = sb("ohP", [17, 128], F32); ohS = sb("ohS", [17, 128], F32)
        P = lambda fn: k.op("pool", fn, writes=[Bc])
        P(lambda e: e.memset(identf[:], 1.0))
        P(lambda e: e.affine_select(out=identf[:], in_=identf[:], compare_op=ALU.is_equal, fill=0.0, base=0,
                                    pattern=[[-1, 128]], channel_multiplier=1))
        P(lambda e: e.tensor_copy(out=identb[:], in_=identf[:]))
        P(lambda e: e.memset(onesb[:], 1.0))
        P(lambda e: e.memset(onesf[:], 1.0))
        P(lambda e: e.memset(causal[:], 1.0))
        P(lambda e: e.affine_select(out=causal[:], in_=causal[:], compare_op=ALU.is_ge, fill=0.0, base=0,
                                    pattern=[[1, 128]], channel_multiplier=-1))
        P(lambda e: e.memset(seqmask[:], 1.0))
        P(lambda e: e.affine_select(out=seqmask[:], in_=seqmask[:], compare_op=ALU.is_ge, fill=0.0, base=0,
                                    pattern=[[-8, NSEQ]], channel_multiplier=1))
        P(lambda e: e.affine_select(out=seqmask[:], in_=seqmask[:], compare_op=ALU.is_ge, fill=0.0, base=7,
                                    pattern=[[8, NSEQ]], channel_multiplier=-1))
        P(lambda e: e.tensor_tensor(out=smask[:].rearrange("p (j r) -> p j r", r=8),
                                    in0=causal[:].rearrange("p (j r) -> p j r", r=8),
                                    in1=seqmask[:].unsqueeze(2).to_broadcast([128, NSEQ, 8]), op=ALU.mult))
        P(lambda e: e.memset(rstmask[:], 1.0))
        P(lambda e: e.affine_select(out=rstmask[:].rearrange("p (j r) -> p j r", r=8),
                                    in_=rstmask[:].rearrange("p (j r) -> p j r", r=8),
                                    compare_op=ALU.is_ge, fill=0.0, base=-1,
                                    pattern=[[0, NSEQ], [1, 8]], channel_multiplier=0))
        P(lambda e: e.memset(mask3[:], 1.0))
        P(lambda e: e.affine_select(out=mask3[:], in_=mask3[:], compare_op=ALU.is_ge, fill=0.0, base=15,
                                    pattern=[[16, 8]], channel_multiplier=-1))
        P(lambda e: e.memset(ohP[:], 0.0))
        P(lambda e: e.memset(ohP[0:1, :], 1.0))
        P(lambda e: e.memset(ohS[:], 1.0))
        P(lambda e: e.affine_select(out=ohS[:], in_=ohS[:], compare_op=ALU.is_ge, fill=0.0, base=8,
                                    pattern=[[1, 128]], channel_multiplier=-8))
        P(lambda e: e.affine_select(out=ohS[:], in_=ohS[:], compare_op=ALU.is_ge, fill=0.0, base=-1,
                                    pattern=[[-1, 128]], channel_multiplier=8))

        hrst = sb("hrst", [128, NH, 128], F32); hrst_s = sb("hrst_s", [128, NH, 128], F32)
        P(lambda e: e.memset(hrst[:], 1.0))
        for hd_ in range(NH):
            P(lambda e, hd_=hd_: e.memset(hrst[:, hd_, 0:1], 0.0))
        P(lambda e: e.tensor_copy(out=hrst_s[:], in_=rstmask[:].unsqueeze(1).to_broadcast([128, NH, 128])))
        Bp = Buf("params")
        bglu_col = sb("bglu_col", [128, 8], F32); bgate_col = sb("bgate_col", [128, 4], F32)
        nbgate = sb("nbgate", [128, 4], F32); ggain_col = sb("ggain_col", [128, 8], F32)
        wup = sb("wup", [17, NH * DK], BF16); flag = sb("flag", [128, 1], F32)
        k.dma("sp", bglu_col[:], b_glu.rearrange("(m p) -> p m", p=128), writes=[Bp], allow_slow_non_contiguous=True)
        k.dma("sp", bgate_col[:], b_gate.rearrange("(m p) -> p m", p=128), writes=[Bp], allow_slow_non_contiguous=True)
        k.dma("sp", ggain_col[:], gla_gain.rearrange("(m p) -> p m", p=128), writes=[Bp], allow_slow_non_contiguous=True)
        k.dma("sp", flag[:], flag_d[:, :], writes=[Bp])
        Bwup = Buf("wup")
        k.dma("pool", wup[0:16, :], w_gate_up[:, :], writes=[Bwup])
        k.dma("pool", wup[16:17, :], b_gate.rearrange("(o n) -> o n", o=1), writes=[Bwup])
        k.op("dve", lambda e: e.tensor_scalar(out=nbgate[:], in0=bgate_col[:], scalar1=-1.0, scalar2=None, op0=ALU.mult),
             reads=[Bp], writes=[Bc])

        wslot = [sb("wslot%d" % i, [128, KT, WC], BF16) for i in range(NSLOT)]
        wbuf = [Buf("wslot%d" % i) for i in range(NSLOT)]
        specs = []

        def wspec(w, c0, ncol, kt):
            specs.append((w, c0, ncol, kt))

        for blk in range(24):
            wspec(w_ada, blk * WC, WC, KT)
        for full in (False, True):
            for i in range(4):
                wspec(w_in, i * WC, WC, KT)
            wspec(w_in, 5120, 16, KT)
            for i in range(2):
                wspec(w_in, 2560 + i * WC, WC, KT)
            for i in range(4):
                wspec(w_in, 3072 + i * WC, WC, KT)
            if full:
                for i in range(2):
                    wspec(w_in, 2048 + i * WC, WC, KT)
                for i in range(4):
                    wspec(w_glu, i * WC, WC, 8)
                for i in range(4):
                    wspec(w_in, 1024 + i * WC, WC, KT)
                for i in range(4):
                    wspec(w_in, 4096 + i * WC, WC, KT)
                for fb in range(8):
                    wspec(w_in, 5136 + fb * WC, WC, KT)
                    wspec(w_a_out, fb * WC, WC, 8)
                    wspec(w_in, 7184 + fb * WC, WC, KT)
                    wspec(w_b_out, fb * WC, WC, 8)
                for i in range(8):
                    wspec(w_out, i * WC, WC, KT)
        wstate = {"issued": 0, "used": 0}

        def w_issue():
            i = wstate["issued"]
            if i >= len(specs):
                return
            w, c0, ncol, kt = specs[i]
            s = i % NSLOT
            src = w.rearrange("(kt p) n -> p kt n", p=128)[:, :, c0:c0 + ncol]
            k.dma("pool", wslot[s][:, 0:kt, 0:ncol], src, writes=[wbuf[s]])
            wstate["issued"] = i + 1

        def w_next(expect=None):
            i = wstate["used"]
            if expect is not None:
                assert specs[i][0] is expect[0] and specs[i][1] == expect[1], (i, specs[i][1:], expect[1:])
            while wstate["issued"] < min(i + NSLOT - 1, len(specs)) or wstate["issued"] <= i:
                w_issue()
            wstate["used"] = i + 1
            s = i % NSLOT
            return wslot[s], wbuf[s]

        shiftT = sb("shiftT", [128, KT, 17], F32); scaleT = sb("scaleT", [128, KT, 17], F32)
        Bmod = Buf("modT")
        hT = sb("hT", [128, KT, TM], BF16)
        hTb = [Buf("hT%d" % j) for j in range(9)]
        Hrun = sb("Hrun", [128, 2, 32], F32); BH = Buf("Hrun")
        Sgl = sb("Sgl", [128, NH, DV], F32); Sglb = sb("Sglb", [128, NH, DV], BF16); BS = Buf("Sgl")
        epsb = sb("epsb", [128, 1], F32)
        k.op("pool", lambda e: e.memset(epsb[:], EPS), writes=[Bc])
        esR = ExitStack()
        k.esR = esR
        esW = ExitStack()
        k.es = esW
        Toep = sb("Toep", [128, NG, 128], BF16)
        Mw = sb("Mw", [128, NG, 2, 64], BF16)
        Ybr = sb("Ybr", [128, 32, 9, 16], BF16); Ybn = sb("Ybn", [128, 32, 9, 16], BF16)
        ARR = sb("ARR", [128, 32], F32); AIp = sb("AIp", [128, 32], F32); AIn = sb("AIn", [128, 32], F32)
        A64 = sb("A64", [128, 3, 32], F32)
        Bs5 = Buf("s5w")

        Bgs = Buf("gate_scr")

        def ada_emit():
            esA = ExitStack()
            k.esR = esA
            c17s = sb("c17s", [17, D], F32, side="right"); Bp2 = Buf("c17s")
            siluT = sb("siluT", [128, KT, 17], BF16, side="right"); Bsl = Buf("silu")
            brow = [sb("brow%d" % i, [1, WC], F32, side="right") for i in range(2)]; Bbr = [Buf("brow%d" % i) for i in range(2)]
            modr = [sb("modr%d" % i, [17, WC], F32, side="right") for i in range(2)]; Bmr = [Buf("modr%d" % i) for i in range(2)]
            k.dma("sp", c17s[:], c17[:, :], writes=[Bp2])
            k.op("act", lambda e: e.activation(out=c17s[:], in_=c17s[:], func=AF.Silu), reads=[Bp2], writes=[Bp2])
            pt, pb = bank()
            for kt in range(KT):
                k.op("pe", lambda e, kt=kt, pt=pt: e.transpose(out=pt[:, kt * 17:(kt + 1) * 17], in_=c17s[0:17, kt * 128:(kt + 1) * 128],
                                                               identity=identf[0:17, 0:17]), reads=[Bp2, Bc], writes=[pb])
            k.op("act", lambda e, pt=pt: e.activation(out=siluT[:].rearrange("p a b -> p (a b)"), in_=pt[:, 0:KT * 17], func=AF.Copy), reads=[pb], writes=[Bsl])
            for blk in range(24):
                a = blk % 2
                ws, wb = w_next((w_ada, blk * WC))
                k.dma("sp", brow[a][:], b_ada[blk * WC:(blk + 1) * WC].rearrange("(o n) -> o n", o=1), writes=[Bbr[a]])
                pt, pb = bank()
                for kt in range(KT):
                    k.op("pe", lambda e, kt=kt, ws=ws, pt=pt: e.matmul(pt[0:17, 0:WC], lhsT=siluT[:, kt, :], rhs=ws[:, kt, :],
                                                                      start=(kt == 0), stop=False), reads=[Bsl, wb], writes=[pb])
                k.op("pe", lambda e, a=a, pt=pt: e.matmul(pt[0:17, 0:WC], lhsT=onesf[0:1, 0:17], rhs=brow[a][0:1, :], start=False, stop=True),
                     reads=[Bbr[a], Bc], writes=[pb])
                which = blk // 8
                if which == 1:
                    k.op("act", lambda e, a=a, pt=pt: e.activation(out=modr[a][:], in_=pt[0:17, 0:WC], func=AF.Identity, bias=onesf[0:17, 0:1]), reads=[pb, Bc], writes=[Bmr[a]])
                else:
                    k.op("act", lambda e, a=a, pt=pt: e.activation(out=modr[a][:], in_=pt[0:17, 0:WC], func=AF.Copy), reads=[pb], writes=[Bmr[a]])
                if which == 2:
                    c0 = (blk - 16) * WC
                    k.dma("sp", gate_scr[:, c0:c0 + WC], modr[a][:], reads=[Bmr[a]], writes=[Bgs], sbuf=Bmr[a])
                else:
                    dst = shiftT if which == 0 else scaleT
                    ft0 = (blk % 8) * 2
                    pt2, pb2 = bank()
                    for hh in range(2):
                        k.op("pe", lambda e, a=a, hh=hh, pt2=pt2: e.transpose(out=pt2[:, hh * 17:(hh + 1) * 17], in_=modr[a][0:17, hh * 128:(hh + 1) * 128],
                                                                             identity=identf[0:17, 0:17]), reads=[Bmr[a], Bc], writes=[pb2])
                    k.op("act", lambda e, dst=dst, ft0=ft0, pt2=pt2: e.activation(out=dst[:, ft0:ft0 + 2, :].rearrange("p a b -> p (a b)"), in_=pt2[:, 0:34], func=AF.Copy),
                         reads=[pb2], writes=[Bmod])
            return esA, [Bp2, Bsl] + Bbr + Bmr

        with ExitStack() as es2:
            k.es = es2
            Bt = Buf("s5tmp")
            L2 = sb("L2", [32, 2, 128], F32)
            Cn = sb("Cn", [128, 2, 4, 128], F32)
            Bpk = sb("Bpk", [128, 2, 32, 16], F32)
            Cpk = sb("Cpk", [128, 2, 32, 16], F32)
            ldtb = sb("ldtb", [128, 32], F32)
            dpk = sb("dpk", [128, NG], F32)
            k.dma("sp", L2[:, 0, :].rearrange("g (a p) -> g a p", a=2), lam_re.rearrange("(a g) p -> g a p", a=2), writes=[Bt])
            k.dma("sp", L2[:, 1, :].rearrange("g (a p) -> g a p", a=2), lam_im.rearrange("(a g) p -> g a p", a=2), writes=[Bt])
            for ri, cc in enumerate((c_re, c_im)):
                for a_ in range(2):
                    k.dma("sp", Cn[:, ri, :, a_ * 64:(a_ + 1) * 64],
                          cc[a_ * 32:(a_ + 1) * 32, :, :].rearrange("(c gl) h p -> (gl h) c p", c=4), writes=[Bt])
            for ri, bb in enumerate((b_re, b_im)):
                for gh in range(2):
                    k.dma("sp", Bpk[gh * 64:(gh + 1) * 64, ri, :, :],
                          bb[gh * 32:(gh + 1) * 32, :, :].rearrange("g p h -> p g h"), writes=[Bt])
            for gh in range(2):
                k.dma("sp", ldtb[gh * 64:(gh + 1) * 64, :], log_dt[gh * 32:(gh + 1) * 32].partition_broadcast(64), writes=[Bt])
            for s in range(8):
                k.dma("sp", dpk[s * 16:(s + 1) * 16, :], d_skip.rearrange("(g h) -> h g", h=16), writes=[Bt],
                      allow_slow_non_contiguous=True)
            lrli = sb("lrli", [128, 2, 32], F32)
            pt, pb = bank()
            for ri in range(2):
                k.op("pe", lambda e, ri=ri: e.transpose(out=pt[:, ri * 32:(ri + 1) * 32], in_=L2[:, ri, :], identity=identf[0:32, 0:32]),
                     reads=[Bt, Bc], writes=[pb])
            k.op("dve", lambda e: e.tensor_copy(out=lrli[:].rearrange("p a g -> p (a g)"), in_=pt[:, 0:64]), reads=[pb], writes=[Bt])
            for ri in range(2):
                pt, pb = bank()
                for c4 in range(4):
                    k.op("pe", lambda e, ri=ri, c4=c4: e.transpose(out=pt[:, c4 * 128:(c4 + 1) * 128], in_=Cn[:, ri, c4, :], identity=identf[:]),
                         reads=[Bt, Bc], writes=[pb])
                k.op("dve", lambda e, ri=ri: e.tensor_copy(out=Cpk[:, ri, :, :].rearrange("p g h -> p (g h)"), in_=pt[:, :]),
                     reads=[pb], writes=[Bt])
            lr = lrli[:, 0, :]; li = lrli[:, 1, :]

            def V(fn):
                return k.op("dve", fn, reads=[Bt, Bc], writes=[Bt])

            def A(fn):
                return k.op("act", fn, reads=[Bt, Bc], writes=[Bt])

            sm = sb("s5sm", [128, 24, 32], F32)
            dt = sm[:, 0, :]; ldr = sm[:, 1, :]; th = sm[:, 2, :]; t0 = sm[:, 3, :]; t1 = sm[:, 4, :]
            kk1 = sm[:, 5, :]; thr = sm[:, 6, :]; thc = sm[:, 7, :]; s1 = sm[:, 8, :]; c1 = sm[:, 9, :]
            cfr = sm[:, 10, :]; cfi = sm[:, 11, :]; nr = sm[:, 12, :]; den = sm[:, 13, :]; t2 = sm[:, 14, :]
            A(lambda e: e.activation(out=dt, in_=ldtb[:], func=AF.Exp))
            V(lambda e: e.tensor_tensor(out=ldr, in0=lr, in1=dt, op=ALU.mult))
            V(lambda e: e.tensor_tensor(out=th, in0=li, in1=dt, op=ALU.mult))

            def range_reduce(dst, shift):
                V(lambda e: e.tensor_scalar(out=t0, in0=th, scalar1=float(shift), scalar2=None, op0=ALU.add))
                V(lambda e: e.memset(kk1, 0.0))
                for m in (1, 3, 5, 7):
                    V(lambda e, m=m: e.tensor_scalar(out=t1, in0=t0, scalar1=float(m * PI), scalar2=None, op0=ALU.is_gt))
                    V(lambda e: e.tensor_tensor(out=kk1, in0=kk1, in1=t1, op=ALU.add))
                V(lambda e: e.scalar_tensor_tensor(out=dst, in0=kk1, scalar=float(-2 * PI), in1=t0, op0=ALU.mult, op1=ALU.add))

            range_reduce(thr, 0.0)
            z = sm[:, 15, :]; z2 = sm[:, 16, :]; ps_ = sm[:, 17, :]; pc_ = sm[:, 18, :]
            V(lambda e: e.tensor_scalar(out=z, in0=thr, scalar1=0.125, scalar2=None, op0=ALU.mult))
            V(lambda e: e.tensor_tensor(out=z2, in0=z, in1=z, op=ALU.mult))
            V(lambda e: e.tensor_scalar(out=ps_, in0=z2, scalar1=-1.0 / 72.0, scalar2=1.0, op0=ALU.mult, op1=ALU.add))
            for dv_ in (42.0, 20.0, 6.0):
                V(lambda e: e.tensor_tensor(out=ps_, in0=ps_, in1=z2, op=ALU.mult))
                V(lambda e, dv_=dv_: e.tensor_scalar(out=ps_, in0=ps_, scalar1=-1.0 / dv_, scalar2=1.0, op0=ALU.mult, op1=ALU.add))
            V(lambda e: e.tensor_tensor(out=s1, in0=ps_, in1=z, op=ALU.mult))
            V(lambda e: e.tensor_scalar(out=pc_, in0=z2, scalar1=-1.0 / 56.0, scalar2=1.0, op0=ALU.mult, op1=ALU.add))
            for dv_ in (30.0, 12.0, 2.0):
                V(lambda e: e.tensor_tensor(out=pc_, in0=pc_, in1=z2, op=ALU.mult))
                V(lambda e, dv_=dv_: e.tensor_scalar(out=pc_, in0=pc_, scalar1=-1.0 / dv_, scalar2=1.0, op0=ALU.mult, op1=ALU.add))
            V(lambda e: e.tensor_copy(out=c1, in_=pc_))
            for _ in range(3):
                V(lambda e: e.tensor_tensor(out=t0, in0=s1, in1=c1, op=ALU.mult))
                V(lambda e: e.tensor_tensor(out=t1, in0=s1, in1=s1, op=ALU.mult))
                V(lambda e: e.tensor_scalar(out=s1, in0=t0, scalar1=2.0, scalar2=None, op0=ALU.mult))
                V(lambda e: e.tensor_scalar(out=c1, in0=t1, scalar1=-2.0, scalar2=1.0, op0=ALU.mult, op1=ALU.add))
            Ur = sb("Ur", [128, 9, 32], F32); Ui = sb("Ui", [128, 9, 32], F32)
            V(lambda e: e.memset(Ur[:, 0, :], 1.0)); V(lambda e: e.memset(Ui[:, 0, :], 0.0))
            V(lambda e: e.tensor_copy(out=Ur[:, 1, :], in_=c1)); V(lambda e: e.tensor_copy(out=Ui[:, 1, :], in_=s1))
            for t in range(1, 8):
                V(lambda e, t=t: e.tensor_tensor(out=t0, in0=Ur[:, t, :], in1=c1, op=ALU.mult))
                V(lambda e, t=t: e.tensor_tensor(out=t1, in0=Ui[:, t, :], in1=s1, op=ALU.mult))
                V(lambda e, t=t: e.tensor_tensor(out=Ur[:, t + 1, :], in0=t0, in1=t1, op=ALU.subtract))
                V(lambda e, t=t: e.tensor_tensor(out=t0, in0=Ur[:, t, :], in1=s1, op=ALU.mult))
                V(lambda e, t=t: e.tensor_tensor(out=t1, in0=Ui[:, t, :], in1=c1, op=ALU.mult))
                V(lambda e, t=t: e.tensor_tensor(out=Ui[:, t + 1, :], in0=t0, in1=t1, op=ALU.add))
            Epr = sb("Epr", [128, 9, 32], F32); Epi = sb("Epi", [128, 9, 32], F32)
            Enr = sb("Enr", [128, 8, 32], F32); Eni = sb("Eni", [128, 8, 32], F32)
            MG = sb("MG", [128, 9, 32], F32); MGn = sb("MGn", [128, 8, 32], F32)
            for t in range(9):
                A(lambda e, t=t: e.activation(out=MG[:, t, :], in_=ldr, func=AF.Exp, scale=float(t)))
            for t in range(8):
                A(lambda e, t=t: e.activation(out=MGn[:, t, :], in_=ldr, func=AF.Exp, scale=float(-t)))
            esA, ada_bufs = ada_emit()
            V(lambda e: e.tensor_tensor(out=Epr[:], in0=MG[:], in1=Ur[:], op=ALU.mult))
            V(lambda e: e.tensor_tensor(out=Epi[:], in0=MG[:], in1=Ui[:], op=ALU.mult))
            V(lambda e: e.tensor_tensor(out=Enr[:], in0=MGn[:], in1=Ur[:, 0:8, :], op=ALU.mult))
            V(lambda e: e.tensor_tensor(out=Eni[:], in0=MGn[:], in1=Ui[:, 0:8, :], op=ALU.mult))
            V(lambda e: e.tensor_scalar(out=Eni[:], in0=Eni[:], scalar1=-1.0, scalar2=None, op0=ALU.mult))
            V(lambda e: e.tensor_scalar(out=nr, in0=Epr[:, 1, :], scalar1=-1.0, scalar2=None, op0=ALU.add))
            ni = Epi[:, 1, :]
            V(lambda e: e.tensor_tensor(out=t0, in0=lr, in1=lr, op=ALU.mult))
            V(lambda e: e.tensor_tensor(out=t1, in0=li, in1=li, op=ALU.mult))
            V(lambda e: e.tensor_tensor(out=den, in0=t0, in1=t1, op=ALU.add))
            V(lambda e: e.reciprocal(out=den, in_=den))
            V(lambda e: e.tensor_tensor(out=t0, in0=nr, in1=lr, op=ALU.mult))
            V(lambda e: e.tensor_tensor(out=t1, in0=ni, in1=li, op=ALU.mult))
            V(lambda e: e.tensor_tensor(out=t0, in0=t0, in1=t1, op=ALU.add))
            V(lambda e: e.tensor_tensor(out=cfr, in0=t0, in1=den, op=ALU.mult))
            V(lambda e: e.tensor_tensor(out=t0, in0=ni, in1=lr, op=ALU.mult))
            V(lambda e: e.tensor_tensor(out=t1, in0=nr, in1=li, op=ALU.mult))
            V(lambda e: e.tensor_tensor(out=t0, in0=t0, in1=t1, op=ALU.subtract))
            V(lambda e: e.tensor_tensor(out=cfi, in0=t0, in1=den, op=ALU.mult))
            Bpr = sb("Bpr", [128, 32, 16], F32); Bpi = sb("Bpi", [128, 32, 16], F32)
            tb0 = sb("tb0", [128, 32, 16], F32); tb1 = sb("tb1", [128, 32, 16], F32)
            bc16 = lambda ap: ap.unsqueeze(2).to_broadcast([128, 32, 16])
            V(lambda e: e.tensor_tensor(out=tb0[:], in0=Bpk[:, 0, :, :], in1=bc16(cfr), op=ALU.mult))
            V(lambda e: e.tensor_tensor(out=tb1[:], in0=Bpk[:, 1, :, :], in1=bc16(cfi), op=ALU.mult))
            V(lambda e: e.tensor_tensor(out=Bpr[:], in0=tb0[:], in1=tb1[:], op=ALU.subtract))
            V(lambda e: e.tensor_tensor(out=tb0[:], in0=Bpk[:, 1, :, :], in1=bc16(cfr), op=ALU.mult))
            V(lambda e: e.tensor_tensor(out=tb1[:], in0=Bpk[:, 0, :, :], in1=bc16(cfi), op=ALU.mult))
            V(lambda e: e.tensor_tensor(out=Bpi[:], in0=tb0[:], in1=tb1[:], op=ALU.add))
            Gr = sb("Gr", [128, 32, 8, 16], BF16); Gi = sb("Gi", [128, 32, 8, 16], BF16)
            G7r = sb("G7r", [128, 32, 8, 16], BF16); G7i = sb("G7i", [128, 32, 8, 16], BF16)
            for s in range(8):
                for (er, ei, outr, outi) in ((Enr[:, s, :], Eni[:, s, :], Gr, Gi), (Epr[:, 7 - s, :], Epi[:, 7 - s, :], G7r, G7i)):
                    V(lambda e, er=er: e.tensor_tensor(out=tb0[:], in0=Bpr[:], in1=bc16(er), op=ALU.mult))
                    V(lambda e, ei=ei: e.tensor_tensor(out=tb1[:], in0=Bpi[:], in1=bc16(ei), op=ALU.mult))
                    V(lambda e, outr=outr, s=s: e.tensor_tensor(out=outr[:, :, s, :], in0=tb0[:], in1=tb1[:], op=ALU.subtract))
                    V(lambda e, ei=ei: e.tensor_tensor(out=tb0[:], in0=Bpr[:], in1=bc16(ei), op=ALU.mult))
                    V(lambda e, er=er: e.tensor_tensor(out=tb1[:], in0=Bpi[:], in1=bc16(er), op=ALU.mult))
                    V(lambda e, outi=outi, s=s: e.tensor_tensor(out=outi[:, :, s, :], in0=tb0[:], in1=tb1[:], op=ALU.add))
            for t in range(9):
                er, ei = Epr[:, t, :], Epi[:, t, :]
                V(lambda e, er=er: e.tensor_tensor(out=tb0[:], in0=Cpk[:, 0, :, :], in1=bc16(er), op=ALU.mult))
                V(lambda e, ei=ei: e.tensor_tensor(out=tb1[:], in0=Cpk[:, 1, :, :], in1=bc16(ei), op=ALU.mult))
                k.op("dve", lambda e, t=t: e.tensor_tensor(out=Ybr[:, :, t, :], in0=tb0[:], in1=tb1[:], op=ALU.subtract),
                     reads=[Bt], writes=[Bt, Bs5])
                V(lambda e, ei=ei: e.tensor_tensor(out=tb0[:], in0=Cpk[:, 0, :, :], in1=bc16(ei), op=ALU.mult))
                V(lambda e, er=er: e.tensor_tensor(out=tb1[:], in0=Cpk[:, 1, :, :], in1=bc16(er), op=ALU.mult))
                V(lambda e: e.tensor_tensor(out=tb0[:], in0=tb0[:], in1=tb1[:], op=ALU.add))
                k.op("dve", lambda e, t=t: e.tensor_scalar(out=Ybn[:, :, t, :], in0=tb0[:], scalar1=-1.0, scalar2=None, op0=ALU.mult),
                     reads=[Bt], writes=[Bt, Bs5])
            k.op("dve", lambda e: e.tensor_copy(out=ARR[:], in_=Epr[:, 8, :]), reads=[Bt], writes=[Bs5])
            k.op("dve", lambda e: e.tensor_copy(out=AIp[:], in_=Epi[:, 8, :]), reads=[Bt], writes=[Bs5])
            k.op("dve", lambda e: e.tensor_scalar(out=AIn[:], in0=Epi[:, 8, :], scalar1=-1.0, scalar2=None, op0=ALU.mult),
                 reads=[Bt], writes=[Bs5])
            sqr = sm[:, 19, :]; sqi = sm[:, 20, :]
            V(lambda e: e.tensor_copy(out=sqr, in_=Epr[:, 8, :])); V(lambda e: e.tensor_copy(out=sqi, in_=Epi[:, 8, :]))
            for _ in range(6):
                V(lambda e: e.tensor_tensor(out=t0, in0=sqr, in1=sqr, op=ALU.mult))
                V(lambda e: e.tensor_tensor(out=t1, in0=sqi, in1=sqi, op=ALU.mult))
                V(lambda e: e.tensor_tensor(out=t2, in0=sqr, in1=sqi, op=ALU.mult))
                V(lambda e: e.tensor_tensor(out=sqr, in0=t0, in1=t1, op=ALU.subtract))
                V(lambda e: e.tensor_scalar(out=sqi, in0=t2, scalar1=2.0, scalar2=None, op0=ALU.mult))
            k.op("dve", lambda e: e.tensor_copy(out=A64[:, 0, :], in_=sqr), reads=[Bt], writes=[Bs5])
            k.op("dve", lambda e: e.tensor_copy(out=A64[:, 1, :], in_=sqi), reads=[Bt], writes=[Bs5])
            k.op("dve", lambda e: e.tensor_scalar(out=A64[:, 2, :], in0=sqi, scalar1=-1.0, scalar2=None, op0=ALU.mult), reads=[Bt], writes=[Bs5])
            for g0 in range(0, NG, 4):
                pt, pb = bank()
                for gi in range(4):
                    g = g0 + gi
                    gh, gp = g // 32, g % 32
                    rows = slice(gh * 64, gh * 64 + 64)
                    k.op("pe", lambda e, gi=gi, rows=rows, gp=gp: e.matmul(
                        pt[:, gi * 128:(gi + 1) * 128], lhsT=Gr[rows, gp, :, :], rhs=Ybr[rows, gp, 0:8, :], start=True, stop=False),
                        reads=[Bt, Bs5], writes=[pb])
                    k.op("pe", lambda e, gi=gi, rows=rows, gp=gp: e.matmul(
                        pt[:, gi * 128:(gi + 1) * 128], lhsT=Gi[rows, gp, :, :], rhs=Ybn[rows, gp, 0:8, :], start=False, stop=True),
                        reads=[Bt, Bs5], writes=[pb])
                k.op("dve", lambda e, g0=g0: e.tensor_tensor(
                    out=Toep[:, g0:g0 + 4, :].rearrange("p g (t h) -> p g t h", h=16),
                    in0=pt[:, :].rearrange("p (g t h) -> p g t h", g=4, h=16),
                    in1=mask3[:].unsqueeze(1).unsqueeze(3).to_broadcast([128, 4, 8, 16]), op=ALU.mult),
                    reads=[pb, Bc], writes=[Bs5])
            for g in range(NG):
                k.op("dve", lambda e, g=g: e.scalar_tensor_tensor(out=Toep[:, g, :], in0=identf[:], scalar=dpk[:, g:g + 1],
                                                                  in1=Toep[:, g, :], op0=ALU.mult, op1=ALU.add),
                     reads=[Bt, Bc, Bs5], writes=[Bs5])
            for g0 in range(0, NG, 8):
                pt, pb = bank()
                ptb = pt[:].bitcast(BF16)
                for gi in range(8):
                    g = g0 + gi
                    gh, gp = g // 32, g % 32
                    rows = slice(gh * 64, gh * 64 + 64)
                    for ri, GG in enumerate((G7r, G7i)):
                        col = (gi * 2 + ri) * 64
                        k.op("pe", lambda e, rows=rows, gp=gp, GG=GG, col=col: e.transpose(
                            out=ptb[:, col:col + 64], in_=GG[rows, gp, :, :], identity=identb[rows, rows]),
                            reads=[Bt, Bc], writes=[pb])
                k.op("dve", lambda e, g0=g0: e.tensor_copy(out=Mw[:, g0:g0 + 8, :, :].rearrange("p g r q -> p (g r q)"), in_=ptb[:, 0:1024]),
                     reads=[pb], writes=[Bs5])
            for e_ in ("pe", "dve", "act", "pool", "sp"):
                k.wait_all(e_, [Bt] + ada_bufs + psb)
            k.es = esW
        esA.close()
        k.esR = esR

        dump("Toep", Toep[:], [128, NG, 128], [Bs5], BF16)
        dump("Mw", Mw[:], [128, NG, 2, 64], [Bs5], BF16)
        dump("Ybr", Ybr[:], [128, 32, 9, 16], [Bs5], BF16)
        dump("Ybn", Ybn[:], [128, 32, 9, 16], [Bs5], BF16)
        dump("ARR", ARR[:], [128, 32], [Bs5])
        dump("AIp", AIp[:], [128, 32], [Bs5])
        stage_end(1)
        dump("shiftT", shiftT[:], [128, KT, 17], [Bmod])
        dump("scaleT", scaleT[:], [128, KT, 17], [Bmod])
        stage_end(2)
        ENG = ("pe", "dve", "act", "pool", "sp")

        def barrier(bufs):
            for e_ in ENG:
                k.wait_all(e_, bufs)

        def prep(full):
            ntile = 9 if full else 8
            xsrc = [(xmain if full else xpre)[j * 128:(j + 1) * 128, :] for j in range(8)]
            if full:
                xsrc.append(xsmp[:, :])
            with ExitStack() as es4:
                k.es = es4
                gainbc = sb("gainbc", [128, D], F32); Bg = Buf("gain")
                k.dma("sp", gainbc[:], norm_gain.partition_broadcast(128), writes=[Bg])
                xs = [sb("xs%d" % i, [128, D], F32) for i in range(2)]; xb_ = [Buf("xs%d" % i) for i in range(2)]
                xn = [sb("xn%d" % i, [128, D], BF16) for i in range(2)]; xnb = [Buf("xn%d" % i) for i in range(2)]
                st = sb("prst", [128, 9, 4], F32); Bst = Buf("prst")
                tmpm = [sb("tmpm%d" % i, [128, 8, 128], F32) for i in range(2)]; tmb = [Buf("tmpm%d" % i) for i in range(2)]
                for j in range(ntile):
                    a = j % 2
                    k.dma("sp", xs[a][:], xsrc[j], writes=[xb_[a]])
                    k.op("act", lambda e, a=a, j=j: e.activation(out=xn[a][:], in_=xs[a][:], func=AF.Square, accum_out=st[:, j, 0:1]),
                         reads=[xb_[a]], writes=[xnb[a], Bst])
                    k.op("act", lambda e, j=j: e.activation(out=st[:, j, 2:3], in_=st[:, j, 0:1], func=AF.Ln, scale=1.0 / D, bias=epsb[:, 0:1]), reads=[Bst, Bc], writes=[Bst])
                    k.op("act", lambda e, j=j: e.activation(out=st[:, j, 3:4], in_=st[:, j, 2:3], func=AF.Exp, scale=-0.5), reads=[Bst], writes=[Bst])
                    k.op("dve", lambda e, a=a, j=j: e.scalar_tensor_tensor(out=xn[a][:], in0=xs[a][:], scalar=st[:, j, 3:4], in1=gainbc[:],
                                                                           op0=ALU.mult, op1=ALU.mult),
                         reads=[xb_[a], Bst, Bg], writes=[xnb[a]])
                    for half in range(2):
                        pt, pb = bank()
                        ptb = pt[:].bitcast(BF16)
                        for q in range(8):
                            kt = half * 8 + q
                            k.op("pe", lambda e, a=a, kt=kt, q=q, ptb=ptb: e.transpose(
                                out=ptb[:, q * 128:(q + 1) * 128], in_=xn[a][:, kt * 128:(kt + 1) * 128], identity=identb[:]),
                                reads=[xnb[a], Bc], writes=[pb])
                        m = half
                        src = ptb[:, 0:1024].rearrange("p (q t) -> p q t", q=8)
                        if j < 8:
                            sc = scaleT[:, half * 8:half * 8 + 8, 0:1].to_broadcast([128, 8, 128])
                            sh = shiftT[:, half * 8:half * 8 + 8, 0:1].to_broadcast([128, 8, 128])
                            o1 = tmpm[m][:]
                            o2 = hT[:, half * 8:half * 8 + 8, j * 128:(j + 1) * 128]
                        else:
                            src = src.rearrange("p q (j r) -> p q j r", r=8)
                            sc = scaleT[:, half * 8:half * 8 + 8, 1:17].unsqueeze(3).to_broadcast([128, 8, NSEQ, 8])
                            sh = shiftT[:, half * 8:half * 8 + 8, 1:17].unsqueeze(3).to_broadcast([128, 8, NSEQ, 8])
                            o1 = tmpm[m][:].rearrange("p q (j r) -> p q j r", r=8)
                            o2 = hT[:, half * 8:half * 8 + 8, j * 128:(j + 1) * 128].rearrange("p q (j r) -> p q j r", r=8)
                        k.op("dve", lambda e, o1=o1, src=src, sc=sc: e.tensor_tensor(out=o1, in0=src, in1=sc, op=ALU.mult),
                             reads=[pb, Bmod], writes=[tmb[m]])
                        k.op("dve", lambda e, o1=o1, o2=o2, sh=sh: e.tensor_tensor(out=o2, in0=o1, in1=sh, op=ALU.add),
                             reads=[tmb[m], Bmod], writes=[hTb[j]])
                barrier([Bg] + xb_ + xnb + [Bst] + tmb)
            return ntile

        def s5_front(full, U2, BU2, VH, BV):
            ntile = 9 if full else 8
            hall = hTb[0:ntile]
            ctiles = [(0, 128, 0)] + ([(TP, NSEQ, 128)] if full else [])
            with ExitStack() as es4:
                k.es = es4
                Uc = sb("Uc", [128, 16, 8, 16], BF16); BUc = Buf("Uc")
                for cb in range(4):
                    ws, wb = w_next((w_in, cb * WC))
                    for (toff, C, coff) in ctiles:
                        for sp2 in range(4):
                            pt, pb = bank()
                            for si in range(2):
                                s = sp2 * 2 + si
                                for kt in range(KT):
                                    k.op("pe", lambda e, kt=kt, s=s, si=si, toff=toff, C=C, ws=ws, pt=pt: e.matmul(
                                        pt[0:C, si * WC:(si + 1) * WC], lhsT=hT[:, kt, toff + s:toff + 8 * C:8], rhs=ws[:, kt, :],
                                        start=(kt == 0), stop=(kt == KT - 1)), reads=hall + [wb], writes=[pb])
                            k.op("act", lambda e, sp2=sp2, C=C, pt=pt: e.activation(
                                out=Uc[0:C, :, sp2 * 2:sp2 * 2 + 2, :].rearrange("c g s h -> c s g h"),
                                in_=pt[0:C, :].rearrange("c (s g h) -> c s g h", s=2, h=16), func=AF.Copy),
                                reads=[pb], writes=[BUc])
                        for gq in range(2):
                            pt, pb = bank()
                            ptb = pt[:].bitcast(BF16)
                            for gi in range(8):
                                gl = gq * 8 + gi
                                k.op("pe", lambda e, gl=gl, gi=gi, C=C, ptb=ptb: e.transpose(
                                    out=ptb[:, gi * 128:gi * 128 + C], in_=Uc[0:C, gl, :, :], identity=identb[0:C, 0:C]),
                                    reads=[BUc, Bc], writes=[pb])
                            g0 = cb * 16 + gq * 8
                            k.op("act", lambda e, g0=g0, C=C, coff=coff, ptb=ptb: e.activation(
                                out=U2[:, g0:g0 + 8, coff:coff + C], in_=ptb[:, 0:1024].rearrange("p (g c) -> p g c", g=8)[:, :, 0:C],
                                func=AF.Copy), reads=[pb], writes=[BU2[g0 // 8]])
                barrier([BUc])
            CT = 128 + (NSEQ if full else 0)
            for gp in range(32):
                pt, pb = bank()
                for gh in range(2):
                    g = gh * 32 + gp
                    for ri in range(2):
                        k.op("pe", lambda e, g=g, gh=gh, ri=ri, pt=pt: e.matmul(
                            pt[gh * 64:(gh + 1) * 64, ri * 256:ri * 256 + CT], lhsT=Mw[:, g, ri, :], rhs=U2[:, g, 0:CT],
                            start=True, stop=True), reads=[Bs5, BU2[g // 8]], writes=[pb])
                k.op("act", lambda e, gp=gp, pt=pt: e.activation(
                    out=VH[:, 1:1 + CT, :, gp].rearrange("p c r -> p r c"),
                    in_=pt[:, :].rearrange("p (r c) -> p r c", r=2)[:, :, 0:CT], func=AF.Copy), reads=[pb], writes=[BV])

        def s5_recur(VH, BV, BHh, hist):
            with ExitStack() as es4:
                k.es = es4
                ta = sb("rec_ta", [128, 2, 32], F32); tb = sb("rec_tb", [128, 2, 32], F32); Br = Buf("rec")
                arb = ARR[:].unsqueeze(1).to_broadcast([128, 2, 32])
                for c in range(128):
                    k.op("dve", lambda e: e.tensor_tensor(out=ta[:], in0=Hrun[:], in1=arb, op=ALU.mult), reads=[BH, Bs5], writes=[Br])
                    k.op("dve", lambda e: e.tensor_tensor(out=tb[:, 0, :], in0=Hrun[:, 1, :], in1=AIn[:], op=ALU.mult), reads=[BH, Bs5], writes=[Br])
                    k.op("dve", lambda e: e.tensor_tensor(out=tb[:, 1, :], in0=Hrun[:, 0, :], in1=AIp[:], op=ALU.mult), reads=[BH, Bs5], writes=[Br])
                    k.op("dve", lambda e: e.tensor_tensor(out=ta[:], in0=ta[:], in1=tb[:], op=ALU.add), reads=[Br], writes=[Br])
                    k.op("dve", lambda e, c=c: e.tensor_tensor(out=Hrun[:], in0=ta[:], in1=VH[:, 1 + c, :, :], op=ALU.add), reads=[Br, BV], writes=[BH])
                    if hist and c < 127:
                        k.op("act", lambda e, c=c: e.activation(out=VH[:, 1 + c, :, :], in_=Hrun[:], func=AF.Copy), reads=[BH], writes=[BHh])
                barrier([Br])

        def s5_recur2_state(VH, BV):
            with ExitStack() as es4:
                k.es = es4
                H2 = sb("rec_H2", [128, 2, 2, 32], F32); ta = sb("rec_ta2", [128, 2, 2, 32], F32); tb = sb("rec_tb2", [128, 2, 2, 32], F32)
                Br = Buf("rec2")
                arb = ARR[:].unsqueeze(1).unsqueeze(1).to_broadcast([128, 2, 2, 32])
                ainb = AIn[:].unsqueeze(1).to_broadcast([128, 2, 32]); aipb = AIp[:].unsqueeze(1).to_broadcast([128, 2, 32])
                k.op("dve", lambda e: e.memset(H2[:], 0.0), writes=[Br])
                k.op("dve", lambda e: e.tensor_copy(out=H2[:, 0, :, :], in_=Hrun[:]), reads=[BH], writes=[Br])
                for c in range(64):
                    k.op("dve", lambda e: e.tensor_tensor(out=ta[:], in0=H2[:], in1=arb, op=ALU.mult), reads=[Br, Bs5], writes=[Br])
                    k.op("dve", lambda e: e.tensor_tensor(out=tb[:, :, 0, :], in0=H2[:, :, 1, :], in1=ainb, op=ALU.mult), reads=[Br, Bs5], writes=[Br])
                    k.op("dve", lambda e: e.tensor_tensor(out=tb[:, :, 1, :], in0=H2[:, :, 0, :], in1=aipb, op=ALU.mult), reads=[Br, Bs5], writes=[Br])
                    k.op("dve", lambda e: e.tensor_tensor(out=ta[:], in0=ta[:], in1=tb[:], op=ALU.add), reads=[Br], writes=[Br])
                    k.op("dve", lambda e, c=c: e.tensor_tensor(out=H2[:], in0=ta[:], in1=VH[:, 1 + c:1 + c + 65:64, :, :], op=ALU.add), reads=[Br, BV], writes=[Br])
                k.op("dve", lambda e: e.tensor_tensor(out=ta[:, 0, :, :], in0=H2[:, 0, :, :], in1=A64[:, 0, :].unsqueeze(1).to_broadcast([128, 2, 32]), op=ALU.mult),
                     reads=[Br, Bs5], writes=[Br])
                k.op("dve", lambda e: e.tensor_tensor(out=tb[:, 0, 0, :], in0=H2[:, 0, 1, :], in1=A64[:, 2, :], op=ALU.mult), reads=[Br, Bs5], writes=[Br])
                k.op("dve", lambda e: e.tensor_tensor(out=tb[:, 0, 1, :], in0=H2[:, 0, 0, :], in1=A64[:, 1, :], op=ALU.mult), reads=[Br, Bs5], writes=[Br])
                k.op("dve", lambda e: e.tensor_tensor(out=ta[:, 0, :, :], in0=ta[:, 0, :, :], in1=tb[:, 0, :, :], op=ALU.add), reads=[Br], writes=[Br])
                k.op("dve", lambda e: e.tensor_tensor(out=Hrun[:], in0=ta[:, 0, :, :], in1=H2[:, 1, :, :], op=ALU.add), reads=[Br], writes=[BH])
                barrier([Br])

        def s5_recur2_hist(VH, BV, BHh):
            SB_ = 8
            with ExitStack() as es4:
                k.es = es4
                H2 = sb("rec_H2", [128, 2, 2, 32], F32); ta = sb("rec_ta2", [128, 2, 2, 32], F32); tb = sb("rec_tb2", [128, 2, 2, 32], F32)
                PW = sb("rec_PW", [128, 64, 2, 32], BF16); c1 = sb("rec_c1", [128, SB_, 32], F32); c2 = sb("rec_c2", [128, SB_, 32], F32)
                pw = sb("rec_pw", [128, 5, 32], F32)
                Br = Buf("rec2"); Bpw = Buf("rec_pw")
                W = lambda fn: k.op("dve", fn, reads=[Bpw, Bs5], writes=[Bpw])
                pr, pi_, q0_, q1_, q2_ = (pw[:, i, :] for i in range(5))
                W(lambda e: e.tensor_copy(out=pr, in_=ARR[:])); W(lambda e: e.tensor_copy(out=pi_, in_=AIp[:]))
                W(lambda e: e.tensor_copy(out=PW[:, 0, 0, :], in_=ARR[:])); W(lambda e: e.tensor_copy(out=PW[:, 0, 1, :], in_=AIp[:]))
                n = 1
                while n < 64:
                    for b0 in range(0, n, SB_):
                        nb = min(SB_, n - b0)
                        src_r = PW[:, b0:b0 + nb, 0, :]; src_i = PW[:, b0:b0 + nb, 1, :]
                        prb = pr.unsqueeze(1).to_broadcast([128, nb, 32]); pib = pi_.unsqueeze(1).to_broadcast([128, nb, 32])
                        W(lambda e, src_r=src_r, prb=prb, nb=nb: e.tensor_tensor(out=c1[:, 0:nb, :], in0=src_r, in1=prb, op=ALU.mult))
                        W(lambda e, src_i=src_i, pib=pib, nb=nb: e.tensor_tensor(out=c2[:, 0:nb, :], in0=src_i, in1=pib, op=ALU.mult))
                        W(lambda e, nb=nb, b0=b0, n=n: e.tensor_tensor(out=PW[:, n + b0:n + b0 + nb, 0, :], in0=c1[:, 0:nb, :], in1=c2[:, 0:nb, :], op=ALU.subtract))
                        W(lambda e, src_r=src_r, pib=pib, nb=nb: e.tensor_tensor(out=c1[:, 0:nb, :], in0=src_r, in1=pib, op=ALU.mult))
                        W(lambda e, src_i=src_i, prb=prb, nb=nb: e.tensor_tensor(out=c2[:, 0:nb, :], in0=src_i, in1=prb, op=ALU.mult))
                        W(lambda e, nb=nb, b0=b0, n=n: e.tensor_tensor(out=PW[:, n + b0:n + b0 + nb, 1, :], in0=c1[:, 0:nb, :], in1=c2[:, 0:nb, :], op=ALU.add))
                    n *= 2
                    if n < 64:
                        W(lambda e: e.tensor_tensor(out=q0_, in0=pr, in1=pr, op=ALU.mult))
                        W(lambda e: e.tensor_tensor(out=q1_, in0=pi_, in1=pi_, op=ALU.mult))
                        W(lambda e: e.tensor_tensor(out=q2_, in0=pr, in1=pi_, op=ALU.mult))
                        W(lambda e: e.tensor_tensor(out=pr, in0=q0_, in1=q1_, op=ALU.subtract))
                        W(lambda e: e.tensor_scalar(out=pi_, in0=q2_, scalar1=2.0, scalar2=None, op0=ALU.mult))
                arb = ARR[:].unsqueeze(1).unsqueeze(1).to_broadcast([128, 2, 2, 32])
                ainb = AIn[:].unsqueeze(1).to_broadcast([128, 2, 32]); aipb = AIp[:].unsqueeze(1).to_broadcast([128, 2, 32])
                k.op("dve", lambda e: e.memset(H2[:], 0.0), writes=[Br])
                k.op("dve", lambda e: e.tensor_copy(out=H2[:, 0, :, :], in_=Hrun[:]), reads=[BH], writes=[Br])
                Bh2 = Buf("H2")
                for c in range(64):
                    k.op("dve", lambda e: e.tensor_tensor(out=ta[:], in0=H2[:], in1=arb, op=ALU.mult), reads=[Bh2, Bs5], writes=[Br])
                    k.op("dve", lambda e: e.tensor_tensor(out=tb[:, :, 0, :], in0=H2[:, :, 1, :], in1=ainb, op=ALU.mult), reads=[Bh2, Bs5], writes=[Br])
                    k.op("dve", lambda e: e.tensor_tensor(out=tb[:, :, 1, :], in0=H2[:, :, 0, :], in1=aipb, op=ALU.mult), reads=[Bh2, Bs5], writes=[Br])
                    k.op("dve", lambda e: e.tensor_tensor(out=ta[:], in0=ta[:], in1=tb[:], op=ALU.add), reads=[Br], writes=[Br])
                    k.op("dve", lambda e, c=c: e.tensor_tensor(out=H2[:], in0=ta[:], in1=VH[:, 1 + c:1 + c + 65:64, :, :], op=ALU.add), reads=[Br, BV], writes=[Bh2])
                    k.op("act", lambda e, c=c: e.activation(out=VH[:, 1 + c:1 + c + 65:64, :, :], in_=H2[:], func=AF.Copy), reads=[Bh2], writes=[BHh])
                hr_b = lambda nb: H2[:, 0, 0, :].unsqueeze(1).to_broadcast([128, nb, 32])
                hi_b = lambda nb: H2[:, 0, 1, :].unsqueeze(1).to_broadcast([128, nb, 32])
                for b0 in range(0, 64, SB_):
                    pwr = PW[:, b0:b0 + SB_, 0, :]; pwi = PW[:, b0:b0 + SB_, 1, :]
                    tgt_r = VH[:, 65 + b0:65 + b0 + SB_, 0, :]; tgt_i = VH[:, 65 + b0:65 + b0 + SB_, 1, :]
                    X = lambda fn: k.op("dve", fn, reads=[Bh2, Bpw, BHh, Br], writes=[Br, BHh])
                    X(lambda e, pwr=pwr: e.tensor_tensor(out=c1[:], in0=pwr, in1=hr_b(SB_), op=ALU.mult))
                    X(lambda e, pwi=pwi: e.tensor_tensor(out=c2[:], in0=pwi, in1=hi_b(SB_), op=ALU.mult))
                    X(lambda e: e.tensor_tensor(out=c1[:], in0=c1[:], in1=c2[:], op=ALU.subtract))
                    X(lambda e, tgt_r=tgt_r: e.tensor_tensor(out=tgt_r, in0=tgt_r, in1=c1[:], op=ALU.add))
                    X(lambda e, pwr=pwr: e.tensor_tensor(out=c1[:], in0=pwr, in1=hi_b(SB_), op=ALU.mult))
                    X(lambda e, pwi=pwi: e.tensor_tensor(out=c2[:], in0=pwi, in1=hr_b(SB_), op=ALU.mult))
                    X(lambda e: e.tensor_tensor(out=c1[:], in0=c1[:], in1=c2[:], op=ALU.add))
                    X(lambda e, tgt_i=tgt_i: e.tensor_tensor(out=tgt_i, in0=tgt_i, in1=c1[:], op=ALU.add))
                k.op("dve", lambda e: e.tensor_tensor(out=ta[:, 0, :, :], in0=H2[:, 0, :, :], in1=A64[:, 0, :].unsqueeze(1).to_broadcast([128, 2, 32]), op=ALU.mult),
                     reads=[Bh2, Bs5, Br], writes=[Br])
                k.op("dve", lambda e: e.tensor_tensor(out=tb[:, 0, 0, :], in0=H2[:, 0, 1, :], in1=A64[:, 2, :], op=ALU.mult), reads=[Bh2, Bs5, Br], writes=[Br])
                k.op("dve", lambda e: e.tensor_tensor(out=tb[:, 0, 1, :], in0=H2[:, 0, 0, :], in1=A64[:, 1, :], op=ALU.mult), reads=[Bh2, Bs5, Br], writes=[Br])
                k.op("dve", lambda e: e.tensor_tensor(out=ta[:, 0, :, :], in0=ta[:, 0, :, :], in1=tb[:, 0, :, :], op=ALU.add), reads=[Br], writes=[Br])
                k.op("dve", lambda e: e.tensor_tensor(out=Hrun[:], in0=ta[:, 0, :, :], in1=H2[:, 1, :, :], op=ALU.add), reads=[Br, Bh2], writes=[BH])
                barrier([Br, Bpw, Bh2, BHh])

        def proj_fm(wlist, T, hall, dstfn, func=AF.Copy):
            nblocks = [(n0, min(512, T - n0)) for n0 in range(0, T, 512)]
            for i, spec in enumerate(wlist):
                ws, wb = w_next(spec)
                for mt in range(2):
                    for (n0, nn) in nblocks:
                        pt, pb = bank()
                        for kt in range(KT):
                            k.op("pe", lambda e, kt=kt, n0=n0, nn=nn, mt=mt, ws=ws, pt=pt: e.matmul(
                                pt[:, 0:nn], lhsT=ws[:, kt, mt * 128:(mt + 1) * 128], rhs=hT[:, kt, n0:n0 + nn],
                                start=(kt == 0), stop=(kt == KT - 1)), reads=hall + [wb], writes=[pb])
                        oap, ob = dstfn(i * 2 + mt, n0, nn)
                        k.op("act", lambda e, oap=oap, nn=nn, pt=pt: e.activation(out=oap, in_=pt[:, 0:nn], func=func), reads=[pb], writes=[ob])

        def gla_alloc(full):
            glr = sb("glr", [17, TM], BF16); Bglr = Buf("glr")
            kT = sb("kT", [128, NH, TM], BF16); BkT = Buf("kT")
            vtok = sb("vtok", [128, 9, NH * DV], BF16); Bvt = [Buf("vtok%d" % j) for j in range(9)]
            qT = sb("qT", [128, NH, TM], BF16) if full else None
            BqT = Buf("qT")
            return dict(glr=glr, Bglr=Bglr, kT=kT, BkT=BkT, vtok=vtok, Bvt=Bvt, qT=qT, BqT=BqT)

        def gla_proj(gl, full):
            glr, Bglr, kT, BkT, vtok, Bvt, qT, BqT = (gl[x] for x in ("glr", "Bglr", "kT", "BkT", "vtok", "Bvt", "qT", "BqT"))
            ntile = 9 if full else 8
            T = ntile * 128
            hall = hTb[0:ntile]
            nblocks = [(n0, min(512, T - n0)) for n0 in range(0, T, 512)]
            if True:
                k.op("dve", lambda e: e.memset(glr[:], 1.0), writes=[Bglr])
                ws, wb = w_next((w_in, 5120))
                for (n0, nn) in nblocks:
                    pt, pb = bank()
                    for kt in range(KT):
                        k.op("pe", lambda e, kt=kt, n0=n0, nn=nn, ws=ws, pt=pt: e.matmul(pt[0:16, 0:nn], lhsT=ws[:, kt, 0:16], rhs=hT[:, kt, n0:n0 + nn],
                                                                                       start=(kt == 0), stop=(kt == KT - 1)), reads=hall + [wb], writes=[pb])
                    k.op("act", lambda e, n0=n0, nn=nn, pt=pt: e.activation(out=glr[0:16, n0:n0 + nn], in_=pt[0:16, 0:nn], func=AF.Copy), reads=[pb], writes=[Bglr])
                proj_fm([(w_in, 2560), (w_in, 2560 + WC)], T, hall, lambda m, n0, nn: (kT[:, m, n0:n0 + nn], BkT))
                for i in range(4):
                    ws, wb = w_next((w_in, 3072 + i * WC))
                    for j in range(ntile):
                        pt, pb = bank()
                        for kt in range(KT):
                            k.op("pe", lambda e, kt=kt, j=j, ws=ws, pt=pt: e.matmul(pt[:, 0:WC], lhsT=hT[:, kt, j * 128:(j + 1) * 128], rhs=ws[:, kt, :],
                                                                                   start=(kt == 0), stop=(kt == KT - 1)), reads=[hTb[j], wb], writes=[pb])
                        k.op("act", lambda e, j=j, i=i, pt=pt: e.activation(out=vtok[:, j, i * WC:(i + 1) * WC], in_=pt[:, 0:WC], func=AF.Copy),
                             reads=[pb], writes=[Bvt[j]])
                if full:
                    proj_fm([(w_in, 2048), (w_in, 2048 + WC)], T, hall, lambda m, n0, nn: (qT[:, m, n0:n0 + nn], BqT))

        def gla_tiles(gl, full, by, Bby):
            glr, Bglr, kT, BkT, vtok, Bvt, qT, BqT = (gl[x] for x in ("glr", "Bglr", "kT", "BkT", "vtok", "Bvt", "qT", "BqT"))
            ntile = 9 if full else 8
            with ExitStack() as es4:
                k.es = es4
                G = {}
                gl_ = [("e1", [128, NH, 128], F32), ("cs", [128, NH, 128], F32), ("eb", [128, NH, 128], F32), ("einv", [128, NH, 128], F32),
                       ("kt", [128, NH, 128], BF16), ("ktok", [128, NH, 128], BF16), ("stmp", [128, 2, DV], F32)]
                if full:
                    gl_ += [("qt", [128, NH, 128], BF16), ("scm", [128, NH, 128], BF16), ("sq", [128, 2, 4, 128], BF16),
                            ("rr", [128, NH, 128], F32), ("qtf", [128, NH, 128], F32), ("KM", [128, NSEQ, 128], BF16), ("ebe", [128, NH, NSEQ], F32)]
                for nm, shp, dt_ in gl_:
                    G[nm] = sb("g_" + nm, shp, dt_)
                GB = {nm: Buf("g_" + nm) for nm in G}
                if full:
                    S0h = [sb("S0h%d" % i, [128, 8, DV], F32) for i in range(2)]; BS0h = [Buf("S0h%d" % i) for i in range(2)]
                    Snew = [sb("Snew%d" % i, [128, 2, DV], F32) for i in range(2)]; BSn2 = [Buf("Snew%d" % i) for i in range(2)]
                rot = [0]; rot2 = [0]
                fl = lambda ap: ap.rearrange("p a b -> p (a b)")
                for j in range(ntile):
                    sample = (j == 8)
                    toks = slice(j * 128, (j + 1) * 128)
                    pg, pgb = bank()
                    for hd in range(NH):
                        k.op("pe", lambda e, hd=hd, pg=pg: e.matmul(pg[:, hd * 128:(hd + 1) * 128], lhsT=wup[0:17, hd * 128:(hd + 1) * 128], rhs=glr[0:17, toks],
                                                                    start=True, stop=True), reads=[Bwup, Bglr], writes=[pgb])
                    k.op("act", lambda e, pg=pg: e.activation(out=fl(G["e1"][:]), in_=pg[:, :], func=AF.Exp, scale=-1.0), reads=[pgb], writes=[GB["e1"]])
                    k.op("act", lambda e: e.activation(out=fl(G["e1"][:]), in_=fl(G["e1"][:]), func=AF.Ln, bias=1.0), reads=[GB["e1"]], writes=[GB["e1"]])
                    d0 = hrst_s if sample else hrst
                    k.op("dve", lambda e, d0=d0: e.tensor_tensor_scan(out=fl(G["cs"][:]), data0=fl(d0[:]), data1=fl(G["e1"][:]), initial=0.0,
                                                                      op0=ALU.mult, op1=ALU.add), reads=[GB["e1"], Bc], writes=[GB["cs"]])
                    k.op("act", lambda e: e.activation(out=fl(G["einv"][:]), in_=fl(G["cs"][:]), func=AF.Exp, scale=1.0 / 16.0), reads=[GB["cs"]], writes=[GB["einv"]])
                    k.op("act", lambda e: e.activation(out=fl(G["eb"][:]), in_=fl(G["cs"][:]), func=AF.Exp, scale=-1.0 / 16.0), reads=[GB["cs"]], writes=[GB["eb"]])
                    k.op("dve", lambda e: e.tensor_tensor(out=G["kt"][:], in0=kT[:, :, toks], in1=G["einv"][:], op=ALU.mult),
                         reads=[BkT, GB["einv"]], writes=[GB["kt"]])
                    pt2, pb2 = bank()
                    pt2b = pt2[:].bitcast(BF16)
                    for hd in range(NH):
                        k.op("pe", lambda e, hd=hd, pt2b=pt2b: e.transpose(out=pt2b[:, hd * 128:(hd + 1) * 128], in_=G["kt"][:, hd, :], identity=identb[:]),
                             reads=[GB["kt"], Bc], writes=[pb2])
                    k.op("act", lambda e, pt2b=pt2b: e.activation(out=fl(G["ktok"][:]), in_=pt2b[:, 0:512], func=AF.Copy), reads=[pb2], writes=[GB["ktok"]])
                    if full:
                        k.op("dve", lambda e: e.scalar_tensor_tensor(out=G["qt"][:], in0=qT[:, :, toks], scalar=float(DK ** -0.5),
                                                                     in1=G["eb"][:], op0=ALU.mult, op1=ALU.mult),
                             reads=[BqT, GB["eb"]], writes=[GB["qt"]])
                        psc, pscb = bank()
                        for hd in range(NH):
                            k.op("pe", lambda e, hd=hd, psc=psc: e.matmul(psc[:, hd * 128:(hd + 1) * 128], lhsT=G["kt"][:, hd, :], rhs=G["qt"][:, hd, :], start=True, stop=True),
                                 reads=[GB["kt"], GB["qt"]], writes=[pscb])
                        mk = smask if sample else causal
                        k.op("dve", lambda e, psc=psc, mk=mk: e.tensor_tensor(out=G["scm"][:], in0=psc[:, :].rearrange("p (a b) -> p a b", a=NH),
                                                                              in1=mk[:].unsqueeze(1).to_broadcast([128, NH, 128]), op=ALU.mult),
                             reads=[pscb, Bc], writes=[GB["scm"]])
                        if sample:
                            k.op("dve", lambda e: e.scalar_tensor_tensor(out=G["qtf"][:], in0=qT[:, :, toks], scalar=float(DK ** -0.5),
                                                                         in1=G["eb"][:], op0=ALU.mult, op1=ALU.mult),
                                 reads=[BqT, GB["eb"]], writes=[GB["qtf"]])
                            k.op("dve", lambda e: e.tensor_copy(out=G["ebe"][:], in_=G["eb"][:].rearrange("p a (j r) -> p a j r", r=8)[:, :, :, 7]),
                                 reads=[GB["eb"]], writes=[GB["ebe"]])
                        pos = []
                        for half in range(2):
                            po, pob = bank(hold=True)
                            pos.append((po, pob))
                            for hdl in range(2):
                                hd = half * 2 + hdl
                                if sample:
                                    k.op("dve", lambda e, hd=hd: e.tensor_tensor(out=G["KM"][:], in0=G["ktok"][:, hd, :].unsqueeze(1).to_broadcast([128, NSEQ, 128]),
                                                                                 in1=seqmask[:].unsqueeze(2).to_broadcast([128, NSEQ, 128]), op=ALU.mult),
                                         reads=[GB["ktok"], Bc], writes=[GB["KM"]])
                                for qh in (range(2) if sample else [None]):
                                    if sample:
                                        a = rot[0] % 2; rot[0] += 1
                                        k.dma("pool", S0h[a][:], gla_in[qh * 8:(qh + 1) * 8, hd, :, :].rearrange("q k v -> k q v"), writes=[BS0h[a]])
                                    for vh in range(2):
                                        c0 = (hdl * 2 + vh) * 128
                                        vcols = slice(hd * DV + vh * 128, hd * DV + (vh + 1) * 128)
                                        if not sample:
                                            k.op("pe", lambda e, c0=c0, vcols=vcols, po=po, hd=hd: e.matmul(po[:, c0:c0 + 128], lhsT=vtok[:, j, vcols], rhs=G["scm"][:, hd, :],
                                                                                                            start=True, stop=False), reads=[Bvt[j], GB["scm"]], writes=[pob])
                                            k.op("pe", lambda e, c0=c0, vh=vh, hd=hd, po=po: e.matmul(po[:, c0:c0 + 128], lhsT=Sglb[:, hd, vh * 128:(vh + 1) * 128],
                                                                                                      rhs=G["qt"][:, hd, :], start=False, stop=True), reads=[BS, GB["qt"]], writes=[pob])
                                        else:
                                            cc0 = c0 + qh * 64
                                            k.op("pe", lambda e, cc0=cc0, vcols=vcols, po=po, hd=hd, qh=qh: e.matmul(
                                                po[:, cc0:cc0 + 64], lhsT=vtok[:, j, vcols], rhs=G["scm"][:, hd, qh * 64:(qh + 1) * 64],
                                                start=True, stop=False), reads=[Bvt[j], GB["scm"]], writes=[pob])
                                            for ql in range(8):
                                                q = qh * 8 + ql
                                                k.op("pe", lambda e, cc0=cc0, ql=ql, q=q, a=a, vh=vh, hd=hd, po=po: e.matmul(
                                                    po[:, cc0 + ql * 8:cc0 + ql * 8 + 8], lhsT=S0h[a][:, ql, vh * 128:(vh + 1) * 128],
                                                    rhs=G["qtf"][:, hd, q * 8:(q + 1) * 8], start=False, stop=(ql == 7)),
                                                    reads=[BS0h[a], GB["qtf"]], writes=[pob])
                                    if sample:
                                        for qp in range(4):
                                            pu, pub = bank()
                                            for qi in range(2):
                                                q = qh * 8 + qp * 2 + qi
                                                k.op("pe", lambda e, q=q, qi=qi, pu=pu, hd=hd: e.matmul(pu[:, qi * DV:(qi + 1) * DV], lhsT=G["KM"][:, q, :],
                                                                                                         rhs=vtok[:, 8, hd * DV:(hd + 1) * DV], start=True, stop=True),
                                                     reads=[GB["KM"], Bvt[8]], writes=[pub])
                                            b_ = rot2[0] % 2; rot2[0] += 1
                                            q0 = qh * 8 + qp * 2
                                            k.op("dve", lambda e, b_=b_, pu=pu, a=a, qp=qp: e.tensor_tensor(out=Snew[b_][:], in0=pu[:, :].rearrange("p (q v) -> p q v", q=2),
                                                                                                          in1=S0h[a][:, qp * 2:qp * 2 + 2, :], op=ALU.add),
                                                 reads=[pub, BS0h[a]], writes=[BSn2[b_]])
                                            k.op("dve", lambda e, b_=b_, hd=hd, q0=q0: e.tensor_tensor(out=Snew[b_][:], in0=Snew[b_][:],
                                                                                                     in1=G["ebe"][:, hd, q0:q0 + 2].unsqueeze(2).to_broadcast([128, 2, DV]), op=ALU.mult),
                                                 reads=[BSn2[b_], GB["ebe"]], writes=[BSn2[b_]])
                                            k.dma("sp", gla_s[q0:q0 + 2, hd, :, :].rearrange("q k v -> k q v"), Snew[b_][:], reads=[BSn2[b_]], sbuf=BSn2[b_])
                            k.op("act", lambda e, po=po, half=half: e.activation(out=G["sq"][:, half, :, :].rearrange("p a b -> p (a b)"), in_=po[:, :], func=AF.Square),
                                 reads=[pob], writes=[GB["sq"]])
                        pss, pssb = bank()
                        for hd in range(NH):
                            for vh in range(2):
                                k.op("pe", lambda e, hd=hd, vh=vh, pss=pss: e.matmul(pss[:, hd * 128:(hd + 1) * 128], lhsT=onesb[:], rhs=G["sq"][:, hd // 2, (hd % 2) * 2 + vh, :],
                                                                                     start=(vh == 0), stop=(vh == 1)), reads=[GB["sq"], Bc], writes=[pssb])
                        k.op("act", lambda e, pss=pss: e.activation(out=fl(G["rr"][:]), in_=pss[:, :], func=AF.Ln, scale=1.0 / DV, bias=epsb[:, 0:1]),
                             reads=[pssb, Bc], writes=[GB["rr"]])
                        k.op("act", lambda e: e.activation(out=fl(G["rr"][:]), in_=fl(G["rr"][:]), func=AF.Exp, scale=-0.5), reads=[GB["rr"]], writes=[GB["rr"]])
                        for half in range(2):
                            po, pob = pos[half]
                            k.op("dve", lambda e, half=half, po=po: e.tensor_tensor(
                                out=by[:, half * 4:(half + 1) * 4, toks].rearrange("p (h v) t -> p h v t", v=2),
                                in0=po[:, :].rearrange("p (h v t) -> p h v t", h=2, v=2),
                                in1=G["rr"][:, half * 2:half * 2 + 2, :].unsqueeze(2).to_broadcast([128, 2, 2, 128]), op=ALU.mult),
                                reads=[pob, GB["rr"]], writes=[Bby])
                            unhold(pob)
                    if not sample:
                        for half in range(2):
                            pu, pub = bank()
                            for hdl in range(2):
                                hd = half * 2 + hdl
                                k.op("pe", lambda e, hd=hd, hdl=hdl, pu=pu: e.matmul(pu[:, hdl * DV:(hdl + 1) * DV], lhsT=G["ktok"][:, hd, :], rhs=vtok[:, j, hd * DV:(hd + 1) * DV],
                                                                                     start=True, stop=True), reads=[GB["ktok"], Bvt[j]], writes=[pub])
                            k.op("dve", lambda e, half=half, pu=pu: e.tensor_tensor(out=G["stmp"][:], in0=pu[:, :].rearrange("p (h v) -> p h v", h=2),
                                                                                    in1=Sgl[:, half * 2:half * 2 + 2, :], op=ALU.add),
                                 reads=[pub, BS], writes=[GB["stmp"]])
                            k.op("dve", lambda e, half=half: e.tensor_tensor(out=Sgl[:, half * 2:half * 2 + 2, :], in0=G["stmp"][:],
                                                                             in1=G["eb"][:, half * 2:half * 2 + 2, 127:128].to_broadcast([128, 2, DV]), op=ALU.mult),
                                 reads=[GB["stmp"], GB["eb"]], writes=[BS])
                        if full:
                            k.op("act", lambda e: e.activation(out=fl(Sglb[:]), in_=fl(Sgl[:]), func=AF.Copy), reads=[BS], writes=[BS])
                if full:
                    outs_wait.extend(BSn2)
                    barrier(BS0h + BSn2)
                barrier([Bdump] + list(GB.values()) + psb)


        def gla_pass(full, by, Bby):
            with ExitStack() as es4:
                k.es = es4
                gl = gla_alloc(full)
                gla_proj(gl, full)
                gla_tiles(gl, full, by, Bby)
                k.es = es4
                barrier([gl["Bglr"], gl["BkT"], gl["BqT"]] + gl["Bvt"] + psb)

        outs_wait = []
        k.op("dve", lambda e: e.memset(Hrun[:], 0.0), writes=[BH])
        k.op("dve", lambda e: e.memset(Sgl[:], 0.0), writes=[BS])
        k.op("dve", lambda e: e.memset(Sglb[:], 0.0), writes=[BS])
        prep(False)
        dump("hTpre", hT[:, :, 0:TP], [128, KT, TP], hTb, BF16)
        stage_end(3)
        with ExitStack() as esG:
            k.es = esG
            glp = gla_alloc(False)
            with ExitStack() as esP:
                k.es = esP
                U2 = sb("U2p", [128, NG, 128], BF16); BU2 = [Buf("U2_%d" % i) for i in range(8)]
                VH = sb("VHp", [128, 129, 2, 32], BF16); BV = Buf("VH"); BHh = Buf("Hh")
                s5_front(False, U2, BU2, VH, BV)
                k.es = esP
                gla_proj(glp, False)
                s5_recur2_state(VH, BV)
                k.es = esP
                dump("U2p", U2[:], [128, NG, 128], BU2, BF16)
                dump("VHp", VH[:], [128, 129, 2, 32], [BV], BF16)
                dump("Hrun_pre", Hrun[:], [128, 2, 32], [BH])
                barrier([BV, BHh, Bdump] + BU2 + psb)
                stage_end(4)
            k.es = esG
            gla_tiles(glp, False, None, None)
            k.es = esG
            barrier([glp["Bglr"], glp["BkT"], glp["BqT"]] + glp["Bvt"] + psb)
        k.es = esW
        dump("Sgl_pre", Sgl[:], [128, NH, DV], [BS])
        stage_end(5)
        k.op("dve", lambda e: e.tensor_scalar(out=Hrun[:], in0=Hrun[:], scalar1=flag[:, 0:1], scalar2=None, op0=ALU.mult), reads=[BH, Bp], writes=[BH])
        k.op("dve", lambda e: e.tensor_scalar(out=Sgl[:], in0=Sgl[:], scalar1=flag[:, 0:1], scalar2=None, op0=ALU.mult), reads=[BS, Bp], writes=[BS])
        k.op("act", lambda e: e.activation(out=Sglb[:], in_=Sgl[:], func=AF.Copy), reads=[BS], writes=[BS])

        ay = sb("ay", [128, 8, TM], BF16, side="right"); Bay = Buf("ay")
        prep(True)
        dump("hTmain", hT[:], [128, KT, TM], hTb, BF16)
        stage_end(6)
        with ExitStack() as esM:
            k.es = esM
            U2 = sb("U2m", [128, NG, 128 + NSEQ], BF16); BU2 = [Buf("U2m_%d" % i) for i in range(8)]
            VH = sb("VHm", [128, 129 + NSEQ, 2, 32], BF16); BV = Buf("VHm"); BHh = Buf("Hhm")
            Hsin = sb("Hsin", [128, NSEQ, 2, 32], F32); BHs = Buf("Hsin")
            Hsb = sb("Hsb", [128, NSEQ, 2, 32], BF16)
            with ExitStack() as es5:
                k.es = es5
                Sn = sb("Sn", [32, 8, 2, 128], F32); BSn = Buf("Sn")
                for q0 in range(0, NSEQ, 8):
                    for ri, src in enumerate((ssm_re_in, ssm_im_in)):
                        for a_ in range(2):
                            k.dma("sp", Sn[:, :, ri, a_ * 64:(a_ + 1) * 64], src[q0:q0 + 8, a_ * 32:(a_ + 1) * 32, :].rearrange("j g p -> g j p"), writes=[BSn])
                    pt, pb = bank()
                    for qi in range(8):
                        for ri in range(2):
                            col = (qi * 2 + ri) * 32
                            k.op("pe", lambda e, qi=qi, ri=ri, col=col, pt=pt: e.transpose(out=pt[:, col:col + 32], in_=Sn[:, qi, ri, :],
                                                                                           identity=identf[0:32, 0:32]), reads=[BSn, Bc], writes=[pb])
                    k.op("dve", lambda e, q0=q0, pt=pt: e.tensor_copy(out=Hsin[:, q0:q0 + 8, :, :].rearrange("p j r g -> p (j r g)"), in_=pt[:, :]),
                         reads=[pb], writes=[BHs])
                barrier([BSn])
            k.es = esM
            s5_front(True, U2, BU2, VH, BV)
            k.es = esM
            k.op("act", lambda e: e.activation(out=Hsb[:], in_=Hsin[:], func=AF.Copy), reads=[BHs], writes=[BHs])
            k.op("act", lambda e: e.activation(out=VH[:, 0, :, :], in_=Hrun[:], func=AF.Copy), reads=[BH], writes=[BHh])
            s5_recur2_hist(VH, BV, BHh)
            k.es = esM
            with ExitStack() as es5:
                k.es = es5
                Hso = sb("Hso", [128, NSEQ, 2, 32], F32); BHo = Buf("Hso")
                t16b = sb("t16b", [128, NSEQ, 2, 32], F32); Bt16 = Buf("t16")
                arb16 = ARR[:].unsqueeze(1).unsqueeze(1).to_broadcast([128, NSEQ, 2, 32])
                k.op("dve", lambda e: e.tensor_tensor(out=Hso[:], in0=Hsin[:], in1=arb16, op=ALU.mult), reads=[BHs, Bs5], writes=[BHo])
                k.op("dve", lambda e: e.tensor_tensor(out=t16b[:, :, 0, :], in0=Hsin[:, :, 1, :], in1=AIn[:].unsqueeze(1).to_broadcast([128, NSEQ, 32]), op=ALU.mult),
                     reads=[BHs, Bs5], writes=[Bt16])
                k.op("dve", lambda e: e.tensor_tensor(out=t16b[:, :, 1, :], in0=Hsin[:, :, 0, :], in1=AIp[:].unsqueeze(1).to_broadcast([128, NSEQ, 32]), op=ALU.mult),
                     reads=[BHs, Bs5], writes=[Bt16])
                k.op("dve", lambda e: e.tensor_tensor(out=Hso[:], in0=Hso[:], in1=t16b[:], op=ALU.add), reads=[Bt16, BHo], writes=[BHo])
                k.op("dve", lambda e: e.tensor_tensor(out=Hso[:], in0=Hso[:], in1=VH[:, 129:129 + NSEQ, :, :], op=ALU.add), reads=[BHo, BV], writes=[BHo])
                So = [sb("So%d" % i, [32, 2, 2, 128], F32) for i in range(2)]; BSo = [Buf("So%d" % i) for i in range(2)]
                for bi, q0 in enumerate(range(0, NSEQ + 1, 2)):
                    a = bi % 2
                    pt, pb = bank()
                    nq = min(2, NSEQ + 1 - q0)
                    for qi in range(nq):
                        q = q0 + qi
                        for ri in range(2):
                            src = Hso[:, q, ri, :] if q < NSEQ else Hrun[:, ri, :]
                            col = (qi * 2 + ri) * 128
                            k.op("pe", lambda e, src=src, col=col, pt=pt: e.transpose(out=pt[0:32, col:col + 128], in_=src, identity=identf[:]),
                                 reads=[BHo, BH, Bc], writes=[pb])
                    k.op("dve", lambda e, a=a, nq=nq, pt=pt: e.tensor_copy(out=So[a][:, 0:nq, :, :].rearrange("g j r q -> g (j r q)"), in_=pt[0:32, 0:nq * 256]),
                         reads=[pb], writes=[BSo[a]])
                    if q0 < NSEQ:
                        for ri, dst in enumerate((ssm_re_s, ssm_im_s)):
                            for a_ in range(2):
                                k.dma("sp", dst[q0:q0 + 2, a_ * 32:(a_ + 1) * 32, :].rearrange("j g p -> g j p"), So[a][:, :, ri, a_ * 64:(a_ + 1) * 64],
                                      reads=[BSo[a]], sbuf=BSo[a])
                    else:
                        for ri, dst in enumerate((ssm_re_p, ssm_im_p)):
                            k.dma("sp", dst.rearrange("(a g) p -> g a p", a=2), So[a][:, 0, ri, :].rearrange("g (a p) -> g a p", a=2), reads=[BSo[a]], sbuf=BSo[a])
                outs_wait.extend(BSo)
                barrier([BHo, Bt16] + BSo)
            k.es = esM
            with ExitStack() as es5:
                k.es = es5
                Yc = [sb("Yc%d" % i, [128, 8, 128], BF16) for i in range(2)]; BYc = [Buf("Yc%d" % i) for i in range(2)]
                Ycs = [sb("Ycs%d" % i, [NSEQ, 8, 128], BF16) for i in range(2)]; BYcs = [Buf("Ycs%d" % i) for i in range(2)]
                for ct in range(8):
                    for ci, (toff, C, coff) in enumerate(((0, 128, 0), (TP, NSEQ, 128))):
                        yc, byc = (Yc[ct % 2], BYc[ct % 2]) if ci == 0 else (Ycs[ct % 2], BYcs[ct % 2])
                        for gq in range(2):
                            pt, pb = bank()
                            for gi in range(4):
                                g = ct * 8 + gq * 4 + gi
                                gh, gp = g // 32, g % 32
                                rows = slice(gh * 64, gh * 64 + 64)
                                out = pt[0:C, gi * 128:(gi + 1) * 128]
                                k.op("pe", lambda e, out=out, g=g, C=C, coff=coff: e.matmul(out, lhsT=U2[:, g, coff:coff + C], rhs=Toep[:, g, :], start=True, stop=False),
                                     reads=[BU2[g // 8], Bs5], writes=[pb])
                                for ri, YY in enumerate((Ybr, Ybn)):
                                    hp = VH[rows, 0:128, ri, gp] if ci == 0 else Hsb[rows, :, ri, gp]
                                    k.op("pe", lambda e, out=out, hp=hp, YY=YY, rows=rows, gp=gp, ri=ri: e.matmul(out, lhsT=hp, rhs=YY[rows, gp, 1:9, :], start=False, stop=(ri == 1)),
                                         reads=[BHh, BHs, Bs5], writes=[pb])
                            k.op("act", lambda e, pt=pt, C=C, yc=yc, gq=gq: e.activation(
                                out=yc[0:C, :, gq * 64:(gq + 1) * 64].rearrange("c t (g h) -> c g t h", h=16),
                                in_=pt[0:C, :].rearrange("c (g t h) -> c g t h", g=4, h=16), func=AF.Gelu_apprx_tanh), reads=[pb], writes=[byc])
                        pt, pb = bank()
                        ptb = pt[:].bitcast(BF16)
                        for t in range(8):
                            k.op("pe", lambda e, t=t, C=C, yc=yc, ptb=ptb: e.transpose(out=ptb[:, t * 128:t * 128 + C], in_=yc[0:C, t, :], identity=identb[0:C, 0:C]),
                                 reads=[byc, Bc], writes=[pb])
                        k.op("act", lambda e, C=C, toff=toff, ct=ct, ptb=ptb: e.activation(
                            out=ay[:, ct, toff:toff + 8 * C].rearrange("p (c t) -> p t c", t=8),
                            in_=ptb[:, 0:1024].rearrange("p (t c) -> p t c", t=8)[:, :, 0:C], func=AF.Copy), reads=[pb], writes=[Bay])
                barrier(BYc + BYcs)
            k.es = esM
            dump("ay", ay[:], [128, 8, TM], [Bay], BF16)
            dump("Hrun_main", Hrun[:], [128, 2, 32], [BH])
            barrier([BV, BHh, BHs, Bdump] + BU2 + [Bs5] + psb)
            stage_end(7)
        esW.close()
        k.es = es
        by = sb("by", [128, 8, TM], BF16, side="right"); Bby = Buf("by")
        gla_pass(True, by, Bby)
        k.es = es
        dump("by", by[:], [128, 8, TM], [Bby], BF16)
        dump("Sgl_main", Sgl[:], [128, NH, DV], [BS])
        stage_end(8)
        k.dma("sp", gla_p.rearrange("h k v -> k h v"), Sgl[:], reads=[BS], sbuf=BS)
        outs_wait.append(BS)

        T = TM
        nblocks = [(n0, min(512, T - n0)) for n0 in range(0, T, 512)]
        hall = hTb[0:9]
        mg = sb("mg", [128, KT, TM], BF16, side="right"); Bmg = [Buf("mg%d" % j) for j in range(9)]
        By_d = [Buf("ydram%d" % j) for j in range(9)]
        with ExitStack() as esC:
            k.es = esC
            sg = [sb("sg%d" % i, [128, 512], F32) for i in range(2)]; Bsg = [Buf("sg%d" % i) for i in range(2)]
            m1 = [sb("m1_%d" % i, [128, 512], F32) for i in range(2)]; Bm1 = [Buf("m1_%d" % i) for i in range(2)]
            rot = [0]
            ay2 = sb("ay2", [128, 8, TM], BF16); Bay2 = Buf("ay2")
            for i in range(4):
                ws, wb = w_next((w_glu, i * WC))
                for mt in range(2):
                    m = i * 2 + mt
                    for (n0, nn) in nblocks:
                        pt, pb = bank()
                        for kt in range(8):
                            k.op("pe", lambda e, kt=kt, n0=n0, nn=nn, mt=mt, ws=ws, pt=pt: e.matmul(
                                pt[:, 0:nn], lhsT=ws[:, kt, mt * 128:(mt + 1) * 128], rhs=ay[:, kt, n0:n0 + nn], start=(kt == 0), stop=(kt == 7)),
                                reads=[Bay, wb], writes=[pb])
                        a = rot[0] % 2; rot[0] += 1
                        k.op("act", lambda e, a=a, nn=nn, m=m, pt=pt: e.activation(out=sg[a][:, 0:nn], in_=pt[:, 0:nn], func=AF.Sigmoid, bias=bglu_col[:, m:m + 1]),
                             reads=[pb, Bp], writes=[Bsg[a]])
                        k.op("dve", lambda e, a=a, n0=n0, nn=nn, m=m: e.tensor_tensor(out=ay2[:, m, n0:n0 + nn], in0=ay[:, m, n0:n0 + nn], in1=sg[a][:, 0:nn], op=ALU.mult),
                             reads=[Bay, Bsg[a]], writes=[Bay2])
            for (c00, tgt, Btgt) in ((1024, ay2, Bay2), (4096, by, Bby)):
                for i in range(4):
                    ws, wb = w_next((w_in, c00 + i * WC))
                    for mt in range(2):
                        m = i * 2 + mt
                        for (n0, nn) in nblocks:
                            pt, pb = bank()
                            for kt in range(KT):
                                k.op("pe", lambda e, kt=kt, n0=n0, nn=nn, mt=mt, ws=ws, pt=pt: e.matmul(
                                    pt[:, 0:nn], lhsT=ws[:, kt, mt * 128:(mt + 1) * 128], rhs=hT[:, kt, n0:n0 + nn], start=(kt == 0), stop=(kt == KT - 1)),
                                    reads=hall + [wb], writes=[pb])
                            a = rot[0] % 2; rot[0] += 1
                            k.op("act", lambda e, a=a, nn=nn, pt=pt: e.activation(out=sg[a][:, 0:nn], in_=pt[:, 0:nn], func=AF.Silu), reads=[pb], writes=[Bsg[a]])
                            if tgt is by:
                                k.op("dve", lambda e, a=a, n0=n0, nn=nn, m=m, tgt=tgt: e.scalar_tensor_tensor(out=tgt[:, m, n0:n0 + nn], in0=tgt[:, m, n0:n0 + nn], scalar=ggain_col[:, m:m + 1],
                                                                                                               in1=sg[a][:, 0:nn], op0=ALU.mult, op1=ALU.mult),
                                     reads=[Btgt, Bsg[a], Bp], writes=[Btgt])
                            else:
                                k.op("dve", lambda e, a=a, n0=n0, nn=nn, m=m, tgt=tgt: e.tensor_tensor(out=tgt[:, m, n0:n0 + nn], in0=tgt[:, m, n0:n0 + nn], in1=sg[a][:, 0:nn], op=ALU.mult),
                                     reads=[Btgt, Bsg[a]], writes=[Btgt])
            m1A = sb("m1A", [128, 2, TM], F32); Bm1A = Buf("m1A")
            for fb in range(8):
                for br, (cg, wo_d, src, Bsrc) in enumerate(((5136, w_a_out, ay2, Bay2), (7184, w_b_out, by, Bby))):
                    wg, wgb = w_next((w_in, cg + fb * WC))
                    wo, wob = w_next((wo_d, fb * WC))
                    for mt in range(2):
                        f = fb * 2 + mt
                        for (n0, nn) in nblocks:
                            pg, pgb = bank()
                            for kt in range(KT):
                                k.op("pe", lambda e, kt=kt, pg=pg, wg=wg, mt=mt, n0=n0, nn=nn: e.matmul(
                                    pg[:, 0:nn], lhsT=wg[:, kt, mt * 128:(mt + 1) * 128], rhs=hT[:, kt, n0:n0 + nn],
                                    start=(kt == 0), stop=(kt == KT - 1)), reads=hall + [wgb], writes=[pgb])
                            po, pob = bank()
                            for kt in range(8):
                                k.op("pe", lambda e, kt=kt, po=po, wo=wo, src=src, mt=mt, n0=n0, nn=nn: e.matmul(
                                    po[:, 0:nn], lhsT=wo[:, kt, mt * 128:(mt + 1) * 128], rhs=src[:, kt, n0:n0 + nn],
                                    start=(kt == 0), stop=(kt == 7)), reads=[Bsrc, wob], writes=[pob])
                            a = rot[0] % 2; rot[0] += 1
                            k.op("act", lambda e, a=a, pg=pg, nn=nn: e.activation(out=sg[a][:, 0:nn], in_=pg[:, 0:nn], func=AF.Sigmoid), reads=[pgb], writes=[Bsg[a]])
                            if br == 0:
                                k.op("dve", lambda e, a=a, po=po, mt=mt, n0=n0, nn=nn: e.tensor_tensor(out=m1A[:, mt, n0:n0 + nn], in0=po[:, 0:nn], in1=sg[a][:, 0:nn], op=ALU.mult),
                                     reads=[pob, Bsg[a]], writes=[Bm1A])
                            else:
                                k.op("dve", lambda e, a=a, po=po, nn=nn: e.tensor_tensor(out=m1[a][:, 0:nn], in0=po[:, 0:nn], in1=sg[a][:, 0:nn], op=ALU.mult),
                                     reads=[pob, Bsg[a]], writes=[Bm1[a]])
                                tiles = list(range(n0 // 128, (n0 + nn) // 128))
                                k.op("dve", lambda e, a=a, f=f, mt=mt, n0=n0, nn=nn: e.tensor_tensor(out=mg[:, f, n0:n0 + nn], in0=m1[a][:, 0:nn], in1=m1A[:, mt, n0:n0 + nn], op=ALU.add),
                                     reads=[Bm1[a], Bm1A], writes=[Bmg[t_] for t_ in tiles])
            dump("ay2", ay2[:], [128, 8, TM], [Bay2], BF16)
            dump("by2", by[:], [128, 8, TM], [Bby], BF16)
            dump("mg", mg[:], [128, KT, TM], Bmg, BF16)
            barrier(Bsg + Bm1 + [Bm1A, Bay2, Bay, Bby, Bdump] + hall + psb)
            stage_end(9)
        with ExitStack() as esO:
            k.es = esO
            gts = sb("gts", [17, D], F32); Bgts = Buf("gts")
            k.dma("sp", gts[:], gate_scr[:, :], reads=[Bgs], writes=[Bgts])
            gtk = [sb("gtk%d" % i, [128, WC], F32) for i in range(2)]; Bgtk = [Buf("gtk%d" % i) for i in range(2)]
            fgb = sb("fgb", [128, D], F32); Bfg = Buf("fgb")
            k.dma("sp", fgb[:], fgain.partition_broadcast(128), writes=[Bfg])
            yacc = hT[:].rearrange("p a b -> p (a b)").rearrange("p (j f) -> p j f", j=9)
            Bya = [Buf("yacc%d" % j) for j in range(9)]
            xs = [sb("fxs%d" % i, [128, D], F32) for i in range(2)]; xb_ = [Buf("fxs%d" % i) for i in range(2)]
            jk = sb("fjk", [128, D], BF16); Bjk = Buf("fjk")
            st = sb("fst", [128, 9, 4], F32); Bst = [Buf("fst%d" % j_) for j_ in range(9)]
            for j in range(2):
                k.dma("sp", xs[j][:], xmain[j * 128:(j + 1) * 128, :], writes=[xb_[j]])

            def final_tile(j):
                a = j % 2
                dst = y_main[j * 128:(j + 1) * 128, :] if j < 8 else y_smp[:, :]
                k.op("dve", lambda e: e.tensor_tensor(out=xs[a][:], in0=xs[a][:], in1=yacc[:, j, :], op=ALU.add), reads=[xb_[a], Bya[j]], writes=[xb_[a]])
                k.op("act", lambda e: e.activation(out=jk[:], in_=xs[a][:], func=AF.Square, accum_out=st[:, j, 0:1]), reads=[xb_[a]], writes=[Bjk, Bst[j]])
                k.op("act", lambda e: e.activation(out=st[:, j, 2:3], in_=st[:, j, 0:1], func=AF.Ln, scale=1.0 / D, bias=epsb[:, 0:1]), reads=[Bst[j], Bc], writes=[Bst[j]])
                k.op("act", lambda e: e.activation(out=st[:, j, 3:4], in_=st[:, j, 2:3], func=AF.Exp, scale=-0.5), reads=[Bst[j]], writes=[Bst[j]])
                k.op("dve", lambda e: e.scalar_tensor_tensor(out=xs[a][:], in0=xs[a][:], scalar=st[:, j, 3:4], in1=fgb[:], op0=ALU.mult, op1=ALU.mult),
                     reads=[xb_[a], Bst[j], Bfg], writes=[xb_[a]])
                k.dma("sp", dst, xs[a][:], reads=[xb_[a]], writes=[By_d[j]], sbuf=xb_[a])
                if j + 2 < 9:
                    jn = j + 2
                    srcn = xmain[jn * 128:(jn + 1) * 128, :] if jn < 8 else xsmp[:, :]
                    k.dma("sp", xs[a][:], srcn, writes=[xb_[a]])

            for i in range(8):
                ws, wb = w_next((w_out, i * WC))
                for wi, oh in enumerate((ohP, ohS)):
                    pt, pb = bank()
                    k.op("pe", lambda e, oh=oh, i=i, pt=pt: e.matmul(pt[:, 0:WC], lhsT=oh[:, :], rhs=gts[:, i * WC:(i + 1) * WC], start=True, stop=True),
                         reads=[Bgts, Bc], writes=[pb])
                    k.op("act", lambda e, wi=wi, pt=pt: e.activation(out=gtk[wi][:], in_=pt[:, 0:WC], func=AF.Copy), reads=[pb], writes=[Bgtk[wi]])
                for j in range(9):
                    pt, pb = bank()
                    for kt in range(KT):
                        k.op("pe", lambda e, kt=kt, j=j, ws=ws, pt=pt: e.matmul(pt[:, 0:WC], lhsT=mg[:, kt, j * 128:(j + 1) * 128], rhs=ws[:, kt, :],
                                                                               start=(kt == 0), stop=(kt == KT - 1)), reads=[Bmg[j], wb], writes=[pb])
                    wi = 0 if j < 8 else 1
                    k.op("dve", lambda e, j=j, i=i, wi=wi, pt=pt: e.tensor_tensor(out=yacc[:, j, i * WC:(i + 1) * WC], in0=pt[:, 0:WC], in1=gtk[wi][:], op=ALU.mult),
                         reads=[pb, Bgtk[wi]], writes=[Bya[j]])
                    if i == 7:
                        final_tile(j)
            barrier(outs_wait + By_d + xb_ + Bya + [Bfg, Bgts, Bjk] + Bst + Bgtk + Bmg + psb)
            stage_end(10)
        esR.close()
        k.es = es
    except _Stop:
        pass
    return nc


_NC_CACHE = {}


def _shard_inputs(inp):
    f = lambda a: np.ascontiguousarray(np.asarray(a, dtype=np.float32))
    xp = f(inp["x_prompt"]); xs = f(inp["x_sample"]); cp = f(inp["c_prompt"]); cs = f(inp["c_sample"])
    sre = f(inp["state_ssm_re"])[0]; sim = f(inp["state_ssm_im"])[0]; sgl = f(inp["state_gla"])[0]
    shared = {
        "w_ada": f(inp["w_ada"])[0], "b_ada": f(inp["b_ada"])[0], "norm_gain": f(inp["norm_gain"])[0], "w_in": f(inp["w_in"])[0],
        "lambda_re": f(inp["lambda_re"])[0], "lambda_im": f(inp["lambda_im"])[0], "log_dt": f(inp["log_dt"])[0],
        "ssm_b_re": f(inp["ssm_b_re"])[0], "ssm_b_im": f(inp["ssm_b_im"])[0], "ssm_c_re": f(inp["ssm_c_re"])[0], "ssm_c_im": f(inp["ssm_c_im"])[0],
        "d_skip": f(inp["d_skip"])[0], "w_glu": f(inp["w_glu"])[0], "b_glu": f(inp["b_glu"])[0], "w_gate_up": f(inp["w_gate_up"])[0],
        "b_gate": f(inp["b_gate"])[0], "gla_norm_gain": f(inp["gla_norm_gain"])[0], "w_a_out": f(inp["w_a_out"])[0],
        "w_b_out": f(inp["w_b_out"])[0], "w_out": f(inp["w_out"])[0], "final_norm_gain": f(inp["final_norm_gain"]),
    }
    maps = []
    for core in range(8):
        b, half = core // 2, core % 2
        m = dict(shared)
        m["xpre"] = np.ascontiguousarray(xp[b, 0:TP])
        m["xmain"] = np.ascontiguousarray(xp[b, half * TP:(half + 1) * TP])
        sl = slice(core * NSEQ, (core + 1) * NSEQ)
        m["xsmp"] = np.ascontiguousarray(xs[sl].reshape(NSEQ * 8, D))
        m["c17"] = np.ascontiguousarray(np.concatenate([cp[b:b + 1], cs[sl]], axis=0))
        m["flag"] = np.full((128, 1), float(half), np.float32)
        m["ssm_re_in"] = np.ascontiguousarray(sre[sl]); m["ssm_im_in"] = np.ascontiguousarray(sim[sl])
        m["gla_in"] = np.ascontiguousarray(sgl[sl])
        maps.append(m)
    return maps


def run_debug(inputs, stop, core=1):
    nc = build_nc(dbg=True, stop=stop)
    maps = _shard_inputs(inputs)
    res = run_bass_kernel_spmd(nc, [maps[core]], core_ids=[0])
    return res.results[0], maps[core]


def kernel(**inputs):
    if "nc" not in _NC_CACHE:
        _NC_CACHE["nc"] = build_nc()
    nc = _NC_CACHE["nc"]
    maps = _shard_inputs(inputs)
    res = run_bass_kernel_spmd(nc, maps, core_ids=list(range(8)))
    R = res.results
    y_prompt = np.zeros((4, 2048, D), np.float32); y_sample = np.zeros((128, 8, D), np.float32)
    re_p = np.zeros((1, 4, NG, 64), np.float32); im_p = np.zeros((1, 4, NG, 64), np.float32)
    gl_p = np.zeros((1, 4, NH, DK, DV), np.float32)
    re_s = np.zeros((1, 128, NG, 64), np.float32); im_s = np.zeros((1, 128, NG, 64), np.float32)
    gl_s = np.zeros((1, 128, NH, DK, DV), np.float32)
    for core in range(8):
        b, half = core // 2, core % 2
        r = R[core]
        y_prompt[b, half * TP:(half + 1) * TP] = r["y_main"]
        sl = slice(core * NSEQ, (core + 1) * NSEQ)
        y_sample[sl] = r["y_smp"].reshape(NSEQ, 8, D)
        re_s[0, sl] = r["ssm_re_s"]; im_s[0, sl] = r["ssm_im_s"]; gl_s[0, sl] = r["gla_s"]
        if half == 1:
            re_p[0, b] = r["ssm_re_p"]; im_p[0, b] = r["ssm_im_p"]; gl_p[0, b] = r["gla_p"]
    return (y_prompt, y_sample, re_p, im_p, gl_p, re_s, im_s, gl_s)
```

```python
import math
from contextlib import ExitStack

import numpy as np
import concourse.bass as bass
import concourse.mybir as mybir
from concourse.bass_utils import run_bass_kernel_spmd

F32 = mybir.dt.float32
BF16 = mybir.dt.bfloat16
AF = mybir.ActivationFunctionType
ALU = mybir.AluOpType

D = 2048
KT = 16
WA = 1024
NG = 64
NH = 4
DK = 128
DV = 256
EPS = 1e-6
INC = 9232
NSEQ = 16
TP = 1024
TM = TP + 128
WC = 256
NSLOT = 3
PI = math.pi


class Ev:
    __slots__ = ("sem", "val", "eng")

    def __init__(self, sem, val, eng):
        self.sem, self.val, self.eng = sem, val, eng


class Buf:
    def __init__(self, name):
        self.name = name
        self.w = None
        self.r = {}
        self.dsem = None
        self.dcnt = 0


class K:
    def __init__(self, nc, es):
        self.nc, self.es = nc, es
        self.E = {"pe": nc.tensor, "dve": nc.vector, "act": nc.scalar, "pool": nc.gpsimd, "sp": nc.sync}
        self.sem = {e: es.enter_context(nc.semaphore("s_" + e)) for e in ("pe", "dve", "act", "pool")}
        self.cnt = {e: 0 for e in self.sem}
        self.seen = {e: {} for e in self.E}
        self.nbuf = 0
        self.esR = es
        self.es_sem = es

    def sb(self, name, shape, dt, side="left"):
        st = self.es if side == "left" else self.esR
        self.nname = getattr(self, "nname", 0) + 1
        return st.enter_context(self.nc.sbuf_tensor("sb%d_%s" % (self.nname, name), shape, dt, side=side))

    def _deps(self, reads, writes):
        evs = []
        for b in reads:
            if b.w is not None:
                evs.append(b.w)
        for b in writes:
            if b.w is not None:
                evs.append(b.w)
            evs.extend(b.r.values())
        return evs

    def _waits(self, e, evs):
        best = {}
        for ev in evs:
            if ev.eng == "pe" and e == "pe":
                continue
            kk = id(ev.sem)
            if kk not in best or best[kk].val < ev.val:
                best[kk] = ev
        for kk, ev in best.items():
            if self.seen[e].get(kk, 0) >= ev.val:
                continue
            self.E[e].wait_ge(ev.sem, ev.val)
            self.seen[e][kk] = ev.val

    def _commit(self, ev, reads, writes):
        for b in writes:
            b.w = ev
            b.r = {}
        for b in reads:
            if b not in writes:
                kk = id(ev.sem)
                if kk not in b.r or b.r[kk].val < ev.val:
                    b.r[kk] = ev

    def op(self, e, fn, reads=(), writes=()):
        self._waits(e, self._deps(reads, writes))
        ins = fn(self.E[e])
        self.cnt[e] += 1
        ins.then_inc(self.sem[e], 1)
        ev = Ev(self.sem[e], self.cnt[e], e)
        self._commit(ev, reads, writes)
        return ev

    def dma(self, q, out, in_, reads=(), writes=(), sbuf=None, **kw):
        self._waits(q, self._deps(reads, writes))
        tb = sbuf if sbuf is not None else (writes[0] if writes else reads[0])
        if tb.dsem is None:
            self.nbuf += 1
            tb.dsem = self.es_sem.enter_context(self.nc.semaphore("d%d" % self.nbuf))
        tb.dcnt += 16
        self.E[q].dma_start(out=out, in_=in_, **kw).then_inc(tb.dsem, 16)
        ev = Ev(tb.dsem, tb.dcnt, "dma")
        self._commit(ev, reads, writes)
        return ev

    def wait_all(self, e, bufs):
        evs = []
        for b in bufs:
            if b.w is not None:
                evs.append(b.w)
            evs.extend(b.r.values())
        self._waits(e, evs)


class _Stop(Exception):
    pass


def build_nc(dbg=None, stop=None):
    nc = bass.Bass("TRN2", target_bir_lowering=False)
    dumps = []

    def din(name, shape):
        return nc.dram_tensor(name, list(shape), F32, kind="ExternalInput").ap()

    def dout(name, shape):
        return nc.dram_tensor(name, list(shape), F32, kind="ExternalOutput").ap()

    xpre = din("xpre", [TP, D]); xmain = din("xmain", [TP, D]); xsmp = din("xsmp", [128, D])
    c17 = din("c17", [17, D]); flag_d = din("flag", [128, 1])
    ssm_re_in = din("ssm_re_in", [NSEQ, NG, 64]); ssm_im_in = din("ssm_im_in", [NSEQ, NG, 64])
    gla_in = din("gla_in", [NSEQ, NH, DK, DV])
    w_ada = din("w_ada", [D, 3 * D]); b_ada = din("b_ada", [3 * D]); norm_gain = din("norm_gain", [D])
    w_in = din("w_in", [D, INC])
    lam_re = din("lambda_re", [NG, 64]); lam_im = din("lambda_im", [NG, 64]); log_dt = din("log_dt", [NG])
    b_re = din("ssm_b_re", [NG, 64, 16]); b_im = din("ssm_b_im", [NG, 64, 16])
    c_re = din("ssm_c_re", [NG, 16, 64]); c_im = din("ssm_c_im", [NG, 16, 64])
    d_skip = din("d_skip", [WA]); w_glu = din("w_glu", [WA, WA]); b_glu = din("b_glu", [WA])
    w_gate_up = din("w_gate_up", [16, NH * DK]); b_gate = din("b_gate", [NH * DK])
    gla_gain = din("gla_norm_gain", [WA])
    w_a_out = din("w_a_out", [WA, D]); w_b_out = din("w_b_out", [WA, D]); w_out = din("w_out", [D, D])
    fgain = din("final_norm_gain", [D])

    y_main = dout("y_main", [TP, D]); y_smp = dout("y_smp", [128, D])
    ssm_re_p = dout("ssm_re_p", [NG, 64]); ssm_im_p = dout("ssm_im_p", [NG, 64])
    gla_p = dout("gla_p", [NH, DK, DV])
    ssm_re_s = dout("ssm_re_s", [NSEQ, NG, 64]); ssm_im_s = dout("ssm_im_s", [NSEQ, NG, 64])
    gla_s = dout("gla_s", [NSEQ, NH, DK, DV])
    gate_scr = nc.dram_tensor("gate_scr", [17, D], F32, kind="Internal").ap()
    es = ExitStack()
    try:
      with es:
        k = K(nc, es)
        sb = k.sb
        Bdump = Buf("dump")

        def dump(name, ap, shape, bufs, dt=F32):
            if not dbg:
                return
            d = nc.dram_tensor("dbg_" + name, list(shape), dt, kind="ExternalOutput").ap()
            k.dma("sp", d, ap, reads=list(bufs), writes=[Bdump])
            dumps.append(name)

        def stage_end(n):
            if stop is not None and n >= stop:
                for e_ in ("pe", "dve", "act", "pool", "sp"):
                    k.wait_all(e_, [Bdump])
                raise _Stop()
        psum = [es.enter_context(nc.psum_tensor("ps%d" % i, [128, 512], F32)) for i in range(8)]
        psb = [Buf("ps%d" % i) for i in range(8)]
        pcur = [0]
        held = set()

        def bank(hold=False):
            i = pcur[0]
            while i in held:
                i = (i + 1) % 8
            pcur[0] = (i + 1) % 8
            if hold:
                held.add(i)
            return psum[i], psb[i]

        def unhold(pbuf):
            held.discard(psb.index(pbuf))

        Bc = Buf("consts")
        identf = sb("identf", [128, 128], F32); identb = sb("identb", [128, 128], BF16)
        onesb = sb("onesb", [128, 128], BF16); onesf = sb("onesf", [128, 128], F32)
        causal = sb("causal", [128, 128], F32); seqmask = sb("seqmask", [128, NSEQ], F32)
        smask = sb("smask", [128, 128], F32); rstmask = sb("rstmask", [128, 128], F32)
        mask3 = sb("mask3", [128, 8], F32)
        ohP = sb("ohP", [17, 128], F32); ohS = sb("ohS", [17, 128], F32)
        P = lambda fn: k.op("pool", fn, writes=[Bc])
        P(lambda e: e.memset(identf[:], 1.0))
        P(lambda e: e.affine_select(out=identf[:], in_=identf[:], compare_op=ALU.is_equal, fill=0.0, base=0,
                                    pattern=[[-1, 128]], channel_multiplier=1))
        P(lambda e: e.tensor_copy(out=identb[:], in_=identf[:]))
        P(lambda e: e.memset(onesb[:], 1.0))
        P(lambda e: e.memset(onesf[:], 1.0))
        P(lambda e: e.memset(causal[:], 1.0))
        P(lambda e: e.affine_select(out=causal[:], in_=causal[:], compare_op=ALU.is_ge, fill=0.0, base=0,
                                    pattern=[[1, 128]], channel_multiplier=-1))
        P(lambda e: e.memset(seqmask[:], 1.0))
        P(lambda e: e.affine_select(out=seqmask[:], in_=seqmask[:], compare_op=ALU.is_ge, fill=0.0, base=0,
                                    pattern=[[-8, NSEQ]], channel_multiplier=1))
        P(lambda e: e.affine_select(out=seqmask[:], in_=seqmask[:], compare_op=ALU.is_ge, fill=0.0, base=7,
                                    pattern=[[8, NSEQ]], channel_multiplier=-1))
        P(lambda e: e.tensor_tensor(out=smask[:].rearrange("p (j r) -> p j r", r=8),
                                    in0=causal[:].rearrange("p (j r) -> p j r", r=8),
                                    in1=seqmask[:].unsqueeze(2).to_broadcast([128, NSEQ, 8]), op=ALU.mult))
        P(lambda e: e.memset(rstmask[:], 1.0))
        P(lambda e: e.affine_select(out=rstmask[:].rearrange("p (j r) -> p j r", r=8),
                                    in_=rstmask[:].rearrange("p (j r) -> p j r", r=8),
                                    compare_op=ALU.is_ge, fill=0.0, base=-1,
                                    pattern=[[0, NSEQ], [1, 8]], channel_multiplier=0))
        P(lambda e: e.memset(mask3[:], 1.0))
        P(lambda e: e.affine_select(out=mask3[:], in_=mask3[:], compare_op=ALU.is_ge, fill=0.0, base=15,
                                    pattern=[[16, 8]], channel_multiplier=-1))
        P(lambda e: e.memset(ohP[:], 0.0))
        P(lambda e: e.memset(ohP[0:1, :], 1.0))
        P(lambda e: e.memset(ohS[:], 1.0))
        P(lambda e: e.affine_select(out=ohS[:], in_=ohS[:], compare_op=ALU.is_ge, fill=0.0, base=8,
                                    pattern=[[1, 128]], channel_multiplier=-8))
        P(lambda e: e.affine_select(out=ohS[:], in_=ohS[:], compare_op=ALU.is_ge, fill=0.0, base=-1,
                                    pattern=[[-1, 128]], channel_multiplier=8))

        hrst = sb("hrst", [128, NH, 128], F32); hrst_s = sb("hrst_s", [128, NH, 128], F32)
        P(lambda e: e.memset(hrst[:], 1.0))
        for hd_ in range(NH):
            P(lambda e, hd_=hd_: e.memset(hrst[:, hd_, 0:1], 0.0))
        P(lambda e: e.tensor_copy(out=hrst_s[:], in_=rstmask[:].unsqueeze(1).to_broadcast([128, NH, 128])))
        Bp = Buf("params")
        bglu_col = sb("bglu_col", [128, 8], F32); bgate_col = sb("bgate_col", [128, 4], F32)
        nbgate = sb("nbgate", [128, 4], F32); ggain_col = sb("ggain_col", [128, 8], F32)
        wup = sb("wup", [17, NH * DK], BF16); flag = sb("flag", [128, 1], F32)
        k.dma("sp", bglu_col[:], b_glu.rearrange("(m p) -> p m", p=128), writes=[Bp], allow_slow_non_contiguous=True)
        k.dma("sp", bgate_col[:], b_gate.rearrange("(m p) -> p m", p=128), writes=[Bp], allow_slow_non_contiguous=True)
        k.dma("sp", ggain_col[:], gla_gain.rearrange("(m p) -> p m", p=128), writes=[Bp], allow_slow_non_contiguous=True)
        k.dma("sp", flag[:], flag_d[:, :], writes=[Bp])
        Bwup = Buf("wup")
        k.dma("pool", wup[0:16, :], w_gate_up[:, :], writes=[Bwup])
        k.dma("pool", wup[16:17, :], b_gate.rearrange("(o n) -> o n", o=1), writes=[Bwup])
        k.op("dve", lambda e: e.tensor_scalar(out=nbgate[:], in0=bgate_col[:], scalar1=-1.0, scalar2=None, op0=ALU.mult),
             reads=[Bp], writes=[Bc])

        wslot = [sb("wslot%d" % i, [128, KT, WC], BF16) for i in range(NSLOT)]
        wbuf = [Buf("wslot%d" % i) for i in range(NSLOT)]
        specs = []

        def wspec(w, c0, ncol, kt):
            specs.append((w, c0, ncol, kt))

        for blk in range(24):
            wspec(w_ada, blk * WC, WC, KT)
        for full in (False, True):
            for i in range(4):
                wspec(w_in, i * WC, WC, KT)
            wspec(w_in, 5120, 16, KT)
            for i in range(2):
                wspec(w_in, 2560 + i * WC, WC, KT)
            for i in range(4):
                wspec(w_in, 3072 + i * WC, WC, KT)
            if full:
                for i in range(2):
                    wspec(w_in, 2048 + i * WC, WC, KT)
                for i in range(4):
                    wspec(w_glu, i * WC, WC, 8)
                for i in range(4):
                    wspec(w_in, 1024 + i * WC, WC, KT)
                for i in range(4):
                    wspec(w_in, 4096 + i * WC, WC, KT)
                for fb in range(8):
                    wspec(w_in, 5136 + fb * WC, WC, KT)
                    wspec(w_a_out, fb * WC, WC, 8)
                    wspec(w_in, 7184 + fb * WC, WC, KT)
                    wspec(w_b_out, fb * WC, WC, 8)
                for i in range(8):
                    wspec(w_out, i * WC, WC, KT)
        wstate = {"issued": 0, "used": 0}

        def w_issue():
            i = wstate["issued"]
            if i >= len(specs):
                return
            w, c0, ncol, kt = specs[i]
            s = i % NSLOT
            src = w.rearrange("(kt p) n -> p kt n", p=128)[:, :, c0:c0 + ncol]
            k.dma("pool", wslot[s][:, 0:kt, 0:ncol], src, writes=[wbuf[s]])
            wstate["issued"] = i + 1

        def w_next(expect=None):
            i = wstate["used"]
            if expect is not None:
                assert specs[i][0] is expect[0] and specs[i][1] == expect[1], (i, specs[i][1:], expect[1:])
            while wstate["issued"] < min(i + NSLOT - 1, len(specs)) or wstate["issued"] <= i:
                w_issue()
            wstate["used"] = i + 1
            s = i % NSLOT
            return wslot[s], wbuf[s]

        shiftT = sb("shiftT", [128, KT, 17], F32); scaleT = sb("scaleT", [128, KT, 17], F32)
        Bmod = Buf("modT")
        hT = sb("hT", [128, KT, TM], BF16)
        hTb = [Buf("hT%d" % j) for j in range(9)]
        Hrun = sb("Hrun", [128, 2, 32], F32); BH = Buf("Hrun")
        Sgl = sb("Sgl", [128, NH, DV], F32); Sglb = sb("Sglb", [128, NH, DV], BF16); BS = Buf("Sgl")
        epsb = sb("epsb", [128, 1], F32)
        k.op("pool", lambda e: e.memset(epsb[:], EPS), writes=[Bc])
        esR = ExitStack()
        k.esR = esR
        esW = ExitStack()
        k.es = esW
        Toep = sb("Toep", [128, NG, 128], BF16)
        Mw = sb("Mw", [128, NG, 2, 64], BF16)
        Ybr = sb("Ybr", [128, 32, 9, 16], BF16); Ybn = sb("Ybn", [128, 32, 9, 16], BF16)
        ARR = sb("ARR", [128, 32], F32); AIp = sb("AIp", [128, 32], F32); AIn = sb("AIn", [128, 32], F32)
        A64 = sb("A64", [128, 3, 32], F32)
        Bs5 = Buf("s5w")

        Bgs = Buf("gate_scr")

        def ada_emit():
            esA = ExitStack()
            k.esR = esA
            c17s = sb("c17s", [17, D], F32, side="right"); Bp2 = Buf("c17s")
            siluT = sb("siluT", [128, KT, 17], BF16, side="right"); Bsl = Buf("silu")
            brow = [sb("brow%d" % i, [1, WC], F32, side="right") for i in range(2)]; Bbr = [Buf("brow%d" % i) for i in range(2)]
            modr = [sb("modr%d" % i, [17, WC], F32, side="right") for i in range(2)]; Bmr = [Buf("modr%d" % i) for i in range(2)]
            k.dma("sp", c17s[:], c17[:, :], writes=[Bp2])
            k.op("act", lambda e: e.activation(out=c17s[:], in_=c17s[:], func=AF.Silu), reads=[Bp2], writes=[Bp2])
            pt, pb = bank()
            for kt in range(KT):
                k.op("pe", lambda e, kt=kt, pt=pt: e.transpose(out=pt[:, kt * 17:(kt + 1) * 17], in_=c17s[0:17, kt * 128:(kt + 1) * 128],
                                                               identity=identf[0:17, 0:17]), reads=[Bp2, Bc], writes=[pb])
            k.op("act", lambda e, pt=pt: e.activation(out=siluT[:].rearrange("p a b -> p (a b)"), in_=pt[:, 0:KT * 17], func=AF.Copy), reads=[pb], writes=[Bsl])
            for blk in range(24):
                a = blk % 2
                ws, wb = w_next((w_ada, blk * WC))
                k.dma("sp", brow[a][:], b_ada[blk * WC:(blk + 1) * WC].rearrange("(o n) -> o n", o=1), writes=[Bbr[a]])
                pt, pb = bank()
                for kt in range(KT):
                    k.op("pe", lambda e, kt=kt, ws=ws, pt=pt: e.matmul(pt[0:17, 0:WC], lhsT=siluT[:, kt, :], rhs=ws[:, kt, :],
                                                                      start=(kt == 0), stop=False), reads=[Bsl, wb], writes=[pb])
                k.op("pe", lambda e, a=a, pt=pt: e.matmul(pt[0:17, 0:WC], lhsT=onesf[0:1, 0:17], rhs=brow[a][0:1, :], start=False, stop=True),
                     reads=[Bbr[a], Bc], writes=[pb])
                which = blk // 8
                if which == 1:
                    k.op("act", lambda e, a=a, pt=pt: e.activation(out=modr[a][:], in_=pt[0:17, 0:WC], func=AF.Identity, bias=onesf[0:17, 0:1]), reads=[pb, Bc], writes=[Bmr[a]])
                else:
                    k.op("act", lambda e, a=a, pt=pt: e.activation(out=modr[a][:], in_=pt[0:17, 0:WC], func=AF.Copy), reads=[pb], writes=[Bmr[a]])
                if which == 2:
                    c0 = (blk - 16) * WC
                    k.dma("sp", gate_scr[:, c0:c0 + WC], modr[a][:], reads=[Bmr[a]], writes=[Bgs], sbuf=Bmr[a])
                else:
                    dst = shiftT if which == 0 else scaleT
                    ft0 = (blk % 8) * 2
                    pt2, pb2 = bank()
                    for hh in range(2):
                        k.op("pe", lambda e, a=a, hh=hh, pt2=pt2: e.transpose(out=pt2[:, hh * 17:(hh + 1) * 17], in_=modr[a][0:17, hh * 128:(hh + 1) * 128],
                                                                             identity=identf[0:17, 0:17]), reads=[Bmr[a], Bc], writes=[pb2])
                    k.op("act", lambda e, dst=dst, ft0=ft0, pt2=pt2: e.activation(out=dst[:, ft0:ft0 + 2, :].rearrange("p a b -> p (a b)"), in_=pt2[:, 0:34], func=AF.Copy),
                         reads=[pb2], writes=[Bmod])
            return esA, [Bp2, Bsl] + Bbr + Bmr

        with ExitStack() as es2:
            k.es = es2
            Bt = Buf("s5tmp")
            L2 = sb("L2", [32, 2, 128], F32)
            Cn = sb("Cn", [128, 2, 4, 128], F32)
            Bpk = sb("Bpk", [128, 2, 32, 16], F32)
            Cpk = sb("Cpk", [128, 2, 32, 16], F32)
            ldtb = sb("ldtb", [128, 32], F32)
            dpk = sb("dpk", [128, NG], F32)
            k.dma("sp", L2[:, 0, :].rearrange("g (a p) -> g a p", a=2), lam_re.rearrange("(a g) p -> g a p", a=2), writes=[Bt])
            k.dma("sp", L2[:, 1, :].rearrange("g (a p) -> g a p", a=2), lam_im.rearrange("(a g) p -> g a p", a=2), writes=[Bt])
            for ri, cc in enumerate((c_re, c_im)):
                for a_ in range(2):
                    k.dma("sp", Cn[:, ri, :, a_ * 64:(a_ + 1) * 64],
                          cc[a_ * 32:(a_ + 1) * 32, :, :].rearrange("(c gl) h p -> (gl h) c p", c=4), writes=[Bt])
            for ri, bb in enumerate((b_re, b_im)):
                for gh in range(2):
                    k.dma("sp", Bpk[gh * 64:(gh + 1) * 64, ri, :, :],
                          bb[gh * 32:(gh + 1) * 32, :, :].rearrange("g p h -> p g h"), writes=[Bt])
            for gh in range(2):
                k.dma("sp", ldtb[gh * 64:(gh + 1) * 64, :], log_dt[gh * 32:(gh + 1) * 32].partition_broadcast(64), writes=[Bt])
            for s in range(8):
                k.dma("sp", dpk[s * 16:(s + 1) * 16, :], d_skip.rearrange("(g h) -> h g", h=16), writes=[Bt],
                      allow_slow_non_contiguous=True)
            lrli = sb("lrli", [128, 2, 32], F32)
            pt, pb = bank()
            for ri in range(2):
                k.op("pe", lambda e, ri=ri: e.transpose(out=pt[:, ri * 32:(ri + 1) * 32], in_=L2[:, ri, :], identity=identf[0:32, 0:32]),
                     reads=[Bt, Bc], writes=[pb])
            k.op("dve", lambda e: e.tensor_copy(out=lrli[:].rearrange("p a g -> p (a g)"), in_=pt[:, 0:64]), reads=[pb], writes=[Bt])
            for ri in range(2):
                pt, pb = bank()
                for c4 in range(4):
                    k.op("pe", lambda e, ri=ri, c4=c4: e.transpose(out=pt[:, c4 * 128:(c4 + 1) * 128], in_=Cn[:, ri, c4, :], identity=identf[:]),
                         reads=[Bt, Bc], writes=[pb])
                k.op("dve", lambda e, ri=ri: e.tensor_copy(out=Cpk[:, ri, :, :].rearrange("p g h -> p (g h)"), in_=pt[:, :]),
                     reads=[pb], writes=[Bt])
            lr = lrli[:, 0, :]; li = lrli[:, 1, :]

            def V(fn):
                return k.op("dve", fn, reads=[Bt, Bc], writes=[Bt])

            def A(fn):
                return k.op("act", fn, reads=[Bt, Bc], writes=[Bt])

            sm = sb("s5sm", [128, 24, 32], F32)
            dt = sm[:, 0, :]; ldr = sm[:, 1, :]; th = sm[:, 2, :]; t0 = sm[:, 3, :]; t1 = sm[:, 4, :]
            kk1 = sm[:, 5, :]; thr = sm[:, 6, :]; thc = sm[:, 7, :]; s1 = sm[:, 8, :]; c1 = sm[:, 9, :]
            cfr = sm[:, 10, :]; cfi = sm[:, 11, :]; nr = sm[:, 12, :]; den = sm[:, 13, :]; t2 = sm[:, 14, :]
            A(lambda e: e.activation(out=dt, in_=ldtb[:], func=AF.Exp))
            V(lambda e: e.tensor_tensor(out=ldr, in0=lr, in1=dt, op=ALU.mult))
            V(lambda e: e.tensor_tensor(out=th, in0=li, in1=dt, op=ALU.mult))

            def range_reduce(dst, shift):
                V(lambda e: e.tensor_scalar(out=t0, in0=th, scalar1=float(shift), scalar2=None, op0=ALU.add))
                V(lambda e: e.memset(kk1, 0.0))
                for m in (1, 3, 5, 7):
                    V(lambda e, m=m: e.tensor_scalar(out=t1, in0=t0, scalar1=float(m * PI), scalar2=None, op0=ALU.is_gt))
                    V(lambda e: e.tensor_tensor(out=kk1, in0=kk1, in1=t1, op=ALU.add))
                V(lambda e: e.scalar_tensor_tensor(out=dst, in0=kk1, scalar=float(-2 * PI), in1=t0, op0=ALU.mult, op1=ALU.add))

            range_reduce(thr, 0.0)
            z = sm[:, 15, :]; z2 = sm[:, 16, :]; ps_ = sm[:, 17, :]; pc_ = sm[:, 18, :]
            V(lambda e: e.tensor_scalar(out=z, in0=thr, scalar1=0.125, scalar2=None, op0=ALU.mult))
            V(lambda e: e.tensor_tensor(out=z2, in0=z, in1=z, op=ALU.mult))
            V(lambda e: e.tensor_scalar(out=ps_, in0=z2, scalar1=-1.0 / 72.0, scalar2=1.0, op0=ALU.mult, op1=ALU.add))
            for dv_ in (42.0, 20.0, 6.0):
                V(lambda e: e.tensor_tensor(out=ps_, in0=ps_, in1=z2, op=ALU.mult))
                V(lambda e, dv_=dv_: e.tensor_scalar(out=ps_, in0=ps_, scalar1=-1.0 / dv_, scalar2=1.0, op0=ALU.mult, op1=ALU.add))
            V(lambda e: e.tensor_tensor(out=s1, in0=ps_, in1=z, op=ALU.mult))
            V(lambda e: e.tensor_scalar(out=pc_, in0=z2, scalar1=-1.0 / 56.0, scalar2=1.0, op0=ALU.mult, op1=ALU.add))
            for dv_ in (30.0, 12.0, 2.0):
                V(lambda e: e.tensor_tensor(out=pc_, in0=pc_, in1=z2, op=ALU.mult))
                V(lambda e, dv_=dv_: e.tensor_scalar(out=pc_, in0=pc_, scalar1=-1.0 / dv_, scalar2=1.0, op0=ALU.mult, op1=ALU.add))
            V(lambda e: e.tensor_copy(out=c1, in_=pc_))
            for _ in range(3):
                V(lambda e: e.tensor_tensor(out=t0, in0=s1, in1=c1, op=ALU.mult))
                V(lambda e: e.tensor_tensor(out=t1, in0=s1, in1=s1, op=ALU.mult))
                V(lambda e: e.tensor_scalar(out=s1, in0=t0, scalar1=2.0, scalar2=None, op0=ALU.mult))
                V(lambda e: e.tensor_scalar(out=c1, in0=t1, scalar1=-2.0, scalar2=1.0, op0=ALU.mult, op1=ALU.add))
            Ur = sb("Ur", [128, 9, 32], F32); Ui = sb("Ui", [128, 9, 32], F32)
            V(lambda e: e.memset(Ur[:, 0, :], 1.0)); V(lambda e: e.memset(Ui[:, 0, :], 0.0))
            V(lambda e: e.tensor_copy(out=Ur[:, 1, :], in_=c1)); V(lambda e: e.tensor_copy(out=Ui[:, 1, :], in_=s1))
            for t in range(1, 8):
                V(lambda e, t=t: e.tensor_tensor(out=t0, in0=Ur[:, t, :], in1=c1, op=ALU.mult))
                V(lambda e, t=t: e.tensor_tensor(out=t1, in0=Ui[:, t, :], in1=s1, op=ALU.mult))
                V(lambda e, t=t: e.tensor_tensor(out=Ur[:, t + 1, :], in0=t0, in1=t1, op=ALU.subtract))
                V(lambda e, t=t: e.tensor_tensor(out=t0, in0=Ur[:, t, :], in1=s1, op=ALU.mult))
                V(lambda e, t=t: e.tensor_tensor(out=t1, in0=Ui[:, t, :], in1=c1, op=ALU.mult))
                V(lambda e, t=t: e.tensor_tensor(out=Ui[:, t + 1, :], in0=t0, in1=t1, op=ALU.add))
            Epr = sb("Epr", [128, 9, 32], F32); Epi = sb("Epi", [128, 9, 32], F32)
            Enr = sb("Enr", [128, 8, 32], F32); Eni = sb("Eni", [128, 8, 32], F32)
            MG = sb("MG", [128, 9, 32], F32); MGn = sb("MGn", [128, 8, 32], F32)
            for t in range(9):
                A(lambda e, t=t: e.activation(out=MG[:, t, :], in_=ldr, func=AF.Exp, scale=float(t)))
            for t in range(8):
                A(lambda e, t=t: e.activation(out=MGn[:, t, :], in_=ldr, func=AF.Exp, scale=float(-t)))
            esA, ada_bufs = ada_emit()
            V(lambda e: e.tensor_tensor(out=Epr[:], in0=MG[:], in1=Ur[:], op=ALU.mult))
            V(lambda e: e.tensor_tensor(out=Epi[:], in0=MG[:], in1=Ui[:], op=ALU.mult))
            V(lambda e: e.tensor_tensor(out=Enr[:], in0=MGn[:], in1=Ur[:, 0:8, :], op=ALU.mult))
            V(lambda e: e.tensor_tensor(out=Eni[:], in0=MGn[:], in1=Ui[:, 0:8, :], op=ALU.mult))
            V(lambda e: e.tensor_scalar(out=Eni[:], in0=Eni[:], scalar1=-1.0, scalar2=None, op0=ALU.mult))
            V(lambda e: e.tensor_scalar(out=nr, in0=Epr[:, 1, :], scalar1=-1.0, scalar2=None, op0=ALU.add))
            ni = Epi[:, 1, :]
            V(lambda e: e.tensor_tensor(out=t0, in0=lr, in1=lr, op=ALU.mult))
            V(lambda e: e.tensor_tensor(out=t1, in0=li, in1=li, op=ALU.mult))
            V(lambda e: e.tensor_tensor(out=den, in0=t0, in1=t1, op=ALU.add))
            V(lambda e: e.reciprocal(out=den, in_=den))
            V(lambda e: e.tensor_tensor(out=t0, in0=nr, in1=lr, op=ALU.mult))
            V(lambda e: e.tensor_tensor(out=t1, in0=ni, in1=li, op=ALU.mult))
            V(lambda e: e.tensor_tensor(out=t0, in0=t0, in1=t1, op=ALU.add))
            V(lambda e: e.tensor_tensor(out=cfr, in0=t0, in1=den, op=ALU.mult))
            V(lambda e: e.tensor_tensor(out=t0, in0=ni, in1=lr, op=ALU.mult))
            V(lambda e: e.tensor_tensor(out=t1, in0=nr, in1=li, op=ALU.mult))
            V(lambda e: e.tensor_tensor(out=t0, in0=t0, in1=t1, op=ALU.subtract))
            V(lambda e: e.tensor_tensor(out=cfi, in0=t0, in1=den, op=ALU.mult))
            Bpr = sb("Bpr", [128, 32, 16], F32); Bpi = sb("Bpi", [128, 32, 16], F32)
            tb0 = sb("tb0", [128, 32, 16], F32); tb1 = sb("tb1", [128, 32, 16], F32)
            bc16 = lambda ap: ap.unsqueeze(2).to_broadcast([128, 32, 16])
            V(lambda e: e.tensor_tensor(out=tb0[:], in0=Bpk[:, 0, :, :], in1=bc16(cfr), op=ALU.mult))
            V(lambda e: e.tensor_tensor(out=tb1[:], in0=Bpk[:, 1, :, :], in1=bc16(cfi), op=ALU.mult))
            V(lambda e: e.tensor_tensor(out=Bpr[:], in0=tb0[:], in1=tb1[:], op=ALU.subtract))
            V(lambda e: e.tensor_tensor(out=tb0[:], in0=Bpk[:, 1, :, :], in1=bc16(cfr), op=ALU.mult))
            V(lambda e: e.tensor_tensor(out=tb1[:], in0=Bpk[:, 0, :, :], in1=bc16(cfi), op=ALU.mult))
            V(lambda e: e.tensor_tensor(out=Bpi[:], in0=tb0[:], in1=tb1[:], op=ALU.add))
            Gr = sb("Gr", [128, 32, 8, 16], BF16); Gi = sb("Gi", [128, 32, 8, 16], BF16)
            G7r = sb("G7r", [128, 32, 8, 16], BF16); G7i = sb("G7i", [128, 32, 8, 16], BF16)
            for s in range(8):
                for (er, ei, outr, outi) in ((Enr[:, s, :], Eni[:, s, :], Gr, Gi), (Epr[:, 7 - s, :], Epi[:, 7 - s, :], G7r, G7i)):
                    V(lambda e, er=er: e.tensor_tensor(out=tb0[:], in0=Bpr[:], in1=bc16(er), op=ALU.mult))
                    V(lambda e, ei=ei: e.tensor_tensor(out=tb1[:], in0=Bpi[:], in1=bc16(ei), op=ALU.mult))
                    V(lambda e, outr=outr, s=s: e.tensor_tensor(out=outr[:, :, s, :], in0=tb0[:], in1=tb1[:], op=ALU.subtract))
                    V(lambda e, ei=ei: e.tensor_tensor(out=tb0[:], in0=Bpr[:], in1=bc16(ei), op=ALU.mult))
                    V(lambda e, er=er: e.tensor_tensor(out=tb1[:], in0=Bpi[:], in1=bc16(er), op=ALU.mult))
                    V(lambda e, outi=outi, s=s: e.tensor_tensor(out=outi[:, :, s, :], in0=tb0[:], in1=tb1[:], op=ALU.add))
            for t in range(9):
                er, ei = Epr[:, t, :], Epi[:, t, :]
                V(lambda e, er=er: e.tensor_tensor(out=tb0[:], in0=Cpk[:, 0, :, :], in1=bc16(er), op=ALU.mult))
                V(lambda e, ei=ei: e.tensor_tensor(out=tb1[:], in0=Cpk[:, 1, :, :], in1=bc16(ei), op=ALU.mult))
                k.op("dve", lambda e, t=t: e.tensor_tensor(out=Ybr[:, :, t, :], in0=tb0[:], in1=tb1[:], op=ALU.subtract),
                     reads=[Bt], writes=[Bt, Bs5])
                V(lambda e, ei=ei: e.tensor_tensor(out=tb0[:], in0=Cpk[:, 0, :, :], in1=bc16(ei), op=ALU.mult))
                V(lambda e, er=er: e.tensor_tensor(out=tb1[:], in0=Cpk[:, 1, :, :], in1=bc16(er), op=ALU.mult))
                V(lambda e: e.tensor_tensor(out=tb0[:], in0=tb0[:], in1=tb1[:], op=ALU.add))
                k.op("dve", lambda e, t=t: e.tensor_scalar(out=Ybn[:, :, t, :], in0=tb0[:], scalar1=-1.0, scalar2=None, op0=ALU.mult),
                     reads=[Bt], writes=[Bt, Bs5])
            k.op("dve", lambda e: e.tensor_copy(out=ARR[:], in_=Epr[:, 8, :]), reads=[Bt], writes=[Bs5])
            k.op("dve", lambda e: e.tensor_copy(out=AIp[:], in_=Epi[:, 8, :]), reads=[Bt], writes=[Bs5])
            k.op("dve", lambda e: e.tensor_scalar(out=AIn[:], in0=Epi[:, 8, :], scalar1=-1.0, scalar2=None, op0=ALU.mult),
                 reads=[Bt], writes=[Bs5])
            sqr = sm[:, 19, :]; sqi = sm[:, 20, :]
            V(lambda e: e.tensor_copy(out=sqr, in_=Epr[:, 8, :])); V(lambda e: e.tensor_copy(out=sqi, in_=Epi[:, 8, :]))
            for _ in range(6):
                V(lambda e: e.tensor_tensor(out=t0, in0=sqr, in1=sqr, op=ALU.mult))
                V(lambda e: e.tensor_tensor(out=t1, in0=sqi, in1=sqi, op=ALU.mult))
                V(lambda e: e.tensor_tensor(out=t2, in0=sqr, in1=sqi, op=ALU.mult))
                V(lambda e: e.tensor_tensor(out=sqr, in0=t0, in1=t1, op=ALU.subtract))
                V(lambda e: e.tensor_scalar(out=sqi, in0=t2, scalar1=2.0, scalar2=None, op0=ALU.mult))
            k.op("dve", lambda e: e.tensor_copy(out=A64[:, 0, :], in_=sqr), reads=[Bt], writes=[Bs5])
            k.op("dve", lambda e: e.tensor_copy(out=A64[:, 1, :], in_=sqi), reads=[Bt], writes=[Bs5])
            k.op("dve", lambda e: e.tensor_scalar(out=A64[:, 2, :], in0=sqi, scalar1=-1.0, scalar2=None, op0=ALU.mult), reads=[Bt], writes=[Bs5])
            for g0 in range(0, NG, 4):
                pt, pb = bank()
                for gi in range(4):
                    g = g0 + gi
                    gh, gp = g // 32, g % 32
                    rows = slice(gh * 64, gh * 64 + 64)
                    k.op("pe", lambda e, gi=gi, rows=rows, gp=gp: e.matmul(
                        pt[:, gi * 128:(gi + 1) * 128], lhsT=Gr[rows, gp, :, :], rhs=Ybr[rows, gp, 0:8, :], start=True, stop=False),
                        reads=[Bt, Bs5], writes=[pb])
                    k.op("pe", lambda e, gi=gi, rows=rows, gp=gp: e.matmul(
                        pt[:, gi * 128:(gi + 1) * 128], lhsT=Gi[rows, gp, :, :], rhs=Ybn[rows, gp, 0:8, :], start=False, stop=True),
                        reads=[Bt, Bs5], writes=[pb])
                k.op("dve", lambda e, g0=g0: e.tensor_tensor(
                    out=Toep[:, g0:g0 + 4, :].rearrange("p g (t h) -> p g t h", h=16),
                    in0=pt[:, :].rearrange("p (g t h) -> p g t h", g=4, h=16),
                    in1=mask3[:].unsqueeze(1).unsqueeze(3).to_broadcast([128, 4, 8, 16]), op=ALU.mult),
                    reads=[pb, Bc], writes=[Bs5])
            for g in range(NG):
                k.op("dve", lambda e, g=g: e.scalar_tensor_tensor(out=Toep[:, g, :], in0=identf[:], scalar=dpk[:, g:g + 1],
                                                                  in1=Toep[:, g, :], op0=ALU.mult, op1=ALU.add),
                     reads=[Bt, Bc, Bs5], writes=[Bs5])
            for g0 in range(0, NG, 8):
                pt, pb = bank()
                ptb = pt[:].bitcast(BF16)
                for gi in range(8):
                    g = g0 + gi
                    gh, gp = g // 32, g % 32
                    rows = slice(gh * 64, gh * 64 + 64)
                    for ri, GG in enumerate((G7r, G7i)):
                        col = (gi * 2 + ri) * 64
                        k.op("pe", lambda e, rows=rows, gp=gp, GG=GG, col=col: e.transpose(
                            out=ptb[:, col:col + 64], in_=GG[rows, gp, :, :], identity=identb[rows, rows]),
                            reads=[Bt, Bc], writes=[pb])
                k.op("dve", lambda e, g0=g0: e.tensor_copy(out=Mw[:, g0:g0 + 8, :, :].rearrange("p g r q -> p (g r q)"), in_=ptb[:, 0:1024]),
                     reads=[pb], writes=[Bs5])
            for e_ in ("pe", "dve", "act", "pool", "sp"):
                k.wait_all(e_, [Bt] + ada_bufs + psb)
            k.es = esW
        esA.close()
        k.esR = esR

        dump("Toep", Toep[:], [128, NG, 128], [Bs5], BF16)
        dump("Mw", Mw[:], [128, NG, 2, 64], [Bs5], BF16)
        dump("Ybr", Ybr[:], [128, 32, 9, 16], [Bs5], BF16)
        dump("Ybn", Ybn[:], [128, 32, 9, 16], [Bs5], BF16)
        dump("ARR", ARR[:], [128, 32], [Bs5])
        dump("AIp", AIp[:], [128, 32], [Bs5])
        stage_end(1)
        dump("shiftT", shiftT[:], [128, KT, 17], [Bmod])
        dump("scaleT", scaleT[:], [128, KT, 17], [Bmod])
        stage_end(2)
        ENG = ("pe", "dve", "act", "pool", "sp")

        def barrier(bufs):
            for e_ in ENG:
                k.wait_all(e_, bufs)

        def prep(full):
            ntile = 9 if full else 8
            xsrc = [(xmain if full else xpre)[j * 128:(j + 1) * 128, :] for j in range(8)]
            if full:
                xsrc.append(xsmp[:, :])
            with ExitStack() as es4:
                k.es = es4
                gainbc = sb("gainbc", [128, D], F32); Bg = Buf("gain")
                k.dma("sp", gainbc[:], norm_gain.partition_broadcast(128), writes=[Bg])
                xs = [sb("xs%d" % i, [128, D], F32) for i in range(2)]; xb_ = [Buf("xs%d" % i) for i in range(2)]
                xn = [sb("xn%d" % i, [128, D], BF16) for i in range(2)]; xnb = [Buf("xn%d" % i) for i in range(2)]
                st = sb("prst", [128, 9, 4], F32); Bst = Buf("prst")
                tmpm = [sb("tmpm%d" % i, [128, 8, 128], F32) for i in range(2)]; tmb = [Buf("tmpm%d" % i) for i in range(2)]
                for j in range(ntile):
                    a = j % 2
                    k.dma("sp", xs[a][:], xsrc[j], writes=[xb_[a]])
                    k.op("act", lambda e, a=a, j=j: e.activation(out=xn[a][:], in_=xs[a][:], func=AF.Square, accum_out=st[:, j, 0:1]),
                         reads=[xb_[a]], writes=[xnb[a], Bst])
                    k.op("act", lambda e, j=j: e.activation(out=st[:, j, 2:3], in_=st[:, j, 0:1], func=AF.Ln, scale=1.0 / D, bias=epsb[:, 0:1]), reads=[Bst, Bc], writes=[Bst])
                    k.op("act", lambda e, j=j: e.activation(out=st[:, j, 3:4], in_=st[:, j, 2:3], func=AF.Exp, scale=-0.5), reads=[Bst], writes=[Bst])
                    k.op("dve", lambda e, a=a, j=j: e.scalar_tensor_tensor(out=xn[a][:], in0=xs[a][:], scalar=st[:, j, 3:4], in1=gainbc[:],
                                                                           op0=ALU.mult, op1=ALU.mult),
                         reads=[xb_[a], Bst, Bg], writes=[xnb[a]])
                    for half in range(2):
                        pt, pb = bank()
                        ptb = pt[:].bitcast(BF16)
                        for q in range(8):
                            kt = half * 8 + q
                            k.op("pe", lambda e, a=a, kt=kt, q=q, ptb=ptb: e.transpose(
                                out=ptb[:, q * 128:(q + 1) * 128], in_=xn[a][:, kt * 128:(kt + 1) * 128], identity=identb[:]),
                                reads=[xnb[a], Bc], writes=[pb])
                        m = half
                        src = ptb[:, 0:1024].rearrange("p (q t) -> p q t", q=8)
                        if j < 8:
                            sc = scaleT[:, half * 8:half * 8 + 8, 0:1].to_broadcast([128, 8, 128])
                            sh = shiftT[:, half * 8:half * 8 + 8, 0:1].to_broadcast([128, 8, 128])
                            o1 = tmpm[m][:]
                            o2 = hT[:, half * 8:half * 8 + 8, j * 128:(j + 1) * 128]
                        else:
                            src = src.rearrange("p q (j r) -> p q j r", r=8)
                            sc = scaleT[:, half * 8:half * 8 + 8, 1:17].unsqueeze(3).to_broadcast([128, 8, NSEQ, 8])
                            sh = shiftT[:, half * 8:half * 8 + 8, 1:17].unsqueeze(3).to_broadcast([128, 8, NSEQ, 8])
                            o1 = tmpm[m][:].rearrange("p q (j r) -> p q j r", r=8)
                            o2 = hT[:, half * 8:half * 8 + 8, j * 128:(j + 1) * 128].rearrange("p q (j r) -> p q j r", r=8)
                        k.op("dve", lambda e, o1=o1, src=src, sc=sc: e.tensor_tensor(out=o1, in0=src, in1=sc, op=ALU.mult),
                             reads=[pb, Bmod], writes=[tmb[m]])
                        k.op("dve", lambda e, o1=o1, o2=o2, sh=sh: e.tensor_tensor(out=o2, in0=o1, in1=sh, op=ALU.add),
                             reads=[tmb[m], Bmod], writes=[hTb[j]])
                barrier([Bg] + xb_ + xnb + [Bst] + tmb)
            return ntile

        def s5_front(full, U2, BU2, VH, BV):
            ntile = 9 if full else 8
            hall = hTb[0:ntile]
            ctiles = [(0, 128, 0)] + ([(TP, NSEQ, 128)] if full else [])
            with ExitStack() as es4:
                k.es = es4
                Uc = sb("Uc", [128, 16, 8, 16], BF16); BUc = Buf("Uc")
                for cb in range(4):
                    ws, wb = w_next((w_in, cb * WC))
                    for (toff, C, coff) in ctiles:
                        for sp2 in range(4):
                            pt, pb = bank()
                            for si in range(2):
                                s = sp2 * 2 + si
                                for kt in range(KT):
                                    k.op("pe", lambda e, kt=kt, s=s, si=si, toff=toff, C=C, ws=ws, pt=pt: e.matmul(
                                        pt[0:C, si * WC:(si + 1) * WC], lhsT=hT[:, kt, toff + s:toff + 8 * C:8], rhs=ws[:, kt, :],
                                        start=(kt == 0), stop=(kt == KT - 1)), reads=hall + [wb], writes=[pb])
                            k.op("act", lambda e, sp2=sp2, C=C, pt=pt: e.activation(
                                out=Uc[0:C, :, sp2 * 2:sp2 * 2 + 2, :].rearrange("c g s h -> c s g h"),
                                in_=pt[0:C, :].rearrange("c (s g h) -> c s g h", s=2, h=16), func=AF.Copy),
                                reads=[pb], writes=[BUc])
                        for gq in range(2):
                            pt, pb = bank()
                            ptb = pt[:].bitcast(BF16)
                            for gi in range(8):
                                gl = gq * 8 + gi
                                k.op("pe", lambda e, gl=gl, gi=gi, C=C, ptb=ptb: e.transpose(
                                    out=ptb[:, gi * 128:gi * 128 + C], in_=Uc[0:C, gl, :, :], identity=identb[0:C, 0:C]),
                                    reads=[BUc, Bc], writes=[pb])
                            g0 = cb * 16 + gq * 8
                            k.op("act", lambda e, g0=g0, C=C, coff=coff, ptb=ptb: e.activation(
                                out=U2[:, g0:g0 + 8, coff:coff + C], in_=ptb[:, 0:1024].rearrange("p (g c) -> p g c", g=8)[:, :, 0:C],
                                func=AF.Copy), reads=[pb], writes=[BU2[g0 // 8]])
                barrier([BUc])
            CT = 128 + (NSEQ if full else 0)
            for gp in range(32):
                pt, pb = bank()
                for gh in range(2):
                    g = gh * 32 + gp
                    for ri in range(2):
                        k.op("pe", lambda e, g=g, gh=gh, ri=ri, pt=pt: e.matmul(
                            pt[gh * 64:(gh + 1) * 64, ri * 256:ri * 256 + CT], lhsT=Mw[:, g, ri, :], rhs=U2[:, g, 0:CT],
                            start=True, stop=True), reads=[Bs5, BU2[g // 8]], writes=[pb])
                k.op("act", lambda e, gp=gp, pt=pt: e.activation(
                    out=VH[:, 1:1 + CT, :, gp].rearrange("p c r -> p r c"),
                    in_=pt[:, :].rearrange("p (r c) -> p r c", r=2)[:, :, 0:CT], func=AF.Copy), reads=[pb], writes=[BV])

        def s5_recur(VH, BV, BHh, hist):
            with ExitStack() as es4:
                k.es = es4
                ta = sb("rec_ta", [128, 2, 32], F32); tb = sb("rec_tb", [128, 2, 32], F32); Br = Buf("rec")
                arb = ARR[:].unsqueeze(1).to_broadcast([128, 2, 32])
                for c in range(128):
                    k.op("dve", lambda e: e.tensor_tensor(out=ta[:], in0=Hrun[:], in1=arb, op=ALU.mult), reads=[BH, Bs5], writes=[Br])
                    k.op("dve", lambda e: e.tensor_tensor(out=tb[:, 0, :], in0=Hrun[:, 1, :], in1=AIn[:], op=ALU.mult), reads=[BH, Bs5], writes=[Br])
                    k.op("dve", lambda e: e.tensor_tensor(out=tb[:, 1, :], in0=Hrun[:, 0, :], in1=AIp[:], op=ALU.mult), reads=[BH, Bs5], writes=[Br])
                    k.op("dve", lambda e: e.tensor_tensor(out=ta[:], in0=ta[:], in1=tb[:], op=ALU.add), reads=[Br], writes=[Br])
                    k.op("dve", lambda e, c=c: e.tensor_tensor(out=Hrun[:], in0=ta[:], in1=VH[:, 1 + c, :, :], op=ALU.add), reads=[Br, BV], writes=[BH])
                    if hist and c < 127:
                        k.op("act", lambda e, c=c: e.activation(out=VH[:, 1 + c, :, :], in_=Hrun[:], func=AF.Copy), reads=[BH], writes=[BHh])
                barrier([Br])

        def s5_recur2_state(VH, BV):
            with ExitStack() as es4:
                k.es = es4
                H2 = sb("rec_H2", [128, 2, 2, 32], F32); ta = sb("rec_ta2", [128, 2, 2, 32], F32); tb = sb("rec_tb2", [128, 2, 2, 32], F32)
                Br = Buf("rec2")
                arb = ARR[:].unsqueeze(1).unsqueeze(1).to_broadcast([128, 2, 2, 32])
                ainb = AIn[:].unsqueeze(1).to_broadcast([128, 2, 32]); aipb = AIp[:].unsqueeze(1).to_broadcast([128, 2, 32])
                k.op("dve", lambda e: e.memset(H2[:], 0.0), writes=[Br])
                k.op("dve", lambda e: e.tensor_copy(out=H2[:, 0, :, :], in_=Hrun[:]), reads=[BH], writes=[Br])
                for c in range(64):
                    k.op("dve", lambda e: e.tensor_tensor(out=ta[:], in0=H2[:], in1=arb, op=ALU.mult), reads=[Br, Bs5], writes=[Br])
                    k.op("dve", lambda e: e.tensor_tensor(out=tb[:, :, 0, :], in0=H2[:, :, 1, :], in1=ainb, op=ALU.mult), reads=[Br, Bs5], writes=[Br])
                    k.op("dve", lambda e: e.tensor_tensor(out=tb[:, :, 1, :], in0=H2[:, :, 0, :], in1=aipb, op=ALU.mult), reads=[Br, Bs5], writes=[Br])
                    k.op("dve", lambda e: e.tensor_tensor(out=ta[:], in0=ta[:], in1=tb[:], op=ALU.add), reads=[Br], writes=[Br])
                    k.op("dve", lambda e, c=c: e.tensor_tensor(out=H2[:], in0=ta[:], in1=VH[:, 1 + c:1 + c + 65:64, :, :], op=ALU.add), reads=[Br, BV], writes=[Br])
                k.op("dve", lambda e: e.tensor_tensor(out=ta[:, 0, :, :], in0=H2[:, 0, :, :], in1=A64[:, 0, :].unsqueeze(1).to_broadcast([128, 2, 32]), op=ALU.mult),
                     reads=[Br, Bs5], writes=[Br])
                k.op("dve", lambda e: e.tensor_tensor(out=tb[:, 0, 0, :], in0=H2[:, 0, 1, :], in1=A64[:, 2, :], op=ALU.mult), reads=[Br, Bs5], writes=[Br])
                k.op("dve", lambda e: e.tensor_tensor(out=tb[:, 0, 1, :], in0=H2[:, 0, 0, :], in1=A64[:, 1, :], op=ALU.mult), reads=[Br, Bs5], writes=[Br])
                k.op("dve", lambda e: e.tensor_tensor(out=ta[:, 0, :, :], in0=ta[:, 0, :, :], in1=tb[:, 0, :, :], op=ALU.add), reads=[Br], writes=[Br])
                k.op("dve", lambda e: e.tensor_tensor(out=Hrun[:], in0=ta[:, 0, :, :], in1=H2[:, 1, :, :], op=ALU.add), reads=[Br], writes=[BH])
                barrier([Br])

        def s5_recur2_hist(VH, BV, BHh):
            SB_ = 8; NQ = 4; L = 128 // NQ
            with ExitStack() as es4:
                k.es = es4
                H2 = sb("rec_H2", [128, NQ, 2, 32], F32); ta = sb("rec_ta2", [128, NQ, 2, 32], F32); tb = sb("rec_tb2", [128, NQ, 2, 32], F32)
                PW = sb("rec_PW", [128, L, 2, 32], BF16); c1 = sb("rec_c1", [128, SB_, 32], F32); c2 = sb("rec_c2", [128, SB_, 32], F32)
                pw = sb("rec_pw", [128, 8, 32], F32)
                He = [sb("rec_He%d" % i, [128, 2, 32], F32) for i in range(2)]
                Br = Buf("rec2"); Bpw = Buf("rec_pw")
                W = lambda fn: k.op("dve", fn, reads=[Bpw, Bs5], writes=[Bpw])
                pr, pi_, q0_, q1_, q2_, aLr, aLi, aLn = (pw[:, i, :] for i in range(8))
                W(lambda e: e.tensor_copy(out=pr, in_=ARR[:])); W(lambda e: e.tensor_copy(out=pi_, in_=AIp[:]))
                W(lambda e: e.tensor_copy(out=PW[:, 0, 0, :], in_=ARR[:])); W(lambda e: e.tensor_copy(out=PW[:, 0, 1, :], in_=AIp[:]))

                def square():
                    W(lambda e: e.tensor_tensor(out=q0_, in0=pr, in1=pr, op=ALU.mult))
                    W(lambda e: e.tensor_tensor(out=q1_, in0=pi_, in1=pi_, op=ALU.mult))
                    W(lambda e: e.tensor_tensor(out=q2_, in0=pr, in1=pi_, op=ALU.mult))
                    W(lambda e: e.tensor_tensor(out=pr, in0=q0_, in1=q1_, op=ALU.subtract))
                    W(lambda e: e.tensor_scalar(out=pi_, in0=q2_, scalar1=2.0, scalar2=None, op0=ALU.mult))

                n = 1
                while n < L:
                    for b0 in range(0, n, SB_):
                        nb = min(SB_, n - b0)
                        src_r = PW[:, b0:b0 + nb, 0, :]; src_i = PW[:, b0:b0 + nb, 1, :]
                        prb = pr.unsqueeze(1).to_broadcast([128, nb, 32]); pib = pi_.unsqueeze(1).to_broadcast([128, nb, 32])
                        W(lambda e, src_r=src_r, prb=prb, nb=nb: e.tensor_tensor(out=c1[:, 0:nb, :], in0=src_r, in1=prb, op=ALU.mult))
                        W(lambda e, src_i=src_i, pib=pib, nb=nb: e.tensor_tensor(out=c2[:, 0:nb, :], in0=src_i, in1=pib, op=ALU.mult))
                        W(lambda e, nb=nb, b0=b0, n=n: e.tensor_tensor(out=PW[:, n + b0:n + b0 + nb, 0, :], in0=c1[:, 0:nb, :], in1=c2[:, 0:nb, :], op=ALU.subtract))
                        W(lambda e, src_r=src_r, pib=pib, nb=nb: e.tensor_tensor(out=c1[:, 0:nb, :], in0=src_r, in1=pib, op=ALU.mult))
                        W(lambda e, src_i=src_i, prb=prb, nb=nb: e.tensor_tensor(out=c2[:, 0:nb, :], in0=src_i, in1=prb, op=ALU.mult))
                        W(lambda e, nb=nb, b0=b0, n=n: e.tensor_tensor(out=PW[:, n + b0:n + b0 + nb, 1, :], in0=c1[:, 0:nb, :], in1=c2[:, 0:nb, :], op=ALU.add))
                    n *= 2
                    square()
                W(lambda e: e.tensor_copy(out=aLr, in_=pr)); W(lambda e: e.tensor_copy(out=aLi, in_=pi_))
                W(lambda e: e.tensor_scalar(out=aLn, in0=pi_, scalar1=-1.0, scalar2=None, op0=ALU.mult))
                arb = ARR[:].unsqueeze(1).unsqueeze(1).to_broadcast([128, NQ, 2, 32])
                ainb = AIn[:].unsqueeze(1).to_broadcast([128, NQ, 32]); aipb = AIp[:].unsqueeze(1).to_broadcast([128, NQ, 32])
                k.op("dve", lambda e: e.memset(H2[:], 0.0), writes=[Br])
                k.op("dve", lambda e: e.tensor_copy(out=H2[:, 0, :, :], in_=Hrun[:]), reads=[BH], writes=[Br])
                Bh2 = Buf("H2")
                hi_sl = lambda c: slice(1 + c, 1 + c + L * (NQ - 1) + 1, L)
                for c in range(L):
                    k.op("dve", lambda e: e.tensor_tensor(out=ta[:], in0=H2[:], in1=arb, op=ALU.mult), reads=[Bh2, Bs5], writes=[Br])
                    k.op("dve", lambda e: e.tensor_tensor(out=tb[:, :, 0, :], in0=H2[:, :, 1, :], in1=ainb, op=ALU.mult), reads=[Bh2, Bs5], writes=[Br])
                    k.op("dve", lambda e: e.tensor_tensor(out=tb[:, :, 1, :], in0=H2[:, :, 0, :], in1=aipb, op=ALU.mult), reads=[Bh2, Bs5], writes=[Br])
                    k.op("dve", lambda e: e.tensor_tensor(out=ta[:], in0=ta[:], in1=tb[:], op=ALU.add), reads=[Br], writes=[Br])
                    k.op("dve", lambda e, c=c: e.tensor_tensor(out=H2[:], in0=ta[:], in1=VH[:, hi_sl(c), :, :], op=ALU.add), reads=[Br, BV], writes=[Bh2])
                    k.op("act", lambda e, c=c: e.activation(out=VH[:, hi_sl(c), :, :], in_=H2[:], func=AF.Copy), reads=[Bh2], writes=[BHh])
                X = lambda fn: k.op("dve", fn, reads=[Bh2, Bpw, BHh, Br], writes=[Br, BHh])
                hprev = H2[:, 0, :, :]
                for q in range(1, NQ):
                    hr_b = hprev[:, 0, :].unsqueeze(1).to_broadcast([128, SB_, 32])
                    hi_b = hprev[:, 1, :].unsqueeze(1).to_broadcast([128, SB_, 32])
                    for b0 in range(0, L, SB_):
                        pwr = PW[:, b0:b0 + SB_, 0, :]; pwi = PW[:, b0:b0 + SB_, 1, :]
                        s0_ = 1 + q * L + b0
                        tgt_r = VH[:, s0_:s0_ + SB_, 0, :]; tgt_i = VH[:, s0_:s0_ + SB_, 1, :]
                        X(lambda e, pwr=pwr, hr_b=hr_b: e.tensor_tensor(out=c1[:], in0=pwr, in1=hr_b, op=ALU.mult))
                        X(lambda e, pwi=pwi, hi_b=hi_b: e.tensor_tensor(out=c2[:], in0=pwi, in1=hi_b, op=ALU.mult))
                        X(lambda e: e.tensor_tensor(out=c1[:], in0=c1[:], in1=c2[:], op=ALU.subtract))
                        X(lambda e, tgt_r=tgt_r: e.tensor_tensor(out=tgt_r, in0=tgt_r, in1=c1[:], op=ALU.add))
                        X(lambda e, pwr=pwr, hi_b=hi_b: e.tensor_tensor(out=c1[:], in0=pwr, in1=hi_b, op=ALU.mult))
                        X(lambda e, pwi=pwi, hr_b=hr_b: e.tensor_tensor(out=c2[:], in0=pwi, in1=hr_b, op=ALU.mult))
                        X(lambda e: e.tensor_tensor(out=c1[:], in0=c1[:], in1=c2[:], op=ALU.add))
                        X(lambda e, tgt_i=tgt_i: e.tensor_tensor(out=tgt_i, in0=tgt_i, in1=c1[:], op=ALU.add))
                    hn = He[q % 2]
                    X(lambda e, hprev=hprev: e.tensor_tensor(out=ta[:, 0, :, :], in0=hprev, in1=aLr.unsqueeze(1).to_broadcast([128, 2, 32]), op=ALU.mult))
                    X(lambda e, hprev=hprev: e.tensor_tensor(out=tb[:, 0, 0, :], in0=hprev[:, 1, :], in1=aLn, op=ALU.mult))
                    X(lambda e, hprev=hprev: e.tensor_tensor(out=tb[:, 0, 1, :], in0=hprev[:, 0, :], in1=aLi, op=ALU.mult))
                    X(lambda e: e.tensor_tensor(out=ta[:, 0, :, :], in0=ta[:, 0, :, :], in1=tb[:, 0, :, :], op=ALU.add))
                    X(lambda e, hn=hn, q=q: e.tensor_tensor(out=hn[:], in0=ta[:, 0, :, :], in1=H2[:, q, :, :], op=ALU.add))
                    hprev = hn[:]
                k.op("dve", lambda e, hprev=hprev: e.tensor_copy(out=Hrun[:], in_=hprev), reads=[Br, Bh2], writes=[BH])
                barrier([Br, Bpw, Bh2, BHh])

        def proj_fm(wlist, T, hall, dstfn, func=AF.Copy):
            nblocks = [(n0, min(512, T - n0)) for n0 in range(0, T, 512)]
            for i, spec in enumerate(wlist):
                ws, wb = w_next(spec)
                for mt in range(2):
                    for (n0, nn) in nblocks:
                        pt, pb = bank()
                        for kt in range(KT):
                            k.op("pe", lambda e, kt=kt, n0=n0, nn=nn, mt=mt, ws=ws, pt=pt: e.matmul(
                                pt[:, 0:nn], lhsT=ws[:, kt, mt * 128:(mt + 1) * 128], rhs=hT[:, kt, n0:n0 + nn],
                                start=(kt == 0), stop=(kt == KT - 1)), reads=hall + [wb], writes=[pb])
                        oap, ob = dstfn(i * 2 + mt, n0, nn)
                        k.op("act", lambda e, oap=oap, nn=nn, pt=pt: e.activation(out=oap, in_=pt[:, 0:nn], func=func), reads=[pb], writes=[ob])

        def gla_alloc(full):
            glr = sb("glr", [17, TM], BF16); Bglr = Buf("glr")
            kT = sb("kT", [128, NH, TM], BF16); BkT = Buf("kT")
            vtok = sb("vtok", [128, 9, NH * DV], BF16); Bvt = [Buf("vtok%d" % j) for j in range(9)]
            qT = sb("qT", [128, NH, TM], BF16) if full else None
            BqT = Buf("qT")
            return dict(glr=glr, Bglr=Bglr, kT=kT, BkT=BkT, vtok=vtok, Bvt=Bvt, qT=qT, BqT=BqT)

        def gla_proj(gl, full):
            glr, Bglr, kT, BkT, vtok, Bvt, qT, BqT = (gl[x] for x in ("glr", "Bglr", "kT", "BkT", "vtok", "Bvt", "qT", "BqT"))
            ntile = 9 if full else 8
            T = ntile * 128
            hall = hTb[0:ntile]
            nblocks = [(n0, min(512, T - n0)) for n0 in range(0, T, 512)]
            if True:
                k.op("dve", lambda e: e.memset(glr[:], 1.0), writes=[Bglr])
                ws, wb = w_next((w_in, 5120))
                for (n0, nn) in nblocks:
                    pt, pb = bank()
                    for kt in range(KT):
                        k.op("pe", lambda e, kt=kt, n0=n0, nn=nn, ws=ws, pt=pt: e.matmul(pt[0:16, 0:nn], lhsT=ws[:, kt, 0:16], rhs=hT[:, kt, n0:n0 + nn],
                                                                                       start=(kt == 0), stop=(kt == KT - 1)), reads=hall + [wb], writes=[pb])
                    k.op("act", lambda e, n0=n0, nn=nn, pt=pt: e.activation(out=glr[0:16, n0:n0 + nn], in_=pt[0:16, 0:nn], func=AF.Copy), reads=[pb], writes=[Bglr])
                proj_fm([(w_in, 2560), (w_in, 2560 + WC)], T, hall, lambda m, n0, nn: (kT[:, m, n0:n0 + nn], BkT))
                for i in range(4):
                    ws, wb = w_next((w_in, 3072 + i * WC))
                    for j in range(ntile):
                        pt, pb = bank()
                        for kt in range(KT):
                            k.op("pe", lambda e, kt=kt, j=j, ws=ws, pt=pt: e.matmul(pt[:, 0:WC], lhsT=hT[:, kt, j * 128:(j + 1) * 128], rhs=ws[:, kt, :],
                                                                                   start=(kt == 0), stop=(kt == KT - 1)), reads=[hTb[j], wb], writes=[pb])
                        k.op("act", lambda e, j=j, i=i, pt=pt: e.activation(out=vtok[:, j, i * WC:(i + 1) * WC], in_=pt[:, 0:WC], func=AF.Copy),
                             reads=[pb], writes=[Bvt[j]])
                if full:
                    proj_fm([(w_in, 2048), (w_in, 2048 + WC)], T, hall, lambda m, n0, nn: (qT[:, m, n0:n0 + nn], BqT))

        def gla_tiles(gl, full, by, Bby):
            glr, Bglr, kT, BkT, vtok, Bvt, qT, BqT = (gl[x] for x in ("glr", "Bglr", "kT", "BkT", "vtok", "Bvt", "qT", "BqT"))
            ntile = 9 if full else 8
            with ExitStack() as es4:
                k.es = es4
                G = {}
                gl_ = [("e1", [128, NH, 128], F32), ("cs", [128, NH, 128], F32), ("eb", [128, NH, 128], F32), ("einv", [128, NH, 128], F32),
                       ("kt", [128, NH, 128], BF16), ("ktok", [128, NH, 128], BF16), ("stmp", [128, 2, DV], F32)]
                if full:
                    gl_ += [("qt", [128, NH, 128], BF16), ("scm", [128, NH, 128], BF16), ("sq", [128, 2, 4, 128], BF16),
                            ("rr", [128, NH, 128], F32), ("qtf", [128, NH, 128], F32), ("KM", [128, NSEQ, 128], BF16), ("ebe", [128, NH, NSEQ], F32)]
                for nm, shp, dt_ in gl_:
                    G[nm] = sb("g_" + nm, shp, dt_)
                GB = {nm: Buf("g_" + nm) for nm in G}
                if full:
                    S0h = [sb("S0h%d" % i, [128, 8, DV], F32) for i in range(2)]; BS0h = [Buf("S0h%d" % i) for i in range(2)]
                    Snew = [sb("Snew%d" % i, [128, 2, DV], F32) for i in range(2)]; BSn2 = [Buf("Snew%d" % i) for i in range(2)]
                rot = [0]; rot2 = [0]
                fl = lambda ap: ap.rearrange("p a b -> p (a b)")
                for j in range(ntile):
                    sample = (j == 8)
                    toks = slice(j * 128, (j + 1) * 128)
                    pg, pgb = bank()
                    for hd in range(NH):
                        k.op("pe", lambda e, hd=hd, pg=pg: e.matmul(pg[:, hd * 128:(hd + 1) * 128], lhsT=wup[0:17, hd * 128:(hd + 1) * 128], rhs=glr[0:17, toks],
                                                                    start=True, stop=True), reads=[Bwup, Bglr], writes=[pgb])
                    k.op("act", lambda e, pg=pg: e.activation(out=fl(G["e1"][:]), in_=pg[:, :], func=AF.Exp, scale=-1.0), reads=[pgb], writes=[GB["e1"]])
                    k.op("act", lambda e: e.activation(out=fl(G["e1"][:]), in_=fl(G["e1"][:]), func=AF.Ln, bias=1.0), reads=[GB["e1"]], writes=[GB["e1"]])
                    d0 = hrst_s if sample else hrst
                    k.op("dve", lambda e, d0=d0: e.tensor_tensor_scan(out=fl(G["cs"][:]), data0=fl(d0[:]), data1=fl(G["e1"][:]), initial=0.0,
                                                                      op0=ALU.mult, op1=ALU.add), reads=[GB["e1"], Bc], writes=[GB["cs"]])
                    k.op("act", lambda e: e.activation(out=fl(G["einv"][:]), in_=fl(G["cs"][:]), func=AF.Exp, scale=1.0 / 16.0), reads=[GB["cs"]], writes=[GB["einv"]])
                    k.op("act", lambda e: e.activation(out=fl(G["eb"][:]), in_=fl(G["cs"][:]), func=AF.Exp, scale=-1.0 / 16.0), reads=[GB["cs"]], writes=[GB["eb"]])
                    k.op("dve", lambda e: e.tensor_tensor(out=G["kt"][:], in0=kT[:, :, toks], in1=G["einv"][:], op=ALU.mult),
                         reads=[BkT, GB["einv"]], writes=[GB["kt"]])
                    pt2, pb2 = bank()
                    pt2b = pt2[:].bitcast(BF16)
                    for hd in range(NH):
                        k.op("pe", lambda e, hd=hd, pt2b=pt2b: e.transpose(out=pt2b[:, hd * 128:(hd + 1) * 128], in_=G["kt"][:, hd, :], identity=identb[:]),
                             reads=[GB["kt"], Bc], writes=[pb2])
                    k.op("act", lambda e, pt2b=pt2b: e.activation(out=fl(G["ktok"][:]), in_=pt2b[:, 0:512], func=AF.Copy), reads=[pb2], writes=[GB["ktok"]])
                    if full:
                        k.op("dve", lambda e: e.scalar_tensor_tensor(out=G["qt"][:], in0=qT[:, :, toks], scalar=float(DK ** -0.5),
                                                                     in1=G["eb"][:], op0=ALU.mult, op1=ALU.mult),
                             reads=[BqT, GB["eb"]], writes=[GB["qt"]])
                        psc, pscb = bank()
                        for hd in range(NH):
                            k.op("pe", lambda e, hd=hd, psc=psc: e.matmul(psc[:, hd * 128:(hd + 1) * 128], lhsT=G["kt"][:, hd, :], rhs=G["qt"][:, hd, :], start=True, stop=True),
                                 reads=[GB["kt"], GB["qt"]], writes=[pscb])
                        mk = smask if sample else causal
                        k.op("dve", lambda e, psc=psc, mk=mk: e.tensor_tensor(out=G["scm"][:], in0=psc[:, :].rearrange("p (a b) -> p a b", a=NH),
                                                                              in1=mk[:].unsqueeze(1).to_broadcast([128, NH, 128]), op=ALU.mult),
                             reads=[pscb, Bc], writes=[GB["scm"]])
                        if sample:
                            k.op("dve", lambda e: e.scalar_tensor_tensor(out=G["qtf"][:], in0=qT[:, :, toks], scalar=float(DK ** -0.5),
                                                                         in1=G["eb"][:], op0=ALU.mult, op1=ALU.mult),
                                 reads=[BqT, GB["eb"]], writes=[GB["qtf"]])
                            k.op("dve", lambda e: e.tensor_copy(out=G["ebe"][:], in_=G["eb"][:].rearrange("p a (j r) -> p a j r", r=8)[:, :, :, 7]),
                                 reads=[GB["eb"]], writes=[GB["ebe"]])
                        pos = []
                        for half in range(2):
                            po, pob = bank(hold=True)
                            pos.append((po, pob))
                            for hdl in range(2):
                                hd = half * 2 + hdl
                                if sample:
                                    k.op("dve", lambda e, hd=hd: e.tensor_tensor(out=G["KM"][:], in0=G["ktok"][:, hd, :].unsqueeze(1).to_broadcast([128, NSEQ, 128]),
                                                                                 in1=seqmask[:].unsqueeze(2).to_broadcast([128, NSEQ, 128]), op=ALU.mult),
                                         reads=[GB["ktok"], Bc], writes=[GB["KM"]])
                                for qh in (range(2) if sample else [None]):
                                    if sample:
                                        a = rot[0] % 2; rot[0] += 1
                                        k.dma("pool", S0h[a][:], gla_in[qh * 8:(qh + 1) * 8, hd, :, :].rearrange("q k v -> k q v"), writes=[BS0h[a]])
                                    for vh in range(2):
                                        c0 = (hdl * 2 + vh) * 128
                                        vcols = slice(hd * DV + vh * 128, hd * DV + (vh + 1) * 128)
                                        if not sample:
                                            k.op("pe", lambda e, c0=c0, vcols=vcols, po=po, hd=hd: e.matmul(po[:, c0:c0 + 128], lhsT=vtok[:, j, vcols], rhs=G["scm"][:, hd, :],
                                                                                                            start=True, stop=False), reads=[Bvt[j], GB["scm"]], writes=[pob])
                                            k.op("pe", lambda e, c0=c0, vh=vh, hd=hd, po=po: e.matmul(po[:, c0:c0 + 128], lhsT=Sglb[:, hd, vh * 128:(vh + 1) * 128],
                                                                                                      rhs=G["qt"][:, hd, :], start=False, stop=True), reads=[BS, GB["qt"]], writes=[pob])
                                        else:
                                            cc0 = c0 + qh * 64
                                            k.op("pe", lambda e, cc0=cc0, vcols=vcols, po=po, hd=hd, qh=qh: e.matmul(
                                                po[:, cc0:cc0 + 64], lhsT=vtok[:, j, vcols], rhs=G["scm"][:, hd, qh * 64:(qh + 1) * 64],
                                                start=True, stop=False), reads=[Bvt[j], GB["scm"]], writes=[pob])
                                            for ql in range(8):
                                                q = qh * 8 + ql
                                                k.op("pe", lambda e, cc0=cc0, ql=ql, q=q, a=a, vh=vh, hd=hd, po=po: e.matmul(
                                                    po[:, cc0 + ql * 8:cc0 + ql * 8 + 8], lhsT=S0h[a][:, ql, vh * 128:(vh + 1) * 128],
                                                    rhs=G["qtf"][:, hd, q * 8:(q + 1) * 8], start=False, stop=(ql == 7)),
                                                    reads=[BS0h[a], GB["qtf"]], writes=[pob])
                                    if sample:
                                        for qp in range(4):
                                            pu, pub = bank()
                                            for qi in range(2):
                                                q = qh * 8 + qp * 2 + qi
                                                k.op("pe", lambda e, q=q, qi=qi, pu=pu, hd=hd: e.matmul(pu[:, qi * DV:(qi + 1) * DV], lhsT=G["KM"][:, q, :],
                                                                                                         rhs=vtok[:, 8, hd * DV:(hd + 1) * DV], start=True, stop=True),
                                                     reads=[GB["KM"], Bvt[8]], writes=[pub])
                                            b_ = rot2[0] % 2; rot2[0] += 1
                                            q0 = qh * 8 + qp * 2
                                            k.op("dve", lambda e, b_=b_, pu=pu, a=a, qp=qp: e.tensor_tensor(out=Snew[b_][:], in0=pu[:, :].rearrange("p (q v) -> p q v", q=2),
                                                                                                          in1=S0h[a][:, qp * 2:qp * 2 + 2, :], op=ALU.add),
                                                 reads=[pub, BS0h[a]], writes=[BSn2[b_]])
                                            k.op("dve", lambda e, b_=b_, hd=hd, q0=q0: e.tensor_tensor(out=Snew[b_][:], in0=Snew[b_][:],
                                                                                                     in1=G["ebe"][:, hd, q0:q0 + 2].unsqueeze(2).to_broadcast([128, 2, DV]), op=ALU.mult),
                                                 reads=[BSn2[b_], GB["ebe"]], writes=[BSn2[b_]])
                                            k.dma("sp", gla_s[q0:q0 + 2, hd, :, :].rearrange("q k v -> k q v"), Snew[b_][:], reads=[BSn2[b_]], sbuf=BSn2[b_])
                            k.op("act", lambda e, po=po, half=half: e.activation(out=G["sq"][:, half, :, :].rearrange("p a b -> p (a b)"), in_=po[:, :], func=AF.Square),
                                 reads=[pob], writes=[GB["sq"]])
                        pss, pssb = bank()
                        for hd in range(NH):
                            for vh in range(2):
                                k.op("pe", lambda e, hd=hd, vh=vh, pss=pss: e.matmul(pss[:, hd * 128:(hd + 1) * 128], lhsT=onesb[:], rhs=G["sq"][:, hd // 2, (hd % 2) * 2 + vh, :],
                                                                                     start=(vh == 0), stop=(vh == 1)), reads=[GB["sq"], Bc], writes=[pssb])
                        k.op("act", lambda e, pss=pss: e.activation(out=fl(G["rr"][:]), in_=pss[:, :], func=AF.Ln, scale=1.0 / DV, bias=epsb[:, 0:1]),
                             reads=[pssb, Bc], writes=[GB["rr"]])
                        k.op("act", lambda e: e.activation(out=fl(G["rr"][:]), in_=fl(G["rr"][:]), func=AF.Exp, scale=-0.5), reads=[GB["rr"]], writes=[GB["rr"]])
                        for half in range(2):
                            po, pob = pos[half]
                            k.op("dve", lambda e, half=half, po=po: e.tensor_tensor(
                                out=by[:, half * 4:(half + 1) * 4, toks].rearrange("p (h v) t -> p h v t", v=2),
                                in0=po[:, :].rearrange("p (h v t) -> p h v t", h=2, v=2),
                                in1=G["rr"][:, half * 2:half * 2 + 2, :].unsqueeze(2).to_broadcast([128, 2, 2, 128]), op=ALU.mult),
                                reads=[pob, GB["rr"]], writes=[Bby])
                            unhold(pob)
                    if not sample:
                        for half in range(2):
                            pu, pub = bank()
                            for hdl in range(2):
                                hd = half * 2 + hdl
                                k.op("pe", lambda e, hd=hd, hdl=hdl, pu=pu: e.matmul(pu[:, hdl * DV:(hdl + 1) * DV], lhsT=G["ktok"][:, hd, :], rhs=vtok[:, j, hd * DV:(hd + 1) * DV],
                                                                                     start=True, stop=True), reads=[GB["ktok"], Bvt[j]], writes=[pub])
                            k.op("dve", lambda e, half=half, pu=pu: e.tensor_tensor(out=G["stmp"][:], in0=pu[:, :].rearrange("p (h v) -> p h v", h=2),
                                                                                    in1=Sgl[:, half * 2:half * 2 + 2, :], op=ALU.add),
                                 reads=[pub, BS], writes=[GB["stmp"]])
                            k.op("dve", lambda e, half=half: e.tensor_tensor(out=Sgl[:, half * 2:half * 2 + 2, :], in0=G["stmp"][:],
                                                                             in1=G["eb"][:, half * 2:half * 2 + 2, 127:128].to_broadcast([128, 2, DV]), op=ALU.mult),
                                 reads=[GB["stmp"], GB["eb"]], writes=[BS])
                        if full:
                            k.op("act", lambda e: e.activation(out=fl(Sglb[:]), in_=fl(Sgl[:]), func=AF.Copy), reads=[BS], writes=[BS])
                if full:
                    outs_wait.extend(BSn2)
                    barrier(BS0h + BSn2)
                barrier([Bdump] + list(GB.values()) + psb)


        def gla_pass(full, by, Bby):
            with ExitStack() as es4:
                k.es = es4
                gl = gla_alloc(full)
                gla_proj(gl, full)
                gla_tiles(gl, full, by, Bby)
                k.es = es4
                barrier([gl["Bglr"], gl["BkT"], gl["BqT"]] + gl["Bvt"] + psb)

        outs_wait = []
        k.op("dve", lambda e: e.memset(Hrun[:], 0.0), writes=[BH])
        k.op("dve", lambda e: e.memset(Sgl[:], 0.0), writes=[BS])
        k.op("dve", lambda e: e.memset(Sglb[:], 0.0), writes=[BS])
        prep(False)
        dump("hTpre", hT[:, :, 0:TP], [128, KT, TP], hTb, BF16)
        stage_end(3)
        with ExitStack() as esG:
            k.es = esG
            glp = gla_alloc(False)
            with ExitStack() as esP:
                k.es = esP
                U2 = sb("U2p", [128, NG, 128], BF16); BU2 = [Buf("U2_%d" % i) for i in range(8)]
                VH = sb("VHp", [128, 129, 2, 32], BF16); BV = Buf("VH"); BHh = Buf("Hh")
                s5_front(False, U2, BU2, VH, BV)
                k.es = esP
                gla_proj(glp, False)
                s5_recur2_state(VH, BV)
                k.es = esP
                dump("U2p", U2[:], [128, NG, 128], BU2, BF16)
                dump("VHp", VH[:], [128, 129, 2, 32], [BV], BF16)
                dump("Hrun_pre", Hrun[:], [128, 2, 32], [BH])
                barrier([BV, BHh, Bdump] + BU2 + psb)
                stage_end(4)
            k.es = esG
            gla_tiles(glp, False, None, None)
            k.es = esG
            barrier([glp["Bglr"], glp["BkT"], glp["BqT"]] + glp["Bvt"] + psb)
        k.es = esW
        dump("Sgl_pre", Sgl[:], [128, NH, DV], [BS])
        stage_end(5)
        k.op("dve", lambda e: e.tensor_scalar(out=Hrun[:], in0=Hrun[:], scalar1=flag[:, 0:1], scalar2=None, op0=ALU.mult), reads=[BH, Bp], writes=[BH])
        k.op("dve", lambda e: e.tensor_scalar(out=Sgl[:], in0=Sgl[:], scalar1=flag[:, 0:1], scalar2=None, op0=ALU.mult), reads=[BS, Bp], writes=[BS])
        k.op("act", lambda e: e.activation(out=Sglb[:], in_=Sgl[:], func=AF.Copy), reads=[BS], writes=[BS])

        ay = sb("ay", [128, 8, TM], BF16, side="right"); Bay = Buf("ay")
        prep(True)
        dump("hTmain", hT[:], [128, KT, TM], hTb, BF16)
        stage_end(6)
        with ExitStack() as esM:
            k.es = esM
            U2 = sb("U2m", [128, NG, 128 + NSEQ], BF16); BU2 = [Buf("U2m_%d" % i) for i in range(8)]
            VH = sb("VHm", [128, 129 + NSEQ, 2, 32], BF16); BV = Buf("VHm"); BHh = Buf("Hhm")
            Hsin = sb("Hsin", [128, NSEQ, 2, 32], F32); BHs = Buf("Hsin")
            Hsb = sb("Hsb", [128, NSEQ, 2, 32], BF16)
            with ExitStack() as es5:
                k.es = es5
                Sn = sb("Sn", [32, 8, 2, 128], F32); BSn = Buf("Sn")
                for q0 in range(0, NSEQ, 8):
                    for ri, src in enumerate((ssm_re_in, ssm_im_in)):
                        for a_ in range(2):
                            k.dma("sp", Sn[:, :, ri, a_ * 64:(a_ + 1) * 64], src[q0:q0 + 8, a_ * 32:(a_ + 1) * 32, :].rearrange("j g p -> g j p"), writes=[BSn])
                    pt, pb = bank()
                    for qi in range(8):
                        for ri in range(2):
                            col = (qi * 2 + ri) * 32
                            k.op("pe", lambda e, qi=qi, ri=ri, col=col, pt=pt: e.transpose(out=pt[:, col:col + 32], in_=Sn[:, qi, ri, :],
                                                                                           identity=identf[0:32, 0:32]), reads=[BSn, Bc], writes=[pb])
                    k.op("dve", lambda e, q0=q0, pt=pt: e.tensor_copy(out=Hsin[:, q0:q0 + 8, :, :].rearrange("p j r g -> p (j r g)"), in_=pt[:, :]),
                         reads=[pb], writes=[BHs])
                barrier([BSn])
            k.es = esM
            s5_front(True, U2, BU2, VH, BV)
            k.es = esM
            k.op("act", lambda e: e.activation(out=Hsb[:], in_=Hsin[:], func=AF.Copy), reads=[BHs], writes=[BHs])
            k.op("act", lambda e: e.activation(out=VH[:, 0, :, :], in_=Hrun[:], func=AF.Copy), reads=[BH], writes=[BHh])
            s5_recur2_hist(VH, BV, BHh)
            k.es = esM
            with ExitStack() as es5:
                k.es = es5
                Hso = sb("Hso", [128, NSEQ, 2, 32], F32); BHo = Buf("Hso")
                t16b = sb("t16b", [128, NSEQ, 2, 32], F32); Bt16 = Buf("t16")
                arb16 = ARR[:].unsqueeze(1).unsqueeze(1).to_broadcast([128, NSEQ, 2, 32])
                k.op("dve", lambda e: e.tensor_tensor(out=Hso[:], in0=Hsin[:], in1=arb16, op=ALU.mult), reads=[BHs, Bs5], writes=[BHo])
                k.op("dve", lambda e: e.tensor_tensor(out=t16b[:, :, 0, :], in0=Hsin[:, :, 1, :], in1=AIn[:].unsqueeze(1).to_broadcast([128, NSEQ, 32]), op=ALU.mult),
                     reads=[BHs, Bs5], writes=[Bt16])
                k.op("dve", lambda e: e.tensor_tensor(out=t16b[:, :, 1, :], in0=Hsin[:, :, 0, :], in1=AIp[:].unsqueeze(1).to_broadcast([128, NSEQ, 32]), op=ALU.mult),
                     reads=[BHs, Bs5], writes=[Bt16])
                k.op("dve", lambda e: e.tensor_tensor(out=Hso[:], in0=Hso[:], in1=t16b[:], op=ALU.add), reads=[Bt16, BHo], writes=[BHo])
                k.op("dve", lambda e: e.tensor_tensor(out=Hso[:], in0=Hso[:], in1=VH[:, 129:129 + NSEQ, :, :], op=ALU.add), reads=[BHo, BV], writes=[BHo])
                So = [sb("So%d" % i, [32, 2, 2, 128], F32) for i in range(2)]; BSo = [Buf("So%d" % i) for i in range(2)]
                for bi, q0 in enumerate(range(0, NSEQ + 1, 2)):
                    a = bi % 2
                    pt, pb = bank()
                    nq = min(2, NSEQ + 1 - q0)
                    for qi in range(nq):
                        q = q0 + qi
                        for ri in range(2):
                            src = Hso[:, q, ri, :] if q < NSEQ else Hrun[:, ri, :]
                            col = (qi * 2 + ri) * 128
                            k.op("pe", lambda e, src=src, col=col, pt=pt: e.transpose(out=pt[0:32, col:col + 128], in_=src, identity=identf[:]),
                                 reads=[BHo, BH, Bc], writes=[pb])
                    k.op("dve", lambda e, a=a, nq=nq, pt=pt: e.tensor_copy(out=So[a][:, 0:nq, :, :].rearrange("g j r q -> g (j r q)"), in_=pt[0:32, 0:nq * 256]),
                         reads=[pb], writes=[BSo[a]])
                    if q0 < NSEQ:
                        for ri, dst in enumerate((ssm_re_s, ssm_im_s)):
                            for a_ in range(2):
                                k.dma("sp", dst[q0:q0 + 2, a_ * 32:(a_ + 1) * 32, :].rearrange("j g p -> g j p"), So[a][:, :, ri, a_ * 64:(a_ + 1) * 64],
                                      reads=[BSo[a]], sbuf=BSo[a])
                    else:
                        for ri, dst in enumerate((ssm_re_p, ssm_im_p)):
                            k.dma("sp", dst.rearrange("(a g) p -> g a p", a=2), So[a][:, 0, ri, :].rearrange("g (a p) -> g a p", a=2), reads=[BSo[a]], sbuf=BSo[a])
                outs_wait.extend(BSo)
                barrier([BHo, Bt16] + BSo)
            k.es = esM
            with ExitStack() as es5:
                k.es = es5
                Yc = [sb("Yc%d" % i, [128, 8, 128], BF16) for i in range(2)]; BYc = [Buf("Yc%d" % i) for i in range(2)]
                Ycs = [sb("Ycs%d" % i, [NSEQ, 8, 128], BF16) for i in range(2)]; BYcs = [Buf("Ycs%d" % i) for i in range(2)]
                for ct in range(8):
                    for ci, (toff, C, coff) in enumerate(((0, 128, 0), (TP, NSEQ, 128))):
                        yc, byc = (Yc[ct % 2], BYc[ct % 2]) if ci == 0 else (Ycs[ct % 2], BYcs[ct % 2])
                        for gq in range(2):
                            pt, pb = bank()
                            for gi in range(4):
                                g = ct * 8 + gq * 4 + gi
                                gh, gp = g // 32, g % 32
                                rows = slice(gh * 64, gh * 64 + 64)
                                out = pt[0:C, gi * 128:(gi + 1) * 128]
                                k.op("pe", lambda e, out=out, g=g, C=C, coff=coff: e.matmul(out, lhsT=U2[:, g, coff:coff + C], rhs=Toep[:, g, :], start=True, stop=False),
                                     reads=[BU2[g // 8], Bs5], writes=[pb])
                                for ri, YY in enumerate((Ybr, Ybn)):
                                    hp = VH[rows, 0:128, ri, gp] if ci == 0 else Hsb[rows, :, ri, gp]
                                    k.op("pe", lambda e, out=out, hp=hp, YY=YY, rows=rows, gp=gp, ri=ri: e.matmul(out, lhsT=hp, rhs=YY[rows, gp, 1:9, :], start=False, stop=(ri == 1)),
                                         reads=[BHh, BHs, Bs5], writes=[pb])
                            k.op("act", lambda e, pt=pt, C=C, yc=yc, gq=gq: e.activation(
                                out=yc[0:C, :, gq * 64:(gq + 1) * 64].rearrange("c t (g h) -> c g t h", h=16),
                                in_=pt[0:C, :].rearrange("c (g t h) -> c g t h", g=4, h=16), func=AF.Gelu_apprx_tanh), reads=[pb], writes=[byc])
                        pt, pb = bank()
                        ptb = pt[:].bitcast(BF16)
                        for t in range(8):
                            k.op("pe", lambda e, t=t, C=C, yc=yc, ptb=ptb: e.transpose(out=ptb[:, t * 128:t * 128 + C], in_=yc[0:C, t, :], identity=identb[0:C, 0:C]),
                                 reads=[byc, Bc], writes=[pb])
                        k.op("act", lambda e, C=C, toff=toff, ct=ct, ptb=ptb: e.activation(
                            out=ay[:, ct, toff:toff + 8 * C].rearrange("p (c t) -> p t c", t=8),
                            in_=ptb[:, 0:1024].rearrange("p (t c) -> p t c", t=8)[:, :, 0:C], func=AF.Copy), reads=[pb], writes=[Bay])
                barrier(BYc + BYcs)
            k.es = esM
            dump("ay", ay[:], [128, 8, TM], [Bay], BF16)
            dump("Hrun_main", Hrun[:], [128, 2, 32], [BH])
            barrier([BV, BHh, BHs, Bdump] + BU2 + [Bs5] + psb)
            stage_end(7)
        esW.close()
        k.es = es
        by = sb("by", [128, 8, TM], BF16, side="right"); Bby = Buf("by")
        gla_pass(True, by, Bby)
        k.es = es
        dump("by", by[:], [128, 8, TM], [Bby], BF16)
        dump("Sgl_main", Sgl[:], [128, NH, DV], [BS])
        stage_end(8)
        k.dma("sp", gla_p.rearrange("h k v -> k h v"), Sgl[:], reads=[BS], sbuf=BS)
        outs_wait.append(BS)

        T = TM
        nblocks = [(n0, min(512, T - n0)) for n0 in range(0, T, 512)]
        hall = hTb[0:9]
        mg = sb("mg", [128, KT, TM], BF16, side="right"); Bmg = [Buf("mg%d" % j) for j in range(9)]
        By_d = [Buf("ydram%d" % j) for j in range(9)]
        with ExitStack() as esC:
            k.es = esC
            sg = [sb("sg%d" % i, [128, 512], F32) for i in range(2)]; Bsg = [Buf("sg%d" % i) for i in range(2)]
            m1 = [sb("m1_%d" % i, [128, 512], F32) for i in range(2)]; Bm1 = [Buf("m1_%d" % i) for i in range(2)]
            rot = [0]
            ay2 = sb("ay2", [128, 8, TM], BF16); Bay2 = Buf("ay2")
            for i in range(4):
                ws, wb = w_next((w_glu, i * WC))
                for mt in range(2):
                    m = i * 2 + mt
                    for (n0, nn) in nblocks:
                        pt, pb = bank()
                        for kt in range(8):
                            k.op("pe", lambda e, kt=kt, n0=n0, nn=nn, mt=mt, ws=ws, pt=pt: e.matmul(
                                pt[:, 0:nn], lhsT=ws[:, kt, mt * 128:(mt + 1) * 128], rhs=ay[:, kt, n0:n0 + nn], start=(kt == 0), stop=(kt == 7)),
                                reads=[Bay, wb], writes=[pb])
                        a = rot[0] % 2; rot[0] += 1
                        k.op("act", lambda e, a=a, nn=nn, m=m, pt=pt: e.activation(out=sg[a][:, 0:nn], in_=pt[:, 0:nn], func=AF.Sigmoid, bias=bglu_col[:, m:m + 1]),
                             reads=[pb, Bp], writes=[Bsg[a]])
                        k.op("dve", lambda e, a=a, n0=n0, nn=nn, m=m: e.tensor_tensor(out=ay2[:, m, n0:n0 + nn], in0=ay[:, m, n0:n0 + nn], in1=sg[a][:, 0:nn], op=ALU.mult),
                             reads=[Bay, Bsg[a]], writes=[Bay2])
            for (c00, tgt, Btgt) in ((1024, ay2, Bay2), (4096, by, Bby)):
                for i in range(4):
                    ws, wb = w_next((w_in, c00 + i * WC))
                    for mt in range(2):
                        m = i * 2 + mt
                        for (n0, nn) in nblocks:
                            pt, pb = bank()
                            for kt in range(KT):
                                k.op("pe", lambda e, kt=kt, n0=n0, nn=nn, mt=mt, ws=ws, pt=pt: e.matmul(
                                    pt[:, 0:nn], lhsT=ws[:, kt, mt * 128:(mt + 1) * 128], rhs=hT[:, kt, n0:n0 + nn], start=(kt == 0), stop=(kt == KT - 1)),
                                    reads=hall + [wb], writes=[pb])
                            a = rot[0] % 2; rot[0] += 1
                            k.op("act", lambda e, a=a, nn=nn, pt=pt: e.activation(out=sg[a][:, 0:nn], in_=pt[:, 0:nn], func=AF.Silu), reads=[pb], writes=[Bsg[a]])
                            if tgt is by:
                                k.op("dve", lambda e, a=a, n0=n0, nn=nn, m=m, tgt=tgt: e.scalar_tensor_tensor(out=tgt[:, m, n0:n0 + nn], in0=tgt[:, m, n0:n0 + nn], scalar=ggain_col[:, m:m + 1],
                                                                                                               in1=sg[a][:, 0:nn], op0=ALU.mult, op1=ALU.mult),
                                     reads=[Btgt, Bsg[a], Bp], writes=[Btgt])
                            else:
                                k.op("dve", lambda e, a=a, n0=n0, nn=nn, m=m, tgt=tgt: e.tensor_tensor(out=tgt[:, m, n0:n0 + nn], in0=tgt[:, m, n0:n0 + nn], in1=sg[a][:, 0:nn], op=ALU.mult),
                                     reads=[Btgt, Bsg[a]], writes=[Btgt])
            m1A = sb("m1A", [128, 2, TM], F32); Bm1A = Buf("m1A")
            for fb in range(8):
                for br, (cg, wo_d, src, Bsrc) in enumerate(((5136, w_a_out, ay2, Bay2), (7184, w_b_out, by, Bby))):
                    wg, wgb = w_next((w_in, cg + fb * WC))
                    wo, wob = w_next((wo_d, fb * WC))
                    for mt in range(2):
                        f = fb * 2 + mt
                        for (n0, nn) in nblocks:
                            pg, pgb = bank()
                            for kt in range(KT):
                                k.op("pe", lambda e, kt=kt, pg=pg, wg=wg, mt=mt, n0=n0, nn=nn: e.matmul(
                                    pg[:, 0:nn], lhsT=wg[:, kt, mt * 128:(mt + 1) * 128], rhs=hT[:, kt, n0:n0 + nn],
                                    start=(kt == 0), stop=(kt == KT - 1)), reads=hall + [wgb], writes=[pgb])
                            po, pob = bank()
                            for kt in range(8):
                                k.op("pe", lambda e, kt=kt, po=po, wo=wo, src=src, mt=mt, n0=n0, nn=nn: e.matmul(
                                    po[:, 0:nn], lhsT=wo[:, kt, mt * 128:(mt + 1) * 128], rhs=src[:, kt, n0:n0 + nn],
                                    start=(kt == 0), stop=(kt == 7)), reads=[Bsrc, wob], writes=[pob])
                            a = rot[0] % 2; rot[0] += 1
                            k.op("act", lambda e, a=a, pg=pg, nn=nn: e.activation(out=sg[a][:, 0:nn], in_=pg[:, 0:nn], func=AF.Sigmoid), reads=[pgb], writes=[Bsg[a]])
                            if br == 0:
                                k.op("dve", lambda e, a=a, po=po, mt=mt, n0=n0, nn=nn: e.tensor_tensor(out=m1A[:, mt, n0:n0 + nn], in0=po[:, 0:nn], in1=sg[a][:, 0:nn], op=ALU.mult),
                                     reads=[pob, Bsg[a]], writes=[Bm1A])
                            else:
                                k.op("dve", lambda e, a=a, po=po, nn=nn: e.tensor_tensor(out=m1[a][:, 0:nn], in0=po[:, 0:nn], in1=sg[a][:, 0:nn], op=ALU.mult),
                                     reads=[pob, Bsg[a]], writes=[Bm1[a]])
                                tiles = list(range(n0 // 128, (n0 + nn) // 128))
                                k.op("dve", lambda e, a=a, f=f, mt=mt, n0=n0, nn=nn: e.tensor_tensor(out=mg[:, f, n0:n0 + nn], in0=m1[a][:, 0:nn], in1=m1A[:, mt, n0:n0 + nn], op=ALU.add),
                                     reads=[Bm1[a], Bm1A], writes=[Bmg[t_] for t_ in tiles])
            dump("ay2", ay2[:], [128, 8, TM], [Bay2], BF16)
            dump("by2", by[:], [128, 8, TM], [Bby], BF16)
            dump("mg", mg[:], [128, KT, TM], Bmg, BF16)
            barrier(Bsg + Bm1 + [Bm1A, Bay2, Bay, Bby, Bdump] + hall + psb)
            stage_end(9)
        with ExitStack() as esO:
            k.es = esO
            gts = sb("gts", [17, D], F32); Bgts = Buf("gts")
            k.dma("sp", gts[:], gate_scr[:, :], reads=[Bgs], writes=[Bgts])
            gtk = [sb("gtk%d" % i, [128, WC], F32) for i in range(2)]; Bgtk = [Buf("gtk%d" % i) for i in range(2)]
            fgb = sb("fgb", [128, D], F32); Bfg = Buf("fgb")
            k.dma("sp", fgb[:], fgain.partition_broadcast(128), writes=[Bfg])
            yacc = hT[:].rearrange("p a b -> p (a b)").rearrange("p (j f) -> p j f", j=9)
            Bya = [Buf("yacc%d" % j) for j in range(9)]
            xs = [sb("fxs%d" % i, [128, D], F32) for i in range(2)]; xb_ = [Buf("fxs%d" % i) for i in range(2)]
            jk = sb("fjk", [128, D], BF16); Bjk = Buf("fjk")
            st = sb("fst", [128, 9, 4], F32); Bst = [Buf("fst%d" % j_) for j_ in range(9)]
            for j in range(2):
                k.dma("sp", xs[j][:], xmain[j * 128:(j + 1) * 128, :], writes=[xb_[j]])

            def final_tile(j):
                a = j % 2
                dst = y_main[j * 128:(j + 1) * 128, :] if j < 8 else y_smp[:, :]
                k.op("dve", lambda e: e.tensor_tensor(out=xs[a][:], in0=xs[a][:], in1=yacc[:, j, :], op=ALU.add), reads=[xb_[a], Bya[j]], writes=[xb_[a]])
                k.op("act", lambda e: e.activation(out=jk[:], in_=xs[a][:], func=AF.Square, accum_out=st[:, j, 0:1]), reads=[xb_[a]], writes=[Bjk, Bst[j]])
                k.op("act", lambda e: e.activation(out=st[:, j, 2:3], in_=st[:, j, 0:1], func=AF.Ln, scale=1.0 / D, bias=epsb[:, 0:1]), reads=[Bst[j], Bc], writes=[Bst[j]])
                k.op("act", lambda e: e.activation(out=st[:, j, 3:4], in_=st[:, j, 2:3], func=AF.Exp, scale=-0.5), reads=[Bst[j]], writes=[Bst[j]])
                k.op("dve", lambda e: e.scalar_tensor_tensor(out=xs[a][:], in0=xs[a][:], scalar=st[:, j, 3:4], in1=fgb[:], op0=ALU.mult, op1=ALU.mult),
                     reads=[xb_[a], Bst[j], Bfg], writes=[xb_[a]])
                k.dma("sp", dst, xs[a][:], reads=[xb_[a]], writes=[By_d[j]], sbuf=xb_[a])
                if j + 2 < 9:
                    jn = j + 2
                    srcn = xmain[jn * 128:(jn + 1) * 128, :] if jn < 8 else xsmp[:, :]
                    k.dma("sp", xs[a][:], srcn, writes=[xb_[a]])

            for i in range(8):
                ws, wb = w_next((w_out, i * WC))
                for wi, oh in enumerate((ohP, ohS)):
                    pt, pb = bank()
                    k.op("pe", lambda e, oh=oh, i=i, pt=pt: e.matmul(pt[:, 0:WC], lhsT=oh[:, :], rhs=gts[:, i * WC:(i + 1) * WC], start=True, stop=True),
                         reads=[Bgts, Bc], writes=[pb])
                    k.op("act", lambda e, wi=wi, pt=pt: e.activation(out=gtk[wi][:], in_=pt[:, 0:WC], func=AF.Copy), reads=[pb], writes=[Bgtk[wi]])
                for j in range(9):
                    pt, pb = bank()
                    for kt in range(KT):
                        k.op("pe", lambda e, kt=kt, j=j, ws=ws, pt=pt: e.matmul(pt[:, 0:WC], lhsT=mg[:, kt, j * 128:(j + 1) * 128], rhs=ws[:, kt, :],
                                                                               start=(kt == 0), stop=(kt == KT - 1)), reads=[Bmg[j], wb], writes=[pb])
                    wi = 0 if j < 8 else 1
                    k.op("dve", lambda e, j=j, i=i, wi=wi, pt=pt: e.tensor_tensor(out=yacc[:, j, i * WC:(i + 1) * WC], in0=pt[:, 0:WC], in1=gtk[wi][:], op=ALU.mult),
                         reads=[pb, Bgtk[wi]], writes=[Bya[j]])
                    if i == 7:
                        final_tile(j)
            barrier(outs_wait + By_d + xb_ + Bya + [Bfg, Bgts, Bjk] + Bst + Bgtk + Bmg + psb)
            stage_end(10)
        esR.close()
        k.es = es
    except _Stop:
        pass
    return nc


_NC_CACHE = {}


def _shard_inputs(inp):
    f = lambda a: np.ascontiguousarray(np.asarray(a, dtype=np.float32))
    xp = f(inp["x_prompt"]); xs = f(inp["x_sample"]); cp = f(inp["c_prompt"]); cs = f(inp["c_sample"])
    sre = f(inp["state_ssm_re"])[0]; sim = f(inp["state_ssm_im"])[0]; sgl = f(inp["state_gla"])[0]
    shared = {
        "w_ada": f(inp["w_ada"])[0], "b_ada": f(inp["b_ada"])[0], "norm_gain": f(inp["norm_gain"])[0], "w_in": f(inp["w_in"])[0],
        "lambda_re": f(inp["lambda_re"])[0], "lambda_im": f(inp["lambda_im"])[0], "log_dt": f(inp["log_dt"])[0],
        "ssm_b_re": f(inp["ssm_b_re"])[0], "ssm_b_im": f(inp["ssm_b_im"])[0], "ssm_c_re": f(inp["ssm_c_re"])[0], "ssm_c_im": f(inp["ssm_c_im"])[0],
        "d_skip": f(inp["d_skip"])[0], "w_glu": f(inp["w_glu"])[0], "b_glu": f(inp["b_glu"])[0], "w_gate_up": f(inp["w_gate_up"])[0],
        "b_gate": f(inp["b_gate"])[0], "gla_norm_gain": f(inp["gla_norm_gain"])[0], "w_a_out": f(inp["w_a_out"])[0],
        "w_b_out": f(inp["w_b_out"])[0], "w_out": f(inp["w_out"])[0], "final_norm_gain": f(inp["final_norm_gain"]),
    }
    maps = []
    for core in range(8):
        b, half = core // 2, core % 2
        m = dict(shared)
        m["xpre"] = np.ascontiguousarray(xp[b, 0:TP])
        m["xmain"] = np.ascontiguousarray(xp[b, half * TP:(half + 1) * TP])
        sl = slice(core * NSEQ, (core + 1) * NSEQ)
        m["xsmp"] = np.ascontiguousarray(xs[sl].reshape(NSEQ * 8, D))
        m["c17"] = np.ascontiguousarray(np.concatenate([cp[b:b + 1], cs[sl]], axis=0))
        m["flag"] = np.full((128, 1), float(half), np.float32)
        m["ssm_re_in"] = np.ascontiguousarray(sre[sl]); m["ssm_im_in"] = np.ascontiguousarray(sim[sl])
        m["gla_in"] = np.ascontiguousarray(sgl[sl])
        maps.append(m)
    return maps


def run_debug(inputs, stop, core=1):
    nc = build_nc(dbg=True, stop=stop)
    maps = _shard_inputs(inputs)
    res = run_bass_kernel_spmd(nc, [maps[core]], core_ids=[0])
    return res.results[0], maps[core]


def kernel(**inputs):
    if "nc" not in _NC_CACHE:
        _NC_CACHE["nc"] = build_nc()
    nc = _NC_CACHE["nc"]
    maps = _shard_inputs(inputs)
    res = run_bass_kernel_spmd(nc, maps, core_ids=list(range(8)))
    R = res.results
    y_prompt = np.zeros((4, 2048, D), np.float32); y_sample = np.zeros((128, 8, D), np.float32)
    re_p = np.zeros((1, 4, NG, 64), np.float32); im_p = np.zeros((1, 4, NG, 64), np.float32)
    gl_p = np.zeros((1, 4, NH, DK, DV), np.float32)
    re_s = np.zeros((1, 128, NG, 64), np.float32); im_s = np.zeros((1, 128, NG, 64), np.float32)
    gl_s = np.zeros((1, 128, NH, DK, DV), np.float32)
    for core in range(8):
        b, half = core // 2, core % 2
        r = R[core]
        y_prompt[b, half * TP:(half + 1) * TP] = r["y_main"]
        sl = slice(core * NSEQ, (core + 1) * NSEQ)
        y_sample[sl] = r["y_smp"].reshape(NSEQ, 8, D)
        re_s[0, sl] = r["ssm_re_s"]; im_s[0, sl] = r["ssm_im_s"]; gl_s[0, sl] = r["gla_s"]
        if half == 1:
            re_p[0, b] = r["ssm_re_p"]; im_p[0, b] = r["ssm_im_p"]; gl_p[0, b] = r["gla_p"]
    return (y_prompt, y_sample, re_p, im_p, gl_p, re_s, im_s, gl_s)
```

```python
import math
from contextlib import ExitStack

import numpy as np
import concourse.bass as bass
import concourse.mybir as mybir
from concourse.bass_utils import run_bass_kernel_spmd

F32 = mybir.dt.float32
BF16 = mybir.dt.bfloat16
AF = mybir.ActivationFunctionType
ALU = mybir.AluOpType

D = 2048
KT = 16
WA = 1024
NG = 64
NH = 4
DK = 128
DV = 256
EPS = 1e-6
INC = 9232
NSEQ = 16
TP = 1024
TM = TP + 128
WC = 256
NSLOT = 3
PI = math.pi


class Ev:
    __slots__ = ("sem", "val", "eng")

    def __init__(self, sem, val, eng):
        self.sem, self.val, self.eng = sem, val, eng


class Buf:
    def __init__(self, name):
        self.name = name
        self.w = None
        self.r = {}
        self.dsem = None
        self.dcnt = 0


class K:
    def __init__(self, nc, es):
        self.nc, self.es = nc, es
        self.E = {"pe": nc.tensor, "dve": nc.vector, "act": nc.scalar, "pool": nc.gpsimd, "sp": nc.sync}
        self.sem = {e: es.enter_context(nc.semaphore("s_" + e)) for e in ("pe", "dve", "act", "pool")}
        self.cnt = {e: 0 for e in self.sem}
        self.seen = {e: {} for e in self.E}
        self.nbuf = 0
        self.esR = es
        self.es_sem = es

    def sb(self, name, shape, dt, side="left"):
        st = self.es if side == "left" else self.esR
        self.nname = getattr(self, "nname", 0) + 1
        return st.enter_context(self.nc.sbuf_tensor("sb%d_%s" % (self.nname, name), shape, dt, side=side))

    def _deps(self, reads, writes):
        evs = []
        for b in reads:
            if b.w is not None:
                evs.append(b.w)
        for b in writes:
            if b.w is not None:
                evs.append(b.w)
            evs.extend(b.r.values())
        return evs

    def _waits(self, e, evs):
        best = {}
        for ev in evs:
            if ev.eng == "pe" and e == "pe":
                continue
            kk = id(ev.sem)
            if kk not in best or best[kk].val < ev.val:
                best[kk] = ev
        for kk, ev in best.items():
            if self.seen[e].get(kk, 0) >= ev.val:
                continue
            self.E[e].wait_ge(ev.sem, ev.val)
            self.seen[e][kk] = ev.val

    def _commit(self, ev, reads, writes):
        for b in writes:
            b.w = ev
            b.r = {}
        for b in reads:
            if b not in writes:
                kk = id(ev.sem)
                if kk not in b.r or b.r[kk].val < ev.val:
                    b.r[kk] = ev

    def op(self, e, fn, reads=(), writes=()):
        self._waits(e, self._deps(reads, writes))
        ins = fn(self.E[e])
        self.cnt[e] += 1
        ins.then_inc(self.sem[e], 1)
        ev = Ev(self.sem[e], self.cnt[e], e)
        self._commit(ev, reads, writes)
        return ev

    def dma(self, q, out, in_, reads=(), writes=(), sbuf=None, **kw):
        self._waits(q, self._deps(reads, writes))
        tb = sbuf if sbuf is not None else (writes[0] if writes else reads[0])
        if tb.dsem is None:
            self.nbuf += 1
            tb.dsem = self.es_sem.enter_context(self.nc.semaphore("d%d" % self.nbuf))
        tb.dcnt += 16
        self.E[q].dma_start(out=out, in_=in_, **kw).then_inc(tb.dsem, 16)
        ev = Ev(tb.dsem, tb.dcnt, "dma")
        self._commit(ev, reads, writes)
        return ev

    def wait_all(self, e, bufs):
        evs = []
        for b in bufs:
            if b.w is not None:
                evs.append(b.w)
            evs.extend(b.r.values())
        self._waits(e, evs)


class _Stop(Exception):
    pass


def build_nc(dbg=None, stop=None):
    nc = bass.Bass("TRN2", target_bir_lowering=False)
    dumps = []

    def din(name, shape):
        return nc.dram_tensor(name, list(shape), F32, kind="ExternalInput").ap()

    def dout(name, shape):
        return nc.dram_tensor(name, list(shape), F32, kind="ExternalOutput").ap()

    xpre = din("xpre", [TP, D]); xmain = din("xmain", [TP, D]); xsmp = din("xsmp", [128, D])
    c17 = din("c17", [17, D]); flag_d = din("flag", [128, 1])
    ssm_re_in = din("ssm_re_in", [NSEQ, NG, 64]); ssm_im_in = din("ssm_im_in", [NSEQ, NG, 64])
    gla_in = din("gla_in", [NSEQ, NH, DK, DV])
    w_ada = din("w_ada", [D, 3 * D]); b_ada = din("b_ada", [3 * D]); norm_gain = din("norm_gain", [D])
    w_in = din("w_in", [D, INC])
    lam_re = din("lambda_re", [NG, 64]); lam_im = din("lambda_im", [NG, 64]); log_dt = din("log_dt", [NG])
    b_re = din("ssm_b_re", [NG, 64, 16]); b_im = din("ssm_b_im", [NG, 64, 16])
    c_re = din("ssm_c_re", [NG, 16, 64]); c_im = din("ssm_c_im", [NG, 16, 64])
    d_skip = din("d_skip", [WA]); w_glu = din("w_glu", [WA, WA]); b_glu = din("b_glu", [WA])
    w_gate_up = din("w_gate_up", [16, NH * DK]); b_gate = din("b_gate", [NH * DK])
    gla_gain = din("gla_norm_gain", [WA])
    w_a_out = din("w_a_out", [WA, D]); w_b_out = din("w_b_out", [WA, D]); w_out = din("w_out", [D, D])
    fgain = din("final_norm_gain", [D])

    y_main = dout("y_main", [TP, D]); y_smp = dout("y_smp", [128, D])
    ssm_re_p = dout("ssm_re_p", [NG, 64]); ssm_im_p = dout("ssm_im_p", [NG, 64])
    gla_p = dout("gla_p", [NH, DK, DV])
    ssm_re_s = dout("ssm_re_s", [NSEQ, NG, 64]); ssm_im_s = dout("ssm_im_s", [NSEQ, NG, 64])
    gla_s = dout("gla_s", [NSEQ, NH, DK, DV])
    gate_scr = nc.dram_tensor("gate_scr", [17, D], F32, kind="Internal").ap()
    es = ExitStack()
    try:
      with es:
        k = K(nc, es)
        sb = k.sb
        Bdump = Buf("dump")

        def dump(name, ap, shape, bufs, dt=F32):
            if not dbg:
                return
            d = nc.dram_tensor("dbg_" + name, list(shape), dt, kind="ExternalOutput").ap()
            k.dma("sp", d, ap, reads=list(bufs), writes=[Bdump])
            dumps.append(name)

        def stage_end(n):
            if stop is not None and n >= stop:
                for e_ in ("pe", "dve", "act", "pool", "sp"):
                    k.wait_all(e_, [Bdump])
                raise _Stop()
        psum = [es.enter_context(nc.psum_tensor("ps%d" % i, [128, 512], F32)) for i in range(8)]
        psb = [Buf("ps%d" % i) for i in range(8)]
        pcur = [0]
        held = set()

        def bank(hold=False):
            i = pcur[0]
            while i in held:
                i = (i + 1) % 8
            pcur[0] = (i + 1) % 8
            if hold:
                held.add(i)
            return psum[i], psb[i]

        def unhold(pbuf):
            held.discard(psb.index(pbuf))

        Bc = Buf("consts")
        identf = sb("identf", [128, 128], F32); identb = sb("identb", [128, 128], BF16)
        onesb = sb("onesb", [128, 128], BF16); onesf = sb("onesf", [128, 128], F32)
        causal = sb("causal", [128, 128], F32); seqmask = sb("seqmask", [128, NSEQ], F32)
        smask = sb("smask", [128, 128], F32); rstmask = sb("rstmask", [128, 128], F32)
        mask3 = sb("mask3", [128, 8], F32)
        ohP = sb("ohP", [17, 128], F32); ohS = sb("ohS", [17, 128], F32)
        P = lambda fn: k.op("pool", fn, writes=[Bc])
        P(lambda e: e.memset(identf[:], 1.0))
        P(lambda e: e.affine_select(out=identf[:], in_=identf[:], compare_op=ALU.is_equal, fill=0.0, base=0,
                                    pattern=[[-1, 128]], channel_multiplier=1))
        P(lambda e: e.tensor_copy(out=identb[:], in_=identf[:]))
        P(lambda e: e.memset(onesb[:], 1.0))
        P(lambda e: e.memset(onesf[:], 1.0))
        P(lambda e: e.memset(causal[:], 1.0))
        P(lambda e: e.affine_select(out=causal[:], in_=causal[:], compare_op=ALU.is_ge, fill=0.0, base=0,
                                    pattern=[[1, 128]], channel_multiplier=-1))
        P(lambda e: e.memset(seqmask[:], 1.0))
        P(lambda e: e.affine_select(out=seqmask[:], in_=seqmask[:], compare_op=ALU.is_ge, fill=0.0, base=0,
                                    pattern=[[-8, NSEQ]], channel_multiplier=1))
        P(lambda e: e.affine_select(out=seqmask[:], in_=seqmask[:], compare_op=ALU.is_ge, fill=0.0, base=7,
                                    pattern=[[8, NSEQ]], channel_multiplier=-1))
        P(lambda e: e.tensor_tensor(out=smask[:].rearrange("p (j r) -> p j r", r=8),
                                    in0=causal[:].rearrange("p (j r) -> p j r", r=8),
                                    in1=seqmask[:].unsqueeze(2).to_broadcast([128, NSEQ, 8]), op=ALU.mult))
        P(lambda e: e.memset(rstmask[:], 1.0))
        P(lambda e: e.affine_select(out=rstmask[:].rearrange("p (j r) -> p j r", r=8),
                                    in_=rstmask[:].rearrange("p (j r) -> p j r", r=8),
                                    compare_op=ALU.is_ge, fill=0.0, base=-1,
                                    pattern=[[0, NSEQ], [1, 8]], channel_multiplier=0))
        P(lambda e: e.memset(mask3[:], 1.0))
        P(lambda e: e.affine_select(out=mask3[:], in_=mask3[:], compare_op=ALU.is_ge, fill=0.0, base=15,
                                    pattern=[[16, 8]], channel_multiplier=-1))
        P(lambda e: e.memset(ohP[:], 0.0))
        P(lambda e: e.memset(ohP[0:1, :], 1.0))
        P(lambda e: e.memset(ohS[:], 1.0))
        P(lambda e: e.affine_select(out=ohS[:], in_=ohS[:], compare_op=ALU.is_ge, fill=0.0, base=8,
                                    pattern=[[1, 128]], channel_multiplier=-8))
        P(lambda e: e.affine_select(out=ohS[:], in_=ohS[:], compare_op=ALU.is_ge, fill=0.0, base=-1,
                                    pattern=[[-1, 128]], channel_multiplier=8))

        hrst = sb("hrst", [128, NH, 128], F32); hrst_s = sb("hrst_s", [128, NH, 128], F32)
        P(lambda e: e.memset(hrst[:], 1.0))
        for hd_ in range(NH):
            P(lambda e, hd_=hd_: e.memset(hrst[:, hd_, 0:1], 0.0))
        P(lambda e: e.tensor_copy(out=hrst_s[:], in_=rstmask[:].unsqueeze(1).to_broadcast([128, NH, 128])))
        Bp = Buf("params")
        bglu_col = sb("bglu_col", [128, 8], F32); bgate_col = sb("bgate_col", [128, 4], F32)
        nbgate = sb("nbgate", [128, 4], F32); ggain_col = sb("ggain_col", [128, 8], F32)
        wup = sb("wup", [17, NH * DK], BF16); flag = sb("flag", [128, 1], F32)
        k.dma("sp", bglu_col[:], b_glu.rearrange("(m p) -> p m", p=128), writes=[Bp], allow_slow_non_contiguous=True)
        k.dma("sp", bgate_col[:], b_gate.rearrange("(m p) -> p m", p=128), writes=[Bp], allow_slow_non_contiguous=True)
        k.dma("sp", ggain_col[:], gla_gain.rearrange("(m p) -> p m", p=128), writes=[Bp], allow_slow_non_contiguous=True)
        k.dma("sp", flag[:], flag_d[:, :], writes=[Bp])
        Bwup = Buf("wup")
        k.dma("pool", wup[0:16, :], w_gate_up[:, :], writes=[Bwup])
        k.dma("pool", wup[16:17, :], b_gate.rearrange("(o n) -> o n", o=1), writes=[Bwup])
        k.op("dve", lambda e: e.tensor_scalar(out=nbgate[:], in0=bgate_col[:], scalar1=-1.0, scalar2=None, op0=ALU.mult),
             reads=[Bp], writes=[Bc])

        wslot = [sb("wslot%d" % i, [128, KT, WC], BF16) for i in range(NSLOT)]
        wbuf = [Buf("wslot%d" % i) for i in range(NSLOT)]
        specs = []

        def wspec(w, c0, ncol, kt):
            specs.append((w, c0, ncol, kt))

        for blk in range(24):
            wspec(w_ada, blk * WC, WC, KT)
        for full in (False, True):
            for i in range(4):
                wspec(w_in, i * WC, WC, KT)
            wspec(w_in, 5120, 16, KT)
            for i in range(2):
                wspec(w_in, 2560 + i * WC, WC, KT)
            for i in range(4):
                wspec(w_in, 3072 + i * WC, WC, KT)
            if full:
                for i in range(2):
                    wspec(w_in, 2048 + i * WC, WC, KT)
                for i in range(4):
                    wspec(w_glu, i * WC, WC, 8)
                for i in range(4):
                    wspec(w_in, 1024 + i * WC, WC, KT)
                for i in range(4):
                    wspec(w_in, 4096 + i * WC, WC, KT)
                for fb in range(8):
                    wspec(w_in, 5136 + fb * WC, WC, KT)
                    wspec(w_a_out, fb * WC, WC, 8)
                    wspec(w_in, 7184 + fb * WC, WC, KT)
                    wspec(w_b_out, fb * WC, WC, 8)
                for i in range(8):
                    wspec(w_out, i * WC, WC, KT)
        wstate = {"issued": 0, "used": 0}

        def w_issue():
            i = wstate["issued"]
            if i >= len(specs):
                return
            w, c0, ncol, kt = specs[i]
            s = i % NSLOT
            src = w.rearrange("(kt p) n -> p kt n", p=128)[:, :, c0:c0 + ncol]
            k.dma("pool", wslot[s][:, 0:kt, 0:ncol], src, writes=[wbuf[s]])
            wstate["issued"] = i + 1

        def w_next(expect=None):
            i = wstate["used"]
            if expect is not None:
                assert specs[i][0] is expect[0] and specs[i][1] == expect[1], (i, specs[i][1:], expect[1:])
            while wstate["issued"] < min(i + NSLOT - 1, len(specs)) or wstate["issued"] <= i:
                w_issue()
            wstate["used"] = i + 1
            s = i % NSLOT
            return wslot[s], wbuf[s]

        shiftT = sb("shiftT", [128, KT, 17], F32); scaleT = sb("scaleT", [128, KT, 17], F32)
        Bmod = Buf("modT")
        hT = sb("hT", [128, KT, TM], BF16)
        hTb = [Buf("hT%d" % j) for j in range(9)]
        Hrun = sb("Hrun", [128, 2, 32], F32); BH = Buf("Hrun")
        Sgl = sb("Sgl", [128, NH, DV], F32); Sglb = sb("Sglb", [128, NH, DV], BF16); BS = Buf("Sgl")
        epsb = sb("epsb", [128, 1], F32)
        k.op("pool", lambda e: e.memset(epsb[:], EPS), writes=[Bc])
        esR = ExitStack()
        k.esR = esR
        esW = ExitStack()
        k.es = esW
        Toep = sb("Toep", [128, NG, 128], BF16)
        Mw = sb("Mw", [128, NG, 2, 64], BF16)
        Ybr = sb("Ybr", [128, 32, 9, 16], BF16); Ybn = sb("Ybn", [128, 32, 9, 16], BF16)
        ARR = sb("ARR", [128, 32], F32); AIp = sb("AIp", [128, 32], F32); AIn = sb("AIn", [128, 32], F32)
        A64 = sb("A64", [128, 3, 32], F32)
        Bs5 = Buf("s5w")

        Bgs = Buf("gate_scr")

        def ada_emit():
            esA = ExitStack()
            k.esR = esA
            c17s = sb("c17s", [17, D], F32, side="right"); Bp2 = Buf("c17s")
            siluT = sb("siluT", [128, KT, 17], BF16, side="right"); Bsl = Buf("silu")
            brow = [sb("brow%d" % i, [1, WC], F32, side="right") for i in range(2)]; Bbr = [Buf("brow%d" % i) for i in range(2)]
            modr = [sb("modr%d" % i, [17, WC], F32, side="right") for i in range(2)]; Bmr = [Buf("modr%d" % i) for i in range(2)]
            k.dma("sp", c17s[:], c17[:, :], writes=[Bp2])
            k.op("act", lambda e: e.activation(out=c17s[:], in_=c17s[:], func=AF.Silu), reads=[Bp2], writes=[Bp2])
            pt, pb = bank()
            for kt in range(KT):
                k.op("pe", lambda e, kt=kt, pt=pt: e.transpose(out=pt[:, kt * 17:(kt + 1) * 17], in_=c17s[0:17, kt * 128:(kt + 1) * 128],
                                                               identity=identf[0:17, 0:17]), reads=[Bp2, Bc], writes=[pb])
            k.op("act", lambda e, pt=pt: e.activation(out=siluT[:].rearrange("p a b -> p (a b)"), in_=pt[:, 0:KT * 17], func=AF.Copy), reads=[pb], writes=[Bsl])
            for blk in range(24):
                a = blk % 2
                ws, wb = w_next((w_ada, blk * WC))
                k.dma("sp", brow[a][:], b_ada[blk * WC:(blk + 1) * WC].rearrange("(o n) -> o n", o=1), writes=[Bbr[a]])
                pt, pb = bank()
                for kt in range(KT):
                    k.op("pe", lambda e, kt=kt, ws=ws, pt=pt: e.matmul(pt[0:17, 0:WC], lhsT=siluT[:, kt, :], rhs=ws[:, kt, :],
                                                                      start=(kt == 0), stop=False), reads=[Bsl, wb], writes=[pb])
                k.op("pe", lambda e, a=a, pt=pt: e.matmul(pt[0:17, 0:WC], lhsT=onesf[0:1, 0:17], rhs=brow[a][0:1, :], start=False, stop=True),
                     reads=[Bbr[a], Bc], writes=[pb])
                which = blk // 8
                if which == 1:
                    k.op("act", lambda e, a=a, pt=pt: e.activation(out=modr[a][:], in_=pt[0:17, 0:WC], func=AF.Identity, bias=onesf[0:17, 0:1]), reads=[pb, Bc], writes=[Bmr[a]])
                else:
                    k.op("act", lambda e, a=a, pt=pt: e.activation(out=modr[a][:], in_=pt[0:17, 0:WC], func=AF.Copy), reads=[pb], writes=[Bmr[a]])
                if which == 2:
                    c0 = (blk - 16) * WC
                    k.dma("sp", gate_scr[:, c0:c0 + WC], modr[a][:], reads=[Bmr[a]], writes=[Bgs], sbuf=Bmr[a])
                else:
                    dst = shiftT if which == 0 else scaleT
                    ft0 = (blk % 8) * 2
                    pt2, pb2 = bank()
                    for hh in range(2):
                        k.op("pe", lambda e, a=a, hh=hh, pt2=pt2: e.transpose(out=pt2[:, hh * 17:(hh + 1) * 17], in_=modr[a][0:17, hh * 128:(hh + 1) * 128],
                                                                             identity=identf[0:17, 0:17]), reads=[Bmr[a], Bc], writes=[pb2])
                    k.op("act", lambda e, dst=dst, ft0=ft0, pt2=pt2: e.activation(out=dst[:, ft0:ft0 + 2, :].rearrange("p a b -> p (a b)"), in_=pt2[:, 0:34], func=AF.Copy),
                         reads=[pb2], writes=[Bmod])
            return esA, [Bp2, Bsl] + Bbr + Bmr

        with ExitStack() as es2:
            k.es = es2
            Bt = Buf("s5tmp")
            L2 = sb("L2", [32, 2, 128], F32)
            Cn = sb("Cn", [128, 2, 4, 128], F32)
            Bpk = sb("Bpk", [128, 2, 32, 16], F32)
            Cpk = sb("Cpk", [128, 2, 32, 16], F32)
            ldtb = sb("ldtb", [128, 32], F32)
            dpk = sb("dpk", [128, NG], F32)
            k.dma("sp", L2[:, 0, :].rearrange("g (a p) -> g a p", a=2), lam_re.rearrange("(a g) p -> g a p", a=2), writes=[Bt])
            k.dma("sp", L2[:, 1, :].rearrange("g (a p) -> g a p", a=2), lam_im.rearrange("(a g) p -> g a p", a=2), writes=[Bt])
            for ri, cc in enumerate((c_re, c_im)):
                for a_ in range(2):
                    k.dma("sp", Cn[:, ri, :, a_ * 64:(a_ + 1) * 64],
                          cc[a_ * 32:(a_ + 1) * 32, :, :].rearrange("(c gl) h p -> (gl h) c p", c=4), writes=[Bt])
            for ri, bb in enumerate((b_re, b_im)):
                for gh in range(2):
                    k.dma("sp", Bpk[gh * 64:(gh + 1) * 64, ri, :, :],
                          bb[gh * 32:(gh + 1) * 32, :, :].rearrange("g p h -> p g h"), writes=[Bt])
            for gh in range(2):
                k.dma("sp", ldtb[gh * 64:(gh + 1) * 64, :], log_dt[gh * 32:(gh + 1) * 32].partition_broadcast(64), writes=[Bt])
            for s in range(8):
                k.dma("sp", dpk[s * 16:(s + 1) * 16, :], d_skip.rearrange("(g h) -> h g", h=16), writes=[Bt],
                      allow_slow_non_contiguous=True)
            lrli = sb("lrli", [128, 2, 32], F32)
            pt, pb = bank()
            for ri in range(2):
                k.op("pe", lambda e, ri=ri: e.transpose(out=pt[:, ri * 32:(ri + 1) * 32], in_=L2[:, ri, :], identity=identf[0:32, 0:32]),
                     reads=[Bt, Bc], writes=[pb])
            k.op("dve", lambda e: e.tensor_copy(out=lrli[:].rearrange("p a g -> p (a g)"), in_=pt[:, 0:64]), reads=[pb], writes=[Bt])
            for ri in range(2):
                pt, pb = bank()
                for c4 in range(4):
                    k.op("pe", lambda e, ri=ri, c4=c4: e.transpose(out=pt[:, c4 * 128:(c4 + 1) * 128], in_=Cn[:, ri, c4, :], identity=identf[:]),
                         reads=[Bt, Bc], writes=[pb])
                k.op("dve", lambda e, ri=ri: e.tensor_copy(out=Cpk[:, ri, :, :].rearrange("p g h -> p (g h)"), in_=pt[:, :]),
                     reads=[pb], writes=[Bt])
            lr = lrli[:, 0, :]; li = lrli[:, 1, :]

            def V(fn):
                return k.op("dve", fn, reads=[Bt, Bc], writes=[Bt])

            def A(fn):
                return k.op("act", fn, reads=[Bt, Bc], writes=[Bt])

            sm = sb("s5sm", [128, 24, 32], F32)
            dt = sm[:, 0, :]; ldr = sm[:, 1, :]; th = sm[:, 2, :]; t0 = sm[:, 3, :]; t1 = sm[:, 4, :]
            kk1 = sm[:, 5, :]; thr = sm[:, 6, :]; thc = sm[:, 7, :]; s1 = sm[:, 8, :]; c1 = sm[:, 9, :]
            cfr = sm[:, 10, :]; cfi = sm[:, 11, :]; nr = sm[:, 12, :]; den = sm[:, 13, :]; t2 = sm[:, 14, :]
            A(lambda e: e.activation(out=dt, in_=ldtb[:], func=AF.Exp))
            V(lambda e: e.tensor_tensor(out=ldr, in0=lr, in1=dt, op=ALU.mult))
            V(lambda e: e.tensor_tensor(out=th, in0=li, in1=dt, op=ALU.mult))

            def range_reduce(dst, shift):
                V(lambda e: e.tensor_scalar(out=t0, in0=th, scalar1=float(shift), scalar2=None, op0=ALU.add))
                V(lambda e: e.memset(kk1, 0.0))
                for m in (1, 3, 5, 7):
                    V(lambda e, m=m: e.tensor_scalar(out=t1, in0=t0, scalar1=float(m * PI), scalar2=None, op0=ALU.is_gt))
                    V(lambda e: e.tensor_tensor(out=kk1, in0=kk1, in1=t1, op=ALU.add))
                V(lambda e: e.scalar_tensor_tensor(out=dst, in0=kk1, scalar=float(-2 * PI), in1=t0, op0=ALU.mult, op1=ALU.add))

            range_reduce(thr, 0.0)
            z = sm[:, 15, :]; z2 = sm[:, 16, :]; ps_ = sm[:, 17, :]; pc_ = sm[:, 18, :]
            V(lambda e: e.tensor_scalar(out=z, in0=thr, scalar1=0.125, scalar2=None, op0=ALU.mult))
            V(lambda e: e.tensor_tensor(out=z2, in0=z, in1=z, op=ALU.mult))
            V(lambda e: e.tensor_scalar(out=ps_, in0=z2, scalar1=-1.0 / 72.0, scalar2=1.0, op0=ALU.mult, op1=ALU.add))
            for dv_ in (42.0, 20.0, 6.0):
                V(lambda e: e.tensor_tensor(out=ps_, in0=ps_, in1=z2, op=ALU.mult))
                V(lambda e, dv_=dv_: e.tensor_scalar(out=ps_, in0=ps_, scalar1=-1.0 / dv_, scalar2=1.0, op0=ALU.mult, op1=ALU.add))
            V(lambda e: e.tensor_tensor(out=s1, in0=ps_, in1=z, op=ALU.mult))
            V(lambda e: e.tensor_scalar(out=pc_, in0=z2, scalar1=-1.0 / 56.0, scalar2=1.0, op0=ALU.mult, op1=ALU.add))
            for dv_ in (30.0, 12.0, 2.0):
                V(lambda e: e.tensor_tensor(out=pc_, in0=pc_, in1=z2, op=ALU.mult))
                V(lambda e, dv_=dv_: e.tensor_scalar(out=pc_, in0=pc_, scalar1=-1.0 / dv_, scalar2=1.0, op0=ALU.mult, op1=ALU.add))
            V(lambda e: e.tensor_copy(out=c1, in_=pc_))
            for _ in range(3):
                V(lambda e: e.tensor_tensor(out=t0, in0=s1, in1=c1, op=ALU.mult))
                V(lambda e: e.tensor_tensor(out=t1, in0=s1, in1=s1, op=ALU.mult))
                V(lambda e: e.tensor_scalar(out=s1, in0=t0, scalar1=2.0, scalar2=None, op0=ALU.mult))
                V(lambda e: e.tensor_scalar(out=c1, in0=t1, scalar1=-2.0, scalar2=1.0, op0=ALU.mult, op1=ALU.add))
            Ur = sb("Ur", [128, 9, 32], F32); Ui = sb("Ui", [128, 9, 32], F32)
            V(lambda e: e.memset(Ur[:, 0, :], 1.0)); V(lambda e: e.memset(Ui[:, 0, :], 0.0))
            V(lambda e: e.tensor_copy(out=Ur[:, 1, :], in_=c1)); V(lambda e: e.tensor_copy(out=Ui[:, 1, :], in_=s1))
            for t in range(1, 8):
                V(lambda e, t=t: e.tensor_tensor(out=t0, in0=Ur[:, t, :], in1=c1, op=ALU.mult))
                V(lambda e, t=t: e.tensor_tensor(out=t1, in0=Ui[:, t, :], in1=s1, op=ALU.mult))
                V(lambda e, t=t: e.tensor_tensor(out=Ur[:, t + 1, :], in0=t0, in1=t1, op=ALU.subtract))
                V(lambda e, t=t: e.tensor_tensor(out=t0, in0=Ur[:, t, :], in1=s1, op=ALU.mult))
                V(lambda e, t=t: e.tensor_tensor(out=t1, in0=Ui[:, t, :], in1=c1, op=ALU.mult))
                V(lambda e, t=t: e.tensor_tensor(out=Ui[:, t + 1, :], in0=t0, in1=t1, op=ALU.add))
            Epr = sb("Epr", [128, 9, 32], F32); Epi = sb("Epi", [128, 9, 32], F32)
            Enr = sb("Enr", [128, 8, 32], F32); Eni = sb("Eni", [128, 8, 32], F32)
            MG = sb("MG", [128, 9, 32], F32); MGn = sb("MGn", [128, 8, 32], F32)
            for t in range(9):
                A(lambda e, t=t: e.activation(out=MG[:, t, :], in_=ldr, func=AF.Exp, scale=float(t)))
            for t in range(8):
                A(lambda e, t=t: e.activation(out=MGn[:, t, :], in_=ldr, func=AF.Exp, scale=float(-t)))
            esA, ada_bufs = ada_emit()
            V(lambda e: e.tensor_tensor(out=Epr[:], in0=MG[:], in1=Ur[:], op=ALU.mult))
            V(lambda e: e.tensor_tensor(out=Epi[:], in0=MG[:], in1=Ui[:], op=ALU.mult))
            V(lambda e: e.tensor_tensor(out=Enr[:], in0=MGn[:], in1=Ur[:, 0:8, :], op=ALU.mult))
            V(lambda e: e.tensor_tensor(out=Eni[:], in0=MGn[:], in1=Ui[:, 0:8, :], op=ALU.mult))
            V(lambda e: e.tensor_scalar(out=Eni[:], in0=Eni[:], scalar1=-1.0, scalar2=None, op0=ALU.mult))
            V(lambda e: e.tensor_scalar(out=nr, in0=Epr[:, 1, :], scalar1=-1.0, scalar2=None, op0=ALU.add))
            ni = Epi[:, 1, :]
            V(lambda e: e.tensor_tensor(out=t0, in0=lr, in1=lr, op=ALU.mult))
            V(lambda e: e.tensor_tensor(out=t1, in0=li, in1=li, op=ALU.mult))
            V(lambda e: e.tensor_tensor(out=den, in0=t0, in1=t1, op=ALU.add))
            V(lambda e: e.reciprocal(out=den, in_=den))
            V(lambda e: e.tensor_tensor(out=t0, in0=nr, in1=lr, op=ALU.mult))
            V(lambda e: e.tensor_tensor(out=t1, in0=ni, in1=li, op=ALU.mult))
            V(lambda e: e.tensor_tensor(out=t0, in0=t0, in1=t1, op=ALU.add))
            V(lambda e: e.tensor_tensor(out=cfr, in0=t0, in1=den, op=ALU.mult))
            V(lambda e: e.tensor_tensor(out=t0, in0=ni, in1=lr, op=ALU.mult))
            V(lambda e: e.tensor_tensor(out=t1, in0=nr, in1=li, op=ALU.mult))
            V(lambda e: e.tensor_tensor(out=t0, in0=t0, in1=t1, op=ALU.subtract))
            V(lambda e: e.tensor_tensor(out=cfi, in0=t0, in1=den, op=ALU.mult))
            Bpr = sb("Bpr", [128, 32, 16], F32); Bpi = sb("Bpi", [128, 32, 16], F32)
            tb0 = sb("tb0", [128, 32, 16], F32); tb1 = sb("tb1", [128, 32, 16], F32)
            bc16 = lambda ap: ap.unsqueeze(2).to_broadcast([128, 32, 16])
            V(lambda e: e.tensor_tensor(out=tb0[:], in0=Bpk[:, 0, :, :], in1=bc16(cfr), op=ALU.mult))
            V(lambda e: e.tensor_tensor(out=tb1[:], in0=Bpk[:, 1, :, :], in1=bc16(cfi), op=ALU.mult))
            V(lambda e: e.tensor_tensor(out=Bpr[:], in0=tb0[:], in1=tb1[:], op=ALU.subtract))
            V(lambda e: e.tensor_tensor(out=tb0[:], in0=Bpk[:, 1, :, :], in1=bc16(cfr), op=ALU.mult))
            V(lambda e: e.tensor_tensor(out=tb1[:], in0=Bpk[:, 0, :, :], in1=bc16(cfi), op=ALU.mult))
            V(lambda e: e.tensor_tensor(out=Bpi[:], in0=tb0[:], in1=tb1[:], op=ALU.add))
            Gr = sb("Gr", [128, 32, 8, 16], BF16); Gi = sb("Gi", [128, 32, 8, 16], BF16)
            G7r = sb("G7r", [128, 32, 8, 16], BF16); G7i = sb("G7i", [128, 32, 8, 16], BF16)
            for s in range(8):
                for (er, ei, outr, outi) in ((Enr[:, s, :], Eni[:, s, :], Gr, Gi), (Epr[:, 7 - s, :], Epi[:, 7 - s, :], G7r, G7i)):
                    V(lambda e, er=er: e.tensor_tensor(out=tb0[:], in0=Bpr[:], in1=bc16(er), op=ALU.mult))
                    V(lambda e, ei=ei: e.tensor_tensor(out=tb1[:], in0=Bpi[:], in1=bc16(ei), op=ALU.mult))
                    V(lambda e, outr=outr, s=s: e.tensor_tensor(out=outr[:, :, s, :], in0=tb0[:], in1=tb1[:], op=ALU.subtract))
                    V(lambda e, ei=ei: e.tensor_tensor(out=tb0[:], in0=Bpr[:], in1=bc16(ei), op=ALU.mult))
                    V(lambda e, er=er: e.tensor_tensor(out=tb1[:], in0=Bpi[:], in1=bc16(er), op=ALU.mult))
                    V(lambda e, outi=outi, s=s: e.tensor_tensor(out=outi[:, :, s, :], in0=tb0[:], in1=tb1[:], op=ALU.add))
            for t in range(9):
                er, ei = Epr[:, t, :], Epi[:, t, :]
                V(lambda e, er=er: e.tensor_tensor(out=tb0[:], in0=Cpk[:, 0, :, :], in1=bc16(er), op=ALU.mult))
                V(lambda e, ei=ei: e.tensor_tensor(out=tb1[:], in0=Cpk[:, 1, :, :], in1=bc16(ei), op=ALU.mult))
                k.op("dve", lambda e, t=t: e.tensor_tensor(out=Ybr[:, :, t, :], in0=tb0[:], in1=tb1[:], op=ALU.subtract),
                     reads=[Bt], writes=[Bt, Bs5])
                V(lambda e, ei=ei: e.tensor_tensor(out=tb0[:], in0=Cpk[:, 0, :, :], in1=bc16(ei), op=ALU.mult))
                V(lambda e, er=er: e.tensor_tensor(out=tb1[:], in0=Cpk[:, 1, :, :], in1=bc16(er), op=ALU.mult))
                V(lambda e: e.tensor_tensor(out=tb0[:], in0=tb0[:], in1=tb1[:], op=ALU.add))
                k.op("dve", lambda e, t=t: e.tensor_scalar(out=Ybn[:, :, t, :], in0=tb0[:], scalar1=-1.0, scalar2=None, op0=ALU.mult),
                     reads=[Bt], writes=[Bt, Bs5])
            k.op("dve", lambda e: e.tensor_copy(out=ARR[:], in_=Epr[:, 8, :]), reads=[Bt], writes=[Bs5])
            k.op("dve", lambda e: e.tensor_copy(out=AIp[:], in_=Epi[:, 8, :]), reads=[Bt], writes=[Bs5])
            k.op("dve", lambda e: e.tensor_scalar(out=AIn[:], in0=Epi[:, 8, :], scalar1=-1.0, scalar2=None, op0=ALU.mult),
                 reads=[Bt], writes=[Bs5])
            sqr = sm[:, 19, :]; sqi = sm[:, 20, :]
            V(lambda e: e.tensor_copy(out=sqr, in_=Epr[:, 8, :])); V(lambda e: e.tensor_copy(out=sqi, in_=Epi[:, 8, :]))
            for _ in range(6):
                V(lambda e: e.tensor_tensor(out=t0, in0=sqr, in1=sqr, op=ALU.mult))
                V(lambda e: e.tensor_tensor(out=t1, in0=sqi, in1=sqi, op=ALU.mult))
                V(lambda e: e.tensor_tensor(out=t2, in0=sqr, in1=sqi, op=ALU.mult))
                V(lambda e: e.tensor_tensor(out=sqr, in0=t0, in1=t1, op=ALU.subtract))
                V(lambda e: e.tensor_scalar(out=sqi, in0=t2, scalar1=2.0, scalar2=None, op0=ALU.mult))
            k.op("dve", lambda e: e.tensor_copy(out=A64[:, 0, :], in_=sqr), reads=[Bt], writes=[Bs5])
            k.op("dve", lambda e: e.tensor_copy(out=A64[:, 1, :], in_=sqi), reads=[Bt], writes=[Bs5])
            k.op("dve", lambda e: e.tensor_scalar(out=A64[:, 2, :], in0=sqi, scalar1=-1.0, scalar2=None, op0=ALU.mult), reads=[Bt], writes=[Bs5])
            for g0 in range(0, NG, 4):
                pt, pb = bank()
                for gi in range(4):
                    g = g0 + gi
                    gh, gp = g // 32, g % 32
                    rows = slice(gh * 64, gh * 64 + 64)
                    k.op("pe", lambda e, gi=gi, rows=rows, gp=gp: e.matmul(
                        pt[:, gi * 128:(gi + 1) * 128], lhsT=Gr[rows, gp, :, :], rhs=Ybr[rows, gp, 0:8, :], start=True, stop=False),
                        reads=[Bt, Bs5], writes=[pb])
                    k.op("pe", lambda e, gi=gi, rows=rows, gp=gp: e.matmul(
                        pt[:, gi * 128:(gi + 1) * 128], lhsT=Gi[rows, gp, :, :], rhs=Ybn[rows, gp, 0:8, :], start=False, stop=True),
                        reads=[Bt, Bs5], writes=[pb])
                k.op("dve", lambda e, g0=g0: e.tensor_tensor(
                    out=Toep[:, g0:g0 + 4, :].rearrange("p g (t h) -> p g t h", h=16),
                    in0=pt[:, :].rearrange("p (g t h) -> p g t h", g=4, h=16),
                    in1=mask3[:].unsqueeze(1).unsqueeze(3).to_broadcast([128, 4, 8, 16]), op=ALU.mult),
                    reads=[pb, Bc], writes=[Bs5])
            for g in range(NG):
                k.op("dve", lambda e, g=g: e.scalar_tensor_tensor(out=Toep[:, g, :], in0=identf[:], scalar=dpk[:, g:g + 1],
                                                                  in1=Toep[:, g, :], op0=ALU.mult, op1=ALU.add),
                     reads=[Bt, Bc, Bs5], writes=[Bs5])
            for g0 in range(0, NG, 8):
                pt, pb = bank()
                ptb = pt[:].bitcast(BF16)
                for gi in range(8):
                    g = g0 + gi
                    gh, gp = g // 32, g % 32
                    rows = slice(gh * 64, gh * 64 + 64)
                    for ri, GG in enumerate((G7r, G7i)):
                        col = (gi * 2 + ri) * 64
                        k.op("pe", lambda e, rows=rows, gp=gp, GG=GG, col=col: e.transpose(
                            out=ptb[:, col:col + 64], in_=GG[rows, gp, :, :], identity=identb[rows, rows]),
                            reads=[Bt, Bc], writes=[pb])
                k.op("dve", lambda e, g0=g0: e.tensor_copy(out=Mw[:, g0:g0 + 8, :, :].rearrange("p g r q -> p (g r q)"), in_=ptb[:, 0:1024]),
                     reads=[pb], writes=[Bs5])
            for e_ in ("pe", "dve", "act", "pool", "sp"):
                k.wait_all(e_, [Bt] + ada_bufs + psb)
            k.es = esW
        esA.close()
        k.esR = esR

        dump("Toep", Toep[:], [128, NG, 128], [Bs5], BF16)
        dump("Mw", Mw[:], [128, NG, 2, 64], [Bs5], BF16)
        dump("Ybr", Ybr[:], [128, 32, 9, 16], [Bs5], BF16)
        dump("Ybn", Ybn[:], [128, 32, 9, 16], [Bs5], BF16)
        dump("ARR", ARR[:], [128, 32], [Bs5])
        dump("AIp", AIp[:], [128, 32], [Bs5])
        stage_end(1)
        dump("shiftT", shiftT[:], [128, KT, 17], [Bmod])
        dump("scaleT", scaleT[:], [128, KT, 17], [Bmod])
        stage_end(2)
        ENG = ("pe", "dve", "act", "pool", "sp")

        def barrier(bufs):
            for e_ in ENG:
                k.wait_all(e_, bufs)

        def prep(full):
            ntile = 9 if full else 8
            xsrc = [(xmain if full else xpre)[j * 128:(j + 1) * 128, :] for j in range(8)]
            if full:
                xsrc.append(xsmp[:, :])
            with ExitStack() as es4:
                k.es = es4
                gainbc = sb("gainbc", [128, D], F32); Bg = Buf("gain")
                k.dma("sp", gainbc[:], norm_gain.partition_broadcast(128), writes=[Bg])
                xs = [sb("xs%d" % i, [128, D], F32) for i in range(2)]; xb_ = [Buf("xs%d" % i) for i in range(2)]
                xn = [sb("xn%d" % i, [128, D], BF16) for i in range(2)]; xnb = [Buf("xn%d" % i) for i in range(2)]
                st = sb("prst", [128, 9, 4], F32); Bst = Buf("prst")
                tmpm = [sb("tmpm%d" % i, [128, 8, 128], F32) for i in range(2)]; tmb = [Buf("tmpm%d" % i) for i in range(2)]
                for j in range(ntile):
                    a = j % 2
                    k.dma("sp", xs[a][:], xsrc[j], writes=[xb_[a]])
                    k.op("act", lambda e, a=a, j=j: e.activation(out=xn[a][:], in_=xs[a][:], func=AF.Square, accum_out=st[:, j, 0:1]),
                         reads=[xb_[a]], writes=[xnb[a], Bst])
                    k.op("act", lambda e, j=j: e.activation(out=st[:, j, 2:3], in_=st[:, j, 0:1], func=AF.Ln, scale=1.0 / D, bias=epsb[:, 0:1]), reads=[Bst, Bc], writes=[Bst])
                    k.op("act", lambda e, j=j: e.activation(out=st[:, j, 3:4], in_=st[:, j, 2:3], func=AF.Exp, scale=-0.5), reads=[Bst], writes=[Bst])
                    k.op("dve", lambda e, a=a, j=j: e.scalar_tensor_tensor(out=xn[a][:], in0=xs[a][:], scalar=st[:, j, 3:4], in1=gainbc[:],
                                                                           op0=ALU.mult, op1=ALU.mult),
                         reads=[xb_[a], Bst, Bg], writes=[xnb[a]])
                    for half in range(2):
                        pt, pb = bank()
                        ptb = pt[:].bitcast(BF16)
                        for q in range(8):
                            kt = half * 8 + q
                            k.op("pe", lambda e, a=a, kt=kt, q=q, ptb=ptb: e.transpose(
                                out=ptb[:, q * 128:(q + 1) * 128], in_=xn[a][:, kt * 128:(kt + 1) * 128], identity=identb[:]),
                                reads=[xnb[a], Bc], writes=[pb])
                        m = half
                        src = ptb[:, 0:1024].rearrange("p (q t) -> p q t", q=8)
                        if j < 8:
                            sc = scaleT[:, half * 8:half * 8 + 8, 0:1].to_broadcast([128, 8, 128])
                            sh = shiftT[:, half * 8:half * 8 + 8, 0:1].to_broadcast([128, 8, 128])
                            o1 = tmpm[m][:]
                            o2 = hT[:, half * 8:half * 8 + 8, j * 128:(j + 1) * 128]
                        else:
                            src = src.rearrange("p q (j r) -> p q j r", r=8)
                            sc = scaleT[:, half * 8:half * 8 + 8, 1:17].unsqueeze(3).to_broadcast([128, 8, NSEQ, 8])
                            sh = shiftT[:, half * 8:half * 8 + 8, 1:17].unsqueeze(3).to_broadcast([128, 8, NSEQ, 8])
                            o1 = tmpm[m][:].rearrange("p q (j r) -> p q j r", r=8)
                            o2 = hT[:, half * 8:half * 8 + 8, j * 128:(j + 1) * 128].rearrange("p q (j r) -> p q j r", r=8)
                        k.op("dve", lambda e, o1=o1, src=src, sc=sc: e.tensor_tensor(out=o1, in0=src, in1=sc, op=ALU.mult),
                             reads=[pb, Bmod], writes=[tmb[m]])
                        k.op("dve", lambda e, o1=o1, o2=o2, sh=sh: e.tensor_tensor(out=o2, in0=o1, in1=sh, op=ALU.add),
                             reads=[tmb[m], Bmod], writes=[hTb[j]])
                barrier([Bg] + xb_ + xnb + [Bst] + tmb)
            return ntile

        def s5_front(full, U2, BU2, VH, BV):
            ntile = 9 if full else 8
            hall = hTb[0:ntile]
            ctiles = [(0, 128, 0)] + ([(TP, NSEQ, 128)] if full else [])
            with ExitStack() as es4:
                k.es = es4
                Uc = sb("Uc", [128, 16, 8, 16], BF16); BUc = Buf("Uc")
                for cb in range(4):
                    ws, wb = w_next((w_in, cb * WC))
                    for (toff, C, coff) in ctiles:
                        for sp2 in range(4):
                            pt, pb = bank()
                            for si in range(2):
                                s = sp2 * 2 + si
                                for kt in range(KT):
                                    k.op("pe", lambda e, kt=kt, s=s, si=si, toff=toff, C=C, ws=ws, pt=pt: e.matmul(
                                        pt[0:C, si * WC:(si + 1) * WC], lhsT=hT[:, kt, toff + s:toff + 8 * C:8], rhs=ws[:, kt, :],
                                        start=(kt == 0), stop=(kt == KT - 1)), reads=hall + [wb], writes=[pb])
                            k.op("act", lambda e, sp2=sp2, C=C, pt=pt: e.activation(
                                out=Uc[0:C, :, sp2 * 2:sp2 * 2 + 2, :].rearrange("c g s h -> c s g h"),
                                in_=pt[0:C, :].rearrange("c (s g h) -> c s g h", s=2, h=16), func=AF.Copy),
                                reads=[pb], writes=[BUc])
                        for gq in range(2):
                            pt, pb = bank()
                            ptb = pt[:].bitcast(BF16)
                            for gi in range(8):
                                gl = gq * 8 + gi
                                k.op("pe", lambda e, gl=gl, gi=gi, C=C, ptb=ptb: e.transpose(
                                    out=ptb[:, gi * 128:gi * 128 + C], in_=Uc[0:C, gl, :, :], identity=identb[0:C, 0:C]),
                                    reads=[BUc, Bc], writes=[pb])
                            g0 = cb * 16 + gq * 8
                            k.op("act", lambda e, g0=g0, C=C, coff=coff, ptb=ptb: e.activation(
                                out=U2[:, g0:g0 + 8, coff:coff + C], in_=ptb[:, 0:1024].rearrange("p (g c) -> p g c", g=8)[:, :, 0:C],
                                func=AF.Copy), reads=[pb], writes=[BU2[g0 // 8]])
                barrier([BUc])
            CT = 128 + (NSEQ if full else 0)
            for gp in range(32):
                pt, pb = bank()
                for gh in range(2):
                    g = gh * 32 + gp
                    for ri in range(2):
                        k.op("pe", lambda e, g=g, gh=gh, ri=ri, pt=pt: e.matmul(
                            pt[gh * 64:(gh + 1) * 64, ri * 256:ri * 256 + CT], lhsT=Mw[:, g, ri, :], rhs=U2[:, g, 0:CT],
                            start=True, stop=True), reads=[Bs5, BU2[g // 8]], writes=[pb])
                k.op("act", lambda e, gp=gp, pt=pt: e.activation(
                    out=VH[:, 1:1 + CT, :, gp].rearrange("p c r -> p r c"),
                    in_=pt[:, :].rearrange("p (r c) -> p r c", r=2)[:, :, 0:CT], func=AF.Copy), reads=[pb], writes=[BV])

        def s5_recur(VH, BV, BHh, hist):
            with ExitStack() as es4:
                k.es = es4
                ta = sb("rec_ta", [128, 2, 32], F32); tb = sb("rec_tb", [128, 2, 32], F32); Br = Buf("rec")
                arb = ARR[:].unsqueeze(1).to_broadcast([128, 2, 32])
                for c in range(128):
                    k.op("dve", lambda e: e.tensor_tensor(out=ta[:], in0=Hrun[:], in1=arb, op=ALU.mult), reads=[BH, Bs5], writes=[Br])
                    k.op("dve", lambda e: e.tensor_tensor(out=tb[:, 0, :], in0=Hrun[:, 1, :], in1=AIn[:], op=ALU.mult), reads=[BH, Bs5], writes=[Br])
                    k.op("dve", lambda e: e.tensor_tensor(out=tb[:, 1, :], in0=Hrun[:, 0, :], in1=AIp[:], op=ALU.mult), reads=[BH, Bs5], writes=[Br])
                    k.op("dve", lambda e: e.tensor_tensor(out=ta[:], in0=ta[:], in1=tb[:], op=ALU.add), reads=[Br], writes=[Br])
                    k.op("dve", lambda e, c=c: e.tensor_tensor(out=Hrun[:], in0=ta[:], in1=VH[:, 1 + c, :, :], op=ALU.add), reads=[Br, BV], writes=[BH])
                    if hist and c < 127:
                        k.op("act", lambda e, c=c: e.activation(out=VH[:, 1 + c, :, :], in_=Hrun[:], func=AF.Copy), reads=[BH], writes=[BHh])
                barrier([Br])

        def s5_recur2_state(VH, BV):
            with ExitStack() as es4:
                k.es = es4
                H2 = sb("rec_H2", [128, 2, 2, 32], F32); ta = sb("rec_ta2", [128, 2, 2, 32], F32); tb = sb("rec_tb2", [128, 2, 2, 32], F32)
                Br = Buf("rec2")
                arb = ARR[:].unsqueeze(1).unsqueeze(1).to_broadcast([128, 2, 2, 32])
                ainb = AIn[:].unsqueeze(1).to_broadcast([128, 2, 32]); aipb = AIp[:].unsqueeze(1).to_broadcast([128, 2, 32])
                k.op("dve", lambda e: e.memset(H2[:], 0.0), writes=[Br])
                k.op("dve", lambda e: e.tensor_copy(out=H2[:, 0, :, :], in_=Hrun[:]), reads=[BH], writes=[Br])
                for c in range(64):
                    k.op("dve", lambda e: e.tensor_tensor(out=ta[:], in0=H2[:], in1=arb, op=ALU.mult), reads=[Br, Bs5], writes=[Br])
                    k.op("dve", lambda e: e.tensor_tensor(out=tb[:, :, 0, :], in0=H2[:, :, 1, :], in1=ainb, op=ALU.mult), reads=[Br, Bs5], writes=[Br])
                    k.op("dve", lambda e: e.tensor_tensor(out=tb[:, :, 1, :], in0=H2[:, :, 0, :], in1=aipb, op=ALU.mult), reads=[Br, Bs5], writes=[Br])
                    k.op("dve", lambda e: e.tensor_tensor(out=ta[:], in0=ta[:], in1=tb[:], op=ALU.add), reads=[Br], writes=[Br])
                    k.op("dve", lambda e, c=c: e.tensor_tensor(out=H2[:], in0=ta[:], in1=VH[:, 1 + c:1 + c + 65:64, :, :], op=ALU.add), reads=[Br, BV], writes=[Br])
                k.op("dve", lambda e: e.tensor_tensor(out=ta[:, 0, :, :], in0=H2[:, 0, :, :], in1=A64[:, 0, :].unsqueeze(1).to_broadcast([128, 2, 32]), op=ALU.mult),
                     reads=[Br, Bs5], writes=[Br])
                k.op("dve", lambda e: e.tensor_tensor(out=tb[:, 0, 0, :], in0=H2[:, 0, 1, :], in1=A64[:, 2, :], op=ALU.mult), reads=[Br, Bs5], writes=[Br])
                k.op("dve", lambda e: e.tensor_tensor(out=tb[:, 0, 1, :], in0=H2[:, 0, 0, :], in1=A64[:, 1, :], op=ALU.mult), reads=[Br, Bs5], writes=[Br])
                k.op("dve", lambda e: e.tensor_tensor(out=ta[:, 0, :, :], in0=ta[:, 0, :, :], in1=tb[:, 0, :, :], op=ALU.add), reads=[Br], writes=[Br])
                k.op("dve", lambda e: e.tensor_tensor(out=Hrun[:], in0=ta[:, 0, :, :], in1=H2[:, 1, :, :], op=ALU.add), reads=[Br], writes=[BH])
                barrier([Br])

        def s5_recur2_hist(VH, BV, BHh):
            SB_ = 16; NQ = 4; L = 128 // NQ
            with ExitStack() as es4:
                k.es = es4
                H2 = sb("rec_H2", [128, NQ, 2, 32], F32); ta = sb("rec_ta2", [128, NQ, 2, 32], F32); tb = sb("rec_tb2", [128, NQ, 2, 32], F32)
                PW = sb("rec_PW", [128, L, 2, 32], BF16); c1 = sb("rec_c1", [128, SB_, 32], F32); c2 = sb("rec_c2", [128, SB_, 32], F32)
                pw = sb("rec_pw", [128, 8, 32], F32)
                He = [sb("rec_He%d" % i, [128, 2, 32], F32) for i in range(2)]
                Br = Buf("rec2"); Bpw = Buf("rec_pw")
                W = lambda fn: k.op("dve", fn, reads=[Bpw, Bs5], writes=[Bpw])
                pr, pi_, q0_, q1_, q2_, aLr, aLi, aLn = (pw[:, i, :] for i in range(8))
                W(lambda e: e.tensor_copy(out=pr, in_=ARR[:])); W(lambda e: e.tensor_copy(out=pi_, in_=AIp[:]))
                W(lambda e: e.tensor_copy(out=PW[:, 0, 0, :], in_=ARR[:])); W(lambda e: e.tensor_copy(out=PW[:, 0, 1, :], in_=AIp[:]))

                def square():
                    W(lambda e: e.tensor_tensor(out=q0_, in0=pr, in1=pr, op=ALU.mult))
                    W(lambda e: e.tensor_tensor(out=q1_, in0=pi_, in1=pi_, op=ALU.mult))
                    W(lambda e: e.tensor_tensor(out=q2_, in0=pr, in1=pi_, op=ALU.mult))
                    W(lambda e: e.tensor_tensor(out=pr, in0=q0_, in1=q1_, op=ALU.subtract))
                    W(lambda e: e.tensor_scalar(out=pi_, in0=q2_, scalar1=2.0, scalar2=None, op0=ALU.mult))

                n = 1
                while n < L:
                    for b0 in range(0, n, SB_):
                        nb = min(SB_, n - b0)
                        src_r = PW[:, b0:b0 + nb, 0, :]; src_i = PW[:, b0:b0 + nb, 1, :]
                        prb = pr.unsqueeze(1).to_broadcast([128, nb, 32]); pib = pi_.unsqueeze(1).to_broadcast([128, nb, 32])
                        W(lambda e, src_r=src_r, prb=prb, nb=nb: e.tensor_tensor(out=c1[:, 0:nb, :], in0=src_r, in1=prb, op=ALU.mult))
                        W(lambda e, src_i=src_i, pib=pib, nb=nb: e.tensor_tensor(out=c2[:, 0:nb, :], in0=src_i, in1=pib, op=ALU.mult))
                        W(lambda e, nb=nb, b0=b0, n=n: e.tensor_tensor(out=PW[:, n + b0:n + b0 + nb, 0, :], in0=c1[:, 0:nb, :], in1=c2[:, 0:nb, :], op=ALU.subtract))
                        W(lambda e, src_r=src_r, pib=pib, nb=nb: e.tensor_tensor(out=c1[:, 0:nb, :], in0=src_r, in1=pib, op=ALU.mult))
                        W(lambda e, src_i=src_i, prb=prb, nb=nb: e.tensor_tensor(out=c2[:, 0:nb, :], in0=src_i, in1=prb, op=ALU.mult))
                        W(lambda e, nb=nb, b0=b0, n=n: e.tensor_tensor(out=PW[:, n + b0:n + b0 + nb, 1, :], in0=c1[:, 0:nb, :], in1=c2[:, 0:nb, :], op=ALU.add))
                    n *= 2
                    square()
                W(lambda e: e.tensor_copy(out=aLr, in_=pr)); W(lambda e: e.tensor_copy(out=aLi, in_=pi_))
                W(lambda e: e.tensor_scalar(out=aLn, in0=pi_, scalar1=-1.0, scalar2=None, op0=ALU.mult))
                arb = ARR[:].unsqueeze(1).unsqueeze(1).to_broadcast([128, NQ, 2, 32])
                ainb = AIn[:].unsqueeze(1).to_broadcast([128, NQ, 32]); aipb = AIp[:].unsqueeze(1).to_broadcast([128, NQ, 32])
                k.op("dve", lambda e: e.memset(H2[:], 0.0), writes=[Br])
                k.op("dve", lambda e: e.tensor_copy(out=H2[:, 0, :, :], in_=Hrun[:]), reads=[BH], writes=[Br])
                Bh2 = Buf("H2")
                hi_sl = lambda c: slice(1 + c, 1 + c + L * (NQ - 1) + 1, L)
                for c in range(L):
                    k.op("dve", lambda e: e.tensor_tensor(out=ta[:], in0=H2[:], in1=arb, op=ALU.mult), reads=[Bh2, Bs5], writes=[Br])
                    k.op("dve", lambda e: e.tensor_tensor(out=tb[:, :, 0, :], in0=H2[:, :, 1, :], in1=ainb, op=ALU.mult), reads=[Bh2, Bs5], writes=[Br])
                    k.op("dve", lambda e: e.tensor_tensor(out=tb[:, :, 1, :], in0=H2[:, :, 0, :], in1=aipb, op=ALU.mult), reads=[Bh2, Bs5], writes=[Br])
                    k.op("dve", lambda e: e.tensor_tensor(out=ta[:], in0=ta[:], in1=tb[:], op=ALU.add), reads=[Br], writes=[Br])
                    k.op("dve", lambda e, c=c: e.tensor_tensor(out=H2[:], in0=ta[:], in1=VH[:, hi_sl(c), :, :], op=ALU.add), reads=[Br, BV], writes=[Bh2])
                    k.op("act", lambda e, c=c: e.activation(out=VH[:, hi_sl(c), :, :], in_=H2[:], func=AF.Copy), reads=[Bh2], writes=[BHh])
                X = lambda fn: k.op("dve", fn, reads=[Bh2, Bpw, BHh, Br], writes=[Br, BHh])
                hprev = H2[:, 0, :, :]
                for q in range(1, NQ):
                    hr_b = hprev[:, 0, :].unsqueeze(1).to_broadcast([128, SB_, 32])
                    hi_b = hprev[:, 1, :].unsqueeze(1).to_broadcast([128, SB_, 32])
                    for b0 in range(0, L, SB_):
                        pwr = PW[:, b0:b0 + SB_, 0, :]; pwi = PW[:, b0:b0 + SB_, 1, :]
                        s0_ = 1 + q * L + b0
                        tgt_r = VH[:, s0_:s0_ + SB_, 0, :]; tgt_i = VH[:, s0_:s0_ + SB_, 1, :]
                        X(lambda e, pwr=pwr, hr_b=hr_b: e.tensor_tensor(out=c1[:], in0=pwr, in1=hr_b, op=ALU.mult))
                        X(lambda e, pwi=pwi, hi_b=hi_b: e.tensor_tensor(out=c2[:], in0=pwi, in1=hi_b, op=ALU.mult))
                        X(lambda e: e.tensor_tensor(out=c1[:], in0=c1[:], in1=c2[:], op=ALU.subtract))
                        X(lambda e, tgt_r=tgt_r: e.tensor_tensor(out=tgt_r, in0=tgt_r, in1=c1[:], op=ALU.add))
                        X(lambda e, pwr=pwr, hi_b=hi_b: e.tensor_tensor(out=c1[:], in0=pwr, in1=hi_b, op=ALU.mult))
                        X(lambda e, pwi=pwi, hr_b=hr_b: e.tensor_tensor(out=c2[:], in0=pwi, in1=hr_b, op=ALU.mult))
                        X(lambda e: e.tensor_tensor(out=c1[:], in0=c1[:], in1=c2[:], op=ALU.add))
                        X(lambda e, tgt_i=tgt_i: e.tensor_tensor(out=tgt_i, in0=tgt_i, in1=c1[:], op=ALU.add))
                    hn = He[q % 2]
                    X(lambda e, hprev=hprev: e.tensor_tensor(out=ta[:, 0, :, :], in0=hprev, in1=aLr.unsqueeze(1).to_broadcast([128, 2, 32]), op=ALU.mult))
                    X(lambda e, hprev=hprev: e.tensor_tensor(out=tb[:, 0, 0, :], in0=hprev[:, 1, :], in1=aLn, op=ALU.mult))
                    X(lambda e, hprev=hprev: e.tensor_tensor(out=tb[:, 0, 1, :], in0=hprev[:, 0, :], in1=aLi, op=ALU.mult))
                    X(lambda e: e.tensor_tensor(out=ta[:, 0, :, :], in0=ta[:, 0, :, :], in1=tb[:, 0, :, :], op=ALU.add))
                    X(lambda e, hn=hn, q=q: e.tensor_tensor(out=hn[:], in0=ta[:, 0, :, :], in1=H2[:, q, :, :], op=ALU.add))
                    hprev = hn[:]
                k.op("dve", lambda e, hprev=hprev: e.tensor_copy(out=Hrun[:], in_=hprev), reads=[Br, Bh2], writes=[BH])
                barrier([Br, Bpw, Bh2, BHh])

        def proj_fm(wlist, T, hall, dstfn, func=AF.Copy):
            nblocks = [(n0, min(512, T - n0)) for n0 in range(0, T, 512)]
            for i, spec in enumerate(wlist):
                ws, wb = w_next(spec)
                for mt in range(2):
                    for (n0, nn) in nblocks:
                        pt, pb = bank()
                        for kt in range(KT):
                            k.op("pe", lambda e, kt=kt, n0=n0, nn=nn, mt=mt, ws=ws, pt=pt: e.matmul(
                                pt[:, 0:nn], lhsT=ws[:, kt, mt * 128:(mt + 1) * 128], rhs=hT[:, kt, n0:n0 + nn],
                                start=(kt == 0), stop=(kt == KT - 1)), reads=hall + [wb], writes=[pb])
                        oap, ob = dstfn(i * 2 + mt, n0, nn)
                        k.op("act", lambda e, oap=oap, nn=nn, pt=pt: e.activation(out=oap, in_=pt[:, 0:nn], func=func), reads=[pb], writes=[ob])

        def gla_alloc(full):
            glr = sb("glr", [17, TM], BF16); Bglr = Buf("glr")
            kT = sb("kT", [128, NH, TM], BF16); BkT = Buf("kT")
            vtok = sb("vtok", [128, 9, NH * DV], BF16); Bvt = [Buf("vtok%d" % j) for j in range(9)]
            qT = sb("qT", [128, NH, TM], BF16) if full else None
            BqT = Buf("qT")
            return dict(glr=glr, Bglr=Bglr, kT=kT, BkT=BkT, vtok=vtok, Bvt=Bvt, qT=qT, BqT=BqT)

        def gla_proj(gl, full):
            glr, Bglr, kT, BkT, vtok, Bvt, qT, BqT = (gl[x] for x in ("glr", "Bglr", "kT", "BkT", "vtok", "Bvt", "qT", "BqT"))
            ntile = 9 if full else 8
            T = ntile * 128
            hall = hTb[0:ntile]
            nblocks = [(n0, min(512, T - n0)) for n0 in range(0, T, 512)]
            if True:
                k.op("dve", lambda e: e.memset(glr[:], 1.0), writes=[Bglr])
                ws, wb = w_next((w_in, 5120))
                for (n0, nn) in nblocks:
                    pt, pb = bank()
                    for kt in range(KT):
                        k.op("pe", lambda e, kt=kt, n0=n0, nn=nn, ws=ws, pt=pt: e.matmul(pt[0:16, 0:nn], lhsT=ws[:, kt, 0:16], rhs=hT[:, kt, n0:n0 + nn],
                                                                                       start=(kt == 0), stop=(kt == KT - 1)), reads=hall + [wb], writes=[pb])
                    k.op("act", lambda e, n0=n0, nn=nn, pt=pt: e.activation(out=glr[0:16, n0:n0 + nn], in_=pt[0:16, 0:nn], func=AF.Copy), reads=[pb], writes=[Bglr])
                proj_fm([(w_in, 2560), (w_in, 2560 + WC)], T, hall, lambda m, n0, nn: (kT[:, m, n0:n0 + nn], BkT))
                for i in range(4):
                    ws, wb = w_next((w_in, 3072 + i * WC))
                    for j in range(ntile):
                        pt, pb = bank()
                        for kt in range(KT):
                            k.op("pe", lambda e, kt=kt, j=j, ws=ws, pt=pt: e.matmul(pt[:, 0:WC], lhsT=hT[:, kt, j * 128:(j + 1) * 128], rhs=ws[:, kt, :],
                                                                                   start=(kt == 0), stop=(kt == KT - 1)), reads=[hTb[j], wb], writes=[pb])
                        k.op("act", lambda e, j=j, i=i, pt=pt: e.activation(out=vtok[:, j, i * WC:(i + 1) * WC], in_=pt[:, 0:WC], func=AF.Copy),
                             reads=[pb], writes=[Bvt[j]])
                if full:
                    proj_fm([(w_in, 2048), (w_in, 2048 + WC)], T, hall, lambda m, n0, nn: (qT[:, m, n0:n0 + nn], BqT))

        def gla_tiles(gl, full, by, Bby):
            glr, Bglr, kT, BkT, vtok, Bvt, qT, BqT = (gl[x] for x in ("glr", "Bglr", "kT", "BkT", "vtok", "Bvt", "qT", "BqT"))
            ntile = 9 if full else 8
            with ExitStack() as es4:
                k.es = es4
                G = {}
                gl_ = [("e1", [128, NH, 128], F32), ("cs", [128, NH, 128], F32), ("eb", [128, NH, 128], F32), ("einv", [128, NH, 128], F32),
                       ("kt", [128, NH, 128], BF16), ("ktok", [128, NH, 128], BF16), ("stmp", [128, 2, DV], F32)]
                if full:
                    gl_ += [("qt", [128, NH, 128], BF16), ("scm", [128, NH, 128], BF16), ("sq", [128, 2, 4, 128], BF16),
                            ("rr", [128, NH, 128], F32), ("qtf", [128, NH, 128], F32), ("KM", [128, NSEQ, 128], BF16), ("ebe", [128, NH, NSEQ], F32)]
                for nm, shp, dt_ in gl_:
                    G[nm] = sb("g_" + nm, shp, dt_)
                GB = {nm: Buf("g_" + nm) for nm in G}
                if full:
                    S0h = [sb("S0h%d" % i, [128, 8, DV], F32) for i in range(2)]; BS0h = [Buf("S0h%d" % i) for i in range(2)]
                    Snew = [sb("Snew%d" % i, [128, 2, DV], F32) for i in range(2)]; BSn2 = [Buf("Snew%d" % i) for i in range(2)]
                rot = [0]; rot2 = [0]
                fl = lambda ap: ap.rearrange("p a b -> p (a b)")
                for j in range(ntile):
                    sample = (j == 8)
                    toks = slice(j * 128, (j + 1) * 128)
                    pg, pgb = bank()
                    for hd in range(NH):
                        k.op("pe", lambda e, hd=hd, pg=pg: e.matmul(pg[:, hd * 128:(hd + 1) * 128], lhsT=wup[0:17, hd * 128:(hd + 1) * 128], rhs=glr[0:17, toks],
                                                                    start=True, stop=True), reads=[Bwup, Bglr], writes=[pgb])
                    k.op("act", lambda e, pg=pg: e.activation(out=fl(G["e1"][:]), in_=pg[:, :], func=AF.Exp, scale=-1.0), reads=[pgb], writes=[GB["e1"]])
                    k.op("act", lambda e: e.activation(out=fl(G["e1"][:]), in_=fl(G["e1"][:]), func=AF.Ln, bias=1.0), reads=[GB["e1"]], writes=[GB["e1"]])
                    d0 = hrst_s if sample else hrst
                    k.op("dve", lambda e, d0=d0: e.tensor_tensor_scan(out=fl(G["cs"][:]), data0=fl(d0[:]), data1=fl(G["e1"][:]), initial=0.0,
                                                                      op0=ALU.mult, op1=ALU.add), reads=[GB["e1"], Bc], writes=[GB["cs"]])
                    k.op("act", lambda e: e.activation(out=fl(G["einv"][:]), in_=fl(G["cs"][:]), func=AF.Exp, scale=1.0 / 16.0), reads=[GB["cs"]], writes=[GB["einv"]])
                    k.op("act", lambda e: e.activation(out=fl(G["eb"][:]), in_=fl(G["cs"][:]), func=AF.Exp, scale=-1.0 / 16.0), reads=[GB["cs"]], writes=[GB["eb"]])
                    k.op("dve", lambda e: e.tensor_tensor(out=G["kt"][:], in0=kT[:, :, toks], in1=G["einv"][:], op=ALU.mult),
                         reads=[BkT, GB["einv"]], writes=[GB["kt"]])
                    pt2, pb2 = bank()
                    pt2b = pt2[:].bitcast(BF16)
                    for hd in range(NH):
                        k.op("pe", lambda e, hd=hd, pt2b=pt2b: e.transpose(out=pt2b[:, hd * 128:(hd + 1) * 128], in_=G["kt"][:, hd, :], identity=identb[:]),
                             reads=[GB["kt"], Bc], writes=[pb2])
                    k.op("act", lambda e, pt2b=pt2b: e.activation(out=fl(G["ktok"][:]), in_=pt2b[:, 0:512], func=AF.Copy), reads=[pb2], writes=[GB["ktok"]])
                    if full:
                        k.op("dve", lambda e: e.scalar_tensor_tensor(out=G["qt"][:], in0=qT[:, :, toks], scalar=float(DK ** -0.5),
                                                                     in1=G["eb"][:], op0=ALU.mult, op1=ALU.mult),
                             reads=[BqT, GB["eb"]], writes=[GB["qt"]])
                        psc, pscb = bank()
                        for hd in range(NH):
                            k.op("pe", lambda e, hd=hd, psc=psc: e.matmul(psc[:, hd * 128:(hd + 1) * 128], lhsT=G["kt"][:, hd, :], rhs=G["qt"][:, hd, :], start=True, stop=True),
                                 reads=[GB["kt"], GB["qt"]], writes=[pscb])
                        mk = smask if sample else causal
                        k.op("dve", lambda e, psc=psc, mk=mk: e.tensor_tensor(out=G["scm"][:], in0=psc[:, :].rearrange("p (a b) -> p a b", a=NH),
                                                                              in1=mk[:].unsqueeze(1).to_broadcast([128, NH, 128]), op=ALU.mult),
                             reads=[pscb, Bc], writes=[GB["scm"]])
                        if sample:
                            k.op("dve", lambda e: e.scalar_tensor_tensor(out=G["qtf"][:], in0=qT[:, :, toks], scalar=float(DK ** -0.5),
                                                                         in1=G["eb"][:], op0=ALU.mult, op1=ALU.mult),
                                 reads=[BqT, GB["eb"]], writes=[GB["qtf"]])
                            k.op("dve", lambda e: e.tensor_copy(out=G["ebe"][:], in_=G["eb"][:].rearrange("p a (j r) -> p a j r", r=8)[:, :, :, 7]),
                                 reads=[GB["eb"]], writes=[GB["ebe"]])
                        pos = []
                        for half in range(2):
                            po, pob = bank(hold=True)
                            pos.append((po, pob))
                            for hdl in range(2):
                                hd = half * 2 + hdl
                                if sample:
                                    k.op("dve", lambda e, hd=hd: e.tensor_tensor(out=G["KM"][:], in0=G["ktok"][:, hd, :].unsqueeze(1).to_broadcast([128, NSEQ, 128]),
                                                                                 in1=seqmask[:].unsqueeze(2).to_broadcast([128, NSEQ, 128]), op=ALU.mult),
                                         reads=[GB["ktok"], Bc], writes=[GB["KM"]])
                                for qh in (range(2) if sample else [None]):
                                    if sample:
                                        a = rot[0] % 2; rot[0] += 1
                                        k.dma("pool", S0h[a][:], gla_in[qh * 8:(qh + 1) * 8, hd, :, :].rearrange("q k v -> k q v"), writes=[BS0h[a]])
                                    for vh in range(2):
                                        c0 = (hdl * 2 + vh) * 128
                                        vcols = slice(hd * DV + vh * 128, hd * DV + (vh + 1) * 128)
                                        if not sample:
                                            k.op("pe", lambda e, c0=c0, vcols=vcols, po=po, hd=hd: e.matmul(po[:, c0:c0 + 128], lhsT=vtok[:, j, vcols], rhs=G["scm"][:, hd, :],
                                                                                                            start=True, stop=False), reads=[Bvt[j], GB["scm"]], writes=[pob])
                                            k.op("pe", lambda e, c0=c0, vh=vh, hd=hd, po=po: e.matmul(po[:, c0:c0 + 128], lhsT=Sglb[:, hd, vh * 128:(vh + 1) * 128],
                                                                                                      rhs=G["qt"][:, hd, :], start=False, stop=True), reads=[BS, GB["qt"]], writes=[pob])
                                        else:
                                            cc0 = c0 + qh * 64
                                            k.op("pe", lambda e, cc0=cc0, vcols=vcols, po=po, hd=hd, qh=qh: e.matmul(
                                                po[:, cc0:cc0 + 64], lhsT=vtok[:, j, vcols], rhs=G["scm"][:, hd, qh * 64:(qh + 1) * 64],
                                                start=True, stop=False), reads=[Bvt[j], GB["scm"]], writes=[pob])
                                            for ql in range(8):
                                                q = qh * 8 + ql
                                                k.op("pe", lambda e, cc0=cc0, ql=ql, q=q, a=a, vh=vh, hd=hd, po=po: e.matmul(
                                                    po[:, cc0 + ql * 8:cc0 + ql * 8 + 8], lhsT=S0h[a][:, ql, vh * 128:(vh + 1) * 128],
                                                    rhs=G["qtf"][:, hd, q * 8:(q + 1) * 8], start=False, stop=(ql == 7)),
                                                    reads=[BS0h[a], GB["qtf"]], writes=[pob])
                                    if sample:
                                        for qp in range(4):
                                            pu, pub = bank()
                                            for qi in range(2):
                                                q = qh * 8 + qp * 2 + qi
                                                k.op("pe", lambda e, q=q, qi=qi, pu=pu, hd=hd: e.matmul(pu[:, qi * DV:(qi + 1) * DV], lhsT=G["KM"][:, q, :],
                                                                                                         rhs=vtok[:, 8, hd * DV:(hd + 1) * DV], start=True, stop=True),
                                                     reads=[GB["KM"], Bvt[8]], writes=[pub])
                                            b_ = rot2[0] % 2; rot2[0] += 1
                                            q0 = qh * 8 + qp * 2
                                            k.op("dve", lambda e, b_=b_, pu=pu, a=a, qp=qp: e.tensor_tensor(out=Snew[b_][:], in0=pu[:, :].rearrange("p (q v) -> p q v", q=2),
                                                                                                          in1=S0h[a][:, qp * 2:qp * 2 + 2, :], op=ALU.add),
                                                 reads=[pub, BS0h[a]], writes=[BSn2[b_]])
                                            k.op("dve", lambda e, b_=b_, hd=hd, q0=q0: e.tensor_tensor(out=Snew[b_][:], in0=Snew[b_][:],
                                                                                                     in1=G["ebe"][:, hd, q0:q0 + 2].unsqueeze(2).to_broadcast([128, 2, DV]), op=ALU.mult),
                                                 reads=[BSn2[b_], GB["ebe"]], writes=[BSn2[b_]])
                                            k.dma("sp", gla_s[q0:q0 + 2, hd, :, :].rearrange("q k v -> k q v"), Snew[b_][:], reads=[BSn2[b_]], sbuf=BSn2[b_])
                            k.op("act", lambda e, po=po, half=half: e.activation(out=G["sq"][:, half, :, :].rearrange("p a b -> p (a b)"), in_=po[:, :], func=AF.Square),
                                 reads=[pob], writes=[GB["sq"]])
                        pss, pssb = bank()
                        for hd in range(NH):
                            for vh in range(2):
                                k.op("pe", lambda e, hd=hd, vh=vh, pss=pss: e.matmul(pss[:, hd * 128:(hd + 1) * 128], lhsT=onesb[:], rhs=G["sq"][:, hd // 2, (hd % 2) * 2 + vh, :],
                                                                                     start=(vh == 0), stop=(vh == 1)), reads=[GB["sq"], Bc], writes=[pssb])
                        k.op("act", lambda e, pss=pss: e.activation(out=fl(G["rr"][:]), in_=pss[:, :], func=AF.Ln, scale=1.0 / DV, bias=epsb[:, 0:1]),
                             reads=[pssb, Bc], writes=[GB["rr"]])
                        k.op("act", lambda e: e.activation(out=fl(G["rr"][:]), in_=fl(G["rr"][:]), func=AF.Exp, scale=-0.5), reads=[GB["rr"]], writes=[GB["rr"]])
                        for half in range(2):
                            po, pob = pos[half]
                            k.op("dve", lambda e, half=half, po=po: e.tensor_tensor(
                                out=by[:, half * 4:(half + 1) * 4, toks].rearrange("p (h v) t -> p h v t", v=2),
                                in0=po[:, :].rearrange("p (h v t) -> p h v t", h=2, v=2),
                                in1=G["rr"][:, half * 2:half * 2 + 2, :].unsqueeze(2).to_broadcast([128, 2, 2, 128]), op=ALU.mult),
                                reads=[pob, GB["rr"]], writes=[Bby])
                            unhold(pob)
                    if not sample:
                        for half in range(2):
                            pu, pub = bank()
                            for hdl in range(2):
                                hd = half * 2 + hdl
                                k.op("pe", lambda e, hd=hd, hdl=hdl, pu=pu: e.matmul(pu[:, hdl * DV:(hdl + 1) * DV], lhsT=G["ktok"][:, hd, :], rhs=vtok[:, j, hd * DV:(hd + 1) * DV],
                                                                                     start=True, stop=True), reads=[GB["ktok"], Bvt[j]], writes=[pub])
                            k.op("dve", lambda e, half=half, pu=pu: e.tensor_tensor(out=G["stmp"][:], in0=pu[:, :].rearrange("p (h v) -> p h v", h=2),
                                                                                    in1=Sgl[:, half * 2:half * 2 + 2, :], op=ALU.add),
                                 reads=[pub, BS], writes=[GB["stmp"]])
                            k.op("dve", lambda e, half=half: e.tensor_tensor(out=Sgl[:, half * 2:half * 2 + 2, :], in0=G["stmp"][:],
                                                                             in1=G["eb"][:, half * 2:half * 2 + 2, 127:128].to_broadcast([128, 2, DV]), op=ALU.mult),
                                 reads=[GB["stmp"], GB["eb"]], writes=[BS])
                        if full:
                            k.op("act", lambda e: e.activation(out=fl(Sglb[:]), in_=fl(Sgl[:]), func=AF.Copy), reads=[BS], writes=[BS])
                if full:
                    outs_wait.extend(BSn2)
                    barrier(BS0h + BSn2)
                barrier([Bdump] + list(GB.values()) + psb)


        def gla_pass(full, by, Bby):
            with ExitStack() as es4:
                k.es = es4
                gl = gla_alloc(full)
                gla_proj(gl, full)
                gla_tiles(gl, full, by, Bby)
                k.es = es4
                barrier([gl["Bglr"], gl["BkT"], gl["BqT"]] + gl["Bvt"] + psb)

        outs_wait = []
        k.op("dve", lambda e: e.memset(Hrun[:], 0.0), writes=[BH])
        k.op("dve", lambda e: e.memset(Sgl[:], 0.0), writes=[BS])
        k.op("dve", lambda e: e.memset(Sglb[:], 0.0), writes=[BS])
        prep(False)
        dump("hTpre", hT[:, :, 0:TP], [128, KT, TP], hTb, BF16)
        stage_end(3)
        with ExitStack() as esG:
            k.es = esG
            glp = gla_alloc(False)
            with ExitStack() as esP:
                k.es = esP
                U2 = sb("U2p", [128, NG, 128], BF16); BU2 = [Buf("U2_%d" % i) for i in range(8)]
                VH = sb("VHp", [128, 129, 2, 32], BF16); BV = Buf("VH"); BHh = Buf("Hh")
                s5_front(False, U2, BU2, VH, BV)
                k.es = esP
                gla_proj(glp, False)
                s5_recur2_state(VH, BV)
                k.es = esP
                dump("U2p", U2[:], [128, NG, 128], BU2, BF16)
                dump("VHp", VH[:], [128, 129, 2, 32], [BV], BF16)
                dump("Hrun_pre", Hrun[:], [128, 2, 32], [BH])
                barrier([BV, BHh, Bdump] + BU2 + psb)
                stage_end(4)
            k.es = esG
            gla_tiles(glp, False, None, None)
            k.es = esG
            barrier([glp["Bglr"], glp["BkT"], glp["BqT"]] + glp["Bvt"] + psb)
        k.es = esW
        dump("Sgl_pre", Sgl[:], [128, NH, DV], [BS])
        stage_end(5)
        k.op("dve", lambda e: e.tensor_scalar(out=Hrun[:], in0=Hrun[:], scalar1=flag[:, 0:1], scalar2=None, op0=ALU.mult), reads=[BH, Bp], writes=[BH])
        k.op("dve", lambda e: e.tensor_scalar(out=Sgl[:], in0=Sgl[:], scalar1=flag[:, 0:1], scalar2=None, op0=ALU.mult), reads=[BS, Bp], writes=[BS])
        k.op("act", lambda e: e.activation(out=Sglb[:], in_=Sgl[:], func=AF.Copy), reads=[BS], writes=[BS])

        ay = sb("ay", [128, 8, TM], BF16, side="right"); Bay = Buf("ay")
        prep(True)
        dump("hTmain", hT[:], [128, KT, TM], hTb, BF16)
        stage_end(6)
        with ExitStack() as esM:
            k.es = esM
            U2 = sb("U2m", [128, NG, 128 + NSEQ], BF16); BU2 = [Buf("U2m_%d" % i) for i in range(8)]
            VH = sb("VHm", [128, 129 + NSEQ, 2, 32], BF16); BV = Buf("VHm"); BHh = Buf("Hhm")
            Hsin = sb("Hsin", [128, NSEQ, 2, 32], F32); BHs = Buf("Hsin")
            Hsb = sb("Hsb", [128, NSEQ, 2, 32], BF16)
            with ExitStack() as es5:
                k.es = es5
                Sn = sb("Sn", [32, 8, 2, 128], F32); BSn = Buf("Sn")
                for q0 in range(0, NSEQ, 8):
                    for ri, src in enumerate((ssm_re_in, ssm_im_in)):
                        for a_ in range(2):
                            k.dma("sp", Sn[:, :, ri, a_ * 64:(a_ + 1) * 64], src[q0:q0 + 8, a_ * 32:(a_ + 1) * 32, :].rearrange("j g p -> g j p"), writes=[BSn])
                    pt, pb = bank()
                    for qi in range(8):
                        for ri in range(2):
                            col = (qi * 2 + ri) * 32
                            k.op("pe", lambda e, qi=qi, ri=ri, col=col, pt=pt: e.transpose(out=pt[:, col:col + 32], in_=Sn[:, qi, ri, :],
                                                                                           identity=identf[0:32, 0:32]), reads=[BSn, Bc], writes=[pb])
                    k.op("dve", lambda e, q0=q0, pt=pt: e.tensor_copy(out=Hsin[:, q0:q0 + 8, :, :].rearrange("p j r g -> p (j r g)"), in_=pt[:, :]),
                         reads=[pb], writes=[BHs])
                barrier([BSn])
            k.es = esM
            s5_front(True, U2, BU2, VH, BV)
            k.es = esM
            k.op("act", lambda e: e.activation(out=Hsb[:], in_=Hsin[:], func=AF.Copy), reads=[BHs], writes=[BHs])
            k.op("act", lambda e: e.activation(out=VH[:, 0, :, :], in_=Hrun[:], func=AF.Copy), reads=[BH], writes=[BHh])
            s5_recur2_hist(VH, BV, BHh)
            k.es = esM
            with ExitStack() as es5:
                k.es = es5
                Hso = sb("Hso", [128, NSEQ, 2, 32], F32); BHo = Buf("Hso")
                t16b = sb("t16b", [128, NSEQ, 2, 32], F32); Bt16 = Buf("t16")
                arb16 = ARR[:].unsqueeze(1).unsqueeze(1).to_broadcast([128, NSEQ, 2, 32])
                k.op("dve", lambda e: e.tensor_tensor(out=Hso[:], in0=Hsin[:], in1=arb16, op=ALU.mult), reads=[BHs, Bs5], writes=[BHo])
                k.op("dve", lambda e: e.tensor_tensor(out=t16b[:, :, 0, :], in0=Hsin[:, :, 1, :], in1=AIn[:].unsqueeze(1).to_broadcast([128, NSEQ, 32]), op=ALU.mult),
                     reads=[BHs, Bs5], writes=[Bt16])
                k.op("dve", lambda e: e.tensor_tensor(out=t16b[:, :, 1, :], in0=Hsin[:, :, 0, :], in1=AIp[:].unsqueeze(1).to_broadcast([128, NSEQ, 32]), op=ALU.mult),
                     reads=[BHs, Bs5], writes=[Bt16])
                k.op("dve", lambda e: e.tensor_tensor(out=Hso[:], in0=Hso[:], in1=t16b[:], op=ALU.add), reads=[Bt16, BHo], writes=[BHo])
                k.op("dve", lambda e: e.tensor_tensor(out=Hso[:], in0=Hso[:], in1=VH[:, 129:129 + NSEQ, :, :], op=ALU.add), reads=[BHo, BV], writes=[BHo])
                So = [sb("So%d" % i, [32, 2, 2, 128], F32) for i in range(2)]; BSo = [Buf("So%d" % i) for i in range(2)]
                for bi, q0 in enumerate(range(0, NSEQ + 1, 2)):
                    a = bi % 2
                    pt, pb = bank()
                    nq = min(2, NSEQ + 1 - q0)
                    for qi in range(nq):
                        q = q0 + qi
                        for ri in range(2):
                            src = Hso[:, q, ri, :] if q < NSEQ else Hrun[:, ri, :]
                            col = (qi * 2 + ri) * 128
                            k.op("pe", lambda e, src=src, col=col, pt=pt: e.transpose(out=pt[0:32, col:col + 128], in_=src, identity=identf[:]),
                                 reads=[BHo, BH, Bc], writes=[pb])
                    k.op("dve", lambda e, a=a, nq=nq, pt=pt: e.tensor_copy(out=So[a][:, 0:nq, :, :].rearrange("g j r q -> g (j r q)"), in_=pt[0:32, 0:nq * 256]),
                         reads=[pb], writes=[BSo[a]])
                    if q0 < NSEQ:
                        for ri, dst in enumerate((ssm_re_s, ssm_im_s)):
                            for a_ in range(2):
                                k.dma("sp", dst[q0:q0 + 2, a_ * 32:(a_ + 1) * 32, :].rearrange("j g p -> g j p"), So[a][:, :, ri, a_ * 64:(a_ + 1) * 64],
                                      reads=[BSo[a]], sbuf=BSo[a])
                    else:
                        for ri, dst in enumerate((ssm_re_p, ssm_im_p)):
                            k.dma("sp", dst.rearrange("(a g) p -> g a p", a=2), So[a][:, 0, ri, :].rearrange("g (a p) -> g a p", a=2), reads=[BSo[a]], sbuf=BSo[a])
                outs_wait.extend(BSo)
                barrier([BHo, Bt16] + BSo)
            k.es = esM
            with ExitStack() as es5:
                k.es = es5
                Yc = [sb("Yc%d" % i, [128, 8, 128], BF16) for i in range(2)]; BYc = [Buf("Yc%d" % i) for i in range(2)]
                Ycs = [sb("Ycs%d" % i, [NSEQ, 8, 128], BF16) for i in range(2)]; BYcs = [Buf("Ycs%d" % i) for i in range(2)]
                for ct in range(8):
                    for ci, (toff, C, coff) in enumerate(((0, 128, 0), (TP, NSEQ, 128))):
                        yc, byc = (Yc[ct % 2], BYc[ct % 2]) if ci == 0 else (Ycs[ct % 2], BYcs[ct % 2])
                        for gq in range(2):
                            pt, pb = bank()
                            for gi in range(4):
                                g = ct * 8 + gq * 4 + gi
                                gh, gp = g // 32, g % 32
                                rows = slice(gh * 64, gh * 64 + 64)
                                out = pt[0:C, gi * 128:(gi + 1) * 128]
                                k.op("pe", lambda e, out=out, g=g, C=C, coff=coff: e.matmul(out, lhsT=U2[:, g, coff:coff + C], rhs=Toep[:, g, :], start=True, stop=False),
                                     reads=[BU2[g // 8], Bs5], writes=[pb])
                                for ri, YY in enumerate((Ybr, Ybn)):
                                    hp = VH[rows, 0:128, ri, gp] if ci == 0 else Hsb[rows, :, ri, gp]
                                    k.op("pe", lambda e, out=out, hp=hp, YY=YY, rows=rows, gp=gp, ri=ri: e.matmul(out, lhsT=hp, rhs=YY[rows, gp, 1:9, :], start=False, stop=(ri == 1)),
                                         reads=[BHh, BHs, Bs5], writes=[pb])
                            k.op("act", lambda e, pt=pt, C=C, yc=yc, gq=gq: e.activation(
                                out=yc[0:C, :, gq * 64:(gq + 1) * 64].rearrange("c t (g h) -> c g t h", h=16),
                                in_=pt[0:C, :].rearrange("c (g t h) -> c g t h", g=4, h=16), func=AF.Gelu_apprx_tanh), reads=[pb], writes=[byc])
                        pt, pb = bank()
                        ptb = pt[:].bitcast(BF16)
                        for t in range(8):
                            k.op("pe", lambda e, t=t, C=C, yc=yc, ptb=ptb: e.transpose(out=ptb[:, t * 128:t * 128 + C], in_=yc[0:C, t, :], identity=identb[0:C, 0:C]),
                                 reads=[byc, Bc], writes=[pb])
                        k.op("act", lambda e, C=C, toff=toff, ct=ct, ptb=ptb: e.activation(
                            out=ay[:, ct, toff:toff + 8 * C].rearrange("p (c t) -> p t c", t=8),
                            in_=ptb[:, 0:1024].rearrange("p (t c) -> p t c", t=8)[:, :, 0:C], func=AF.Copy), reads=[pb], writes=[Bay])
                barrier(BYc + BYcs)
            k.es = esM
            dump("ay", ay[:], [128, 8, TM], [Bay], BF16)
            dump("Hrun_main", Hrun[:], [128, 2, 32], [BH])
            barrier([BV, BHh, BHs, Bdump] + BU2 + [Bs5] + psb)
            stage_end(7)
        esW.close()
        k.es = es
        by = sb("by", [128, 8, TM], BF16, side="right"); Bby = Buf("by")
        gla_pass(True, by, Bby)
        k.es = es
        dump("by", by[:], [128, 8, TM], [Bby], BF16)
        dump("Sgl_main", Sgl[:], [128, NH, DV], [BS])
        stage_end(8)
        k.dma("sp", gla_p.rearrange("h k v -> k h v"), Sgl[:], reads=[BS], sbuf=BS)
        outs_wait.append(BS)

        T = TM
        nblocks = [(n0, min(512, T - n0)) for n0 in range(0, T, 512)]
        hall = hTb[0:9]
        mg = sb("mg", [128, KT, TM], BF16, side="right"); Bmg = [Buf("mg%d" % j) for j in range(9)]
        By_d = [Buf("ydram%d" % j) for j in range(9)]
        with ExitStack() as esC:
            k.es = esC
            sg = [sb("sg%d" % i, [128, 512], F32) for i in range(2)]; Bsg = [Buf("sg%d" % i) for i in range(2)]
            m1 = [sb("m1_%d" % i, [128, 512], F32) for i in range(2)]; Bm1 = [Buf("m1_%d" % i) for i in range(2)]
            rot = [0]
            ay2 = sb("ay2", [128, 8, TM], BF16); Bay2 = Buf("ay2")
            for i in range(4):
                ws, wb = w_next((w_glu, i * WC))
                for mt in range(2):
                    m = i * 2 + mt
                    for (n0, nn) in nblocks:
                        pt, pb = bank()
                        for kt in range(8):
                            k.op("pe", lambda e, kt=kt, n0=n0, nn=nn, mt=mt, ws=ws, pt=pt: e.matmul(
                                pt[:, 0:nn], lhsT=ws[:, kt, mt * 128:(mt + 1) * 128], rhs=ay[:, kt, n0:n0 + nn], start=(kt == 0), stop=(kt == 7)),
                                reads=[Bay, wb], writes=[pb])
                        a = rot[0] % 2; rot[0] += 1
                        k.op("act", lambda e, a=a, nn=nn, m=m, pt=pt: e.activation(out=sg[a][:, 0:nn], in_=pt[:, 0:nn], func=AF.Sigmoid, bias=bglu_col[:, m:m + 1]),
                             reads=[pb, Bp], writes=[Bsg[a]])
                        k.op("dve", lambda e, a=a, n0=n0, nn=nn, m=m: e.tensor_tensor(out=ay2[:, m, n0:n0 + nn], in0=ay[:, m, n0:n0 + nn], in1=sg[a][:, 0:nn], op=ALU.mult),
                             reads=[Bay, Bsg[a]], writes=[Bay2])
            for (c00, tgt, Btgt) in ((1024, ay2, Bay2), (4096, by, Bby)):
                for i in range(4):
                    ws, wb = w_next((w_in, c00 + i * WC))
                    for mt in range(2):
                        m = i * 2 + mt
                        for (n0, nn) in nblocks:
                            pt, pb = bank()
                            for kt in range(KT):
                                k.op("pe", lambda e, kt=kt, n0=n0, nn=nn, mt=mt, ws=ws, pt=pt: e.matmul(
                                    pt[:, 0:nn], lhsT=ws[:, kt, mt * 128:(mt + 1) * 128], rhs=hT[:, kt, n0:n0 + nn], start=(kt == 0), stop=(kt == KT - 1)),
                                    reads=hall + [wb], writes=[pb])
                            a = rot[0] % 2; rot[0] += 1
                            k.op("act", lambda e, a=a, nn=nn, pt=pt: e.activation(out=sg[a][:, 0:nn], in_=pt[:, 0:nn], func=AF.Silu), reads=[pb], writes=[Bsg[a]])
                            if tgt is by:
                                k.op("dve", lambda e, a=a, n0=n0, nn=nn, m=m, tgt=tgt: e.scalar_tensor_tensor(out=tgt[:, m, n0:n0 + nn], in0=tgt[:, m, n0:n0 + nn], scalar=ggain_col[:, m:m + 1],
                                                                                                               in1=sg[a][:, 0:nn], op0=ALU.mult, op1=ALU.mult),
                                     reads=[Btgt, Bsg[a], Bp], writes=[Btgt])
                            else:
                                k.op("dve", lambda e, a=a, n0=n0, nn=nn, m=m, tgt=tgt: e.tensor_tensor(out=tgt[:, m, n0:n0 + nn], in0=tgt[:, m, n0:n0 + nn], in1=sg[a][:, 0:nn], op=ALU.mult),
                                     reads=[Btgt, Bsg[a]], writes=[Btgt])
            m1A = sb("m1A", [128, 2, TM], F32); Bm1A = Buf("m1A")
            for fb in range(8):
                for br, (cg, wo_d, src, Bsrc) in enumerate(((5136, w_a_out, ay2, Bay2), (7184, w_b_out, by, Bby))):
                    wg, wgb = w_next((w_in, cg + fb * WC))
                    wo, wob = w_next((wo_d, fb * WC))
                    for mt in range(2):
                        f = fb * 2 + mt
                        for (n0, nn) in nblocks:
                            pg, pgb = bank()
                            for kt in range(KT):
                                k.op("pe", lambda e, kt=kt, pg=pg, wg=wg, mt=mt, n0=n0, nn=nn: e.matmul(
                                    pg[:, 0:nn], lhsT=wg[:, kt, mt * 128:(mt + 1) * 128], rhs=hT[:, kt, n0:n0 + nn],
                                    start=(kt == 0), stop=(kt == KT - 1)), reads=hall + [wgb], writes=[pgb])
                            po, pob = bank()
                            for kt in range(8):
                                k.op("pe", lambda e, kt=kt, po=po, wo=wo, src=src, mt=mt, n0=n0, nn=nn: e.matmul(
                                    po[:, 0:nn], lhsT=wo[:, kt, mt * 128:(mt + 1) * 128], rhs=src[:, kt, n0:n0 + nn],
                                    start=(kt == 0), stop=(kt == 7)), reads=[Bsrc, wob], writes=[pob])
                            a = rot[0] % 2; rot[0] += 1
                            k.op("act", lambda e, a=a, pg=pg, nn=nn: e.activation(out=sg[a][:, 0:nn], in_=pg[:, 0:nn], func=AF.Sigmoid), reads=[pgb], writes=[Bsg[a]])
                            if br == 0:
                                k.op("dve", lambda e, a=a, po=po, mt=mt, n0=n0, nn=nn: e.tensor_tensor(out=m1A[:, mt, n0:n0 + nn], in0=po[:, 0:nn], in1=sg[a][:, 0:nn], op=ALU.mult),
                                     reads=[pob, Bsg[a]], writes=[Bm1A])
                            else:
                                k.op("dve", lambda e, a=a, po=po, nn=nn: e.tensor_tensor(out=m1[a][:, 0:nn], in0=po[:, 0:nn], in1=sg[a][:, 0:nn], op=ALU.mult),
                                     reads=[pob, Bsg[a]], writes=[Bm1[a]])
                                tiles = list(range(n0 // 128, (n0 + nn) // 128))
                                k.op("dve", lambda e, a=a, f=f, mt=mt, n0=n0, nn=nn: e.tensor_tensor(out=mg[:, f, n0:n0 + nn], in0=m1[a][:, 0:nn], in1=m1A[:, mt, n0:n0 + nn], op=ALU.add),
                                     reads=[Bm1[a], Bm1A], writes=[Bmg[t_] for t_ in tiles])
            dump("ay2", ay2[:], [128, 8, TM], [Bay2], BF16)
            dump("by2", by[:], [128, 8, TM], [Bby], BF16)
            dump("mg", mg[:], [128, KT, TM], Bmg, BF16)
            barrier(Bsg + Bm1 + [Bm1A, Bay2, Bay, Bby, Bdump] + hall + psb)
            stage_end(9)
        with ExitStack() as esO:
            k.es = esO
            gts = sb("gts", [17, D], F32); Bgts = Buf("gts")
            k.dma("sp", gts[:], gate_scr[:, :], reads=[Bgs], writes=[Bgts])
            gtk = [sb("gtk%d" % i, [128, WC], F32) for i in range(2)]; Bgtk = [Buf("gtk%d" % i) for i in range(2)]
            fgb = sb("fgb", [128, D], F32); Bfg = Buf("fgb")
            k.dma("sp", fgb[:], fgain.partition_broadcast(128), writes=[Bfg])
            yacc = hT[:].rearrange("p a b -> p (a b)").rearrange("p (j f) -> p j f", j=9)
            Bya = [Buf("yacc%d" % j) for j in range(9)]
            xs = [sb("fxs%d" % i, [128, D], F32) for i in range(2)]; xb_ = [Buf("fxs%d" % i) for i in range(2)]
            jk = sb("fjk", [128, D], BF16); Bjk = Buf("fjk")
            st = sb("fst", [128, 9, 4], F32); Bst = [Buf("fst%d" % j_) for j_ in range(9)]
            for j in range(2):
                k.dma("sp", xs[j][:], xmain[j * 128:(j + 1) * 128, :], writes=[xb_[j]])

            def final_tile(j):
                a = j % 2
                dst = y_main[j * 128:(j + 1) * 128, :] if j < 8 else y_smp[:, :]
                k.op("dve", lambda e: e.tensor_tensor(out=xs[a][:], in0=xs[a][:], in1=yacc[:, j, :], op=ALU.add), reads=[xb_[a], Bya[j]], writes=[xb_[a]])
                k.op("act", lambda e: e.activation(out=jk[:], in_=xs[a][:], func=AF.Square, accum_out=st[:, j, 0:1]), reads=[xb_[a]], writes=[Bjk, Bst[j]])
                k.op("act", lambda e: e.activation(out=st[:, j, 2:3], in_=st[:, j, 0:1], func=AF.Ln, scale=1.0 / D, bias=epsb[:, 0:1]), reads=[Bst[j], Bc], writes=[Bst[j]])
                k.op("act", lambda e: e.activation(out=st[:, j, 3:4], in_=st[:, j, 2:3], func=AF.Exp, scale=-0.5), reads=[Bst[j]], writes=[Bst[j]])
                k.op("dve", lambda e: e.scalar_tensor_tensor(out=xs[a][:], in0=xs[a][:], scalar=st[:, j, 3:4], in1=fgb[:], op0=ALU.mult, op1=ALU.mult),
                     reads=[xb_[a], Bst[j], Bfg], writes=[xb_[a]])
                k.dma("sp", dst, xs[a][:], reads=[xb_[a]], writes=[By_d[j]], sbuf=xb_[a])
                if j + 2 < 9:
                    jn = j + 2
                    srcn = xmain[jn * 128:(jn + 1) * 128, :] if jn < 8 else xsmp[:, :]
                    k.dma("sp", xs[a][:], srcn, writes=[xb_[a]])

            for i in range(8):
                ws, wb = w_next((w_out, i * WC))
                for wi, oh in enumerate((ohP, ohS)):
                    pt, pb = bank()
                    k.op("pe", lambda e, oh=oh, i=i, pt=pt: e.matmul(pt[:, 0:WC], lhsT=oh[:, :], rhs=gts[:, i * WC:(i + 1) * WC], start=True, stop=True),
                         reads=[Bgts, Bc], writes=[pb])
                    k.op("act", lambda e, wi=wi, pt=pt: e.activation(out=gtk[wi][:], in_=pt[:, 0:WC], func=AF.Copy), reads=[pb], writes=[Bgtk[wi]])
                for j in range(9):
                    pt, pb = bank()
                    for kt in range(KT):
                        k.op("pe", lambda e, kt=kt, j=j, ws=ws, pt=pt: e.matmul(pt[:, 0:WC], lhsT=mg[:, kt, j * 128:(j + 1) * 128], rhs=ws[:, kt, :],
                                                                               start=(kt == 0), stop=(kt == KT - 1)), reads=[Bmg[j], wb], writes=[pb])
                    wi = 0 if j < 8 else 1
                    k.op("dve", lambda e, j=j, i=i, wi=wi, pt=pt: e.tensor_tensor(out=yacc[:, j, i * WC:(i + 1) * WC], in0=pt[:, 0:WC], in1=gtk[wi][:], op=ALU.mult),
                         reads=[pb, Bgtk[wi]], writes=[Bya[j]])
                    if i == 7:
                        final_tile(j)
            barrier(outs_wait + By_d + xb_ + Bya + [Bfg, Bgts, Bjk] + Bst + Bgtk + Bmg + psb)
            stage_end(10)
        esR.close()
        k.es = es
    except _Stop:
        pass
    return nc


_NC_CACHE = {}


def _shard_inputs(inp):
    f = lambda a: np.ascontiguousarray(np.asarray(a, dtype=np.float32))
    xp = f(inp["x_prompt"]); xs = f(inp["x_sample"]); cp = f(inp["c_prompt"]); cs = f(inp["c_sample"])
    sre = f(inp["state_ssm_re"])[0]; sim = f(inp["state_ssm_im"])[0]; sgl = f(inp["state_gla"])[0]
    shared = {
        "w_ada": f(inp["w_ada"])[0], "b_ada": f(inp["b_ada"])[0], "norm_gain": f(inp["norm_gain"])[0], "w_in": f(inp["w_in"])[0],
        "lambda_re": f(inp["lambda_re"])[0], "lambda_im": f(inp["lambda_im"])[0], "log_dt": f(inp["log_dt"])[0],
        "ssm_b_re": f(inp["ssm_b_re"])[0], "ssm_b_im": f(inp["ssm_b_im"])[0], "ssm_c_re": f(inp["ssm_c_re"])[0], "ssm_c_im": f(inp["ssm_c_im"])[0],
        "d_skip": f(inp["d_skip"])[0], "w_glu": f(inp["w_glu"])[0], "b_glu": f(inp["b_glu"])[0], "w_gate_up": f(inp["w_gate_up"])[0],
        "b_gate": f(inp["b_gate"])[0], "gla_norm_gain": f(inp["gla_norm_gain"])[0], "w_a_out": f(inp["w_a_out"])[0],
        "w_b_out": f(inp["w_b_out"])[0], "w_out": f(inp["w_out"])[0], "final_norm_gain": f(inp["final_norm_gain"]),
    }
    maps = []
    for core in range(8):
        b, half = core // 2, core % 2
        m = dict(shared)
        m["xpre"] = np.ascontiguousarray(xp[b, 0:TP])
        m["xmain"] = np.ascontiguousarray(xp[b, half * TP:(half + 1) * TP])
        sl = slice(core * NSEQ, (core + 1) * NSEQ)
        m["xsmp"] = np.ascontiguousarray(xs[sl].reshape(NSEQ * 8, D))
        m["c17"] = np.ascontiguousarray(np.concatenate([cp[b:b + 1], cs[sl]], axis=0))
        m["flag"] = np.full((128, 1), float(half), np.float32)
        m["ssm_re_in"] = np.ascontiguousarray(sre[sl]); m["ssm_im_in"] = np.ascontiguousarray(sim[sl])
        m["gla_in"] = np.ascontiguousarray(sgl[sl])
        maps.append(m)
    return maps


def run_debug(inputs, stop, core=1):
    nc = build_nc(dbg=True, stop=stop)
    maps = _shard_inputs(inputs)
    res = run_bass_kernel_spmd(nc, [maps[core]], core_ids=[0])
    return res.results[0], maps[core]


def kernel(**inputs):
    if "nc" not in _NC_CACHE:
        _NC_CACHE["nc"] = build_nc()
    nc = _NC_CACHE["nc"]
    maps = _shard_inputs(inputs)
    res = run_bass_kernel_spmd(nc, maps, core_ids=list(range(8)))
    R = res.results
    y_prompt = np.zeros((4, 2048, D), np.float32); y_sample = np.zeros((128, 8, D), np.float32)
    re_p = np.zeros((1, 4, NG, 64), np.float32); im_p = np.zeros((1, 4, NG, 64), np.float32)
    gl_p = np.zeros((1, 4, NH, DK, DV), np.float32)
    re_s = np.zeros((1, 128, NG, 64), np.float32); im_s = np.zeros((1, 128, NG, 64), np.float32)
    gl_s = np.zeros((1, 128, NH, DK, DV), np.float32)
    for core in range(8):
        b, half = core // 2, core % 2
        r = R[core]
        y_prompt[b, half * TP:(half + 1) * TP] = r["y_main"]
        sl = slice(core * NSEQ, (core + 1) * NSEQ)
        y_sample[sl] = r["y_smp"].reshape(NSEQ, 8, D)
        re_s[0, sl] = r["ssm_re_s"]; im_s[0, sl] = r["ssm_im_s"]; gl_s[0, sl] = r["gla_s"]
        if half == 1:
            re_p[0, b] = r["ssm_re_p"]; im_p[0, b] = r["ssm_im_p"]; gl_p[0, b] = r["gla_p"]
    return (y_prompt, y_sample, re_p, im_p, gl_p, re_s, im_s, gl_s)
```

```python
import math
from contextlib import ExitStack

import numpy as np
import concourse.bass as bass
import concourse.mybir as mybir
from concourse.bass_utils import run_bass_kernel_spmd

F32 = mybir.dt.float32
BF16 = mybir.dt.bfloat16
AF = mybir.ActivationFunctionType
ALU = mybir.AluOpType

D = 2048
KT = 16
WA = 1024
NG = 64
NH = 4
DK = 128
DV = 256
EPS = 1e-6
INC = 9232
NSEQ = 16
TP = 1024
TM = TP + 128
WC = 256
NSLOT = 3
PI = math.pi


class Ev:
    __slots__ = ("sem", "val", "eng")

    def __init__(self, sem, val, eng):
        self.sem, self.val, self.eng = sem, val, eng


class Buf:
    def __init__(self, name):
        self.name = name
        self.w = None
        self.r = {}
        self.dsem = None
        self.dcnt = 0


class K:
    def __init__(self, nc, es):
        self.nc, self.es = nc, es
        self.E = {"pe": nc.tensor, "dve": nc.vector, "act": nc.scalar, "pool": nc.gpsimd, "sp": nc.sync}
        self.sem = {e: es.enter_context(nc.semaphore("s_" + e)) for e in ("pe", "dve", "act", "pool")}
        self.cnt = {e: 0 for e in self.sem}
        self.seen = {e: {} for e in self.E}
        self.nbuf = 0
        self.esR = es
        self.es_sem = es

    def sb(self, name, shape, dt, side="left"):
        st = self.es if side == "left" else self.esR
        self.nname = getattr(self, "nname", 0) + 1
        return st.enter_context(self.nc.sbuf_tensor("sb%d_%s" % (self.nname, name), shape, dt, side=side))

    def _deps(self, reads, writes):
        evs = []
        for b in reads:
            if b.w is not None:
                evs.append(b.w)
        for b in writes:
            if b.w is not None:
                evs.append(b.w)
            evs.extend(b.r.values())
        return evs

    def _waits(self, e, evs):
        best = {}
        for ev in evs:
            if ev.eng == "pe" and e == "pe":
                continue
            kk = id(ev.sem)
            if kk not in best or best[kk].val < ev.val:
                best[kk] = ev
        for kk, ev in best.items():
            if self.seen[e].get(kk, 0) >= ev.val:
                continue
            self.E[e].wait_ge(ev.sem, ev.val)
            self.seen[e][kk] = ev.val

    def _commit(self, ev, reads, writes):
        for b in writes:
            b.w = ev
            b.r = {}
        for b in reads:
            if b not in writes:
                kk = id(ev.sem)
                if kk not in b.r or b.r[kk].val < ev.val:
                    b.r[kk] = ev

    def op(self, e, fn, reads=(), writes=(), inc=True):
        self._waits(e, self._deps(reads, writes))
        ins = fn(self.E[e])
        if not inc:
            self.pend = getattr(self, "pend", [])
            self.pend.append((list(reads), list(writes)))
            return None
        self.cnt[e] += 1
        ins.then_inc(self.sem[e], 1)
        ev = Ev(self.sem[e], self.cnt[e], e)
        for (r_, w_) in getattr(self, "pend", []):
            self._commit(ev, r_, w_)
        self.pend = []
        self._commit(ev, reads, writes)
        return ev

    def dma(self, q, out, in_, reads=(), writes=(), sbuf=None, **kw):
        self._waits(q, self._deps(reads, writes))
        tb = sbuf if sbuf is not None else (writes[0] if writes else reads[0])
        if tb.dsem is None:
            self.nbuf += 1
            tb.dsem = self.es_sem.enter_context(self.nc.semaphore("d%d" % self.nbuf))
        tb.dcnt += 16
        self.E[q].dma_start(out=out, in_=in_, **kw).then_inc(tb.dsem, 16)
        ev = Ev(tb.dsem, tb.dcnt, "dma")
        self._commit(ev, reads, writes)
        return ev

    def wait_all(self, e, bufs):
        evs = []
        for b in bufs:
            if b.w is not None:
                evs.append(b.w)
            evs.extend(b.r.values())
        self._waits(e, evs)


class _Stop(Exception):
    pass


def build_nc(dbg=None, stop=None):
    nc = bass.Bass("TRN2", target_bir_lowering=False)
    dumps = []

    def din(name, shape):
        return nc.dram_tensor(name, list(shape), F32, kind="ExternalInput").ap()

    def dout(name, shape):
        return nc.dram_tensor(name, list(shape), F32, kind="ExternalOutput").ap()

    xpre = din("xpre", [TP, D]); xmain = din("xmain", [TP, D]); xsmp = din("xsmp", [128, D])
    c17 = din("c17", [17, D]); flag_d = din("flag", [128, 1])
    ssm_re_in = din("ssm_re_in", [NSEQ, NG, 64]); ssm_im_in = din("ssm_im_in", [NSEQ, NG, 64])
    gla_in = din("gla_in", [NSEQ, NH, DK, DV])
    w_ada = din("w_ada", [D, 3 * D]); b_ada = din("b_ada", [3 * D]); norm_gain = din("norm_gain", [D])
    w_in = din("w_in", [D, INC])
    lam_re = din("lambda_re", [NG, 64]); lam_im = din("lambda_im", [NG, 64]); log_dt = din("log_dt", [NG])
    b_re = din("ssm_b_re", [NG, 64, 16]); b_im = din("ssm_b_im", [NG, 64, 16])
    c_re = din("ssm_c_re", [NG, 16, 64]); c_im = din("ssm_c_im", [NG, 16, 64])
    d_skip = din("d_skip", [WA]); w_glu = din("w_glu", [WA, WA]); b_glu = din("b_glu", [WA])
    w_gate_up = din("w_gate_up", [16, NH * DK]); b_gate = din("b_gate", [NH * DK])
    gla_gain = din("gla_norm_gain", [WA])
    w_a_out = din("w_a_out", [WA, D]); w_b_out = din("w_b_out", [WA, D]); w_out = din("w_out", [D, D])
    fgain = din("final_norm_gain", [D])

    y_main = dout("y_main", [TP, D]); y_smp = dout("y_smp", [128, D])
    ssm_re_p = dout("ssm_re_p", [NG, 64]); ssm_im_p = dout("ssm_im_p", [NG, 64])
    gla_p = dout("gla_p", [NH, DK, DV])
    ssm_re_s = dout("ssm_re_s", [NSEQ, NG, 64]); ssm_im_s = dout("ssm_im_s", [NSEQ, NG, 64])
    gla_s = dout("gla_s", [NSEQ, NH, DK, DV])
    gate_scr = nc.dram_tensor("gate_scr", [17, D], F32, kind="Internal").ap()
    es = ExitStack()
    try:
      with es:
        k = K(nc, es)
        sb = k.sb
        Bdump = Buf("dump")

        def dump(name, ap, shape, bufs, dt=F32):
            if not dbg:
                return
            d = nc.dram_tensor("dbg_" + name, list(shape), dt, kind="ExternalOutput").ap()
            k.dma("sp", d, ap, reads=list(bufs), writes=[Bdump])
            dumps.append(name)

        def stage_end(n):
            if stop is not None and n >= stop:
                for e_ in ("pe", "dve", "act", "pool", "sp"):
                    k.wait_all(e_, [Bdump])
                raise _Stop()
        psum = [es.enter_context(nc.psum_tensor("ps%d" % i, [128, 512], F32)) for i in range(8)]
        psb = [Buf("ps%d" % i) for i in range(8)]
        pcur = [0]
        held = set()

        def bank(hold=False):
            i = pcur[0]
            while i in held:
                i = (i + 1) % 8
            pcur[0] = (i + 1) % 8
            if hold:
                held.add(i)
            return psum[i], psb[i]

        def unhold(pbuf):
            held.discard(psb.index(pbuf))

        Bc = Buf("consts")
        identf = sb("identf", [128, 128], F32); identb = sb("identb", [128, 128], BF16)
        onesb = sb("onesb", [128, 128], BF16); onesf = sb("onesf", [128, 128], F32)
        causal = sb("causal", [128, 128], F32); seqmask = sb("seqmask", [128, NSEQ], F32)
        smask = sb("smask", [128, 128], F32); rstmask = sb("rstmask", [128, 128], F32)
        mask3 = sb("mask3", [128, 8], F32)
        ohP = sb("ohP", [17, 128], F32); ohS = sb("ohS", [17, 128], F32)
        P = lambda fn: k.op("pool", fn, writes=[Bc])
        P(lambda e: e.memset(identf[:], 1.0))
        P(lambda e: e.affine_select(out=identf[:], in_=identf[:], compare_op=ALU.is_equal, fill=0.0, base=0,
                                    pattern=[[-1, 128]], channel_multiplier=1))
        P(lambda e: e.tensor_copy(out=identb[:], in_=identf[:]))
        P(lambda e: e.memset(onesb[:], 1.0))
        P(lambda e: e.memset(onesf[:], 1.0))
        P(lambda e: e.memset(causal[:], 1.0))
        P(lambda e: e.affine_select(out=causal[:], in_=causal[:], compare_op=ALU.is_ge, fill=0.0, base=0,
                                    pattern=[[1, 128]], channel_multiplier=-1))
        P(lambda e: e.memset(seqmask[:], 1.0))
        P(lambda e: e.affine_select(out=seqmask[:], in_=seqmask[:], compare_op=ALU.is_ge, fill=0.0, base=0,
                                    pattern=[[-8, NSEQ]], channel_multiplier=1))
        P(lambda e: e.affine_select(out=seqmask[:], in_=seqmask[:], compare_op=ALU.is_ge, fill=0.0, base=7,
                                    pattern=[[8, NSEQ]], channel_multiplier=-1))
        P(lambda e: e.tensor_tensor(out=smask[:].rearrange("p (j r) -> p j r", r=8),
                                    in0=causal[:].rearrange("p (j r) -> p j r", r=8),
                                    in1=seqmask[:].unsqueeze(2).to_broadcast([128, NSEQ, 8]), op=ALU.mult))
        P(lambda e: e.memset(rstmask[:], 1.0))
        P(lambda e: e.affine_select(out=rstmask[:].rearrange("p (j r) -> p j r", r=8),
                                    in_=rstmask[:].rearrange("p (j r) -> p j r", r=8),
                                    compare_op=ALU.is_ge, fill=0.0, base=-1,
                                    pattern=[[0, NSEQ], [1, 8]], channel_multiplier=0))
        P(lambda e: e.memset(mask3[:], 1.0))
        P(lambda e: e.affine_select(out=mask3[:], in_=mask3[:], compare_op=ALU.is_ge, fill=0.0, base=15,
                                    pattern=[[16, 8]], channel_multiplier=-1))
        P(lambda e: e.memset(ohP[:], 0.0))
        P(lambda e: e.memset(ohP[0:1, :], 1.0))
        P(lambda e: e.memset(ohS[:], 1.0))
        P(lambda e: e.affine_select(out=ohS[:], in_=ohS[:], compare_op=ALU.is_ge, fill=0.0, base=8,
                                    pattern=[[1, 128]], channel_multiplier=-8))
        P(lambda e: e.affine_select(out=ohS[:], in_=ohS[:], compare_op=ALU.is_ge, fill=0.0, base=-1,
                                    pattern=[[-1, 128]], channel_multiplier=8))

        hrst = sb("hrst", [128, NH, 128], F32); hrst_s = sb("hrst_s", [128, NH, 128], F32)
        P(lambda e: e.memset(hrst[:], 1.0))
        for hd_ in range(NH):
            P(lambda e, hd_=hd_: e.memset(hrst[:, hd_, 0:1], 0.0))
        P(lambda e: e.tensor_copy(out=hrst_s[:], in_=rstmask[:].unsqueeze(1).to_broadcast([128, NH, 128])))
        Bp = Buf("params")
        bglu_col = sb("bglu_col", [128, 8], F32); bgate_col = sb("bgate_col", [128, 4], F32)
        nbgate = sb("nbgate", [128, 4], F32); ggain_col = sb("ggain_col", [128, 8], F32)
        wup = sb("wup", [17, NH * DK], BF16); flag = sb("flag", [128, 1], F32)
        k.dma("sp", bglu_col[:], b_glu.rearrange("(m p) -> p m", p=128), writes=[Bp], allow_slow_non_contiguous=True)
        k.dma("sp", bgate_col[:], b_gate.rearrange("(m p) -> p m", p=128), writes=[Bp], allow_slow_non_contiguous=True)
        k.dma("sp", ggain_col[:], gla_gain.rearrange("(m p) -> p m", p=128), writes=[Bp], allow_slow_non_contiguous=True)
        k.dma("sp", flag[:], flag_d[:, :], writes=[Bp])
        Bwup = Buf("wup")
        k.dma("pool", wup[0:16, :], w_gate_up[:, :], writes=[Bwup])
        k.dma("pool", wup[16:17, :], b_gate.rearrange("(o n) -> o n", o=1), writes=[Bwup])
        k.op("dve", lambda e: e.tensor_scalar(out=nbgate[:], in0=bgate_col[:], scalar1=-1.0, scalar2=None, op0=ALU.mult),
             reads=[Bp], writes=[Bc])

        wslot = [sb("wslot%d" % i, [128, KT, WC], BF16) for i in range(NSLOT)]
        wbuf = [Buf("wslot%d" % i) for i in range(NSLOT)]
        specs = []

        def wspec(w, c0, ncol, kt):
            specs.append((w, c0, ncol, kt))

        for blk in range(24):
            wspec(w_ada, blk * WC, WC, KT)
        for full in (False, True):
            for i in range(4):
                wspec(w_in, i * WC, WC, KT)
            wspec(w_in, 5120, 16, KT)
            for i in range(2):
                wspec(w_in, 2560 + i * WC, WC, KT)
            for i in range(4):
                wspec(w_in, 3072 + i * WC, WC, KT)
            if full:
                for i in range(2):
                    wspec(w_in, 2048 + i * WC, WC, KT)
                for i in range(4):
                    wspec(w_glu, i * WC, WC, 8)
                for i in range(4):
                    wspec(w_in, 1024 + i * WC, WC, KT)
                for i in range(4):
                    wspec(w_in, 4096 + i * WC, WC, KT)
                for fb in range(8):
                    wspec(w_in, 5136 + fb * WC, WC, KT)
                    wspec(w_a_out, fb * WC, WC, 8)
                    wspec(w_in, 7184 + fb * WC, WC, KT)
                    wspec(w_b_out, fb * WC, WC, 8)
                for i in range(8):
                    wspec(w_out, i * WC, WC, KT)
        wstate = {"issued": 0, "used": 0}

        def w_issue():
            i = wstate["issued"]
            if i >= len(specs):
                return
            w, c0, ncol, kt = specs[i]
            s = i % NSLOT
            src = w.rearrange("(kt p) n -> p kt n", p=128)[:, :, c0:c0 + ncol]
            k.dma("pool", wslot[s][:, 0:kt, 0:ncol], src, writes=[wbuf[s]])
            wstate["issued"] = i + 1

        def w_next(expect=None):
            i = wstate["used"]
            if expect is not None:
                assert specs[i][0] is expect[0] and specs[i][1] == expect[1], (i, specs[i][1:], expect[1:])
            while wstate["issued"] < min(i + NSLOT - 1, len(specs)) or wstate["issued"] <= i:
                w_issue()
            wstate["used"] = i + 1
            s = i % NSLOT
            return wslot[s], wbuf[s]

        shiftT = sb("shiftT", [128, KT, 17], F32); scaleT = sb("scaleT", [128, KT, 17], F32)
        Bmod = Buf("modT")
        hT = sb("hT", [128, KT, TM], BF16)
        hTb = [Buf("hT%d" % j) for j in range(9)]
        Hrun = sb("Hrun", [128, 2, 32], F32); BH = Buf("Hrun")
        Sgl = sb("Sgl", [128, NH, DV], F32); Sglb = sb("Sglb", [128, NH, DV], BF16); BS = Buf("Sgl")
        epsb = sb("epsb", [128, 1], F32)
        k.op("pool", lambda e: e.memset(epsb[:], EPS), writes=[Bc])
        esR = ExitStack()
        k.esR = esR
        esW = ExitStack()
        k.es = esW
        Toep = sb("Toep", [128, NG, 128], BF16)
        Mw = sb("Mw", [128, NG, 2, 64], BF16)
        Ybr = sb("Ybr", [128, 32, 9, 16], BF16); Ybn = sb("Ybn", [128, 32, 9, 16], BF16)
        ARR = sb("ARR", [128, 32], F32); AIp = sb("AIp", [128, 32], F32); AIn = sb("AIn", [128, 32], F32)
        A64 = sb("A64", [128, 3, 32], F32)
        Bs5 = Buf("s5w")

        Bgs = Buf("gate_scr")

        def ada_emit():
            esA = ExitStack()
            k.esR = esA
            c17s = sb("c17s", [17, D], F32, side="right"); Bp2 = Buf("c17s")
            siluT = sb("siluT", [128, KT, 17], BF16, side="right"); Bsl = Buf("silu")
            brow = [sb("brow%d" % i, [1, WC], F32, side="right") for i in range(2)]; Bbr = [Buf("brow%d" % i) for i in range(2)]
            modr = [sb("modr%d" % i, [17, WC], F32, side="right") for i in range(2)]; Bmr = [Buf("modr%d" % i) for i in range(2)]
            k.dma("sp", c17s[:], c17[:, :], writes=[Bp2])
            k.op("act", lambda e: e.activation(out=c17s[:], in_=c17s[:], func=AF.Silu), reads=[Bp2], writes=[Bp2])
            pt, pb = bank()
            for kt in range(KT):
                k.op("pe", lambda e, kt=kt, pt=pt: e.transpose(out=pt[:, kt * 17:(kt + 1) * 17], in_=c17s[0:17, kt * 128:(kt + 1) * 128],
                                                               identity=identf[0:17, 0:17]), reads=[Bp2, Bc], writes=[pb])
            k.op("act", lambda e, pt=pt: e.activation(out=siluT[:].rearrange("p a b -> p (a b)"), in_=pt[:, 0:KT * 17], func=AF.Copy), reads=[pb], writes=[Bsl])
            for blk in range(24):
                a = blk % 2
                ws, wb = w_next((w_ada, blk * WC))
                k.dma("sp", brow[a][:], b_ada[blk * WC:(blk + 1) * WC].rearrange("(o n) -> o n", o=1), writes=[Bbr[a]])
                pt, pb = bank()
                for kt in range(KT):
                    k.op("pe", lambda e, kt=kt, ws=ws, pt=pt: e.matmul(pt[0:17, 0:WC], lhsT=siluT[:, kt, :], rhs=ws[:, kt, :],
                                                                      start=(kt == 0), stop=False), reads=[Bsl, wb], writes=[pb])
                k.op("pe", lambda e, a=a, pt=pt: e.matmul(pt[0:17, 0:WC], lhsT=onesf[0:1, 0:17], rhs=brow[a][0:1, :], start=False, stop=True),
                     reads=[Bbr[a], Bc], writes=[pb])
                which = blk // 8
                if which == 1:
                    k.op("act", lambda e, a=a, pt=pt: e.activation(out=modr[a][:], in_=pt[0:17, 0:WC], func=AF.Identity, bias=onesf[0:17, 0:1]), reads=[pb, Bc], writes=[Bmr[a]])
                else:
                    k.op("act", lambda e, a=a, pt=pt: e.activation(out=modr[a][:], in_=pt[0:17, 0:WC], func=AF.Copy), reads=[pb], writes=[Bmr[a]])
                if which == 2:
                    c0 = (blk - 16) * WC
                    k.dma("sp", gate_scr[:, c0:c0 + WC], modr[a][:], reads=[Bmr[a]], writes=[Bgs], sbuf=Bmr[a])
                else:
                    dst = shiftT if which == 0 else scaleT
                    ft0 = (blk % 8) * 2
                    pt2, pb2 = bank()
                    for hh in range(2):
                        k.op("pe", lambda e, a=a, hh=hh, pt2=pt2: e.transpose(out=pt2[:, hh * 17:(hh + 1) * 17], in_=modr[a][0:17, hh * 128:(hh + 1) * 128],
                                                                             identity=identf[0:17, 0:17]), reads=[Bmr[a], Bc], writes=[pb2])
                    k.op("act", lambda e, dst=dst, ft0=ft0, pt2=pt2: e.activation(out=dst[:, ft0:ft0 + 2, :].rearrange("p a b -> p (a b)"), in_=pt2[:, 0:34], func=AF.Copy),
                         reads=[pb2], writes=[Bmod])
            return esA, [Bp2, Bsl] + Bbr + Bmr

        with ExitStack() as es2:
            k.es = es2
            Bt = Buf("s5tmp")
            L2 = sb("L2", [32, 2, 128], F32)
            Cn = sb("Cn", [128, 2, 4, 128], F32)
            Bpk = sb("Bpk", [128, 2, 32, 16], F32)
            Cpk = sb("Cpk", [128, 2, 32, 16], F32)
            ldtb = sb("ldtb", [128, 32], F32)
            dpk = sb("dpk", [128, NG], F32)
            k.dma("sp", L2[:, 0, :].rearrange("g (a p) -> g a p", a=2), lam_re.rearrange("(a g) p -> g a p", a=2), writes=[Bt])
            k.dma("sp", L2[:, 1, :].rearrange("g (a p) -> g a p", a=2), lam_im.rearrange("(a g) p -> g a p", a=2), writes=[Bt])
            for ri, cc in enumerate((c_re, c_im)):
                for a_ in range(2):
                    k.dma("sp", Cn[:, ri, :, a_ * 64:(a_ + 1) * 64],
                          cc[a_ * 32:(a_ + 1) * 32, :, :].rearrange("(c gl) h p -> (gl h) c p", c=4), writes=[Bt])
            for ri, bb in enumerate((b_re, b_im)):
                for gh in range(2):
                    k.dma("sp", Bpk[gh * 64:(gh + 1) * 64, ri, :, :],
                          bb[gh * 32:(gh + 1) * 32, :, :].rearrange("g p h -> p g h"), writes=[Bt])
            for gh in range(2):
                k.dma("sp", ldtb[gh * 64:(gh + 1) * 64, :], log_dt[gh * 32:(gh + 1) * 32].partition_broadcast(64), writes=[Bt])
            for s in range(8):
                k.dma("sp", dpk[s * 16:(s + 1) * 16, :], d_skip.rearrange("(g h) -> h g", h=16), writes=[Bt],
                      allow_slow_non_contiguous=True)
            lrli = sb("lrli", [128, 2, 32], F32)
            pt, pb = bank()
            for ri in range(2):
                k.op("pe", lambda e, ri=ri: e.transpose(out=pt[:, ri * 32:(ri + 1) * 32], in_=L2[:, ri, :], identity=identf[0:32, 0:32]),
                     reads=[Bt, Bc], writes=[pb])
            k.op("dve", lambda e: e.tensor_copy(out=lrli[:].rearrange("p a g -> p (a g)"), in_=pt[:, 0:64]), reads=[pb], writes=[Bt])
            for ri in range(2):
                pt, pb = bank()
                for c4 in range(4):
                    k.op("pe", lambda e, ri=ri, c4=c4: e.transpose(out=pt[:, c4 * 128:(c4 + 1) * 128], in_=Cn[:, ri, c4, :], identity=identf[:]),
                         reads=[Bt, Bc], writes=[pb])
                k.op("dve", lambda e, ri=ri: e.tensor_copy(out=Cpk[:, ri, :, :].rearrange("p g h -> p (g h)"), in_=pt[:, :]),
                     reads=[pb], writes=[Bt])
            lr = lrli[:, 0, :]; li = lrli[:, 1, :]

            def V(fn):
                return k.op("dve", fn, reads=[Bt, Bc], writes=[Bt])

            def A(fn):
                return k.op("act", fn, reads=[Bt, Bc], writes=[Bt])

            sm = sb("s5sm", [128, 24, 32], F32)
            dt = sm[:, 0, :]; ldr = sm[:, 1, :]; th = sm[:, 2, :]; t0 = sm[:, 3, :]; t1 = sm[:, 4, :]
            kk1 = sm[:, 5, :]; thr = sm[:, 6, :]; thc = sm[:, 7, :]; s1 = sm[:, 8, :]; c1 = sm[:, 9, :]
            cfr = sm[:, 10, :]; cfi = sm[:, 11, :]; nr = sm[:, 12, :]; den = sm[:, 13, :]; t2 = sm[:, 14, :]
            A(lambda e: e.activation(out=dt, in_=ldtb[:], func=AF.Exp))
            V(lambda e: e.tensor_tensor(out=ldr, in0=lr, in1=dt, op=ALU.mult))
            V(lambda e: e.tensor_tensor(out=th, in0=li, in1=dt, op=ALU.mult))

            def range_reduce(dst, shift):
                V(lambda e: e.tensor_scalar(out=t0, in0=th, scalar1=float(shift), scalar2=None, op0=ALU.add))
                V(lambda e: e.memset(kk1, 0.0))
                for m in (1, 3, 5, 7):
                    V(lambda e, m=m: e.tensor_scalar(out=t1, in0=t0, scalar1=float(m * PI), scalar2=None, op0=ALU.is_gt))
                    V(lambda e: e.tensor_tensor(out=kk1, in0=kk1, in1=t1, op=ALU.add))
                V(lambda e: e.scalar_tensor_tensor(out=dst, in0=kk1, scalar=float(-2 * PI), in1=t0, op0=ALU.mult, op1=ALU.add))

            range_reduce(thr, 0.0)
            z = sm[:, 15, :]; z2 = sm[:, 16, :]; ps_ = sm[:, 17, :]; pc_ = sm[:, 18, :]
            V(lambda e: e.tensor_scalar(out=z, in0=thr, scalar1=0.125, scalar2=None, op0=ALU.mult))
            V(lambda e: e.tensor_tensor(out=z2, in0=z, in1=z, op=ALU.mult))
            V(lambda e: e.tensor_scalar(out=ps_, in0=z2, scalar1=-1.0 / 72.0, scalar2=1.0, op0=ALU.mult, op1=ALU.add))
            for dv_ in (42.0, 20.0, 6.0):
                V(lambda e: e.tensor_tensor(out=ps_, in0=ps_, in1=z2, op=ALU.mult))
                V(lambda e, dv_=dv_: e.tensor_scalar(out=ps_, in0=ps_, scalar1=-1.0 / dv_, scalar2=1.0, op0=ALU.mult, op1=ALU.add))
            V(lambda e: e.tensor_tensor(out=s1, in0=ps_, in1=z, op=ALU.mult))
            V(lambda e: e.tensor_scalar(out=pc_, in0=z2, scalar1=-1.0 / 56.0, scalar2=1.0, op0=ALU.mult, op1=ALU.add))
            for dv_ in (30.0, 12.0, 2.0):
                V(lambda e: e.tensor_tensor(out=pc_, in0=pc_, in1=z2, op=ALU.mult))
                V(lambda e, dv_=dv_: e.tensor_scalar(out=pc_, in0=pc_, scalar1=-1.0 / dv_, scalar2=1.0, op0=ALU.mult, op1=ALU.add))
            V(lambda e: e.tensor_copy(out=c1, in_=pc_))
            for _ in range(3):
                V(lambda e: e.tensor_tensor(out=t0, in0=s1, in1=c1, op=ALU.mult))
                V(lambda e: e.tensor_tensor(out=t1, in0=s1, in1=s1, op=ALU.mult))
                V(lambda e: e.tensor_scalar(out=s1, in0=t0, scalar1=2.0, scalar2=None, op0=ALU.mult))
                V(lambda e: e.tensor_scalar(out=c1, in0=t1, scalar1=-2.0, scalar2=1.0, op0=ALU.mult, op1=ALU.add))
            Ur = sb("Ur", [128, 9, 32], F32); Ui = sb("Ui", [128, 9, 32], F32)
            V(lambda e: e.memset(Ur[:, 0, :], 1.0)); V(lambda e: e.memset(Ui[:, 0, :], 0.0))
            V(lambda e: e.tensor_copy(out=Ur[:, 1, :], in_=c1)); V(lambda e: e.tensor_copy(out=Ui[:, 1, :], in_=s1))
            for t in range(1, 8):
                V(lambda e, t=t: e.tensor_tensor(out=t0, in0=Ur[:, t, :], in1=c1, op=ALU.mult))
                V(lambda e, t=t: e.tensor_tensor(out=t1, in0=Ui[:, t, :], in1=s1, op=ALU.mult))
                V(lambda e, t=t: e.tensor_tensor(out=Ur[:, t + 1, :], in0=t0, in1=t1, op=ALU.subtract))
                V(lambda e, t=t: e.tensor_tensor(out=t0, in0=Ur[:, t, :], in1=s1, op=ALU.mult))
                V(lambda e, t=t: e.tensor_tensor(out=t1, in0=Ui[:, t, :], in1=c1, op=ALU.mult))
                V(lambda e, t=t: e.tensor_tensor(out=Ui[:, t + 1, :], in0=t0, in1=t1, op=ALU.add))
            Epr = sb("Epr", [128, 9, 32], F32); Epi = sb("Epi", [128, 9, 32], F32)
            Enr = sb("Enr", [128, 8, 32], F32); Eni = sb("Eni", [128, 8, 32], F32)
            MG = sb("MG", [128, 9, 32], F32); MGn = sb("MGn", [128, 8, 32], F32)
            for t in range(9):
                A(lambda e, t=t: e.activation(out=MG[:, t, :], in_=ldr, func=AF.Exp, scale=float(t)))
            for t in range(8):
                A(lambda e, t=t: e.activation(out=MGn[:, t, :], in_=ldr, func=AF.Exp, scale=float(-t)))
            esA, ada_bufs = ada_emit()
            V(lambda e: e.tensor_tensor(out=Epr[:], in0=MG[:], in1=Ur[:], op=ALU.mult))
            V(lambda e: e.tensor_tensor(out=Epi[:], in0=MG[:], in1=Ui[:], op=ALU.mult))
            V(lambda e: e.tensor_tensor(out=Enr[:], in0=MGn[:], in1=Ur[:, 0:8, :], op=ALU.mult))
            V(lambda e: e.tensor_tensor(out=Eni[:], in0=MGn[:], in1=Ui[:, 0:8, :], op=ALU.mult))
            V(lambda e: e.tensor_scalar(out=Eni[:], in0=Eni[:], scalar1=-1.0, scalar2=None, op0=ALU.mult))
            V(lambda e: e.tensor_scalar(out=nr, in0=Epr[:, 1, :], scalar1=-1.0, scalar2=None, op0=ALU.add))
            ni = Epi[:, 1, :]
            V(lambda e: e.tensor_tensor(out=t0, in0=lr, in1=lr, op=ALU.mult))
            V(lambda e: e.tensor_tensor(out=t1, in0=li, in1=li, op=ALU.mult))
            V(lambda e: e.tensor_tensor(out=den, in0=t0, in1=t1, op=ALU.add))
            V(lambda e: e.reciprocal(out=den, in_=den))
            V(lambda e: e.tensor_tensor(out=t0, in0=nr, in1=lr, op=ALU.mult))
            V(lambda e: e.tensor_tensor(out=t1, in0=ni, in1=li, op=ALU.mult))
            V(lambda e: e.tensor_tensor(out=t0, in0=t0, in1=t1, op=ALU.add))
            V(lambda e: e.tensor_tensor(out=cfr, in0=t0, in1=den, op=ALU.mult))
            V(lambda e: e.tensor_tensor(out=t0, in0=ni, in1=lr, op=ALU.mult))
            V(lambda e: e.tensor_tensor(out=t1, in0=nr, in1=li, op=ALU.mult))
            V(lambda e: e.tensor_tensor(out=t0, in0=t0, in1=t1, op=ALU.subtract))
            V(lambda e: e.tensor_tensor(out=cfi, in0=t0, in1=den, op=ALU.mult))
            Bpr = sb("Bpr", [128, 32, 16], F32); Bpi = sb("Bpi", [128, 32, 16], F32)
            tb0 = sb("tb0", [128, 32, 16], F32); tb1 = sb("tb1", [128, 32, 16], F32)
            bc16 = lambda ap: ap.unsqueeze(2).to_broadcast([128, 32, 16])
            V(lambda e: e.tensor_tensor(out=tb0[:], in0=Bpk[:, 0, :, :], in1=bc16(cfr), op=ALU.mult))
            V(lambda e: e.tensor_tensor(out=tb1[:], in0=Bpk[:, 1, :, :], in1=bc16(cfi), op=ALU.mult))
            V(lambda e: e.tensor_tensor(out=Bpr[:], in0=tb0[:], in1=tb1[:], op=ALU.subtract))
            V(lambda e: e.tensor_tensor(out=tb0[:], in0=Bpk[:, 1, :, :], in1=bc16(cfr), op=ALU.mult))
            V(lambda e: e.tensor_tensor(out=tb1[:], in0=Bpk[:, 0, :, :], in1=bc16(cfi), op=ALU.mult))
            V(lambda e: e.tensor_tensor(out=Bpi[:], in0=tb0[:], in1=tb1[:], op=ALU.add))
            Gr = sb("Gr", [128, 32, 8, 16], BF16); Gi = sb("Gi", [128, 32, 8, 16], BF16)
            G7r = sb("G7r", [128, 32, 8, 16], BF16); G7i = sb("G7i", [128, 32, 8, 16], BF16)
            for s in range(8):
                for (er, ei, outr, outi) in ((Enr[:, s, :], Eni[:, s, :], Gr, Gi), (Epr[:, 7 - s, :], Epi[:, 7 - s, :], G7r, G7i)):
                    V(lambda e, er=er: e.tensor_tensor(out=tb0[:], in0=Bpr[:], in1=bc16(er), op=ALU.mult))
                    V(lambda e, ei=ei: e.tensor_tensor(out=tb1[:], in0=Bpi[:], in1=bc16(ei), op=ALU.mult))
                    V(lambda e, outr=outr, s=s: e.tensor_tensor(out=outr[:, :, s, :], in0=tb0[:], in1=tb1[:], op=ALU.subtract))
                    V(lambda e, ei=ei: e.tensor_tensor(out=tb0[:], in0=Bpr[:], in1=bc16(ei), op=ALU.mult))
                    V(lambda e, er=er: e.tensor_tensor(out=tb1[:], in0=Bpi[:], in1=bc16(er), op=ALU.mult))
                    V(lambda e, outi=outi, s=s: e.tensor_tensor(out=outi[:, :, s, :], in0=tb0[:], in1=tb1[:], op=ALU.add))
            for t in range(9):
                er, ei = Epr[:, t, :], Epi[:, t, :]
                V(lambda e, er=er: e.tensor_tensor(out=tb0[:], in0=Cpk[:, 0, :, :], in1=bc16(er), op=ALU.mult))
                V(lambda e, ei=ei: e.tensor_tensor(out=tb1[:], in0=Cpk[:, 1, :, :], in1=bc16(ei), op=ALU.mult))
                k.op("dve", lambda e, t=t: e.tensor_tensor(out=Ybr[:, :, t, :], in0=tb0[:], in1=tb1[:], op=ALU.subtract),
                     reads=[Bt], writes=[Bt, Bs5])
                V(lambda e, ei=ei: e.tensor_tensor(out=tb0[:], in0=Cpk[:, 0, :, :], in1=bc16(ei), op=ALU.mult))
                V(lambda e, er=er: e.tensor_tensor(out=tb1[:], in0=Cpk[:, 1, :, :], in1=bc16(er), op=ALU.mult))
                V(lambda e: e.tensor_tensor(out=tb0[:], in0=tb0[:], in1=tb1[:], op=ALU.add))
                k.op("dve", lambda e, t=t: e.tensor_scalar(out=Ybn[:, :, t, :], in0=tb0[:], scalar1=-1.0, scalar2=None, op0=ALU.mult),
                     reads=[Bt], writes=[Bt, Bs5])
            k.op("dve", lambda e: e.tensor_copy(out=ARR[:], in_=Epr[:, 8, :]), reads=[Bt], writes=[Bs5])
            k.op("dve", lambda e: e.tensor_copy(out=AIp[:], in_=Epi[:, 8, :]), reads=[Bt], writes=[Bs5])
            k.op("dve", lambda e: e.tensor_scalar(out=AIn[:], in0=Epi[:, 8, :], scalar1=-1.0, scalar2=None, op0=ALU.mult),
                 reads=[Bt], writes=[Bs5])
            sqr = sm[:, 19, :]; sqi = sm[:, 20, :]
            V(lambda e: e.tensor_copy(out=sqr, in_=Epr[:, 8, :])); V(lambda e: e.tensor_copy(out=sqi, in_=Epi[:, 8, :]))
            for _ in range(6):
                V(lambda e: e.tensor_tensor(out=t0, in0=sqr, in1=sqr, op=ALU.mult))
                V(lambda e: e.tensor_tensor(out=t1, in0=sqi, in1=sqi, op=ALU.mult))
                V(lambda e: e.tensor_tensor(out=t2, in0=sqr, in1=sqi, op=ALU.mult))
                V(lambda e: e.tensor_tensor(out=sqr, in0=t0, in1=t1, op=ALU.subtract))
                V(lambda e: e.tensor_scalar(out=sqi, in0=t2, scalar1=2.0, scalar2=None, op0=ALU.mult))
            k.op("dve", lambda e: e.tensor_copy(out=A64[:, 0, :], in_=sqr), reads=[Bt], writes=[Bs5])
            k.op("dve", lambda e: e.tensor_copy(out=A64[:, 1, :], in_=sqi), reads=[Bt], writes=[Bs5])
            k.op("dve", lambda e: e.tensor_scalar(out=A64[:, 2, :], in0=sqi, scalar1=-1.0, scalar2=None, op0=ALU.mult), reads=[Bt], writes=[Bs5])
            for g0 in range(0, NG, 4):
                pt, pb = bank()
                for gi in range(4):
                    g = g0 + gi
                    gh, gp = g // 32, g % 32
                    rows = slice(gh * 64, gh * 64 + 64)
                    k.op("pe", lambda e, gi=gi, rows=rows, gp=gp: e.matmul(
                        pt[:, gi * 128:(gi + 1) * 128], lhsT=Gr[rows, gp, :, :], rhs=Ybr[rows, gp, 0:8, :], start=True, stop=False),
                        reads=[Bt, Bs5], writes=[pb])
                    k.op("pe", lambda e, gi=gi, rows=rows, gp=gp: e.matmul(
                        pt[:, gi * 128:(gi + 1) * 128], lhsT=Gi[rows, gp, :, :], rhs=Ybn[rows, gp, 0:8, :], start=False, stop=True),
                        reads=[Bt, Bs5], writes=[pb])
                k.op("dve", lambda e, g0=g0: e.tensor_tensor(
                    out=Toep[:, g0:g0 + 4, :].rearrange("p g (t h) -> p g t h", h=16),
                    in0=pt[:, :].rearrange("p (g t h) -> p g t h", g=4, h=16),
                    in1=mask3[:].unsqueeze(1).unsqueeze(3).to_broadcast([128, 4, 8, 16]), op=ALU.mult),
                    reads=[pb, Bc], writes=[Bs5])
            for g in range(NG):
                k.op("dve", lambda e, g=g: e.scalar_tensor_tensor(out=Toep[:, g, :], in0=identf[:], scalar=dpk[:, g:g + 1],
                                                                  in1=Toep[:, g, :], op0=ALU.mult, op1=ALU.add),
                     reads=[Bt, Bc, Bs5], writes=[Bs5])
            for g0 in range(0, NG, 8):
                pt, pb = bank()
                ptb = pt[:].bitcast(BF16)
                for gi in range(8):
                    g = g0 + gi
                    gh, gp = g // 32, g % 32
                    rows = slice(gh * 64, gh * 64 + 64)
                    for ri, GG in enumerate((G7r, G7i)):
                        col = (gi * 2 + ri) * 64
                        k.op("pe", lambda e, rows=rows, gp=gp, GG=GG, col=col: e.transpose(
                            out=ptb[:, col:col + 64], in_=GG[rows, gp, :, :], identity=identb[rows, rows]),
                            reads=[Bt, Bc], writes=[pb])
                k.op("dve", lambda e, g0=g0: e.tensor_copy(out=Mw[:, g0:g0 + 8, :, :].rearrange("p g r q -> p (g r q)"), in_=ptb[:, 0:1024]),
                     reads=[pb], writes=[Bs5])
            for e_ in ("pe", "dve", "act", "pool", "sp"):
                k.wait_all(e_, [Bt] + ada_bufs + psb)
            k.es = esW
        esA.close()
        k.esR = esR

        dump("Toep", Toep[:], [128, NG, 128], [Bs5], BF16)
        dump("Mw", Mw[:], [128, NG, 2, 64], [Bs5], BF16)
        dump("Ybr", Ybr[:], [128, 32, 9, 16], [Bs5], BF16)
        dump("Ybn", Ybn[:], [128, 32, 9, 16], [Bs5], BF16)
        dump("ARR", ARR[:], [128, 32], [Bs5])
        dump("AIp", AIp[:], [128, 32], [Bs5])
        stage_end(1)
        dump("shiftT", shiftT[:], [128, KT, 17], [Bmod])
        dump("scaleT", scaleT[:], [128, KT, 17], [Bmod])
        stage_end(2)
        ENG = ("pe", "dve", "act", "pool", "sp")

        def barrier(bufs):
            for e_ in ENG:
                k.wait_all(e_, bufs)

        def prep(full):
            ntile = 9 if full else 8
            xsrc = [(xmain if full else xpre)[j * 128:(j + 1) * 128, :] for j in range(8)]
            if full:
                xsrc.append(xsmp[:, :])
            with ExitStack() as es4:
                k.es = es4
                gainbc = sb("gainbc", [128, D], F32); Bg = Buf("gain")
                k.dma("sp", gainbc[:], norm_gain.partition_broadcast(128), writes=[Bg])
                xs = [sb("xs%d" % i, [128, D], F32) for i in range(2)]; xb_ = [Buf("xs%d" % i) for i in range(2)]
                xn = [sb("xn%d" % i, [128, D], BF16) for i in range(2)]; xnb = [Buf("xn%d" % i) for i in range(2)]
                st = sb("prst", [128, 9, 4], F32); Bst = Buf("prst")
                tmpm = [sb("tmpm%d" % i, [128, 8, 128], F32) for i in range(2)]; tmb = [Buf("tmpm%d" % i) for i in range(2)]
                for j in range(ntile):
                    a = j % 2
                    k.dma("sp", xs[a][:], xsrc[j], writes=[xb_[a]])
                    k.op("act", lambda e, a=a, j=j: e.activation(out=xn[a][:], in_=xs[a][:], func=AF.Square, accum_out=st[:, j, 0:1]),
                         reads=[xb_[a]], writes=[xnb[a], Bst])
                    k.op("act", lambda e, j=j: e.activation(out=st[:, j, 2:3], in_=st[:, j, 0:1], func=AF.Ln, scale=1.0 / D, bias=epsb[:, 0:1]), reads=[Bst, Bc], writes=[Bst])
                    k.op("act", lambda e, j=j: e.activation(out=st[:, j, 3:4], in_=st[:, j, 2:3], func=AF.Exp, scale=-0.5), reads=[Bst], writes=[Bst])
                    k.op("dve", lambda e, a=a, j=j: e.scalar_tensor_tensor(out=xn[a][:], in0=xs[a][:], scalar=st[:, j, 3:4], in1=gainbc[:],
                                                                           op0=ALU.mult, op1=ALU.mult),
                         reads=[xb_[a], Bst, Bg], writes=[xnb[a]])
                    for half in range(2):
                        pt, pb = bank()
                        ptb = pt[:].bitcast(BF16)
                        for q in range(8):
                            kt = half * 8 + q
                            k.op("pe", lambda e, a=a, kt=kt, q=q, ptb=ptb: e.transpose(
                                out=ptb[:, q * 128:(q + 1) * 128], in_=xn[a][:, kt * 128:(kt + 1) * 128], identity=identb[:]),
                                reads=[xnb[a], Bc], writes=[pb])
                        m = half
                        src = ptb[:, 0:1024].rearrange("p (q t) -> p q t", q=8)
                        if j < 8:
                            sc = scaleT[:, half * 8:half * 8 + 8, 0:1].to_broadcast([128, 8, 128])
                            sh = shiftT[:, half * 8:half * 8 + 8, 0:1].to_broadcast([128, 8, 128])
                            o1 = tmpm[m][:]
                            o2 = hT[:, half * 8:half * 8 + 8, j * 128:(j + 1) * 128]
                        else:
                            src = src.rearrange("p q (j r) -> p q j r", r=8)
                            sc = scaleT[:, half * 8:half * 8 + 8, 1:17].unsqueeze(3).to_broadcast([128, 8, NSEQ, 8])
                            sh = shiftT[:, half * 8:half * 8 + 8, 1:17].unsqueeze(3).to_broadcast([128, 8, NSEQ, 8])
                            o1 = tmpm[m][:].rearrange("p q (j r) -> p q j r", r=8)
                            o2 = hT[:, half * 8:half * 8 + 8, j * 128:(j + 1) * 128].rearrange("p q (j r) -> p q j r", r=8)
                        k.op("dve", lambda e, o1=o1, src=src, sc=sc: e.tensor_tensor(out=o1, in0=src, in1=sc, op=ALU.mult),
                             reads=[pb, Bmod], writes=[tmb[m]])
                        k.op("dve", lambda e, o1=o1, o2=o2, sh=sh: e.tensor_tensor(out=o2, in0=o1, in1=sh, op=ALU.add),
                             reads=[tmb[m], Bmod], writes=[hTb[j]])
                barrier([Bg] + xb_ + xnb + [Bst] + tmb)
            return ntile

        def s5_front(full, U2, BU2, VH, BV):
            ntile = 9 if full else 8
            hall = hTb[0:ntile]
            ctiles = [(0, 128, 0)] + ([(TP, NSEQ, 128)] if full else [])
            with ExitStack() as es4:
                k.es = es4
                Uc = sb("Uc", [128, 16, 8, 16], BF16); BUc = Buf("Uc")
                for cb in range(4):
                    ws, wb = w_next((w_in, cb * WC))
                    for (toff, C, coff) in ctiles:
                        for sp2 in range(4):
                            pt, pb = bank()
                            for si in range(2):
                                s = sp2 * 2 + si
                                for kt in range(KT):
                                    k.op("pe", lambda e, kt=kt, s=s, si=si, toff=toff, C=C, ws=ws, pt=pt: e.matmul(
                                        pt[0:C, si * WC:(si + 1) * WC], lhsT=hT[:, kt, toff + s:toff + 8 * C:8], rhs=ws[:, kt, :],
                                        start=(kt == 0), stop=(kt == KT - 1)), reads=hall + [wb], writes=[pb])
                            k.op("act", lambda e, sp2=sp2, C=C, pt=pt: e.activation(
                                out=Uc[0:C, :, sp2 * 2:sp2 * 2 + 2, :].rearrange("c g s h -> c s g h"),
                                in_=pt[0:C, :].rearrange("c (s g h) -> c s g h", s=2, h=16), func=AF.Copy),
                                reads=[pb], writes=[BUc])
                        for gq in range(2):
                            pt, pb = bank()
                            ptb = pt[:].bitcast(BF16)
                            for gi in range(8):
                                gl = gq * 8 + gi
                                k.op("pe", lambda e, gl=gl, gi=gi, C=C, ptb=ptb: e.transpose(
                                    out=ptb[:, gi * 128:gi * 128 + C], in_=Uc[0:C, gl, :, :], identity=identb[0:C, 0:C]),
                                    reads=[BUc, Bc], writes=[pb])
                            g0 = cb * 16 + gq * 8
                            k.op("act", lambda e, g0=g0, C=C, coff=coff, ptb=ptb: e.activation(
                                out=U2[:, g0:g0 + 8, coff:coff + C], in_=ptb[:, 0:1024].rearrange("p (g c) -> p g c", g=8)[:, :, 0:C],
                                func=AF.Copy), reads=[pb], writes=[BU2[g0 // 8]])
                barrier([BUc])
            CT = 128 + (NSEQ if full else 0)
            for gp in range(32):
                pt, pb = bank()
                for gh in range(2):
                    g = gh * 32 + gp
                    for ri in range(2):
                        k.op("pe", lambda e, g=g, gh=gh, ri=ri, pt=pt: e.matmul(
                            pt[gh * 64:(gh + 1) * 64, ri * 256:ri * 256 + CT], lhsT=Mw[:, g, ri, :], rhs=U2[:, g, 0:CT],
                            start=True, stop=True), reads=[Bs5, BU2[g // 8]], writes=[pb])
                k.op("act", lambda e, gp=gp, pt=pt: e.activation(
                    out=VH[:, 1:1 + CT, :, gp].rearrange("p c r -> p r c"),
                    in_=pt[:, :].rearrange("p (r c) -> p r c", r=2)[:, :, 0:CT], func=AF.Copy), reads=[pb], writes=[BV])

        def s5_recur(VH, BV, BHh, hist):
            with ExitStack() as es4:
                k.es = es4
                ta = sb("rec_ta", [128, 2, 32], F32); tb = sb("rec_tb", [128, 2, 32], F32); Br = Buf("rec")
                arb = ARR[:].unsqueeze(1).to_broadcast([128, 2, 32])
                for c in range(128):
                    k.op("dve", lambda e: e.tensor_tensor(out=ta[:], in0=Hrun[:], in1=arb, op=ALU.mult), reads=[BH, Bs5], writes=[Br])
                    k.op("dve", lambda e: e.tensor_tensor(out=tb[:, 0, :], in0=Hrun[:, 1, :], in1=AIn[:], op=ALU.mult), reads=[BH, Bs5], writes=[Br])
                    k.op("dve", lambda e: e.tensor_tensor(out=tb[:, 1, :], in0=Hrun[:, 0, :], in1=AIp[:], op=ALU.mult), reads=[BH, Bs5], writes=[Br])
                    k.op("dve", lambda e: e.tensor_tensor(out=ta[:], in0=ta[:], in1=tb[:], op=ALU.add), reads=[Br], writes=[Br])
                    k.op("dve", lambda e, c=c: e.tensor_tensor(out=Hrun[:], in0=ta[:], in1=VH[:, 1 + c, :, :], op=ALU.add), reads=[Br, BV], writes=[BH])
                    if hist and c < 127:
                        k.op("act", lambda e, c=c: e.activation(out=VH[:, 1 + c, :, :], in_=Hrun[:], func=AF.Copy), reads=[BH], writes=[BHh])
                barrier([Br])

        def s5_recur2_state(VH, BV):
            with ExitStack() as es4:
                k.es = es4
                H2 = sb("rec_H2", [128, 2, 2, 32], F32); ta = sb("rec_ta2", [128, 2, 2, 32], F32); tb = sb("rec_tb2", [128, 2, 2, 32], F32)
                Br = Buf("rec2")
                arb = ARR[:].unsqueeze(1).unsqueeze(1).to_broadcast([128, 2, 2, 32])
                ainb = AIn[:].unsqueeze(1).to_broadcast([128, 2, 32]); aipb = AIp[:].unsqueeze(1).to_broadcast([128, 2, 32])
                k.op("dve", lambda e: e.memset(H2[:], 0.0), writes=[Br])
                k.op("dve", lambda e: e.tensor_copy(out=H2[:, 0, :, :], in_=Hrun[:]), reads=[BH], writes=[Br])
                for c in range(64):
                    k.op("dve", lambda e: e.tensor_tensor(out=ta[:], in0=H2[:], in1=arb, op=ALU.mult), reads=[Br, Bs5], writes=[Br])
                    k.op("dve", lambda e: e.tensor_tensor(out=tb[:, :, 0, :], in0=H2[:, :, 1, :], in1=ainb, op=ALU.mult), reads=[Br, Bs5], writes=[Br])
                    k.op("dve", lambda e: e.tensor_tensor(out=tb[:, :, 1, :], in0=H2[:, :, 0, :], in1=aipb, op=ALU.mult), reads=[Br, Bs5], writes=[Br])
                    k.op("dve", lambda e: e.tensor_tensor(out=ta[:], in0=ta[:], in1=tb[:], op=ALU.add), reads=[Br], writes=[Br])
                    k.op("dve", lambda e, c=c: e.tensor_tensor(out=H2[:], in0=ta[:], in1=VH[:, 1 + c:1 + c + 65:64, :, :], op=ALU.add), reads=[Br, BV], writes=[Br])
                k.op("dve", lambda e: e.tensor_tensor(out=ta[:, 0, :, :], in0=H2[:, 0, :, :], in1=A64[:, 0, :].unsqueeze(1).to_broadcast([128, 2, 32]), op=ALU.mult),
                     reads=[Br, Bs5], writes=[Br])
                k.op("dve", lambda e: e.tensor_tensor(out=tb[:, 0, 0, :], in0=H2[:, 0, 1, :], in1=A64[:, 2, :], op=ALU.mult), reads=[Br, Bs5], writes=[Br])
                k.op("dve", lambda e: e.tensor_tensor(out=tb[:, 0, 1, :], in0=H2[:, 0, 0, :], in1=A64[:, 1, :], op=ALU.mult), reads=[Br, Bs5], writes=[Br])
                k.op("dve", lambda e: e.tensor_tensor(out=ta[:, 0, :, :], in0=ta[:, 0, :, :], in1=tb[:, 0, :, :], op=ALU.add), reads=[Br], writes=[Br])
                k.op("dve", lambda e: e.tensor_tensor(out=Hrun[:], in0=ta[:, 0, :, :], in1=H2[:, 1, :, :], op=ALU.add), reads=[Br], writes=[BH])
                barrier([Br])

        def s5_recur2_hist(VH, BV, BHh):
            SB_ = 16; NQ = 4; L = 128 // NQ
            with ExitStack() as es4:
                k.es = es4
                H2 = sb("rec_H2", [128, NQ, 2, 32], F32); ta = sb("rec_ta2", [128, NQ, 2, 32], F32); tb = sb("rec_tb2", [128, NQ, 2, 32], F32)
                PW = sb("rec_PW", [128, L, 2, 32], BF16); c1 = sb("rec_c1", [128, SB_, 32], F32); c2 = sb("rec_c2", [128, SB_, 32], F32)
                pw = sb("rec_pw", [128, 8, 32], F32)
                He = [sb("rec_He%d" % i, [128, 2, 32], F32) for i in range(2)]
                Br = Buf("rec2"); Bpw = Buf("rec_pw")
                W = lambda fn: k.op("dve", fn, reads=[Bpw, Bs5], writes=[Bpw])
                pr, pi_, q0_, q1_, q2_, aLr, aLi, aLn = (pw[:, i, :] for i in range(8))
                W(lambda e: e.tensor_copy(out=pr, in_=ARR[:])); W(lambda e: e.tensor_copy(out=pi_, in_=AIp[:]))
                W(lambda e: e.tensor_copy(out=PW[:, 0, 0, :], in_=ARR[:])); W(lambda e: e.tensor_copy(out=PW[:, 0, 1, :], in_=AIp[:]))

                def square():
                    W(lambda e: e.tensor_tensor(out=q0_, in0=pr, in1=pr, op=ALU.mult))
                    W(lambda e: e.tensor_tensor(out=q1_, in0=pi_, in1=pi_, op=ALU.mult))
                    W(lambda e: e.tensor_tensor(out=q2_, in0=pr, in1=pi_, op=ALU.mult))
                    W(lambda e: e.tensor_tensor(out=pr, in0=q0_, in1=q1_, op=ALU.subtract))
                    W(lambda e: e.tensor_scalar(out=pi_, in0=q2_, scalar1=2.0, scalar2=None, op0=ALU.mult))

                n = 1
                while n < L:
                    for b0 in range(0, n, SB_):
                        nb = min(SB_, n - b0)
                        src_r = PW[:, b0:b0 + nb, 0, :]; src_i = PW[:, b0:b0 + nb, 1, :]
                        prb = pr.unsqueeze(1).to_broadcast([128, nb, 32]); pib = pi_.unsqueeze(1).to_broadcast([128, nb, 32])
                        W(lambda e, src_r=src_r, prb=prb, nb=nb: e.tensor_tensor(out=c1[:, 0:nb, :], in0=src_r, in1=prb, op=ALU.mult))
                        W(lambda e, src_i=src_i, pib=pib, nb=nb: e.tensor_tensor(out=c2[:, 0:nb, :], in0=src_i, in1=pib, op=ALU.mult))
                        W(lambda e, nb=nb, b0=b0, n=n: e.tensor_tensor(out=PW[:, n + b0:n + b0 + nb, 0, :], in0=c1[:, 0:nb, :], in1=c2[:, 0:nb, :], op=ALU.subtract))
                        W(lambda e, src_r=src_r, pib=pib, nb=nb: e.tensor_tensor(out=c1[:, 0:nb, :], in0=src_r, in1=pib, op=ALU.mult))
                        W(lambda e, src_i=src_i, prb=prb, nb=nb: e.tensor_tensor(out=c2[:, 0:nb, :], in0=src_i, in1=prb, op=ALU.mult))
                        W(lambda e, nb=nb, b0=b0, n=n: e.tensor_tensor(out=PW[:, n + b0:n + b0 + nb, 1, :], in0=c1[:, 0:nb, :], in1=c2[:, 0:nb, :], op=ALU.add))
                    n *= 2
                    square()
                W(lambda e: e.tensor_copy(out=aLr, in_=pr)); W(lambda e: e.tensor_copy(out=aLi, in_=pi_))
                W(lambda e: e.tensor_scalar(out=aLn, in0=pi_, scalar1=-1.0, scalar2=None, op0=ALU.mult))
                arb = ARR[:].unsqueeze(1).unsqueeze(1).to_broadcast([128, NQ, 2, 32])
                ainb = AIn[:].unsqueeze(1).to_broadcast([128, NQ, 32]); aipb = AIp[:].unsqueeze(1).to_broadcast([128, NQ, 32])
                k.op("dve", lambda e: e.memset(H2[:], 0.0), writes=[Br])
                k.op("dve", lambda e: e.tensor_copy(out=H2[:, 0, :, :], in_=Hrun[:]), reads=[BH], writes=[Br])
                Bh2 = Buf("H2")
                hi_sl = lambda c: slice(1 + c, 1 + c + L * (NQ - 1) + 1, L)
                for c in range(L):
                    k.op("dve", lambda e: e.tensor_tensor(out=ta[:], in0=H2[:], in1=arb, op=ALU.mult), reads=[Bh2, Bs5], writes=[Br])
                    k.op("dve", lambda e: e.tensor_tensor(out=tb[:, :, 0, :], in0=H2[:, :, 1, :], in1=ainb, op=ALU.mult), reads=[Bh2, Bs5], writes=[Br])
                    k.op("dve", lambda e: e.tensor_tensor(out=tb[:, :, 1, :], in0=H2[:, :, 0, :], in1=aipb, op=ALU.mult), reads=[Bh2, Bs5], writes=[Br])
                    k.op("dve", lambda e: e.tensor_tensor(out=ta[:], in0=ta[:], in1=tb[:], op=ALU.add), reads=[Br], writes=[Br])
                    k.op("dve", lambda e, c=c: e.tensor_tensor(out=H2[:], in0=ta[:], in1=VH[:, hi_sl(c), :, :], op=ALU.add), reads=[Br, BV], writes=[Bh2])
                    k.op("act", lambda e, c=c: e.activation(out=VH[:, hi_sl(c), :, :], in_=H2[:], func=AF.Copy), reads=[Bh2], writes=[BHh])
                X = lambda fn: k.op("dve", fn, reads=[Bh2, Bpw, BHh, Br], writes=[Br, BHh])
                hprev = H2[:, 0, :, :]
                for q in range(1, NQ):
                    hr_b = hprev[:, 0, :].unsqueeze(1).to_broadcast([128, SB_, 32])
                    hi_b = hprev[:, 1, :].unsqueeze(1).to_broadcast([128, SB_, 32])
                    for b0 in range(0, L, SB_):
                        pwr = PW[:, b0:b0 + SB_, 0, :]; pwi = PW[:, b0:b0 + SB_, 1, :]
                        s0_ = 1 + q * L + b0
                        tgt_r = VH[:, s0_:s0_ + SB_, 0, :]; tgt_i = VH[:, s0_:s0_ + SB_, 1, :]
                        X(lambda e, pwr=pwr, hr_b=hr_b: e.tensor_tensor(out=c1[:], in0=pwr, in1=hr_b, op=ALU.mult))
                        X(lambda e, pwi=pwi, hi_b=hi_b: e.tensor_tensor(out=c2[:], in0=pwi, in1=hi_b, op=ALU.mult))
                        X(lambda e: e.tensor_tensor(out=c1[:], in0=c1[:], in1=c2[:], op=ALU.subtract))
                        X(lambda e, tgt_r=tgt_r: e.tensor_tensor(out=tgt_r, in0=tgt_r, in1=c1[:], op=ALU.add))
                        X(lambda e, pwr=pwr, hi_b=hi_b: e.tensor_tensor(out=c1[:], in0=pwr, in1=hi_b, op=ALU.mult))
                        X(lambda e, pwi=pwi, hr_b=hr_b: e.tensor_tensor(out=c2[:], in0=pwi, in1=hr_b, op=ALU.mult))
                        X(lambda e: e.tensor_tensor(out=c1[:], in0=c1[:], in1=c2[:], op=ALU.add))
                        X(lambda e, tgt_i=tgt_i: e.tensor_tensor(out=tgt_i, in0=tgt_i, in1=c1[:], op=ALU.add))
                    hn = He[q % 2]
                    X(lambda e, hprev=hprev: e.tensor_tensor(out=ta[:, 0, :, :], in0=hprev, in1=aLr.unsqueeze(1).to_broadcast([128, 2, 32]), op=ALU.mult))
                    X(lambda e, hprev=hprev: e.tensor_tensor(out=tb[:, 0, 0, :], in0=hprev[:, 1, :], in1=aLn, op=ALU.mult))
                    X(lambda e, hprev=hprev: e.tensor_tensor(out=tb[:, 0, 1, :], in0=hprev[:, 0, :], in1=aLi, op=ALU.mult))
                    X(lambda e: e.tensor_tensor(out=ta[:, 0, :, :], in0=ta[:, 0, :, :], in1=tb[:, 0, :, :], op=ALU.add))
                    X(lambda e, hn=hn, q=q: e.tensor_tensor(out=hn[:], in0=ta[:, 0, :, :], in1=H2[:, q, :, :], op=ALU.add))
                    hprev = hn[:]
                k.op("dve", lambda e, hprev=hprev: e.tensor_copy(out=Hrun[:], in_=hprev), reads=[Br, Bh2], writes=[BH])
                barrier([Br, Bpw, Bh2, BHh])

        def proj_fm(wlist, T, hall, dstfn, func=AF.Copy):
            nblocks = [(n0, min(512, T - n0)) for n0 in range(0, T, 512)]
            for i, spec in enumerate(wlist):
                ws, wb = w_next(spec)
                for mt in range(2):
                    for (n0, nn) in nblocks:
                        pt, pb = bank()
                        for kt in range(KT):
                            k.op("pe", lambda e, kt=kt, n0=n0, nn=nn, mt=mt, ws=ws, pt=pt: e.matmul(
                                pt[:, 0:nn], lhsT=ws[:, kt, mt * 128:(mt + 1) * 128], rhs=hT[:, kt, n0:n0 + nn],
                                start=(kt == 0), stop=(kt == KT - 1)), reads=hall + [wb], writes=[pb], inc=(kt == KT - 1))
                        oap, ob = dstfn(i * 2 + mt, n0, nn)
                        k.op("act", lambda e, oap=oap, nn=nn, pt=pt: e.activation(out=oap, in_=pt[:, 0:nn], func=func), reads=[pb], writes=[ob])

        def gla_alloc(full):
            glr = sb("glr", [17, TM], BF16); Bglr = Buf("glr")
            kT = sb("kT", [128, NH, TM], BF16); BkT = Buf("kT")
            vtok = sb("vtok", [128, 9, NH * DV], BF16); Bvt = [Buf("vtok%d" % j) for j in range(9)]
            qT = sb("qT", [128, NH, TM], BF16) if full else None
            BqT = Buf("qT")
            return dict(glr=glr, Bglr=Bglr, kT=kT, BkT=BkT, vtok=vtok, Bvt=Bvt, qT=qT, BqT=BqT)

        def gla_proj(gl, full):
            glr, Bglr, kT, BkT, vtok, Bvt, qT, BqT = (gl[x] for x in ("glr", "Bglr", "kT", "BkT", "vtok", "Bvt", "qT", "BqT"))
            ntile = 9 if full else 8
            T = ntile * 128
            hall = hTb[0:ntile]
            nblocks = [(n0, min(512, T - n0)) for n0 in range(0, T, 512)]
            if True:
                k.op("dve", lambda e: e.memset(glr[:], 1.0), writes=[Bglr])
                ws, wb = w_next((w_in, 5120))
                for (n0, nn) in nblocks:
                    pt, pb = bank()
                    for kt in range(KT):
                        k.op("pe", lambda e, kt=kt, n0=n0, nn=nn, ws=ws, pt=pt: e.matmul(pt[0:16, 0:nn], lhsT=ws[:, kt, 0:16], rhs=hT[:, kt, n0:n0 + nn],
                                                                                       start=(kt == 0), stop=(kt == KT - 1)), reads=hall + [wb], writes=[pb])
                    k.op("act", lambda e, n0=n0, nn=nn, pt=pt: e.activation(out=glr[0:16, n0:n0 + nn], in_=pt[0:16, 0:nn], func=AF.Copy), reads=[pb], writes=[Bglr])
                proj_fm([(w_in, 2560), (w_in, 2560 + WC)], T, hall, lambda m, n0, nn: (kT[:, m, n0:n0 + nn], BkT))
                for i in range(4):
                    ws, wb = w_next((w_in, 3072 + i * WC))
                    for j in range(ntile):
                        pt, pb = bank()
                        for kt in range(KT):
                            k.op("pe", lambda e, kt=kt, j=j, ws=ws, pt=pt: e.matmul(pt[:, 0:WC], lhsT=hT[:, kt, j * 128:(j + 1) * 128], rhs=ws[:, kt, :],
                                                                                   start=(kt == 0), stop=(kt == KT - 1)), reads=[hTb[j], wb], writes=[pb])
                        k.op("act", lambda e, j=j, i=i, pt=pt: e.activation(out=vtok[:, j, i * WC:(i + 1) * WC], in_=pt[:, 0:WC], func=AF.Copy),
                             reads=[pb], writes=[Bvt[j]])
                if full:
                    proj_fm([(w_in, 2048), (w_in, 2048 + WC)], T, hall, lambda m, n0, nn: (qT[:, m, n0:n0 + nn], BqT))

        def gla_tiles(gl, full, by, Bby):
            glr, Bglr, kT, BkT, vtok, Bvt, qT, BqT = (gl[x] for x in ("glr", "Bglr", "kT", "BkT", "vtok", "Bvt", "qT", "BqT"))
            ntile = 9 if full else 8
            with ExitStack() as es4:
                k.es = es4
                G = {}
                gl_ = [("e1", [128, NH, 128], F32), ("cs", [128, NH, 128], F32), ("eb", [128, NH, 128], F32), ("einv", [128, NH, 128], F32),
                       ("kt", [128, NH, 128], BF16), ("ktok", [128, NH, 128], BF16), ("stmp", [128, 2, DV], F32)]
                if full:
                    gl_ += [("qt", [128, NH, 128], BF16), ("scm", [128, NH, 128], BF16), ("sq", [128, 2, 4, 128], BF16),
                            ("rr", [128, NH, 128], F32), ("qtf", [128, NH, 128], F32), ("KM", [128, NSEQ, 128], BF16), ("ebe", [128, NH, NSEQ], F32)]
                for nm, shp, dt_ in gl_:
                    G[nm] = sb("g_" + nm, shp, dt_)
                GB = {nm: Buf("g_" + nm) for nm in G}
                if full:
                    S0h = [sb("S0h%d" % i, [128, 8, DV], F32) for i in range(2)]; BS0h = [Buf("S0h%d" % i) for i in range(2)]
                    Snew = [sb("Snew%d" % i, [128, 2, DV], F32) for i in range(2)]; BSn2 = [Buf("Snew%d" % i) for i in range(2)]
                rot = [0]; rot2 = [0]
                fl = lambda ap: ap.rearrange("p a b -> p (a b)")
                for j in range(ntile):
                    sample = (j == 8)
                    toks = slice(j * 128, (j + 1) * 128)
                    pg, pgb = bank()
                    for hd in range(NH):
                        k.op("pe", lambda e, hd=hd, pg=pg: e.matmul(pg[:, hd * 128:(hd + 1) * 128], lhsT=wup[0:17, hd * 128:(hd + 1) * 128], rhs=glr[0:17, toks],
                                                                    start=True, stop=True), reads=[Bwup, Bglr], writes=[pgb])
                    k.op("act", lambda e, pg=pg: e.activation(out=fl(G["e1"][:]), in_=pg[:, :], func=AF.Exp, scale=-1.0), reads=[pgb], writes=[GB["e1"]])
                    k.op("act", lambda e: e.activation(out=fl(G["e1"][:]), in_=fl(G["e1"][:]), func=AF.Ln, bias=1.0), reads=[GB["e1"]], writes=[GB["e1"]])
                    d0 = hrst_s if sample else hrst
                    k.op("dve", lambda e, d0=d0: e.tensor_tensor_scan(out=fl(G["cs"][:]), data0=fl(d0[:]), data1=fl(G["e1"][:]), initial=0.0,
                                                                      op0=ALU.mult, op1=ALU.add), reads=[GB["e1"], Bc], writes=[GB["cs"]])
                    k.op("act", lambda e: e.activation(out=fl(G["einv"][:]), in_=fl(G["cs"][:]), func=AF.Exp, scale=1.0 / 16.0), reads=[GB["cs"]], writes=[GB["einv"]])
                    k.op("act", lambda e: e.activation(out=fl(G["eb"][:]), in_=fl(G["cs"][:]), func=AF.Exp, scale=-1.0 / 16.0), reads=[GB["cs"]], writes=[GB["eb"]])
                    k.op("dve", lambda e: e.tensor_tensor(out=G["kt"][:], in0=kT[:, :, toks], in1=G["einv"][:], op=ALU.mult),
                         reads=[BkT, GB["einv"]], writes=[GB["kt"]])
                    pt2, pb2 = bank()
                    pt2b = pt2[:].bitcast(BF16)
                    for hd in range(NH):
                        k.op("pe", lambda e, hd=hd, pt2b=pt2b: e.transpose(out=pt2b[:, hd * 128:(hd + 1) * 128], in_=G["kt"][:, hd, :], identity=identb[:]),
                             reads=[GB["kt"], Bc], writes=[pb2])
                    k.op("act", lambda e, pt2b=pt2b: e.activation(out=fl(G["ktok"][:]), in_=pt2b[:, 0:512], func=AF.Copy), reads=[pb2], writes=[GB["ktok"]])
                    if full:
                        k.op("dve", lambda e: e.scalar_tensor_tensor(out=G["qt"][:], in0=qT[:, :, toks], scalar=float(DK ** -0.5),
                                                                     in1=G["eb"][:], op0=ALU.mult, op1=ALU.mult),
                             reads=[BqT, GB["eb"]], writes=[GB["qt"]])
                        psc, pscb = bank()
                        for hd in range(NH):
                            k.op("pe", lambda e, hd=hd, psc=psc: e.matmul(psc[:, hd * 128:(hd + 1) * 128], lhsT=G["kt"][:, hd, :], rhs=G["qt"][:, hd, :], start=True, stop=True),
                                 reads=[GB["kt"], GB["qt"]], writes=[pscb])
                        mk = smask if sample else causal
                        k.op("dve", lambda e, psc=psc, mk=mk: e.tensor_tensor(out=G["scm"][:], in0=psc[:, :].rearrange("p (a b) -> p a b", a=NH),
                                                                              in1=mk[:].unsqueeze(1).to_broadcast([128, NH, 128]), op=ALU.mult),
                             reads=[pscb, Bc], writes=[GB["scm"]])
                        if sample:
                            k.op("dve", lambda e: e.scalar_tensor_tensor(out=G["qtf"][:], in0=qT[:, :, toks], scalar=float(DK ** -0.5),
                                                                         in1=G["eb"][:], op0=ALU.mult, op1=ALU.mult),
                                 reads=[BqT, GB["eb"]], writes=[GB["qtf"]])
                            k.op("dve", lambda e: e.tensor_copy(out=G["ebe"][:], in_=G["eb"][:].rearrange("p a (j r) -> p a j r", r=8)[:, :, :, 7]),
                                 reads=[GB["eb"]], writes=[GB["ebe"]])
                        pos = []
                        for half in range(2):
                            po, pob = bank(hold=True)
                            pos.append((po, pob))
                            for hdl in range(2):
                                hd = half * 2 + hdl
                                if sample:
                                    k.op("dve", lambda e, hd=hd: e.tensor_tensor(out=G["KM"][:], in0=G["ktok"][:, hd, :].unsqueeze(1).to_broadcast([128, NSEQ, 128]),
                                                                                 in1=seqmask[:].unsqueeze(2).to_broadcast([128, NSEQ, 128]), op=ALU.mult),
                                         reads=[GB["ktok"], Bc], writes=[GB["KM"]])
                                for qh in (range(2) if sample else [None]):
                                    if sample:
                                        a = rot[0] % 2; rot[0] += 1
                                        k.dma("pool", S0h[a][:], gla_in[qh * 8:(qh + 1) * 8, hd, :, :].rearrange("q k v -> k q v"), writes=[BS0h[a]])
                                    for vh in range(2):
                                        c0 = (hdl * 2 + vh) * 128
                                        vcols = slice(hd * DV + vh * 128, hd * DV + (vh + 1) * 128)
                                        if not sample:
                                            k.op("pe", lambda e, c0=c0, vcols=vcols, po=po, hd=hd: e.matmul(po[:, c0:c0 + 128], lhsT=vtok[:, j, vcols], rhs=G["scm"][:, hd, :],
                                                                                                            start=True, stop=False), reads=[Bvt[j], GB["scm"]], writes=[pob])
                                            k.op("pe", lambda e, c0=c0, vh=vh, hd=hd, po=po: e.matmul(po[:, c0:c0 + 128], lhsT=Sglb[:, hd, vh * 128:(vh + 1) * 128],
                                                                                                      rhs=G["qt"][:, hd, :], start=False, stop=True), reads=[BS, GB["qt"]], writes=[pob])
                                        else:
                                            cc0 = c0 + qh * 64
                                            k.op("pe", lambda e, cc0=cc0, vcols=vcols, po=po, hd=hd, qh=qh: e.matmul(
                                                po[:, cc0:cc0 + 64], lhsT=vtok[:, j, vcols], rhs=G["scm"][:, hd, qh * 64:(qh + 1) * 64],
                                                start=True, stop=False), reads=[Bvt[j], GB["scm"]], writes=[pob])
                                            for ql in range(8):
                                                q = qh * 8 + ql
                                                k.op("pe", lambda e, cc0=cc0, ql=ql, q=q, a=a, vh=vh, hd=hd, po=po: e.matmul(
                                                    po[:, cc0 + ql * 8:cc0 + ql * 8 + 8], lhsT=S0h[a][:, ql, vh * 128:(vh + 1) * 128],
                                                    rhs=G["qtf"][:, hd, q * 8:(q + 1) * 8], start=False, stop=(ql == 7)),
                                                    reads=[BS0h[a], GB["qtf"]], writes=[pob])
                                    if sample:
                                        for qp in range(4):
                                            pu, pub = bank()
                                            for qi in range(2):
                                                q = qh * 8 + qp * 2 + qi
                                                k.op("pe", lambda e, q=q, qi=qi, pu=pu, hd=hd: e.matmul(pu[:, qi * DV:(qi + 1) * DV], lhsT=G["KM"][:, q, :],
                                                                                                         rhs=vtok[:, 8, hd * DV:(hd + 1) * DV], start=True, stop=True),
                                                     reads=[GB["KM"], Bvt[8]], writes=[pub])
                                            b_ = rot2[0] % 2; rot2[0] += 1
                                            q0 = qh * 8 + qp * 2
                                            k.op("dve", lambda e, b_=b_, pu=pu, a=a, qp=qp: e.tensor_tensor(out=Snew[b_][:], in0=pu[:, :].rearrange("p (q v) -> p q v", q=2),
                                                                                                          in1=S0h[a][:, qp * 2:qp * 2 + 2, :], op=ALU.add),
                                                 reads=[pub, BS0h[a]], writes=[BSn2[b_]])
                                            k.op("dve", lambda e, b_=b_, hd=hd, q0=q0: e.tensor_tensor(out=Snew[b_][:], in0=Snew[b_][:],
                                                                                                     in1=G["ebe"][:, hd, q0:q0 + 2].unsqueeze(2).to_broadcast([128, 2, DV]), op=ALU.mult),
                                                 reads=[BSn2[b_], GB["ebe"]], writes=[BSn2[b_]])
                                            k.dma("sp", gla_s[q0:q0 + 2, hd, :, :].rearrange("q k v -> k q v"), Snew[b_][:], reads=[BSn2[b_]], sbuf=BSn2[b_])
                            k.op("act", lambda e, po=po, half=half: e.activation(out=G["sq"][:, half, :, :].rearrange("p a b -> p (a b)"), in_=po[:, :], func=AF.Square),
                                 reads=[pob], writes=[GB["sq"]])
                        pss, pssb = bank()
                        for hd in range(NH):
                            for vh in range(2):
                                k.op("pe", lambda e, hd=hd, vh=vh, pss=pss: e.matmul(pss[:, hd * 128:(hd + 1) * 128], lhsT=onesb[:], rhs=G["sq"][:, hd // 2, (hd % 2) * 2 + vh, :],
                                                                                     start=(vh == 0), stop=(vh == 1)), reads=[GB["sq"], Bc], writes=[pssb])
                        k.op("act", lambda e, pss=pss: e.activation(out=fl(G["rr"][:]), in_=pss[:, :], func=AF.Ln, scale=1.0 / DV, bias=epsb[:, 0:1]),
                             reads=[pssb, Bc], writes=[GB["rr"]])
                        k.op("act", lambda e: e.activation(out=fl(G["rr"][:]), in_=fl(G["rr"][:]), func=AF.Exp, scale=-0.5), reads=[GB["rr"]], writes=[GB["rr"]])
                        for half in range(2):
                            po, pob = pos[half]
                            k.op("dve", lambda e, half=half, po=po: e.tensor_tensor(
                                out=by[:, half * 4:(half + 1) * 4, toks].rearrange("p (h v) t -> p h v t", v=2),
                                in0=po[:, :].rearrange("p (h v t) -> p h v t", h=2, v=2),
                                in1=G["rr"][:, half * 2:half * 2 + 2, :].unsqueeze(2).to_broadcast([128, 2, 2, 128]), op=ALU.mult),
                                reads=[pob, GB["rr"]], writes=[Bby])
                            unhold(pob)
                    if not sample:
                        for half in range(2):
                            pu, pub = bank()
                            for hdl in range(2):
                                hd = half * 2 + hdl
                                k.op("pe", lambda e, hd=hd, hdl=hdl, pu=pu: e.matmul(pu[:, hdl * DV:(hdl + 1) * DV], lhsT=G["ktok"][:, hd, :], rhs=vtok[:, j, hd * DV:(hd + 1) * DV],
                                                                                     start=True, stop=True), reads=[GB["ktok"], Bvt[j]], writes=[pub])
                            k.op("dve", lambda e, half=half, pu=pu: e.tensor_tensor(out=G["stmp"][:], in0=pu[:, :].rearrange("p (h v) -> p h v", h=2),
                                                                                    in1=Sgl[:, half * 2:half * 2 + 2, :], op=ALU.add),
                                 reads=[pub, BS], writes=[GB["stmp"]])
                            k.op("dve", lambda e, half=half: e.tensor_tensor(out=Sgl[:, half * 2:half * 2 + 2, :], in0=G["stmp"][:],
                                                                             in1=G["eb"][:, half * 2:half * 2 + 2, 127:128].to_broadcast([128, 2, DV]), op=ALU.mult),
                                 reads=[GB["stmp"], GB["eb"]], writes=[BS])
                        if full:
                            k.op("act", lambda e: e.activation(out=fl(Sglb[:]), in_=fl(Sgl[:]), func=AF.Copy), reads=[BS], writes=[BS])
                if full:
                    outs_wait.extend(BSn2)
                    barrier(BS0h + BSn2)
                barrier([Bdump] + list(GB.values()) + psb)


        def gla_pass(full, by, Bby):
            with ExitStack() as es4:
                k.es = es4
                gl = gla_alloc(full)
                gla_proj(gl, full)
                gla_tiles(gl, full, by, Bby)
                k.es = es4
                barrier([gl["Bglr"], gl["BkT"], gl["BqT"]] + gl["Bvt"] + psb)

        outs_wait = []
        k.op("dve", lambda e: e.memset(Hrun[:], 0.0), writes=[BH])
        k.op("dve", lambda e: e.memset(Sgl[:], 0.0), writes=[BS])
        k.op("dve", lambda e: e.memset(Sglb[:], 0.0), writes=[BS])
        prep(False)
        dump("hTpre", hT[:, :, 0:TP], [128, KT, TP], hTb, BF16)
        stage_end(3)
        with ExitStack() as esG:
            k.es = esG
            glp = gla_alloc(False)
            with ExitStack() as esP:
                k.es = esP
                U2 = sb("U2p", [128, NG, 128], BF16); BU2 = [Buf("U2_%d" % i) for i in range(8)]
                VH = sb("VHp", [128, 129, 2, 32], BF16); BV = Buf("VH"); BHh = Buf("Hh")
                s5_front(False, U2, BU2, VH, BV)
                k.es = esP
                gla_proj(glp, False)
                s5_recur2_state(VH, BV)
                k.es = esP
                dump("U2p", U2[:], [128, NG, 128], BU2, BF16)
                dump("VHp", VH[:], [128, 129, 2, 32], [BV], BF16)
                dump("Hrun_pre", Hrun[:], [128, 2, 32], [BH])
                barrier([BV, BHh, Bdump] + BU2 + psb)
                stage_end(4)
            k.es = esG
            gla_tiles(glp, False, None, None)
            k.es = esG
            barrier([glp["Bglr"], glp["BkT"], glp["BqT"]] + glp["Bvt"] + psb)
        k.es = esW
        dump("Sgl_pre", Sgl[:], [128, NH, DV], [BS])
        stage_end(5)
        k.op("dve", lambda e: e.tensor_scalar(out=Hrun[:], in0=Hrun[:], scalar1=flag[:, 0:1], scalar2=None, op0=ALU.mult), reads=[BH, Bp], writes=[BH])
        k.op("dve", lambda e: e.tensor_scalar(out=Sgl[:], in0=Sgl[:], scalar1=flag[:, 0:1], scalar2=None, op0=ALU.mult), reads=[BS, Bp], writes=[BS])
        k.op("act", lambda e: e.activation(out=Sglb[:], in_=Sgl[:], func=AF.Copy), reads=[BS], writes=[BS])

        ay = sb("ay", [128, 8, TM], BF16, side="right"); Bay = Buf("ay")
        prep(True)
        dump("hTmain", hT[:], [128, KT, TM], hTb, BF16)
        stage_end(6)
        with ExitStack() as esM:
            k.es = esM
            U2 = sb("U2m", [128, NG, 128 + NSEQ], BF16); BU2 = [Buf("U2m_%d" % i) for i in range(8)]
            VH = sb("VHm", [128, 129 + NSEQ, 2, 32], BF16); BV = Buf("VHm"); BHh = Buf("Hhm")
            Hsin = sb("Hsin", [128, NSEQ, 2, 32], F32); BHs = Buf("Hsin")
            Hsb = sb("Hsb", [128, NSEQ, 2, 32], BF16)
            with ExitStack() as es5:
                k.es = es5
                Sn = sb("Sn", [32, 8, 2, 128], F32); BSn = Buf("Sn")
                for q0 in range(0, NSEQ, 8):
                    for ri, src in enumerate((ssm_re_in, ssm_im_in)):
                        for a_ in range(2):
                            k.dma("sp", Sn[:, :, ri, a_ * 64:(a_ + 1) * 64], src[q0:q0 + 8, a_ * 32:(a_ + 1) * 32, :].rearrange("j g p -> g j p"), writes=[BSn])
                    pt, pb = bank()
                    for qi in range(8):
                        for ri in range(2):
                            col = (qi * 2 + ri) * 32
                            k.op("pe", lambda e, qi=qi, ri=ri, col=col, pt=pt: e.transpose(out=pt[:, col:col + 32], in_=Sn[:, qi, ri, :],
                                                                                           identity=identf[0:32, 0:32]), reads=[BSn, Bc], writes=[pb])
                    k.op("dve", lambda e, q0=q0, pt=pt: e.tensor_copy(out=Hsin[:, q0:q0 + 8, :, :].rearrange("p j r g -> p (j r g)"), in_=pt[:, :]),
                         reads=[pb], writes=[BHs])
                barrier([BSn])
            k.es = esM
            s5_front(True, U2, BU2, VH, BV)
            k.es = esM
            k.op("act", lambda e: e.activation(out=Hsb[:], in_=Hsin[:], func=AF.Copy), reads=[BHs], writes=[BHs])
            k.op("act", lambda e: e.activation(out=VH[:, 0, :, :], in_=Hrun[:], func=AF.Copy), reads=[BH], writes=[BHh])
            s5_recur2_hist(VH, BV, BHh)
            k.es = esM
            with ExitStack() as es5:
                k.es = es5
                Hso = sb("Hso", [128, NSEQ, 2, 32], F32); BHo = Buf("Hso")
                t16b = sb("t16b", [128, NSEQ, 2, 32], F32); Bt16 = Buf("t16")
                arb16 = ARR[:].unsqueeze(1).unsqueeze(1).to_broadcast([128, NSEQ, 2, 32])
                k.op("dve", lambda e: e.tensor_tensor(out=Hso[:], in0=Hsin[:], in1=arb16, op=ALU.mult), reads=[BHs, Bs5], writes=[BHo])
                k.op("dve", lambda e: e.tensor_tensor(out=t16b[:, :, 0, :], in0=Hsin[:, :, 1, :], in1=AIn[:].unsqueeze(1).to_broadcast([128, NSEQ, 32]), op=ALU.mult),
                     reads=[BHs, Bs5], writes=[Bt16])
                k.op("dve", lambda e: e.tensor_tensor(out=t16b[:, :, 1, :], in0=Hsin[:, :, 0, :], in1=AIp[:].unsqueeze(1).to_broadcast([128, NSEQ, 32]), op=ALU.mult),
                     reads=[BHs, Bs5], writes=[Bt16])
                k.op("dve", lambda e: e.tensor_tensor(out=Hso[:], in0=Hso[:], in1=t16b[:], op=ALU.add), reads=[Bt16, BHo], writes=[BHo])
                k.op("dve", lambda e: e.tensor_tensor(out=Hso[:], in0=Hso[:], in1=VH[:, 129:129 + NSEQ, :, :], op=ALU.add), reads=[BHo, BV], writes=[BHo])
                So = [sb("So%d" % i, [32, 2, 2, 128], F32) for i in range(2)]; BSo = [Buf("So%d" % i) for i in range(2)]
                for bi, q0 in enumerate(range(0, NSEQ + 1, 2)):
                    a = bi % 2
                    pt, pb = bank()
                    nq = min(2, NSEQ + 1 - q0)
                    for qi in range(nq):
                        q = q0 + qi
                        for ri in range(2):
                            src = Hso[:, q, ri, :] if q < NSEQ else Hrun[:, ri, :]
                            col = (qi * 2 + ri) * 128
                            k.op("pe", lambda e, src=src, col=col, pt=pt: e.transpose(out=pt[0:32, col:col + 128], in_=src, identity=identf[:]),
                                 reads=[BHo, BH, Bc], writes=[pb])
                    k.op("dve", lambda e, a=a, nq=nq, pt=pt: e.tensor_copy(out=So[a][:, 0:nq, :, :].rearrange("g j r q -> g (j r q)"), in_=pt[0:32, 0:nq * 256]),
                         reads=[pb], writes=[BSo[a]])
                    if q0 < NSEQ:
                        for ri, dst in enumerate((ssm_re_s, ssm_im_s)):
                            for a_ in range(2):
                                k.dma("sp", dst[q0:q0 + 2, a_ * 32:(a_ + 1) * 32, :].rearrange("j g p -> g j p"), So[a][:, :, ri, a_ * 64:(a_ + 1) * 64],
                                      reads=[BSo[a]], sbuf=BSo[a])
                    else:
                        for ri, dst in enumerate((ssm_re_p, ssm_im_p)):
                            k.dma("sp", dst.rearrange("(a g) p -> g a p", a=2), So[a][:, 0, ri, :].rearrange("g (a p) -> g a p", a=2), reads=[BSo[a]], sbuf=BSo[a])
                outs_wait.extend(BSo)
                barrier([BHo, Bt16] + BSo)
            k.es = esM
            with ExitStack() as es5:
                k.es = es5
                Yc = [sb("Yc%d" % i, [128, 8, 128], BF16) for i in range(2)]; BYc = [Buf("Yc%d" % i) for i in range(2)]
                Ycs = [sb("Ycs%d" % i, [NSEQ, 8, 128], BF16) for i in range(2)]; BYcs = [Buf("Ycs%d" % i) for i in range(2)]
                for ct in range(8):
                    for ci, (toff, C, coff) in enumerate(((0, 128, 0), (TP, NSEQ, 128))):
                        yc, byc = (Yc[ct % 2], BYc[ct % 2]) if ci == 0 else (Ycs[ct % 2], BYcs[ct % 2])
                        for gq in range(2):
                            pt, pb = bank()
                            for gi in range(4):
                                g = ct * 8 + gq * 4 + gi
                                gh, gp = g // 32, g % 32
                                rows = slice(gh * 64, gh * 64 + 64)
                                out = pt[0:C, gi * 128:(gi + 1) * 128]
                                k.op("pe", lambda e, out=out, g=g, C=C, coff=coff: e.matmul(out, lhsT=U2[:, g, coff:coff + C], rhs=Toep[:, g, :], start=True, stop=False),
                                     reads=[BU2[g // 8], Bs5], writes=[pb])
                                for ri, YY in enumerate((Ybr, Ybn)):
                                    hp = VH[rows, 0:128, ri, gp] if ci == 0 else Hsb[rows, :, ri, gp]
                                    k.op("pe", lambda e, out=out, hp=hp, YY=YY, rows=rows, gp=gp, ri=ri: e.matmul(out, lhsT=hp, rhs=YY[rows, gp, 1:9, :], start=False, stop=(ri == 1)),
                                         reads=[BHh, BHs, Bs5], writes=[pb])
                            k.op("act", lambda e, pt=pt, C=C, yc=yc, gq=gq: e.activation(
                                out=yc[0:C, :, gq * 64:(gq + 1) * 64].rearrange("c t (g h) -> c g t h", h=16),
                                in_=pt[0:C, :].rearrange("c (g t h) -> c g t h", g=4, h=16), func=AF.Gelu_apprx_tanh), reads=[pb], writes=[byc])
                        pt, pb = bank()
                        ptb = pt[:].bitcast(BF16)
                        for t in range(8):
                            k.op("pe", lambda e, t=t, C=C, yc=yc, ptb=ptb: e.transpose(out=ptb[:, t * 128:t * 128 + C], in_=yc[0:C, t, :], identity=identb[0:C, 0:C]),
                                 reads=[byc, Bc], writes=[pb])
                        k.op("act", lambda e, C=C, toff=toff, ct=ct, ptb=ptb: e.activation(
                            out=ay[:, ct, toff:toff + 8 * C].rearrange("p (c t) -> p t c", t=8),
                            in_=ptb[:, 0:1024].rearrange("p (t c) -> p t c", t=8)[:, :, 0:C], func=AF.Copy), reads=[pb], writes=[Bay])
                barrier(BYc + BYcs)
            k.es = esM
            dump("ay", ay[:], [128, 8, TM], [Bay], BF16)
            dump("Hrun_main", Hrun[:], [128, 2, 32], [BH])
            barrier([BV, BHh, BHs, Bdump] + BU2 + [Bs5] + psb)
            stage_end(7)
        esW.close()
        k.es = es
        by = sb("by", [128, 8, TM], BF16, side="right"); Bby = Buf("by")
        gla_pass(True, by, Bby)
        k.es = es
        dump("by", by[:], [128, 8, TM], [Bby], BF16)
        dump("Sgl_main", Sgl[:], [128, NH, DV], [BS])
        stage_end(8)
        k.dma("sp", gla_p.rearrange("h k v -> k h v"), Sgl[:], reads=[BS], sbuf=BS)
        outs_wait.append(BS)

        T = TM
        nblocks = [(n0, min(512, T - n0)) for n0 in range(0, T, 512)]
        hall = hTb[0:9]
        mg = sb("mg", [128, KT, TM], BF16, side="right"); Bmg = [Buf("mg%d" % j) for j in range(9)]
        By_d = [Buf("ydram%d" % j) for j in range(9)]
        with ExitStack() as esC:
            k.es = esC
            sg = [sb("sg%d" % i, [128, 512], F32) for i in range(2)]; Bsg = [Buf("sg%d" % i) for i in range(2)]
            m1 = [sb("m1_%d" % i, [128, 512], F32) for i in range(2)]; Bm1 = [Buf("m1_%d" % i) for i in range(2)]
            rot = [0]
            ay2 = sb("ay2", [128, 8, TM], BF16); Bay2 = Buf("ay2")
            for i in range(4):
                ws, wb = w_next((w_glu, i * WC))
                for mt in range(2):
                    m = i * 2 + mt
                    for (n0, nn) in nblocks:
                        pt, pb = bank()
                        for kt in range(8):
                            k.op("pe", lambda e, kt=kt, n0=n0, nn=nn, mt=mt, ws=ws, pt=pt: e.matmul(
                                pt[:, 0:nn], lhsT=ws[:, kt, mt * 128:(mt + 1) * 128], rhs=ay[:, kt, n0:n0 + nn], start=(kt == 0), stop=(kt == 7)),
                                reads=[Bay, wb], writes=[pb], inc=(kt == 7))
                        a = rot[0] % 2; rot[0] += 1
                        k.op("act", lambda e, a=a, nn=nn, m=m, pt=pt: e.activation(out=sg[a][:, 0:nn], in_=pt[:, 0:nn], func=AF.Sigmoid, bias=bglu_col[:, m:m + 1]),
                             reads=[pb, Bp], writes=[Bsg[a]])
                        k.op("dve", lambda e, a=a, n0=n0, nn=nn, m=m: e.tensor_tensor(out=ay2[:, m, n0:n0 + nn], in0=ay[:, m, n0:n0 + nn], in1=sg[a][:, 0:nn], op=ALU.mult),
                             reads=[Bay, Bsg[a]], writes=[Bay2])
            for (c00, tgt, Btgt) in ((1024, ay2, Bay2), (4096, by, Bby)):
                for i in range(4):
                    ws, wb = w_next((w_in, c00 + i * WC))
                    for mt in range(2):
                        m = i * 2 + mt
                        for (n0, nn) in nblocks:
                            pt, pb = bank()
                            for kt in range(KT):
                                k.op("pe", lambda e, kt=kt, n0=n0, nn=nn, mt=mt, ws=ws, pt=pt: e.matmul(
                                    pt[:, 0:nn], lhsT=ws[:, kt, mt * 128:(mt + 1) * 128], rhs=hT[:, kt, n0:n0 + nn], start=(kt == 0), stop=(kt == KT - 1)),
                                    reads=hall + [wb], writes=[pb], inc=(kt == KT - 1))
                            a = rot[0] % 2; rot[0] += 1
                            k.op("act", lambda e, a=a, nn=nn, pt=pt: e.activation(out=sg[a][:, 0:nn], in_=pt[:, 0:nn], func=AF.Silu), reads=[pb], writes=[Bsg[a]])
                            if tgt is by:
                                k.op("dve", lambda e, a=a, n0=n0, nn=nn, m=m, tgt=tgt: e.scalar_tensor_tensor(out=tgt[:, m, n0:n0 + nn], in0=tgt[:, m, n0:n0 + nn], scalar=ggain_col[:, m:m + 1],
                                                                                                               in1=sg[a][:, 0:nn], op0=ALU.mult, op1=ALU.mult),
                                     reads=[Btgt, Bsg[a], Bp], writes=[Btgt])
                            else:
                                k.op("dve", lambda e, a=a, n0=n0, nn=nn, m=m, tgt=tgt: e.tensor_tensor(out=tgt[:, m, n0:n0 + nn], in0=tgt[:, m, n0:n0 + nn], in1=sg[a][:, 0:nn], op=ALU.mult),
                                     reads=[Btgt, Bsg[a]], writes=[Btgt])
            m1A = sb("m1A", [128, 2, TM], F32); Bm1A = Buf("m1A")
            for fb in range(8):
                for br, (cg, wo_d, src, Bsrc) in enumerate(((5136, w_a_out, ay2, Bay2), (7184, w_b_out, by, Bby))):
                    wg, wgb = w_next((w_in, cg + fb * WC))
                    wo, wob = w_next((wo_d, fb * WC))
                    for mt in range(2):
                        f = fb * 2 + mt
                        for (n0, nn) in nblocks:
                            pg, pgb = bank()
                            for kt in range(KT):
                                k.op("pe", lambda e, kt=kt, pg=pg, wg=wg, mt=mt, n0=n0, nn=nn: e.matmul(
                                    pg[:, 0:nn], lhsT=wg[:, kt, mt * 128:(mt + 1) * 128], rhs=hT[:, kt, n0:n0 + nn],
                                    start=(kt == 0), stop=(kt == KT - 1)), reads=hall + [wgb], writes=[pgb], inc=(kt == KT - 1))
                            po, pob = bank()
                            for kt in range(8):
                                k.op("pe", lambda e, kt=kt, po=po, wo=wo, src=src, mt=mt, n0=n0, nn=nn: e.matmul(
                                    po[:, 0:nn], lhsT=wo[:, kt, mt * 128:(mt + 1) * 128], rhs=src[:, kt, n0:n0 + nn],
                                    start=(kt == 0), stop=(kt == 7)), reads=[Bsrc, wob], writes=[pob], inc=(kt == 7))
                            a = rot[0] % 2; rot[0] += 1
                            k.op("act", lambda e, a=a, pg=pg, nn=nn: e.activation(out=sg[a][:, 0:nn], in_=pg[:, 0:nn], func=AF.Sigmoid), reads=[pgb], writes=[Bsg[a]])
                            if br == 0:
                                k.op("dve", lambda e, a=a, po=po, mt=mt, n0=n0, nn=nn: e.tensor_tensor(out=m1A[:, mt, n0:n0 + nn], in0=po[:, 0:nn], in1=sg[a][:, 0:nn], op=ALU.mult),
                                     reads=[pob, Bsg[a]], writes=[Bm1A])
                            else:
                                k.op("dve", lambda e, a=a, po=po, nn=nn: e.tensor_tensor(out=m1[a][:, 0:nn], in0=po[:, 0:nn], in1=sg[a][:, 0:nn], op=ALU.mult),
                                     reads=[pob, Bsg[a]], writes=[Bm1[a]])
                                tiles = list(range(n0 // 128, (n0 + nn) // 128))
                                k.op("dve", lambda e, a=a, f=f, mt=mt, n0=n0, nn=nn: e.tensor_tensor(out=mg[:, f, n0:n0 + nn], in0=m1[a][:, 0:nn], in1=m1A[:, mt, n0:n0 + nn], op=ALU.add),
                                     reads=[Bm1[a], Bm1A], writes=[Bmg[t_] for t_ in tiles])
            dump("ay2", ay2[:], [128, 8, TM], [Bay2], BF16)
            dump("by2", by[:], [128, 8, TM], [Bby], BF16)
            dump("mg", mg[:], [128, KT, TM], Bmg, BF16)
            barrier(Bsg + Bm1 + [Bm1A, Bay2, Bay, Bby, Bdump] + hall + psb)
            stage_end(9)
        with ExitStack() as esO:
            k.es = esO
            gts = sb("gts", [17, D], F32); Bgts = Buf("gts")
            k.dma("sp", gts[:], gate_scr[:, :], reads=[Bgs], writes=[Bgts])
            gtk = [sb("gtk%d" % i, [128, WC], F32) for i in range(2)]; Bgtk = [Buf("gtk%d" % i) for i in range(2)]
            fgb = sb("fgb", [128, D], F32); Bfg = Buf("fgb")
            k.dma("sp", fgb[:], fgain.partition_broadcast(128), writes=[Bfg])
            yacc = hT[:].rearrange("p a b -> p (a b)").rearrange("p (j f) -> p j f", j=9)
            Bya = [Buf("yacc%d" % j) for j in range(9)]
            xs = [sb("fxs%d" % i, [128, D], F32) for i in range(2)]; xb_ = [Buf("fxs%d" % i) for i in range(2)]
            jk = sb("fjk", [128, D], BF16); Bjk = Buf("fjk")
            st = sb("fst", [128, 9, 4], F32); Bst = [Buf("fst%d" % j_) for j_ in range(9)]
            for j in range(2):
                k.dma("sp", xs[j][:], xmain[j * 128:(j + 1) * 128, :], writes=[xb_[j]])

            def final_tile(j):
                a = j % 2
                dst = y_main[j * 128:(j + 1) * 128, :] if j < 8 else y_smp[:, :]
                k.op("dve", lambda e: e.tensor_tensor(out=xs[a][:], in0=xs[a][:], in1=yacc[:, j, :], op=ALU.add), reads=[xb_[a], Bya[j]], writes=[xb_[a]])
                k.op("act", lambda e: e.activation(out=jk[:], in_=xs[a][:], func=AF.Square, accum_out=st[:, j, 0:1]), reads=[xb_[a]], writes=[Bjk, Bst[j]])
                k.op("act", lambda e: e.activation(out=st[:, j, 2:3], in_=st[:, j, 0:1], func=AF.Ln, scale=1.0 / D, bias=epsb[:, 0:1]), reads=[Bst[j], Bc], writes=[Bst[j]])
                k.op("act", lambda e: e.activation(out=st[:, j, 3:4], in_=st[:, j, 2:3], func=AF.Exp, scale=-0.5), reads=[Bst[j]], writes=[Bst[j]])
                k.op("dve", lambda e: e.scalar_tensor_tensor(out=xs[a][:], in0=xs[a][:], scalar=st[:, j, 3:4], in1=fgb[:], op0=ALU.mult, op1=ALU.mult),
                     reads=[xb_[a], Bst[j], Bfg], writes=[xb_[a]])
                k.dma("sp", dst, xs[a][:], reads=[xb_[a]], writes=[By_d[j]], sbuf=xb_[a])
                if j + 2 < 9:
                    jn = j + 2
                    srcn = xmain[jn * 128:(jn + 1) * 128, :] if jn < 8 else xsmp[:, :]
                    k.dma("sp", xs[a][:], srcn, writes=[xb_[a]])

            for i in range(8):
                ws, wb = w_next((w_out, i * WC))
                for wi, oh in enumerate((ohP, ohS)):
                    pt, pb = bank()
                    k.op("pe", lambda e, oh=oh, i=i, pt=pt: e.matmul(pt[:, 0:WC], lhsT=oh[:, :], rhs=gts[:, i * WC:(i + 1) * WC], start=True, stop=True),
                         reads=[Bgts, Bc], writes=[pb])
                    k.op("act", lambda e, wi=wi, pt=pt: e.activation(out=gtk[wi][:], in_=pt[:, 0:WC], func=AF.Copy), reads=[pb], writes=[Bgtk[wi]])
                for j in range(9):
                    pt, pb = bank()
                    for kt in range(KT):
                        k.op("pe", lambda e, kt=kt, j=j, ws=ws, pt=pt: e.matmul(pt[:, 0:WC], lhsT=mg[:, kt, j * 128:(j + 1) * 128], rhs=ws[:, kt, :],
                                                                               start=(kt == 0), stop=(kt == KT - 1)), reads=[Bmg[j], wb], writes=[pb], inc=(kt == KT - 1))
                    wi = 0 if j < 8 else 1
                    k.op("dve", lambda e, j=j, i=i, wi=wi, pt=pt: e.tensor_tensor(out=yacc[:, j, i * WC:(i + 1) * WC], in0=pt[:, 0:WC], in1=gtk[wi][:], op=ALU.mult),
                         reads=[pb, Bgtk[wi]], writes=[Bya[j]])
                    if i == 7:
                        final_tile(j)
            barrier(outs_wait + By_d + xb_ + Bya + [Bfg, Bgts, Bjk] + Bst + Bgtk + Bmg + psb)
            stage_end(10)
        esR.close()
        k.es = es
    except _Stop:
        pass
    return nc


_NC_CACHE = {}


def _shard_inputs(inp):
    f = lambda a: np.ascontiguousarray(np.asarray(a, dtype=np.float32))
    xp = f(inp["x_prompt"]); xs = f(inp["x_sample"]); cp = f(inp["c_prompt"]); cs = f(inp["c_sample"])
    sre = f(inp["state_ssm_re"])[0]; sim = f(inp["state_ssm_im"])[0]; sgl = f(inp["state_gla"])[0]
    shared = {
        "w_ada": f(inp["w_ada"])[0], "b_ada": f(inp["b_ada"])[0], "norm_gain": f(inp["norm_gain"])[0], "w_in": f(inp["w_in"])[0],
        "lambda_re": f(inp["lambda_re"])[0], "lambda_im": f(inp["lambda_im"])[0], "log_dt": f(inp["log_dt"])[0],
        "ssm_b_re": f(inp["ssm_b_re"])[0], "ssm_b_im": f(inp["ssm_b_im"])[0], "ssm_c_re": f(inp["ssm_c_re"])[0], "ssm_c_im": f(inp["ssm_c_im"])[0],
        "d_skip": f(inp["d_skip"])[0], "w_glu": f(inp["w_glu"])[0], "b_glu": f(inp["b_glu"])[0], "w_gate_up": f(inp["w_gate_up"])[0],
        "b_gate": f(inp["b_gate"])[0], "gla_norm_gain": f(inp["gla_norm_gain"])[0], "w_a_out": f(inp["w_a_out"])[0],
        "w_b_out": f(inp["w_b_out"])[0], "w_out": f(inp["w_out"])[0], "final_norm_gain": f(inp["final_norm_gain"]),
    }
    maps = []
    for core in range(8):
        b, half = core // 2, core % 2
        m = dict(shared)
        m["xpre"] = np.ascontiguousarray(xp[b, 0:TP])
        m["xmain"] = np.ascontiguousarray(xp[b, half * TP:(half + 1) * TP])
        sl = slice(core * NSEQ, (core + 1) * NSEQ)
        m["xsmp"] = np.ascontiguousarray(xs[sl].reshape(NSEQ * 8, D))
        m["c17"] = np.ascontiguousarray(np.concatenate([cp[b:b + 1], cs[sl]], axis=0))
        m["flag"] = np.full((128, 1), float(half), np.float32)
        m["ssm_re_in"] = np.ascontiguousarray(sre[sl]); m["ssm_im_in"] = np.ascontiguousarray(sim[sl])
        m["gla_in"] = np.ascontiguousarray(sgl[sl])
        maps.append(m)
    return maps


def run_debug(inputs, stop, core=1):
    nc = build_nc(dbg=True, stop=stop)
    maps = _shard_inputs(inputs)
    res = run_bass_kernel_spmd(nc, [maps[core]], core_ids=[0])
    return res.results[0], maps[core]


def kernel(**inputs):
    if "nc" not in _NC_CACHE:
        _NC_CACHE["nc"] = build_nc()
    nc = _NC_CACHE["nc"]
    maps = _shard_inputs(inputs)
    res = run_bass_kernel_spmd(nc, maps, core_ids=list(range(8)))
    R = res.results
    y_prompt = np.zeros((4, 2048, D), np.float32); y_sample = np.zeros((128, 8, D), np.float32)
    re_p = np.zeros((1, 4, NG, 64), np.float32); im_p = np.zeros((1, 4, NG, 64), np.float32)
    gl_p = np.zeros((1, 4, NH, DK, DV), np.float32)
    re_s = np.zeros((1, 128, NG, 64), np.float32); im_s = np.zeros((1, 128, NG, 64), np.float32)
    gl_s = np.zeros((1, 128, NH, DK, DV), np.float32)
    for core in range(8):
        b, half = core // 2, core % 2
        r = R[core]
        y_prompt[b, half * TP:(half + 1) * TP] = r["y_main"]
        sl = slice(core * NSEQ, (core + 1) * NSEQ)
        y_sample[sl] = r["y_smp"].reshape(NSEQ, 8, D)
        re_s[0, sl] = r["ssm_re_s"]; im_s[0, sl] = r["ssm_im_s"]; gl_s[0, sl] = r["gla_s"]
        if half == 1:
            re_p[0, b] = r["ssm_re_p"]; im_p[0, b] = r["ssm_im_p"]; gl_p[0, b] = r["gla_p"]
    return (y_prompt, y_sample, re_p, im_p, gl_p, re_s, im_s, gl_s)
```
